# Optimizing a Trainium2 kernel written in Bass

```python
import math
import jax, jax.numpy as jnp
from jax import lax
import numpy as np

D_MODEL = 2048
BATCH = 4
SEQ = 2048
DEPTH = 2

CHUNK = 64
N_META = 16
Q_BLOCK = 128
HEAD_DIM = 64
EPS = 1e-6
NEG = -1e30

A_HEADS = 16
A_KV_HEADS = 4
A_GROUP = A_HEADS // A_KV_HEADS
WINDOW = 128
WINDOW_CHUNKS = WINDOW // CHUNK
BAND_BACK = WINDOW + CHUNK
BAND_LEN = BAND_BACK + Q_BLOCK + CHUNK

B_HEADS = 16

C_HEADS = D_MODEL // (2 * HEAD_DIM)

D_FF = 4 * D_MODEL

N_EVEN = (DEPTH + 1) // 2
N_ODD = DEPTH // 2

A_Q = A_HEADS * HEAD_DIM
A_KV = A_KV_HEADS * HEAD_DIM
B_W = B_HEADS * HEAD_DIM
AB_IN = A_Q + 2 * A_KV + 3 * B_W
AB_OUT = A_Q + B_W
AB_SPLITS = [A_Q, A_Q + A_KV, A_Q + 2 * A_KV, A_Q + 2 * A_KV + B_W, A_Q + 2 * A_KV + 2 * B_W]
C_QK = C_HEADS * 2 * HEAD_DIM
C_IN = 3 * C_QK
C_OUT = C_HEADS * 2 * HEAD_DIM

kernel_name = "chunk_causal_hybrid_swa_sink_stickbreak_diffattn"


def rmsnorm(x, g):
    x32 = x.astype(jnp.float32)
    y = x32 * lax.rsqrt(jnp.mean(x32 * x32, axis=-1, keepdims=True) + EPS)
    return (y * g.astype(jnp.float32)).astype(x.dtype)


def chunk_id(pos):
    return jnp.where(pos < N_META, 0, 1 + (pos - N_META) // CHUNK)


def chunk_end(p):
    if p < N_META:
        return N_META
    return N_META + ((p - N_META) // CHUNK + 1) * CHUNK


def alibi_slopes(n):
    return jnp.exp2(-8.0 * (jnp.arange(n, dtype=jnp.float32) + 1.0) / n)


def sliding_sink_attention(q, k, v, sinks):
    bsz, lp = q.shape[0], q.shape[1]
    nb = lp // Q_BLOCK
    scale = HEAD_DIM ** -0.5
    qb = q.reshape(bsz, nb, Q_BLOCK, A_KV_HEADS, A_GROUP, HEAD_DIM)
    q_pos = jnp.arange(lp).reshape(nb, Q_BLOCK)
    k_pos = q_pos[:, :1] - BAND_BACK + jnp.arange(BAND_LEN)[None, :]
    k_idx = jnp.clip(k_pos, 0, lp - 1)
    kb = k[:, k_idx]
    vb = v[:, k_idx]
    km, vm = k[:, :N_META], v[:, :N_META]

    s_meta = jnp.einsum('bnqhgd,bmhd->bnhgqm', qb, km).astype(jnp.float32)
    s_band = jnp.einsum('bnqhgd,bnkhd->bnhgqk', qb, kb).astype(jnp.float32)
    s = jnp.concatenate([s_meta, s_band], axis=-1) * scale

    meta_pos = jnp.broadcast_to(jnp.arange(N_META)[None, :], (nb, N_META))
    kpos_all = jnp.concatenate([meta_pos, k_pos], axis=-1)
    qc = chunk_id(q_pos)[:, :, None]
    kc = chunk_id(k_pos)[:, None, :]
    band_ok = (k_pos[:, None, :] >= N_META) & (k_pos[:, None, :] < lp) & (kc <= qc) & (kc >= qc - WINDOW_CHUNKS)
    mask = jnp.concatenate([jnp.ones((nb, Q_BLOCK, N_META), bool), band_ok], axis=-1)
    dist = jnp.abs(q_pos[:, :, None] - kpos_all[:, None, :]).astype(jnp.float32)
    slopes = alibi_slopes(A_HEADS).reshape(A_KV_HEADS, A_GROUP)
    bias = -slopes[None, :, :, None, None] * dist[:, None, None]
    s = jnp.where(mask[None, :, None, None], s + bias[None], NEG)

    sink = sinks.astype(jnp.float32).reshape(A_KV_HEADS, A_GROUP)[None, None, :, :, None, None]
    sink = jnp.broadcast_to(sink, s.shape[:-1] + (1,))
    p = jax.nn.softmax(jnp.concatenate([s, sink], axis=-1), axis=-1)[..., :-1].astype(v.dtype)
    o = (jnp.einsum('bnhgqm,bmhd->bnqhgd', p[..., :N_META], vm)
         + jnp.einsum('bnhgqk,bnkhd->bnqhgd', p[..., N_META:], vb))
    return o.reshape(bsz, lp, A_HEADS * HEAD_DIM)


def stick_breaking_attention(q, k, v):
    lp = q.shape[1]
    scale = HEAD_DIM ** -0.5
    outs = []
    for q0 in range(0, lp, Q_BLOCK):
        q1 = q0 + Q_BLOCK
        z = jnp.einsum('bqhd,bkhd->bhqk', q[:, q0:q1], k[:, :q1]).astype(jnp.float32) * scale
        t_pos = jnp.arange(q0, q1)[:, None]
        s_pos = jnp.arange(q1)[None, :]
        strict = s_pos < t_pos
        log_keep = jnp.where(strict, jax.nn.log_sigmoid(-z), 0.0)
        between = lax.cumsum(log_keep, axis=3, reverse=True) - log_keep
        w = jnp.where(strict, jnp.exp(jax.nn.log_sigmoid(z) + between), 0.0)
        outs.append(jnp.einsum('bhqk,bkhd->bqhd', w.astype(v.dtype), v[:, :q1]))
    o = jnp.concatenate(outs, axis=1)
    return o.reshape(o.shape[0], lp, B_HEADS * HEAD_DIM)


def differential_attention(q, k, v, lam_vecs, subln_g, lambda_init):
    bsz, lp = q.shape[0], q.shape[1]
    scale = HEAD_DIM ** -0.5
    lv = lam_vecs.astype(jnp.float32)
    lam = jnp.exp(jnp.sum(lv[0] * lv[1])) - jnp.exp(jnp.sum(lv[2] * lv[3])) + lambda_init
    slopes = alibi_slopes(C_HEADS)
    outs = []
    for q0 in range(0, lp, Q_BLOCK):
        q1 = q0 + Q_BLOCK
        kend = min(lp, chunk_end(q1 - 1))
        s = jnp.einsum('bqhcd,bkhcd->bhcqk', q[:, q0:q1], k[:, :kend]).astype(jnp.float32) * scale
        qp = jnp.arange(q0, q1)
        kp = jnp.arange(kend)
        mask = chunk_id(kp)[None, :] <= chunk_id(qp)[:, None]
        dist = jnp.abs(qp[:, None] - kp[None, :]).astype(jnp.float32)
        bias = -slopes[:, None, None, None] * dist[None, None]
        s = jnp.where(mask, s + bias, NEG)
        p = jax.nn.softmax(s, axis=-1)
        w = (p[:, :, 0] - lam * p[:, :, 1]).astype(v.dtype)
        outs.append(jnp.einsum('bhqk,bkhe->bqhe', w, v[:, :kend]))
    o = jnp.concatenate(outs, axis=1)
    o = rmsnorm(o, subln_g) * (1.0 - lambda_init)
    return o.reshape(bsz, lp, C_HEADS * 2 * HEAD_DIM)


def setup_inputs(seed: int = 0) -> dict:
    key = jax.random.key(seed)
    ks = jax.random.split(key, 16)
    f32 = jnp.float32

    def dense(k, shape, fan_in):
        return jax.random.normal(k, shape, f32) * fan_in ** -0.5

    def gain(k, shape):
        return 1.0 + 0.02 * jax.random.normal(k, shape, f32)

    return {
        "x": jax.random.normal(ks[0], (BATCH, SEQ, D_MODEL), f32),
        "meta_tokens": jax.random.normal(ks[1], (N_META, D_MODEL), f32),
        "ab_norm": gain(ks[2], (N_EVEN, D_MODEL)),
        "w_in_ab": dense(ks[3], (N_EVEN, D_MODEL, AB_IN), D_MODEL),
        "attn_sinks": jax.random.normal(ks[4], (N_EVEN, A_HEADS), f32),
        "w_out_ab": dense(ks[5], (N_EVEN, AB_OUT, D_MODEL), AB_OUT),
        "c_norm": gain(ks[6], (N_ODD, D_MODEL)),
        "w_in_c": dense(ks[7], (N_ODD, D_MODEL, C_IN), D_MODEL),
        "diff_lambda": 0.1 * jax.random.normal(ks[8], (N_ODD, 4, HEAD_DIM), f32),
        "diff_subln": gain(ks[9], (N_ODD, 2 * HEAD_DIM)),
        "w_out_c": dense(ks[10], (N_ODD, C_OUT, D_MODEL), C_OUT),
        "mlp_norm": gain(ks[11], (DEPTH, D_MODEL)),
        "w_mlp_in": dense(ks[12], (DEPTH, D_MODEL, D_FF), D_MODEL),
        "w_mlp_out": dense(ks[13], (DEPTH, D_FF, D_MODEL), D_FF),
        "final_norm": gain(ks[14], (D_MODEL,)),
    }


def reference(x, meta_tokens, ab_norm, w_in_ab, attn_sinks, w_out_ab, c_norm, w_in_c,
              diff_lambda, diff_subln, w_out_c, mlp_norm, w_mlp_in, w_mlp_out, final_norm):
    bsz, seq = x.shape[0], x.shape[1]
    total = N_META + seq
    lp = ((total + Q_BLOCK - 1) // Q_BLOCK) * Q_BLOCK
    meta = jnp.broadcast_to(meta_tokens.astype(x.dtype)[None], (bsz, N_META, D_MODEL))
    h = jnp.concatenate([meta, x], axis=1)
    h = jnp.pad(h, ((0, 0), (0, lp - total), (0, 0)))

    for layer in range(DEPTH):
        if layer % 2 == 0:
            i = layer // 2
            hn = rmsnorm(h, ab_norm[i])
            proj = hn @ w_in_ab[i]
            qa, ka, va, qb, kb, vb = jnp.split(proj, AB_SPLITS, axis=-1)
            out_a = sliding_sink_attention(
                qa.reshape(bsz, lp, A_HEADS, HEAD_DIM),
                ka.reshape(bsz, lp, A_KV_HEADS, HEAD_DIM),
                va.reshape(bsz, lp, A_KV_HEADS, HEAD_DIM),
                attn_sinks[i])
            out_b = stick_breaking_attention(
                qb.reshape(bsz, lp, B_HEADS, HEAD_DIM),
                kb.reshape(bsz, lp, B_HEADS, HEAD_DIM),
                vb.reshape(bsz, lp, B_HEADS, HEAD_DIM))
            h = h + jnp.concatenate([out_a, out_b], axis=-1) @ w_out_ab[i]
        else:
            i = layer // 2
            lambda_init = 0.8 - 0.6 * math.exp(-0.3 * layer)
            hn = rmsnorm(h, c_norm[i])
            proj = hn @ w_in_c[i]
            qc, kc, vc = jnp.split(proj, 3, axis=-1)
            out_c = differential_attention(
                qc.reshape(bsz, lp, C_HEADS, 2, HEAD_DIM),
                kc.reshape(bsz, lp, C_HEADS, 2, HEAD_DIM),
                vc.reshape(bsz, lp, C_HEADS, 2 * HEAD_DIM),
                diff_lambda[i], diff_subln[i], lambda_init)
            h = h + out_c @ w_out_c[i]
        hn = rmsnorm(h, mlp_norm[layer])
        h = h + jnp.square(jax.nn.relu(hn @ w_mlp_in[layer])) @ w_mlp_out[layer]

    h = rmsnorm(h, final_norm)
    return h[:, N_META:N_META + seq]
```

```python
import math
import types
from contextlib import ExitStack

import numpy as np
import concourse.bass as bass
import concourse.mybir as mybir
from concourse.bass_utils import run_bass_kernel_spmd

F32 = mybir.dt.float32
BF16 = mybir.dt.bfloat16
AF = mybir.ActivationFunctionType
ALU = mybir.AluOpType
AX = mybir.AxisListType

P = 128
D = 2048
KC = 16
T = 1088
LP = 2176
NTG = 4
TG = 272
NQB = 9
NGB = 17
DFF = 8192
EPS = 1e-6
NEGBIG = -30000.0
REPLICA_GROUPS = [[0, 1], [2, 3], [4, 5], [6, 7]]
NREL = 18
A_SLOPES = [2.0 ** (-8.0 * (i + 1) / 16) for i in range(16)]
C_SLOPES = A_SLOPES
LAMBDA_INIT = 0.8 - 0.6 * math.exp(-0.3 * 1)

A_NEAR = [6, 7, 8, 9, 15, 16, 17]
C_NEAR = [8, 9, 16, 17]
B_NEAR = [8, 16, 17]
B0IDX = [0, 1, 2, 3, 4, 5, 10, 11, 12]


def qw_of(j):
    return 128 if j < 8 else 64


def _freeze(fn):
    if fn is None or fn.__closure__ is None:
        return fn
    cells = []
    for c in fn.__closure__:
        try:
            cells.append(types.CellType(c.cell_contents))
        except ValueError:
            cells.append(c)
    return types.FunctionType(fn.__code__, fn.__globals__, fn.__name__, fn.__defaults__, tuple(cells))


class Op:
    __slots__ = ("eng", "fn", "dma", "waits", "sig", "sem", "val", "idx", "cc")

    def __init__(self, eng, fn, dma, cc=False):
        self.eng = eng
        self.fn = fn
        self.dma = dma
        self.cc = cc
        self.waits = {}
        self.sig = False
        self.sem = None
        self.val = 0


class Sched:
    ENGS = ("pe", "act", "dve", "pool", "sp")
    NDMASEM = 8

    def __init__(self):
        self.ops = {e: [] for e in self.ENGS}
        self.allops = []
        self.last_w = {}
        self.readers = {}
        self.dma_rr = {"pool": 0, "sp": 0}
        self.dma_last = {}

    def op(self, eng, fn, reads=(), writes=(), dma=False, cc=False):
        o = Op(eng, _freeze(fn), dma, cc)
        o.idx = len(self.allops)
        deps = set()
        for k in reads:
            w = self.last_w.get(k)
            if w is not None:
                deps.add(w)
        for k in writes:
            w = self.last_w.get(k)
            if w is not None:
                deps.add(w)
            for r in self.readers.get(k, ()):
                deps.add(r)
        if dma:
            slot = (eng, self.dma_rr[eng] % self.NDMASEM)
            self.dma_rr[eng] += 1
            o.sem = slot
            prev = self.dma_last.get(slot)
            if prev is not None:
                deps.add(prev)
            self.dma_last[slot] = o
        for d in deps:
            if d is o:
                continue
            if d.eng == "pe" and eng == "pe" and not d.dma:
                continue
            o.waits[d.idx] = d
            d.sig = True
        for k in reads:
            self.readers.setdefault(k, []).append(o)
        for k in writes:
            self.last_w[k] = o
            self.readers[k] = []
        self.ops[eng].append(o)
        self.allops.append(o)
        return o

    def barrier(self):
        last = []
        for e in self.ENGS:
            for o_ in reversed(self.ops[e]):
                if o_.fn is not None:
                    last.append(o_)
                    break
        outstanding = [o for o in self.dma_last.values()]
        key = ("__barrier__", len(self.allops))
        for e in self.ENGS:
            o = Op(e, None, False)
            o.idx = len(self.allops)
            for d in last + outstanding:
                if d.eng == e and not d.dma:
                    continue
                o.waits[d.idx] = d
                d.sig = True
            self.ops[e].append(o)
            self.allops.append(o)
        self.readers = {}

    def finalize(self, nc, stack):
        cnt = {e: 0 for e in self.ENGS}
        self.esem = {e: stack.enter_context(nc.semaphore("es_" + e)) for e in ("pe", "act", "dve", "pool", "sp")}
        self.dsem = {}
        for e in ("pool", "sp"):
            for i in range(self.NDMASEM):
                self.dsem[(e, i)] = stack.enter_context(nc.semaphore("ds_%s%d" % (e, i)))
        self.ccsem = stack.enter_context(nc.semaphore("ccsem"))
        dcnt = {}
        cccnt = 0
        for o in self.allops:
            if o.dma:
                dcnt[o.sem] = dcnt.get(o.sem, 0) + 16
                o.val = dcnt[o.sem]
                o.sem = self.dsem[o.sem]
                o.sig = True
            elif o.cc:
                cccnt += 1
                o.val = cccnt
                o.sem = self.ccsem
                o.sig = True
            elif o.sig:
                cnt[o.eng] += 1
                o.val = cnt[o.eng]
                o.sem = self.esem[o.eng]

    def emit(self, eng, e):
        waited = {}
        for o in self.ops[eng]:
            need = {}
            for d in o.waits.values():
                k = id(d.sem)
                if need.get(k, (None, 0))[1] < d.val:
                    need[k] = (d.sem, d.val)
            for k, (sem, val) in need.items():
                if waited.get(k, 0) >= val:
                    continue
                waited[k] = val
                e.wait_ge(sem, val)
            if o.fn is None:
                continue
            ins = o.fn(e)
            if o.sig:
                if o.dma:
                    ins.then_inc(o.sem, 16)
                elif o.cc:
                    ins.then_inc(o.sem)
                else:
                    ins.then_inc(o.sem, 1)


class Builder:
    def __init__(self, debug=None):
        self.debug = debug
        self.nc = nc = bass.Bass("TRN2", target_bir_lowering=False)
        self.S = Sched()
        self.sb_off = 16512
        self.psrr = 0
        self.uid = 0
        self.evrr = 0
        self.declare_io()
        self.alloc()

    def declare_io(self):
        nc = self.nc

        def inp(name, shape, dt=F32):
            return nc.dram_tensor(name, list(shape), dt, kind="ExternalInput").ap()

        self.h0T = inp("h0T", [D, T])
        self.w_in_ab = inp("w_in_ab", [D, 4608])
        self.w_out_ab = inp("w_out_ab", [D, D])
        self.w_in_c = inp("w_in_c", [D, 6144])
        self.w_out_c = inp("w_out_c", [D, D])
        self.w_mlp_in = [inp("w_mlp_in%d" % l, [D, DFF]) for l in range(2)]
        self.w_mlp_out = [inp("w_mlp_out%d" % l, [DFF, D]) for l in range(2)]
        self.gains_d = inp("gains", [P, 5 * KC])
        self.sinks_d = inp("sinks", [P, 16])
        self.lam_d = inp("lamv", [P, 256])
        self.subg_d = inp("subg", [P, 128])
        self.kqneg_d = inp("kqneg", [P, 128])
        self.negd0_d = inp("negd0", [P, NREL])
        self.negdist_a_d = inp("negdist_a", [P, len(A_NEAR), 128])
        self.mask_a_d = inp("mask_a", [P, len(A_NEAR), 128])
        self.mask_a0_d = inp("mask_a0", [P, NQB, 128])
        self.negdist_c_d = inp("negdist_c", [P, len(C_NEAR), 128])
        self.mask_c_d = inp("mask_c", [P, len(C_NEAR), 128])
        self.mask_b_d = inp("mask_b", [P, len(B_NEAR), 128])
        self.flagb_d = inp("flagb", [P, 1])
        self.consts_d = inp("consts", [P, 3, 128])
        self.outT = nc.dram_tensor("outT", [D, T], F32, kind="ExternalOutput").ap()
        self.QT0 = nc.dram_tensor("QT0", [2048, T], BF16)
        self.KT0 = self.chunked("KT0", [0, 640, 1280], True)
        self.V0 = self.chunked("V0", [0, 768, 1280], False)
        self.QT1 = nc.dram_tensor("QT1", [2048, T], BF16)
        self.KT1 = self.chunked("KT1", [0, 512, 1024, 1536, 2048], True)
        self.V1 = self.chunked("V1", [0, 512, 1024, 1536, 2048], False)
        if self.debug:
            self.dbg = nc.dram_tensor("dbg", list(self.debug["shape"]), F32, kind="ExternalOutput").ap()

    def chunked(self, name, bounds, is_k):
        nc = self.nc
        ch = {"name": name, "bounds": bounds, "is_k": is_k, "loc": [], "gat": [], "keys": [[] for _ in bounds[1:]], "done": [False] * (len(bounds) - 1)}
        for i in range(len(bounds) - 1):
            w = bounds[i + 1] - bounds[i]
            if is_k:
                ch["loc"].append(nc.dram_tensor("%s_l%d" % (name, i), [w, T], BF16))
                ch["gat"].append(nc.dram_tensor("%s_g%d" % (name, i), [2 * w, T], BF16))
            else:
                ch["loc"].append(nc.dram_tensor("%s_l%d" % (name, i), [T, w], BF16))
                ch["gat"].append(nc.dram_tensor("%s_g%d" % (name, i), [2 * T, w], BF16))
        return ch

    @staticmethod
    def ch_find(ch, f):
        b = ch["bounds"]
        for i in range(len(b) - 1):
            if b[i] <= f < b[i + 1]:
                return i, f - b[i], b[i + 1] - b[i]
        raise ValueError(f)

    def sb(self, name, shape, dt, off=None):
        n = 1
        for s in shape[1:]:
            n *= s
        nbytes = n * (4 if dt == F32 else 2)
        nbytes = (nbytes + 31) // 32 * 32
        if off is None:
            off = self.sb_off
            self.sb_off += nbytes
        t = self.nc.alloc_sbuf_tensor_at(name, list(shape), dt, offset=off)
        return t

    def alloc(self):
        nc = self.nc
        self.hT = self.sb("hT", [P, KC, T], F32)
        self.xT = self.sb("xT", [P, KC, T], BF16)
        self.gains = self.sb("gains", [P, 5 * KC], F32)
        self.consts_f = self.sb("consts_f", [P, 3, 128], F32)
        self.consts = self.sb("consts_b", [P, 3, 128], BF16)
        self.epsc = self.sb("epsc", [P, 1], F32)
        self.onec = self.sb("onec", [P, 1], F32)
        self.flagb = self.sb("flagb", [P, 1], F32)
        self.nflagb = self.sb("nflagb", [P, 1], F32)
        self.sinks = self.sb("sinks", [P, 16], F32)
        self.esink = self.sb("esink", [P, 16], F32)
        self.lamv = self.sb("lamv", [P, 256], F32)
        self.lamt = self.sb("lamt", [P, 8], F32)
        self.subg = self.sb("subg", [P, 128], F32)
        self.kqneg = self.sb("kqneg", [P, 128], F32)
        self.negd0 = self.sb("negd0", [P, NREL], F32)
        base = self.sb_off
        self.rstd = [self.sb("rstd%d" % i, [P, TG], F32) for i in range(2)]
        self.lnv = [self.sb("lnv%d" % i, [P, TG], F32) for i in range(2)]
        self.sqb = [self.sb("sqb%d" % i, [P, TG], BF16) for i in range(3)]
        self.stage = [self.sb("stage%d" % i, [P, T], BF16) for i in range(2)]
        self.vstage = [self.sb("vstage%d" % i, [P, 256], BF16) for i in range(3)]
        self.relu_t = [self.sb("relu%d" % i, [P, TG], F32) for i in range(3)]
        self.wbuf = [self.sb("wbuf%d" % i, [P, 4096], BF16) for i in range(2)]
        self.hidT = self.sb("hidT", [P, 8, T], BF16)
        lin_end = self.sb_off
        self.sb_off = base
        self.kT = [[self.sb("kT%d_%d" % (i, c), [64, LP], BF16) for c in range(2)] for i in range(2)]
        self.qT = [[self.sb("qT%d_%d" % (i, c), [64, T], BF16) for c in range(2)] for i in range(2)]
        self.Vs = [self.sb("Vs%d" % i, [P, NGB, 130], BF16) for i in range(2)]
        bH_ = self.sb("biasH0", [P, NREL, 128], F32)
        self.biasH = [bH_, bH_]
        self.expH = [self.sb("expH%d" % i, [P, NREL, 128], BF16) for i in range(2)]
        self.tabA = self.sb("tabA", [P, 7, 128], F32)
        self.tabB = self.sb("tabB", [P, 7, 128], F32)
        self.tabA0 = self.sb("tabA0", [P, NQB, 128], F32)
        self.maskb_f = self.sb("maskb_f", [P, 3, 128], F32)
        self.maskb = self.sb("maskb", [P, 3, 128], BF16)
        NS = 3
        self.tmpS2 = [[self.sb("tmpS%d_0" % t, [P, 4, 128], F32)] * 2 for t in range(NS)]
        self.pT2 = [[self.sb("pT%d_%d" % (t, i), [P, 4, 128], BF16) for i in range(2)] for t in range(NS)]
        self.spT2 = [[self.sb("spT%d_%d" % (t, i), [P, 4, 128], BF16) for i in range(2)] for t in range(NS)]
        self.R32s = [self.sb("R32_%d" % t, [P, 128], F32) for t in range(NS)]
        self.Rtmps = [self.sb("Rtmp_%d" % t, [P, 128], F32) for t in range(NS)]
        self.Rbfs = [[self.sb("Rbf%d_%d" % (t, i), [P, 128], BF16) for i in range(2)] for t in range(NS)]
        self.ofin3 = [self.sb("ofin%d" % t, [P, 128], F32) for t in range(NS)]
        self.ob3 = [self.sb("ob%d" % t, [P, 128], BF16) for t in range(NS)]
        self.small3 = [self.sb("small%d" % t, [P, 8], F32) for t in range(NS)]
        self.junk3 = [self.sb("junk%d" % t, [P, 128], F32) for t in range(NS)]
        self.junk = self.junk3[0]
        self.o0buf = [self.sb("o0buf%d" % t, [P, 132], F32) for t in range(NS)]
        self.tmprr = [0] * NS
        self.sprr = [0] * NS
        self.rbrr = [0] * NS
        att_end = self.sb_off
        self.sb_off = max(lin_end, att_end)
        assert self.sb_off <= 229344, self.sb_off
        self.ps = [nc.alloc_psum_tensor("ps%d" % i, [P, 512], F32) for i in range(6)]
        self.psTs = [nc.alloc_psum_tensor("psT%d" % i, [P, 1024], BF16) for i in range(2)]

    def u(self, name):
        self.uid += 1
        return (name, self.uid)

    def next_ps(self, n=6):
        i = self.psrr % n
        self.psrr += 1
        return i

    def load_consts(self):
        S = self.S
        S.op("sp", lambda e: e.dma_start(out=self.gains[:], in_=self.gains_d), writes=["gains"], dma=True)
        S.op("sp", lambda e: e.dma_start(out=self.consts_f[:], in_=self.consts_d), writes=["consts_f"], dma=True)
        S.op("sp", lambda e: e.dma_start(out=self.flagb[:], in_=self.flagb_d), writes=["flagb"], dma=True)
        S.op("sp", lambda e: e.dma_start(out=self.sinks[:], in_=self.sinks_d), writes=["sinks"], dma=True)
        S.op("sp", lambda e: e.dma_start(out=self.lamv[:], in_=self.lam_d), writes=["lamv"], dma=True)
        S.op("sp", lambda e: e.dma_start(out=self.subg[:], in_=self.subg_d), writes=["subg"], dma=True)
        S.op("sp", lambda e: e.dma_start(out=self.kqneg[:], in_=self.kqneg_d), writes=["kqneg"], dma=True)
        S.op("sp", lambda e: e.dma_start(out=self.negd0[:], in_=self.negd0_d), writes=["negd0"], dma=True)
        for tg in range(NTG):
            sl = slice(tg * TG, (tg + 1) * TG)
            S.op("sp", lambda e, sl=sl: e.dma_start(out=self.hT[:, :, sl],
                                                    in_=self.h0T.rearrange("(c p) t -> p c t", p=P)[:, :, sl]),
                 writes=[("hT", c, tg) for c in range(KC)], dma=True)
        S.op("dve", lambda e: e.tensor_copy(out=self.consts[:], in_=self.consts_f[:]), reads=["consts_f"], writes=["consts"])
        S.op("dve", lambda e: e.memset(self.epsc[:], EPS), writes=["epsc"])
        S.op("dve", lambda e: e.memset(self.onec[:], 1.0), writes=["onec"])
        S.op("dve", lambda e: e.tensor_scalar(out=self.nflagb[:], in0=self.flagb[:], scalar1=-1.0, scalar2=None, op0=ALU.mult),
             reads=["flagb"], writes=["nflagb"])
        S.op("act", lambda e: e.activation(out=self.esink[:], in_=self.sinks[:], func=AF.Exp), reads=["sinks"], writes=["esink"])
        S.op("dve", lambda e: e.tensor_tensor(out=self.junk[:, 0:64], in0=self.lamv[:, 0:64], in1=self.lamv[:, 64:128], op=ALU.mult),
             reads=["lamv"], writes=["junk"])
        S.op("dve", lambda e: e.tensor_reduce(out=self.lamt[:, 0:1], in_=self.junk[:, 0:64], axis=AX.X, op=ALU.add),
             reads=["junk"], writes=["lamt0"])
        S.op("dve", lambda e: e.tensor_tensor(out=self.junk[:, 64:128], in0=self.lamv[:, 128:192], in1=self.lamv[:, 192:256], op=ALU.mult),
             reads=["lamv"], writes=["junk2"])
        S.op("dve", lambda e: e.tensor_reduce(out=self.lamt[:, 1:2], in_=self.junk[:, 64:128], axis=AX.X, op=ALU.add),
             reads=["junk2"], writes=["lamt1"])
        S.op("act", lambda e: e.activation(out=self.lamt[:, 2:4], in_=self.lamt[:, 0:2], func=AF.Exp),
             reads=["lamt0", "lamt1"], writes=["lamt23"])
        S.op("dve", lambda e: e.tensor_tensor(out=self.lamt[:, 4:5], in0=self.lamt[:, 3:4], in1=self.lamt[:, 2:3], op=ALU.subtract),
             reads=["lamt23"], writes=["lamt4"])
        S.op("dve", lambda e: e.tensor_scalar(out=self.lamt[:, 5:6], in0=self.lamt[:, 4:5], scalar1=-LAMBDA_INIT, scalar2=None, op0=ALU.add),
             reads=["lamt4"], writes=["nlam"])
        S.op("dve", lambda e: e.tensor_scalar(out=self.subg[:], in0=self.subg[:], scalar1=1.0 - LAMBDA_INIT, scalar2=None, op0=ALU.mult),
             reads=["subg"], writes=["subg"])

    def rmsnorm(self, gi, dst_keyname="xT", final=False):
        S = self.S
        ones = self.consts[:, 0, :]
        for tg in range(NTG):
            sl = slice(tg * TG, (tg + 1) * TG)
            pi = self.next_ps()
            ps = self.ps[pi]
            for c in range(KC):
                sq = self.sqb[c % 3]
                S.op("act", lambda e, sq=sq, c=c, sl=sl: e.activation(out=sq[:], in_=self.hT[:, c, sl], func=AF.Square),
                     reads=[("hT", c, tg)], writes=[("sqb", c % 3)])
                S.op("pe", lambda e, sq=sq, c=c, ps=ps: e.matmul(ps[:, 0:TG], lhsT=ones, rhs=sq[:], start=(c == 0), stop=(c == KC - 1)),
                     reads=[("sqb", c % 3), "consts"], writes=[("ps", pi)])
            lnv = self.lnv[tg % 2]
            rstd = self.rstd[tg % 2]
            S.op("act", lambda e, ps=ps, lnv=lnv: e.activation(out=lnv[:], in_=ps[:, 0:TG], func=AF.Ln, bias=self.epsc[:, 0:1], scale=1.0 / D),
                 reads=[("ps", pi), "epsc"], writes=[("lnv", tg % 2)])
            S.op("act", lambda e, lnv=lnv, rstd=rstd: e.activation(out=rstd[:], in_=lnv[:], func=AF.Exp, scale=-0.5),
                 reads=[("lnv", tg % 2)], writes=[("rstd", tg % 2)])
            for c in range(KC):
                gcol = self.gains[:, gi * KC + c: gi * KC + c + 1]
                if final:
                    S.op("dve", lambda e, c=c, sl=sl, gcol=gcol, rstd=rstd: e.scalar_tensor_tensor(
                        out=self.hT[:, c, sl], in0=self.hT[:, c, sl], scalar=gcol, in1=rstd[:], op0=ALU.mult, op1=ALU.mult),
                        reads=[("hT", c, tg), ("rstd", tg % 2), "gains"], writes=[("hT", c, tg)])
                else:
                    S.op("dve", lambda e, c=c, sl=sl, gcol=gcol, rstd=rstd: e.scalar_tensor_tensor(
                        out=self.xT[:, c, sl], in0=self.hT[:, c, sl], scalar=gcol, in1=rstd[:], op0=ALU.mult, op1=ALU.mult),
                        reads=[("hT", c, tg), ("rstd", tg % 2), "gains"], writes=[("xT", c, tg)])

    def load_w(self, W, r0, nkc, c0, ncols):
        S = self.S
        self.wrr = getattr(self, "wrr", 0)
        slot = self.wrr % 2
        self.wrr += 1
        wb = self.wbuf[slot]
        view = wb[:, 0:nkc * ncols].rearrange("p (k n) -> p k n", n=ncols)
        src = W[r0:r0 + nkc * P, c0:c0 + ncols].rearrange("(k p) n -> p k n", p=P)
        half = nkc // 2
        S.op("pool", lambda e: e.dma_start(out=view[:, 0:half, :], in_=src[:, 0:half, :]), writes=[("wbuf", slot, 0)], dma=True)
        S.op("pool", lambda e: e.dma_start(out=view[:, half:nkc, :], in_=src[:, half:nkc, :]), writes=[("wbuf", slot, 1)], dma=True)
        return slot, view

    def evac_engine(self):
        self.evrr += 1
        return "act" if self.evrr % 2 == 0 else "dve"

    def copy_op(self, eng, out, in_, reads, writes, scale=None):
        S = self.S
        if eng == "act":
            if scale is None:
                S.op("act", lambda e: e.activation(out=out, in_=in_, func=AF.Copy), reads=reads, writes=writes)
            else:
                S.op("act", lambda e: e.activation(out=out, in_=in_, func=AF.Copy, scale=scale), reads=reads, writes=writes)
        else:
            if scale is None:
                S.op("dve", lambda e: e.tensor_copy(out=out, in_=in_), reads=reads, writes=writes)
            else:
                S.op("dve", lambda e: e.tensor_scalar(out=out, in0=in_, scalar1=scale, scalar2=None, op0=ALU.mult), reads=reads, writes=writes)

    def linear_fm(self, W, r0, nkc, c0, ncols_total, rhs_buf, rhs_key, kc0, evac, wcols=256):
        S = self.S
        ntile = ncols_total // wcols
        for wt in range(ntile):
            slot, view = self.load_w(W, r0, nkc, c0 + wt * wcols, wcols)
            for o in range(wcols // P):
                ot = wt * (wcols // P) + o
                for tg in range(NTG):
                    sl = slice(tg * TG, (tg + 1) * TG)
                    pi = self.next_ps()
                    ps = self.ps[pi]
                    for kc in range(nkc):
                        S.op("pe", lambda e, ps=ps, view=view, kc=kc, o=o, sl=sl: e.matmul(
                            ps[:, 0:TG], lhsT=view[:, kc, o * P:(o + 1) * P], rhs=rhs_buf[:, kc0 + kc, sl],
                            start=(kc == 0), stop=(kc == nkc - 1)),
                            reads=[("wbuf", slot, 0 if kc < nkc // 2 else 1), (rhs_key, kc0 + kc, tg)], writes=[("ps", pi)])
                    evac(ot, tg, ps, pi)

    def proj_qk(self, W, c0, ntiles, dst, drow0, scale=None):
        S = self.S

        def evac(ot, tg, ps, pi):
            st = self.stage[ot % 2]
            sl = slice(tg * TG, (tg + 1) * TG)
            self.copy_op(self.evac_engine(), st[:, sl], ps[:, 0:TG], [("ps", pi)], [("stage", ot % 2, tg)], scale=scale)
            if tg == NTG - 1:
                row = drow0 + ot * P
                if isinstance(dst, dict):
                    ci, w0, _ = self.ch_find(dst, row)
                    dap = dst["loc"][ci].ap()[w0:w0 + P, :]
                    key = (dst["name"], row)
                else:
                    dap = dst[row:row + P, :]
                    key = (dst.tensor.name, row)
                S.op("sp", lambda e: e.dma_start(out=dap, in_=st[:]),
                     reads=[("stage", ot % 2, t) for t in range(NTG)], writes=[key], dma=True)
                if isinstance(dst, dict):
                    dst["keys"][ci].append(key)
                    if row + P == dst["bounds"][ci + 1]:
                        self.emit_cc(dst, ci)

        self.linear_fm(W, 0, KC, c0, ntiles * P, self.xT, "xT", 0, evac)

    def proj_v(self, W, c0, ncols, dst, dcol0):
        S = self.S
        for wt in range(ncols // 256):
            slot, view = self.load_w(W, 0, KC, c0 + wt * 256, 256)
            for tb in range(NQB):
                qw = qw_of(tb)
                tsl = slice(tb * P, tb * P + qw)
                pi = self.next_ps()
                ps = self.ps[pi]
                for kc in range(KC):
                    S.op("pe", lambda e, ps=ps, view=view, kc=kc, tsl=tsl, qw=qw: e.matmul(
                        ps[0:qw, 0:256], lhsT=self.xT[:, kc, tsl], rhs=view[:, kc, :], start=(kc == 0), stop=(kc == KC - 1)),
                        reads=[("wbuf", slot, 0 if kc < 8 else 1)] + [("xT", kc, t) for t in range(NTG)], writes=[("ps", pi)])
                self.vsrr = getattr(self, "vsrr", 0) + 1
                vi = self.vsrr % 3
                vs = self.vstage[vi]
                self.copy_op(self.evac_engine(), vs[0:qw, :], ps[0:qw, 0:256], [("ps", pi)], [("vstage", vi)])
                ci, w0, _ = self.ch_find(dst, dcol0 + wt * 256)
                dap = dst["loc"][ci].ap()[tb * P: tb * P + qw, w0:w0 + 256]
                vkey = (dst["name"], "v", tb, dcol0 + wt * 256)
                S.op("sp", lambda e, vs=vs, qw=qw, dap=dap: e.dma_start(out=dap, in_=vs[0:qw, :]),
                     reads=[("vstage", vi)], writes=[vkey], dma=True)
                dst["keys"][ci].append(vkey)
            if dcol0 + (wt + 1) * 256 == dst["bounds"][ci + 1]:
                self.emit_cc(dst, ci)

    def emit_cc(self, ch, ci):
        S = self.S
        src, dst = ch["loc"][ci], ch["gat"][ci]
        S.op("pool", lambda e: e.collective_compute("AllGather", ALU.bypass, replica_groups=REPLICA_GROUPS,
                                                    ins=[src.ap().opt()], outs=[dst.ap().opt()]),
             reads=list(ch["keys"][ci]), writes=[("cc", ch["name"], ci)], cc=True)
        ch["done"][ci] = True

    def allgather(self, ch):
        assert all(ch["done"]), ch["name"]

    def add_into_h(self, ot, tg, ps, pi):
        sl = slice(tg * TG, (tg + 1) * TG)
        self.S.op("dve", lambda e: e.tensor_tensor(out=self.hT[:, ot, sl], in0=self.hT[:, ot, sl], in1=ps[:, 0:TG], op=ALU.add),
                  reads=[("ps", pi), ("hT", ot, tg)], writes=[("hT", ot, tg)])

    def out_proj(self, W):
        self.linear_fm(W, 0, KC, 0, D, self.xT, "xT", 0, self.add_into_h)

    def mlp(self, l):
        S = self.S
        W1 = self.w_mlp_in[l]
        W2 = self.w_mlp_out[l]
        for fc in range(8):
            def evac1(ot, tg, ps, pi):
                sl = slice(tg * TG, (tg + 1) * TG)
                self.rrr = getattr(self, "rrr", 0) + 1
                ri = self.rrr % 3
                rt = self.relu_t[ri]
                S.op("act", lambda e: e.activation(out=rt[:], in_=ps[:, 0:TG], func=AF.Relu), reads=[("ps", pi)], writes=[("relu", ri)])
                S.op("dve", lambda e: e.tensor_tensor(out=self.hidT[:, ot, sl], in0=rt[:], in1=rt[:], op=ALU.mult),
                     reads=[("relu", ri)], writes=[("hidT", ot, tg)])
            self.linear_fm(W1, 0, KC, fc * 1024, 1024, self.xT, "xT", 0, evac1)
            self.linear_fm(W2, fc * 1024, 8, 0, D, self.hidT, "hidT", 0, self.add_into_h, wcols=512)

    def load_k(self, slot, c, ch, row):
        S = self.S
        kt = self.kT[slot][c]
        ci, w0, w = self.ch_find(ch, row)
        g = ch["gat"][ci].ap()
        for r in range(2):
            S.op("sp", lambda e, r=r: e.dma_start(out=kt[:, r * T:(r + 1) * T], in_=g[r * w + w0: r * w + w0 + 64, :]),
                 reads=[("cc", ch["name"], ci)], writes=[("kT", slot, c, r)], dma=True)

    def load_q(self, slot, c, QTd, row):
        S = self.S
        qt = self.qT[slot][c]
        S.op("sp", lambda e: e.dma_start(out=qt[:], in_=QTd.ap()[row:row + 64, :]),
             reads=[(QTd.name, (row // P) * P)], writes=[("qT", slot, c)], dma=True)

    def load_v(self, slot, ch, col, ncol):
        S = self.S
        vs = self.Vs[slot]
        ci, w0, w = self.ch_find(ch, col)
        src = ch["gat"][ci].ap().rearrange("(g p) f -> p g f", p=P)
        for (g0, g1) in ((0, 9), (9, NGB)):
            S.op("sp", lambda e, g0=g0, g1=g1: e.dma_start(out=vs[:, g0:g1, 0:ncol], in_=src[:, g0:g1, w0:w0 + ncol]),
                 reads=[("cc", ch["name"], ci)], writes=[("Vs", slot, g0)], dma=True)

    def set_v_ones(self, slot, col):
        self.S.op("dve", lambda e: e.memset(self.Vs[slot][:, :, col:col + 1], 1.0), writes=[("Vs1", slot)],
                  reads=[])

    def transpose_out(self, obi, qw, ftile, j):
        S = self.S
        self.ptrr = getattr(self, "ptrr", 0) + 1
        pt = self.ptrr % 4
        ident = self.consts[:, 2, :]
        tsl = slice(j * P, j * P + qw)
        tgs = sorted(set([(j * P) // TG, (j * P + qw - 1) // TG]))
        S.op("pe", lambda e: e.transpose(self.psT[:, pt * 128: pt * 128 + qw], self.ob[obi][0:qw, :], ident[0:qw, 0:qw]),
             reads=[("ob", obi), "consts"], writes=[("psT", pt)])
        self.copy_op(self.evac_engine(), self.xT[:, ftile, tsl], self.psT[:, pt * 128: pt * 128 + qw], [("psT", pt)],
                     [("xT", ftile, t) for t in tgs])

    def run_tasks(self, factories, nslots):
        pending = list(factories)
        active = {}
        free = list(range(nslots))
        while pending or active:
            while pending and free:
                sl = free.pop(0)
                active[sl] = pending.pop(0)(sl)
            for sl in sorted(active.keys()):
                try:
                    next(active[sl])
                except StopIteration:
                    del active[sl]
                    free.append(sl)

    @staticmethod
    def split_groups(glist, split0=False, maxn=4):
        groups = []
        cur = []
        for g in glist:
            if cur and (g != cur[-1] + 1 or len(cur) == maxn or (split0 and cur[-1] == 0)):
                groups.append(cur)
                cur = []
            cur.append(g)
        if cur:
            groups.append(cur)
        return groups

    def softmax_task(self, ts, j, glist, bias_ap_of, kts, qts, vs, vcols, split0, fin_steps):
        S = self.S
        qw = qw_of(j)
        qsl = slice(j * P, j * P + qw)
        ncomp = len(kts)
        groups = self.split_groups(glist, split0)
        psS = self.ps[ts]
        psOb = self.ps[3 + ts][:, 0:vcols + 1]
        if ncomp == 1:
            psO = [psOb]
            okeys = [("psO", ts)]
        else:
            psO = [self.o0buf[ts][:, 0:vcols + 1], psOb]
            okeys = [("o0buf", ts), ("psO", ts)]
        seq = [(c, grp) for c in range(ncomp) for grp in groups]
        first = [True] * ncomp
        lastgrp = groups[-1]
        pend = None
        for si in range(len(seq) + 1):
            item = None
            if si < len(seq):
                c, grp = seq[si]
                n = len(grp)
                kt, ktkeys = kts[c]
                qt, qtkeys = qts[c]
                for i, g in enumerate(grp):
                    S.op("pe", lambda e, i=i, g=g: e.matmul(psS[:, i * 128: i * 128 + qw], lhsT=kt[:, g * P:(g + 1) * P], rhs=qt[:, qsl],
                                                            start=True, stop=True),
                         reads=ktkeys + qtkeys, writes=[("psS", ts)])
                self.tmprr[ts] += 1
                tb = self.tmprr[ts] % 2
                tmp = self.tmpS2[ts][tb]
                pt = self.pT2[ts][tb]
                b_ap, bkeys = bias_ap_of(grp, qw)
                p0 = self.spT2[ts][tb]
                S.op("act", lambda e: e.activation(out=p0[:, 0:n, 0:qw], in_=psS[:, 0:n * 128].rearrange("p (n q) -> p n q", q=128)[:, :, 0:qw],
                                                   func=AF.Exp, scale=0.125),
                     reads=[("psS", ts)], writes=[("spT", ts, tb)])
                S.op("dve", lambda e: e.tensor_tensor(out=pt[:, 0:n, 0:qw], in0=p0[:, 0:n, 0:qw], in1=b_ap, op=ALU.mult),
                     reads=[("spT", ts, tb)] + bkeys, writes=[("pT", ts, tb)])
                item = (c, grp, pt, tb)
            if pend is not None:
                pc, pgrp, ppt, ptb = pend
                for i, g in enumerate(pgrp):
                    st = first[pc]
                    first[pc] = False
                    sp_ = (pgrp is lastgrp and i == len(pgrp) - 1)
                    S.op("pe", lambda e, i=i, g=g, st=st, sp_=sp_: e.matmul(
                        psOb[0:qw, :], lhsT=ppt[:, i, 0:qw], rhs=vs[0][:, g, 0:vcols + 1], start=st, stop=sp_),
                        reads=[("pT", ts, ptb)] + vs[1], writes=[("psO", ts)])
                if ncomp == 2 and pc == 0 and pgrp is lastgrp:
                    self.copy_op(self.evac_engine(), self.o0buf[ts][0:qw, 0:vcols + 1], psOb[0:qw, :], [("psO", ts)], [("o0buf", ts)])
            pend = item
            yield
        for step in fin_steps(ts, psO, okeys):
            step()
            yield

    def attn_c(self):
        S = self.S
        S.op("sp", lambda e: e.dma_start(out=self.tabA[:, 0:4, :], in_=self.negdist_c_d), writes=["tabA"], dma=True)
        S.op("sp", lambda e: e.dma_start(out=self.tabB[:, 0:4, :], in_=self.mask_c_d), writes=["tabB"], dma=True)
        for s_ in range(2):
            self.set_v_ones(s_, 128)
        facts = []
        for h in range(16):
            for j in range(NQB):
                facts.append(lambda ts, h=h, j=j: self.task_c(ts, h, j))
        self.run_tasks(facts, 3)

    def head_pre_c(self, h):
        S = self.S
        slot = h % 2
        for c in range(2):
            self.load_k(slot, c, self.KT1, h * 128 + c * 64)
            self.load_q(slot, c, self.QT1, h * 128 + c * 64)
        self.load_v(slot, self.V1, h * 128, 128)
        bH = self.biasH[slot]
        slope = C_SLOPES[h]
        for rel in range(NREL):
            if rel in C_NEAR:
                i = C_NEAR.index(rel)
                S.op("dve", lambda e: e.scalar_tensor_tensor(out=bH[:, rel, :], in0=self.tabA[:, i, :], scalar=slope,
                                                             in1=self.tabB[:, i, :], op0=ALU.mult, op1=ALU.add),
                     reads=["tabA", "tabB"], writes=[("biasH", 0, rel)])
            else:
                S.op("dve", lambda e: e.tensor_scalar(out=bH[:, rel, :], in0=self.kqneg[:], scalar1=self.negd0[:, rel:rel + 1],
                                                      scalar2=slope, op0=ALU.add, op1=ALU.mult),
                     reads=["kqneg", "negd0"], writes=[("biasH", 0, rel)])
        eH = self.expH[slot]
        S.op("act", lambda e: e.activation(out=eH[:], in_=bH[:], func=AF.Exp),
             reads=[("biasH", 0, r) for r in range(NREL)], writes=[("expH", slot, r) for r in range(NREL)])

    def task_c(self, ts, h, j):
        if j == 0:
            self.head_pre_c(h)
        slot = h % 2
        bH = self.biasH[slot]
        kts = [(self.kT[slot][c], [("kT", slot, c, 0), ("kT", slot, c, 1)]) for c in range(2)]
        qts = [(self.qT[slot][c], [("qT", slot, c)]) for c in range(2)]
        vs = (self.Vs[slot], [("Vs", slot, 0), ("Vs", slot, 9), ("Vs1", slot)])
        gmax = min(16, 9 + j)
        glist = list(range(0, gmax + 1))

        eH = self.expH[slot]

        def bias_ap_of(grp, qw):
            r0 = grp[0] - j + 8
            r1 = grp[-1] - j + 8
            return eH[:, r0:r1 + 1, 0:qw], [("expH", slot, r) for r in range(r0, r1 + 1)]

        def fin_steps(ts, psO, okeys):
            return self.fin_c_steps(ts, h, j, psO, okeys)

        return self.softmax_task(ts, j, glist, bias_ap_of, kts, qts, vs, 128, False, fin_steps)

    def fin_c_steps(self, ts, h, j, psO, okeys):
        S = self.S
        qw = qw_of(j)
        sm = self.small3[ts]
        of = self.ofin3[ts]
        ob = self.ob3[ts]
        jk = self.junk3[ts]
        kk = ("fin", ts)
        steps = []
        A = steps.append
        A(lambda: S.op("dve", lambda e: e.reciprocal(out=sm[0:qw, 0:1], in_=psO[0][0:qw, 128:129]), reads=[okeys[0]], writes=[(kk, 0)]))
        A(lambda: S.op("dve", lambda e: e.reciprocal(out=sm[0:qw, 1:2], in_=psO[1][0:qw, 128:129]), reads=[okeys[1]], writes=[(kk, 1)]))
        A(lambda: S.op("dve", lambda e: e.tensor_tensor(out=sm[0:qw, 2:3], in0=sm[0:qw, 1:2], in1=self.lamt[0:qw, 5:6], op=ALU.mult),
                       reads=[(kk, 1), "nlam"], writes=[(kk, 2)]))
        A(lambda: S.op("act", lambda e: e.activation(out=of[0:qw, :], in_=psO[0][0:qw, 0:128], func=AF.Copy, scale=sm[0:qw, 0:1]),
                       reads=[okeys[0], (kk, 0)], writes=[(kk, "of")]))
        A(lambda: S.op("dve", lambda e: e.scalar_tensor_tensor(out=of[0:qw, :], in0=psO[1][0:qw, 0:128], scalar=sm[0:qw, 2:3], in1=of[0:qw, :],
                                                               op0=ALU.mult, op1=ALU.add),
                       reads=[okeys[1], (kk, 2), (kk, "of")], writes=[(kk, "of")]))
        A(lambda: S.op("act", lambda e: e.activation(out=jk[0:qw, :], in_=of[0:qw, :], func=AF.Square),
                       reads=[(kk, "of")], writes=[(kk, "jk")]))
        A(lambda: S.op("dve", lambda e: e.tensor_reduce(out=sm[0:qw, 3:4], in_=jk[0:qw, :], axis=AX.X, op=ALU.add),
                       reads=[(kk, "jk")], writes=[(kk, 3)]))
        A(lambda: S.op("act", lambda e: e.activation(out=sm[0:qw, 4:5], in_=sm[0:qw, 3:4], func=AF.Ln, bias=self.epsc[0:qw, 0:1], scale=1.0 / 128),
                       reads=[(kk, 3), "epsc"], writes=[(kk, 4)]))
        A(lambda: S.op("act", lambda e: e.activation(out=sm[0:qw, 5:6], in_=sm[0:qw, 4:5], func=AF.Exp, scale=-0.5),
                       reads=[(kk, 4)], writes=[(kk, 5)]))
        A(lambda: S.op("dve", lambda e: e.scalar_tensor_tensor(out=ob[0:qw, :], in0=of[0:qw, :], scalar=sm[0:qw, 5:6], in1=self.subg[0:qw, :],
                                                               op0=ALU.mult, op1=ALU.mult),
                       reads=[(kk, "of"), (kk, 5), "subg"], writes=[("ob3", ts)]))
        A(lambda: self.transpose_evac(qw, h, j, self.transpose_pe(ob, [("ob3", ts)], qw)))
        return steps

    def transpose_pe(self, ob, obkeys, qw):
        S = self.S
        self.ptrr = getattr(self, "ptrr", 0) + 1
        pt = self.ptrr % 2
        ident = self.consts[:, 2, :]
        S.op("pe", lambda e: e.transpose(self.psTs[pt][:, 0:qw], ob[0:qw, :], ident[0:qw, 0:qw]),
             reads=obkeys + ["consts"], writes=[("psT", pt)])
        self.last_pt = pt
        return pt

    def transpose_evac(self, qw, ftile, j, pt=None):
        pt = self.last_pt if pt is None else pt
        tsl = slice(j * P, j * P + qw)
        tgs = sorted(set([(j * P) // TG, (j * P + qw - 1) // TG]))
        self.copy_op(self.evac_engine(), self.xT[:, ftile, tsl], self.psTs[pt][:, 0:qw], [("psT", pt)],
                     [("xT", ftile, t) for t in tgs])

    def attn_a(self):
        S = self.S
        S.op("sp", lambda e: e.dma_start(out=self.tabA[:], in_=self.negdist_a_d), writes=["tabA"], dma=True)
        S.op("sp", lambda e: e.dma_start(out=self.tabB[:], in_=self.mask_a_d), writes=["tabB"], dma=True)
        S.op("sp", lambda e: e.dma_start(out=self.tabA0[:], in_=self.mask_a0_d), writes=["tabA0"], dma=True)
        for s_ in range(2):
            self.set_v_ones(s_, 64)
        facts = []
        for h in range(16):
            for j in range(NQB):
                facts.append(lambda ts, h=h, j=j: self.task_a(ts, h, j))
        self.run_tasks(facts, 3)

    def head_pre_a(self, h):
        S = self.S
        kvh = h // 4
        kslot = kvh % 2
        slot = h % 2
        if h % 4 == 0:
            self.load_k(kslot, 0, self.KT0, kvh * 64)
            self.load_v(kslot, self.V0, kvh * 64, 64)
        self.load_q(slot, 0, self.QT0, h * 64)
        bH = self.biasH[slot]
        slope = A_SLOPES[h]
        for i, rel in enumerate(A_NEAR):
            S.op("dve", lambda e: e.scalar_tensor_tensor(out=bH[:, rel, :], in0=self.tabA[:, i, :], scalar=slope,
                                                         in1=self.tabB[:, i, :], op0=ALU.mult, op1=ALU.add),
                 reads=["tabA", "tabB"], writes=[("biasH", 0, rel)])
        for j in range(NQB):
            rel = 8 - j
            if rel in A_NEAR:
                i = A_NEAR.index(rel)
                S.op("dve", lambda e: e.scalar_tensor_tensor(out=bH[:, B0IDX[j], :], in0=self.tabA[:, i, :], scalar=slope,
                                                             in1=self.tabA0[:, j, :], op0=ALU.mult, op1=ALU.add),
                     reads=["tabA", "tabA0"], writes=[("biasH", 0, B0IDX[j])])
            else:
                S.op("dve", lambda e: e.tensor_scalar(out=bH[:, B0IDX[j], :], in0=self.kqneg[:], scalar1=self.negd0[:, rel:rel + 1],
                                                      scalar2=slope, op0=ALU.add, op1=ALU.mult),
                     reads=["kqneg", "negd0"], writes=[("biasH", 0, B0IDX[j])])
                S.op("dve", lambda e: e.tensor_tensor(out=bH[:, B0IDX[j], :], in0=bH[:, B0IDX[j], :], in1=self.tabA0[:, j, :], op=ALU.add),
                     reads=[("biasH", 0, B0IDX[j]), "tabA0"], writes=[("biasH", 0, B0IDX[j])])

    def task_a(self, ts, h, j):
        S = self.S
        if j == 0:
            self.head_pre_a(h)
            bH_, eH_ = self.biasH[h % 2], self.expH[h % 2]
            for (r0, r1) in ((0, 13), (15, 18)):
                S.op("act", lambda e: e.activation(out=eH_[:, r0:r1, :], in_=bH_[:, r0:r1, :], func=AF.Exp),
                     reads=[("biasH", 0, r) for r in range(r0, r1) if r in A_NEAR or r in B0IDX],
                     writes=[("expH", h % 2, r) for r in range(r0, r1)])
        kslot = (h // 4) % 2
        slot = h % 2
        bH = self.biasH[slot]
        kts = [(self.kT[kslot][0], [("kT", kslot, 0, 0), ("kT", kslot, 0, 1)])]
        qts = [(self.qT[slot][0], [("qT", slot, 0)])]
        vs = (self.Vs[kslot], [("Vs", kslot, 0), ("Vs", kslot, 9), ("Vs1", kslot)])
        near = sorted(set([g for g in list(range(j - 2, j + 2)) + list(range(j + 7, j + 10)) if 1 <= g <= 16]))
        glist = [0] + near

        eH = self.expH[slot]

        def bias_ap_of(grp, qw):
            if grp[0] == 0:
                assert len(grp) == 1
                return eH[:, B0IDX[j]:B0IDX[j] + 1, 0:qw], [("expH", slot, B0IDX[j])]
            r0 = grp[0] - j + 8
            r1 = grp[-1] - j + 8
            return eH[:, r0:r1 + 1, 0:qw], [("expH", slot, r) for r in range(r0, r1 + 1)]

        def fin_steps(ts, psO, okeys):
            qw = qw_of(j)
            sm = self.small3[ts]
            ob = self.obA[(h // 2) % 2][j]
            kk = ("finA", ts)
            obk = ("obA", (h // 2) % 2, j)
            steps = []
            A = steps.append
            A(lambda: S.op("dve", lambda e: e.tensor_tensor(out=sm[0:qw, 0:1], in0=psO[0][0:qw, 64:65], in1=self.esink[0:qw, h:h + 1], op=ALU.add),
                           reads=[okeys[0], "esink"], writes=[(kk, 0)]))
            A(lambda: S.op("dve", lambda e: e.reciprocal(out=sm[0:qw, 1:2], in_=sm[0:qw, 0:1]), reads=[(kk, 0)], writes=[(kk, 1)]))
            A(lambda: S.op("act", lambda e: e.activation(out=ob[0:qw, (h % 2) * 64:(h % 2) * 64 + 64], in_=psO[0][0:qw, 0:64], func=AF.Copy,
                                                         scale=sm[0:qw, 1:2]),
                           reads=[okeys[0], (kk, 1)], writes=[obk + (h % 2,)]))
            if h % 2 == 1:
                A(lambda: self.transpose_evac(qw, h // 2, j, self.transpose_pe(ob, [obk + (0,), obk + (1,)], qw)))
            return steps

        return self.softmax_task(ts, j, glist, bias_ap_of, kts, qts, vs, 64, True, fin_steps)

    def attn_b(self):
        S = self.S
        S.op("sp", lambda e: e.dma_start(out=self.maskb_f[:], in_=self.mask_b_d), writes=["maskb_f"], dma=True)
        S.op("dve", lambda e: e.tensor_copy(out=self.maskb[:], in_=self.maskb_f[:]), reads=["maskb_f"], writes=["maskb"])
        facts = []
        for h in range(16):
            for j in range(NQB):
                facts.append(lambda ts, h=h, j=j: self.task_b(ts, h, j))
        self.run_tasks(facts, 3)

    def task_b(self, ts, h, j):
        S = self.S
        ones = self.consts[:, 0, :]
        triu = self.consts[:, 1, :]
        slot = h % 2
        if j == 0:
            self.load_k(slot, 0, self.KT0, 256 + h * 64)
            self.load_v(slot, self.V0, 256 + h * 64, 64)
            self.load_q(slot, 0, self.QT0, 1024 + h * 64)
        kt = self.kT[slot][0]
        ktkeys = [("kT", slot, 0, 0), ("kT", slot, 0, 1)]
        qt = self.qT[slot][0]
        qtkeys = [("qT", slot, 0)]
        vsb = self.Vs[slot]
        vkeys = [("Vs", slot, 0), ("Vs", slot, 9)]
        qw = qw_of(j)
        qsl = slice(j * P, j * P + qw)
        gmax = min(16, 9 + j)
        groups = []
        cur = []
        curcls = None
        for g in range(gmax, -1, -1):
            rel = g - j + 8
            cls = "near" if rel in B_NEAR else ("flag" if rel >= 9 else "free")
            if cur and (cls != curcls or len(cur) == 4 or cls == "near"):
                groups.append((curcls, cur))
                cur = []
            cur.append(g)
            curcls = cls
        if cur:
            groups.append((curcls, cur))
        psS = self.ps[ts]
        psD = psS
        psO = self.ps[3 + ts][:, 0:64]
        okey = ("psOb", ts)
        R32 = self.R32s[ts]
        Rtmp = self.Rtmps[ts]
        S.op("dve", lambda e: e.memset(R32[:], 0.0), writes=[("R32", ts)])
        self.rbrr[ts] += 1
        rb = self.rbrr[ts] % 2
        S.op("dve", lambda e: e.memset(self.Rbfs[ts][rb][:], 0.0), writes=[("Rbf", ts, rb)])
        pend_pv = None
        first_pv = True
        ng = len(groups)

        def emit_pv(pend, first, last):
            wt, wkey, asc = pend
            n = len(asc)
            for i, g in enumerate(asc):
                st = first
                first = False
                sp_ = last and i == n - 1
                S.op("pe", lambda e: e.matmul(psO[0:qw, :], lhsT=wt[:, i, 0:qw], rhs=vsb[:, g, 0:64], start=st, stop=sp_),
                     reads=[wkey] + vkeys, writes=[okey])
            return first

        for gi, (cls, grp) in enumerate(groups):
            n = len(grp)
            asc = grp[::-1]
            if pend_pv is not None:
                first_pv = emit_pv(pend_pv, first_pv, False)
                pend_pv = None
            for i, g in enumerate(asc):
                S.op("pe", lambda e: e.matmul(psS[:, i * 128: i * 128 + qw], lhsT=kt[:, g * P:(g + 1) * P], rhs=qt[:, qsl], start=True, stop=True),
                     reads=ktkeys + qtkeys, writes=[("psS", ts)])
            et = self.tmpS2[ts][0]
            self.sprr[ts] += 1
            sb_ = self.sprr[ts] % 2
            spt = self.spT2[ts][sb_]
            spkey = ("spT", ts, sb_)
            psS3 = psS[:, 0:n * 128].rearrange("p (n q) -> p n q", q=128)[:, :, 0:qw]
            psD3 = psD[:, 0:n * 128].rearrange("p (n q) -> p n q", q=128)[:, :, 0:qw]
            if cls == "flag":
                S.op("act", lambda e: e.activation(out=et[:, 0:n, 0:qw], in_=psS3, func=AF.Exp, scale=-1.0, bias=self.flagb[:, 0:1]),
                     reads=[("psS", ts), "flagb"], writes=[("tmpS", ts, 0)])
            else:
                S.op("act", lambda e: e.activation(out=et[:, 0:n, 0:qw], in_=psS3, func=AF.Exp, scale=-1.0),
                     reads=[("psS", ts)], writes=[("tmpS", ts, 0)])
            S.op("act", lambda e: e.activation(out=spt[:, 0:n, 0:qw], in_=et[:, 0:n, 0:qw], func=AF.Ln, bias=self.onec[:, 0:1], scale=1.0),
                 reads=[("tmpS", ts, 0), "onec"], writes=[spkey])
            if cls == "near":
                mi = B_NEAR.index(grp[0] - j + 8)
                S.op("dve", lambda e: e.tensor_tensor(out=spt[:, 0, 0:qw], in0=spt[:, 0, 0:qw], in1=self.maskb[:, mi, 0:qw], op=ALU.mult),
                     reads=[spkey, "maskb"], writes=[spkey])
            yield
            for i, g in enumerate(asc):
                dsl = slice(i * 128, i * 128 + qw)
                S.op("pe", lambda e: e.matmul(psD[:, dsl], lhsT=triu, rhs=spt[:, i, 0:qw], start=False, stop=False, skip_group_check=True),
                     reads=[spkey, "consts", ("tmpS", ts, 0)], writes=[("psS", ts)])
                for i2 in range(i + 1, n):
                    S.op("pe", lambda e: e.matmul(psD[:, dsl], lhsT=ones, rhs=spt[:, i2, 0:qw], start=False, stop=False, skip_group_check=True),
                         reads=[spkey, "consts"], writes=[("psS", ts)])
                S.op("pe", lambda e: e.matmul(psD[:, dsl], lhsT=ones, rhs=self.Rbfs[ts][rb][:, 0:qw], start=False, stop=True, skip_group_check=True),
                     reads=[("Rbf", ts, rb), "consts"], writes=[("psS", ts)])
            if gi < ng - 1:
                if n > 1:
                    S.op("dve", lambda e: e.tensor_reduce(out=Rtmp[:, 0:qw], in_=spt[:, 0:n, 0:qw].rearrange("p n q -> p q n"), axis=AX.X, op=ALU.add),
                         reads=[spkey], writes=[("Rtmp", ts)])
                    S.op("dve", lambda e: e.tensor_tensor(out=R32[:, 0:qw], in0=R32[:, 0:qw], in1=Rtmp[:, 0:qw], op=ALU.add),
                         reads=[("Rtmp", ts), ("R32", ts)], writes=[("R32", ts)])
                else:
                    S.op("dve", lambda e: e.tensor_tensor(out=R32[:, 0:qw], in0=R32[:, 0:qw], in1=spt[:, 0, 0:qw], op=ALU.add),
                         reads=[spkey, ("R32", ts)], writes=[("R32", ts)])
                self.rbrr[ts] += 1
                rb = self.rbrr[ts] % 2
                S.op("dve", lambda e: e.tensor_copy(out=self.Rbfs[ts][rb][:, 0:qw], in_=R32[:, 0:qw]), reads=[("R32", ts)], writes=[("Rbf", ts, rb)])
            self.tmprr[ts] += 1
            wb = self.tmprr[ts] % 2
            wt = self.pT2[ts][wb]
            wkey = ("pT", ts, wb)
            if cls == "flag":
                S.op("act", lambda e: e.activation(out=wt[:, 0:n, 0:qw], in_=psD3, func=AF.Exp, scale=-1.0, bias=self.flagb[:, 0:1]),
                     reads=[("psS", ts), "flagb"], writes=[wkey])
            else:
                S.op("act", lambda e: e.activation(out=wt[:, 0:n, 0:qw], in_=psD3, func=AF.Exp, scale=-1.0),
                     reads=[("psS", ts)], writes=[wkey])
            if cls == "near":
                S.op("dve", lambda e: e.tensor_tensor(out=wt[:, 0, 0:qw], in0=wt[:, 0, 0:qw], in1=self.maskb[:, mi, 0:qw], op=ALU.mult),
                     reads=[wkey, "maskb"], writes=[wkey])
            pend_pv = (wt, wkey, asc)
            yield
        emit_pv(pend_pv, first_pv, True)
        yield
        ob = self.obA[(h // 2) % 2][j]
        obk = ("obA", (h // 2) % 2, j)
        self.copy_op(self.evac_engine(), ob[0:qw, (h % 2) * 64:(h % 2) * 64 + 64], psO[0:qw, :], [okey], [obk + (h % 2,)])
        yield
        if h % 2 == 1:
            pt_ = self.transpose_pe(ob, [obk + (0,), obk + (1,)], qw)
            self.transpose_evac(qw, 8 + h // 2, j, pt_)
            yield

    def dump_h(self):
        S = self.S
        outv = self.outT.rearrange("(c p) t -> p c t", p=P)
        for tg in range(NTG):
            sl = slice(tg * TG, (tg + 1) * TG)
            S.op("sp", lambda e, sl=sl: e.dma_start(out=outv[:, :, sl], in_=self.hT[:, :, sl]),
                 reads=[("hT", c, tg) for c in range(KC)], writes=[("out", tg)], dma=True)
        S.op("sp", None, reads=[("out", tg) for tg in range(NTG)])

    def dump_x(self):
        S = self.S
        outv = self.dbgx.rearrange("(c p) t -> p c t", p=P)
        for tg in range(NTG):
            sl = slice(tg * TG, (tg + 1) * TG)
            S.op("sp", lambda e, sl=sl: e.dma_start(out=outv[:, :, sl], in_=self.xT[:, :, sl]),
                 reads=[("xT", c, tg) for c in range(KC)], writes=[("outx", tg)], dma=True)
        S.op("sp", None, reads=[("outx", tg) for tg in range(NTG)])

    def stop_here(self, name):
        if self.stop != name:
            return False
        self.S.barrier()
        if name.startswith("x_"):
            self.dump_x()
        self.dump_h()
        self.emit_all()
        return True

    def build(self, stop=None):
        S = self.S
        nc = self.nc
        self.stop = stop
        if stop is not None:
            self.dbgx = nc.dram_tensor("dbgx", [D, T], BF16, kind="ExternalOutput").ap()
        self.obA = [[self.sb("obA%d_%d" % (i, j), [P, 128], BF16) for j in range(NQB)] for i in range(2)]
        assert self.sb_off <= 229344, self.sb_off
        print('sbuf end', self.sb_off)
        self.load_consts()
        self.rmsnorm(0)
        if self.stop_here("x_norm0"):
            return nc
        self.proj_qk(self.w_in_ab, 1024, 2, self.KT0, 0)
        if stop == "x_ka":
            S.barrier()
            S.op("sp", lambda e: e.dma_start(out=self.dbgx[0:256, :], in_=self.KT0["loc"][0].ap()[0:256, :]), reads=[], writes=[("outx", 0)], dma=True)
            S.op("sp", None, reads=[("outx", 0)])
            self.dump_h()
            self.emit_all()
            return nc
        self.proj_qk(self.w_in_ab, 2560, 8, self.KT0, 256)
        self.proj_v(self.w_in_ab, 1280, 256, self.V0, 0)
        if stop == "x_va":
            S.barrier()
            S.op("sp", lambda e: e.dma_start(out=self.dbgx[0:256, :], in_=self.V0.ap()[0:T, 0:256].rearrange("t f -> t f")), reads=[], writes=[("outx", 0)], dma=True) if False else None
            for f0 in range(0, 256, 32):
                S.op("sp", lambda e, f0=f0: e.dma_start(out=self.dbgx[f0:f0 + 32, :], in_=self.V0["loc"][0].ap()[:, f0:f0 + 32].rearrange("t f -> f t"), allow_slow_non_contiguous=True), reads=[], writes=[("outx", 0)], dma=True)
            S.op("sp", None, reads=[("outx", 0)])
            self.dump_h()
            self.emit_all()
            return nc
        self.proj_v(self.w_in_ab, 3584, 1024, self.V0, 256)
        self.allgather(self.KT0)
        self.allgather(self.V0)
        self.proj_qk(self.w_in_ab, 0, 8, self.QT0.ap(), 0)
        self.proj_qk(self.w_in_ab, 1536, 8, self.QT0.ap(), 1024, scale=-0.125)
        S.barrier()
        if stop == "x_proj0":
            S.op("sp", lambda e: e.dma_start(out=self.dbgx, in_=self.QT0.ap()), reads=[], writes=[("outx", 0)], dma=True)
            S.op("sp", None, reads=[("outx", 0)])
            self.dump_h()
            self.emit_all()
            return nc
        if stop == "x_kv0":
            S.op("sp", lambda e: e.dma_start(out=self.dbgx[0:640, :], in_=self.KT0["gat"][0].ap()[640:1280, :]), reads=[], writes=[("outx", 0)], dma=True)
            S.op("sp", None, reads=[("outx", 0)])
            self.dump_h()
            self.emit_all()
            return nc
        if stop == "x_attn_a":
            self.attn_a()
            self.stop_here("x_attn_a")
            return nc
        if stop == "x_attn_b":
            self.attn_b()
            self.stop_here("x_attn_b")
            return nc
        self.attn_a()
        self.attn_b()
        S.barrier()
        if self.stop_here("x_attn0"):
            return nc
        self.out_proj(self.w_out_ab)
        if self.stop_here("h_attn0"):
            return nc
        self.rmsnorm(3)
        self.mlp(0)
        if self.stop_here("h_l0"):
            return nc
        self.rmsnorm(1)
        self.proj_qk(self.w_in_c, 2048, 16, self.KT1, 0)
        self.proj_v(self.w_in_c, 4096, 2048, self.V1, 0)
        self.allgather(self.KT1)
        self.allgather(self.V1)
        self.proj_qk(self.w_in_c, 0, 16, self.QT1.ap(), 0)
        S.barrier()
        self.attn_c()
        S.barrier()
        if self.stop_here("x_attn1"):
            return nc
        self.out_proj(self.w_out_c)
        self.rmsnorm(4)
        self.mlp(1)
        if self.stop_here("h_l1"):
            return nc
        self.rmsnorm(2, final=True)
        self.dump_h()
        self.emit_all()
        return nc

    def emit_all(self):
        S = self.S
        nc = self.nc
        with ExitStack() as stack:
            S.finalize(nc, stack)
            block = stack.enter_context(nc.Block())

            @block.tensor
            def _(e):
                S.emit("pe", e)

            @block.scalar
            def _(e):
                S.emit("act", e)

            @block.vector
            def _(e):
                S.emit("dve", e)

            @block.gpsimd
            def _(e):
                S.emit("pool", e)

            @block.sync
            def _(e):
                S.emit("sp", e)


def chunk_of(p):
    return 1 + np.floor_divide(p - 16, 64)


def make_tables(rank):
    base = rank * T
    k = np.arange(128)[:, None]
    q = np.arange(128)[None, :]
    t = {}
    t["kqneg"] = (-(q - k)).astype(np.float32) * np.ones((128, 128), np.float32)
    negd0 = np.zeros((128, NREL), np.float32)
    for rel in range(NREL):
        dq0 = base - 128 * (rel - 8)
        negd0[:, rel] = -float(dq0) if dq0 >= 128 else -1.0e6
    t["negd0"] = negd0

    def posmats(rel):
        dq0 = base - 128 * (rel - 8)
        diff = dq0 + q - k
        return diff

    def absq(rel):
        jj = 8
        g = rel - 8 + jj
        qpos = base + 128 * jj + q + 0 * k
        kpos = 128 * g + k + 0 * q
        return qpos, kpos

    na = np.zeros((128, len(A_NEAR), 128), np.float32)
    ma = np.zeros((128, len(A_NEAR), 128), np.float32)
    for i, rel in enumerate(A_NEAR):
        qpos, kpos = absq(rel)
        qpos = qpos + 128 * 64
        kpos = kpos + 128 * 64
        na[:, i, :] = -np.abs(qpos - kpos)
        qc, kc = chunk_of(qpos), chunk_of(kpos)
        ok = (kc <= qc) & (kc >= qc - 2)
        ma[:, i, :] = np.where(ok, 0.0, NEGBIG)
    t["negdist_a"] = na
    t["mask_a"] = ma
    ma0 = np.zeros((128, NQB, 128), np.float32)
    for j in range(NQB):
        qpos = base + 128 * j + q + 0 * k
        kpos = k + 0 * q
        qc, kc = chunk_of(qpos), chunk_of(kpos)
        ok = (kpos < 16) | ((kpos >= 16) & (kc <= qc) & (kc >= qc - 2))
        ma0[:, j, :] = np.where(ok, 0.0, NEGBIG)
    t["mask_a0"] = ma0
    ncm = np.zeros((128, len(C_NEAR), 128), np.float32)
    mc = np.zeros((128, len(C_NEAR), 128), np.float32)
    for i, rel in enumerate(C_NEAR):
        qpos, kpos = absq(rel)
        qpos = qpos + 128 * 64
        kpos = kpos + 128 * 64
        ncm[:, i, :] = -np.abs(qpos - kpos)
        ok = chunk_of(kpos) <= chunk_of(qpos)
        mc[:, i, :] = np.where(ok, 0.0, NEGBIG)
    t["negdist_c"] = ncm
    t["mask_c"] = mc
    mb = np.zeros((128, len(B_NEAR), 128), np.float32)
    for i, rel in enumerate(B_NEAR):
        diff = posmats(rel)
        mb[:, i, :] = (diff > 0).astype(np.float32)
    t["mask_b"] = mb
    t["flagb"] = np.full((128, 1), NEGBIG if rank == 0 else 0.0, np.float32)
    cst = np.zeros((128, 3, 128), np.float32)
    cst[:, 0, :] = 1.0
    cst[:, 1, :] = (k >= q).astype(np.float32)
    cst[:, 2, :] = np.eye(128, dtype=np.float32)
    t["consts"] = cst
    return t


_NC_CACHE = {}


def get_nc(stop=None):
    key = "main" if stop is None else str(stop)
    if key not in _NC_CACHE:
        b = Builder()
        _NC_CACHE[key] = b.build(stop)
    return _NC_CACHE[key]


def make_in_maps(x, meta_tokens, ab_norm, w_in_ab, attn_sinks, w_out_ab, c_norm, w_in_c, diff_lambda, diff_subln,
                 w_out_c, mlp_norm, w_mlp_in, w_mlp_out, final_norm):
    f = lambda a: np.ascontiguousarray(np.asarray(a, dtype=np.float32))
    x = f(x)
    B = x.shape[0]
    meta = f(meta_tokens)
    gains = np.stack([f(ab_norm)[0], f(c_norm)[0], f(final_norm), f(mlp_norm)[0], f(mlp_norm)[1]], 0)
    gains_l = np.ascontiguousarray(gains.reshape(5, KC, P).transpose(2, 0, 1).reshape(P, 5 * KC))
    sinks = np.ascontiguousarray(np.broadcast_to(f(attn_sinks)[0][None, :], (P, 16)))
    lamv = np.ascontiguousarray(np.broadcast_to(f(diff_lambda)[0].reshape(1, 256), (P, 256)))
    subg = np.ascontiguousarray(np.broadcast_to(f(diff_subln)[0][None, :], (P, 128)))
    shared = {
        "w_in_ab": f(w_in_ab)[0], "w_out_ab": f(w_out_ab)[0], "w_in_c": f(w_in_c)[0], "w_out_c": f(w_out_c)[0],
        "w_mlp_in0": f(w_mlp_in)[0], "w_mlp_in1": f(w_mlp_in)[1], "w_mlp_out0": f(w_mlp_out)[0], "w_mlp_out1": f(w_mlp_out)[1],
        "gains": gains_l, "sinks": sinks, "lamv": lamv, "subg": subg,
    }
    tabs = [make_tables(0), make_tables(1)]
    in_maps = []
    for core in range(8):
        b, r = core // 2, core % 2
        seq = np.zeros((LP, D), np.float32)
        seq[0:16] = meta
        seq[16:16 + 2048] = x[b]
        h0T = np.ascontiguousarray(seq[r * T:(r + 1) * T].T)
        m = dict(shared)
        m["h0T"] = h0T
        m.update(tabs[r])
        in_maps.append(m)
    return in_maps


def assemble(results):
    out = np.zeros((4, 2048, D), np.float32)
    for core in range(8):
        b, r = core // 2, core % 2
        oT = np.asarray(results[core]["outT"])
        rows = oT.T
        pos0 = r * T
        lo = max(pos0, 16)
        hi = min(pos0 + T, 16 + 2048)
        out[b, lo - 16:hi - 16] = rows[lo - pos0:hi - pos0]
    return out


def kernel(**inputs):
    nc = get_nc()
    in_maps = make_in_maps(**inputs)
    res = run_bass_kernel_spmd(nc, in_maps, core_ids=list(range(8)))
    return assemble(res.results)
```

```python
import math
import types
from contextlib import ExitStack

import numpy as np
import concourse.bass as bass
import concourse.mybir as mybir
from concourse.bass_utils import run_bass_kernel_spmd

F32 = mybir.dt.float32
BF16 = mybir.dt.bfloat16
AF = mybir.ActivationFunctionType
ALU = mybir.AluOpType
AX = mybir.AxisListType

P = 128
D = 2048
KC = 16
T = 1088
LP = 2176
NTG = 4
TG = 272
NQB = 9
NGB = 17
DFF = 8192
EPS = 1e-6
NEGBIG = -30000.0
REPLICA_GROUPS = [[0, 1], [2, 3], [4, 5], [6, 7]]
NREL = 18
A_SLOPES = [2.0 ** (-8.0 * (i + 1) / 16) for i in range(16)]
C_SLOPES = A_SLOPES
LAMBDA_INIT = 0.8 - 0.6 * math.exp(-0.3 * 1)

A_NEAR = [6, 7, 8, 9, 15, 16, 17]
C_NEAR = [8, 9, 16, 17]
B_NEAR = [8, 16, 17]
B0IDX = [0, 1, 2, 3, 4, 5, 10, 11, 12]


def qw_of(j):
    return 128 if j < 8 else 64


def _freeze(fn):
    if fn is None or fn.__closure__ is None:
        return fn
    cells = []
    for c in fn.__closure__:
        try:
            cells.append(types.CellType(c.cell_contents))
        except ValueError:
            cells.append(c)
    return types.FunctionType(fn.__code__, fn.__globals__, fn.__name__, fn.__defaults__, tuple(cells))


class Op:
    __slots__ = ("eng", "fn", "dma", "waits", "sig", "sem", "val", "idx", "cc")

    def __init__(self, eng, fn, dma, cc=False):
        self.eng = eng
        self.fn = fn
        self.dma = dma
        self.cc = cc
        self.waits = {}
        self.sig = False
        self.sem = None
        self.val = 0


class Sched:
    ENGS = ("pe", "act", "dve", "pool", "sp")
    NDMASEM = 8

    def __init__(self):
        self.ops = {e: [] for e in self.ENGS}
        self.allops = []
        self.last_w = {}
        self.readers = {}
        self.dma_rr = {"pool": 0, "sp": 0}
        self.dma_last = {}

    def op(self, eng, fn, reads=(), writes=(), dma=False, cc=False):
        o = Op(eng, _freeze(fn), dma, cc)
        o.idx = len(self.allops)
        deps = set()
        for k in reads:
            w = self.last_w.get(k)
            if w is not None:
                deps.add(w)
        for k in writes:
            w = self.last_w.get(k)
            if w is not None:
                deps.add(w)
            for r in self.readers.get(k, ()):
                deps.add(r)
        if dma:
            slot = (eng, self.dma_rr[eng] % self.NDMASEM)
            self.dma_rr[eng] += 1
            o.sem = slot
            prev = self.dma_last.get(slot)
            if prev is not None:
                deps.add(prev)
            self.dma_last[slot] = o
        for d in deps:
            if d is o:
                continue
            if d.eng == "pe" and eng == "pe" and not d.dma:
                continue
            o.waits[d.idx] = d
            d.sig = True
        for k in reads:
            self.readers.setdefault(k, []).append(o)
        for k in writes:
            self.last_w[k] = o
            self.readers[k] = []
        self.ops[eng].append(o)
        self.allops.append(o)
        return o

    def barrier(self):
        last = []
        for e in self.ENGS:
            for o_ in reversed(self.ops[e]):
                if o_.fn is not None:
                    last.append(o_)
                    break
        outstanding = [o for o in self.dma_last.values()]
        key = ("__barrier__", len(self.allops))
        for e in self.ENGS:
            o = Op(e, None, False)
            o.idx = len(self.allops)
            for d in last + outstanding:
                if d.eng == e and not d.dma:
                    continue
                o.waits[d.idx] = d
                d.sig = True
            self.ops[e].append(o)
            self.allops.append(o)
        self.readers = {}

    def finalize(self, nc, stack):
        cnt = {e: 0 for e in self.ENGS}
        self.esem = {e: stack.enter_context(nc.semaphore("es_" + e)) for e in ("pe", "act", "dve", "pool", "sp")}
        self.dsem = {}
        for e in ("pool", "sp"):
            for i in range(self.NDMASEM):
                self.dsem[(e, i)] = stack.enter_context(nc.semaphore("ds_%s%d" % (e, i)))
        self.ccsem = stack.enter_context(nc.semaphore("ccsem"))
        dcnt = {}
        cccnt = 0
        for o in self.allops:
            if o.dma:
                dcnt[o.sem] = dcnt.get(o.sem, 0) + 16
                o.val = dcnt[o.sem]
                o.sem = self.dsem[o.sem]
                o.sig = True
            elif o.cc:
                cccnt += 1
                o.val = cccnt
                o.sem = self.ccsem
                o.sig = True
            elif o.sig:
                cnt[o.eng] += 1
                o.val = cnt[o.eng]
                o.sem = self.esem[o.eng]

    def emit(self, eng, e):
        waited = {}
        for o in self.ops[eng]:
            need = {}
            for d in o.waits.values():
                k = id(d.sem)
                if need.get(k, (None, 0))[1] < d.val:
                    need[k] = (d.sem, d.val)
            for k, (sem, val) in need.items():
                if waited.get(k, 0) >= val:
                    continue
                waited[k] = val
                e.wait_ge(sem, val)
            if o.fn is None:
                continue
            ins = o.fn(e)
            if o.sig:
                if o.dma:
                    ins.then_inc(o.sem, 16)
                elif o.cc:
                    ins.then_inc(o.sem)
                else:
                    ins.then_inc(o.sem, 1)


class Builder:
    def __init__(self, debug=None):
        self.debug = debug
        self.nc = nc = bass.Bass("TRN2", target_bir_lowering=False)
        self.S = Sched()
        self.sb_off = 16512
        self.psrr = 0
        self.uid = 0
        self.evrr = 0
        self.declare_io()
        self.alloc()

    def declare_io(self):
        nc = self.nc

        def inp(name, shape, dt=F32):
            return nc.dram_tensor(name, list(shape), dt, kind="ExternalInput").ap()

        self.h0T = inp("h0T", [D, T])
        self.w_in_ab = inp("w_in_ab", [D, 4608])
        self.w_out_ab = inp("w_out_ab", [D, D])
        self.w_in_c = inp("w_in_c", [D, 6144])
        self.w_out_c = inp("w_out_c", [D, D])
        self.w_mlp_in = [inp("w_mlp_in%d" % l, [D, DFF]) for l in range(2)]
        self.w_mlp_out = [inp("w_mlp_out%d" % l, [DFF, D]) for l in range(2)]
        self.gains_d = inp("gains", [P, 5 * KC])
        self.sinks_d = inp("sinks", [P, 16])
        self.lam_d = inp("lamv", [P, 256])
        self.subg_d = inp("subg", [P, 128])
        self.kqneg_d = inp("kqneg", [P, 128])
        self.negd0_d = inp("negd0", [P, NREL])
        self.negdist_a_d = inp("negdist_a", [P, len(A_NEAR), 128])
        self.mask_a_d = inp("mask_a", [P, len(A_NEAR), 128])
        self.mask_a0_d = inp("mask_a0", [P, NQB, 128])
        self.negdist_c_d = inp("negdist_c", [P, len(C_NEAR), 128])
        self.mask_c_d = inp("mask_c", [P, len(C_NEAR), 128])
        self.mask_b_d = inp("mask_b", [P, len(B_NEAR), 128])
        self.flagb_d = inp("flagb", [P, 1])
        self.consts_d = inp("consts", [P, 3, 128])
        self.outT = nc.dram_tensor("outT", [D, T], F32, kind="ExternalOutput").ap()
        self.QT0 = nc.dram_tensor("QT0", [2048, T], BF16)
        self.KT0 = self.chunked("KT0", [0, 640, 1280], True)
        self.V0 = self.chunked("V0", [0, 768, 1280], False)
        self.QT1 = nc.dram_tensor("QT1", [2048, T], BF16)
        self.KT1 = self.chunked("KT1", [0, 512, 1024, 1536, 2048], True)
        self.V1 = self.chunked("V1", [0, 512, 1024, 1536, 2048], False)
        if self.debug:
            self.dbg = nc.dram_tensor("dbg", list(self.debug["shape"]), F32, kind="ExternalOutput").ap()

    def chunked(self, name, bounds, is_k):
        nc = self.nc
        ch = {"name": name, "bounds": bounds, "is_k": is_k, "loc": [], "gat": [], "keys": [[] for _ in bounds[1:]], "done": [False] * (len(bounds) - 1)}
        for i in range(len(bounds) - 1):
            w = bounds[i + 1] - bounds[i]
            if is_k:
                ch["loc"].append(nc.dram_tensor("%s_l%d" % (name, i), [w, T], BF16))
                ch["gat"].append(nc.dram_tensor("%s_g%d" % (name, i), [2 * w, T], BF16))
            else:
                ch["loc"].append(nc.dram_tensor("%s_l%d" % (name, i), [T, w], BF16))
                ch["gat"].append(nc.dram_tensor("%s_g%d" % (name, i), [2 * T, w], BF16))
        return ch

    @staticmethod
    def ch_find(ch, f):
        b = ch["bounds"]
        for i in range(len(b) - 1):
            if b[i] <= f < b[i + 1]:
                return i, f - b[i], b[i + 1] - b[i]
        raise ValueError(f)

    def sb(self, name, shape, dt, off=None):
        n = 1
        for s in shape[1:]:
            n *= s
        nbytes = n * (4 if dt == F32 else 2)
        nbytes = (nbytes + 31) // 32 * 32
        if off is None:
            off = self.sb_off
            self.sb_off += nbytes
        t = self.nc.alloc_sbuf_tensor_at(name, list(shape), dt, offset=off)
        return t

    def alloc(self):
        nc = self.nc
        self.hT = self.sb("hT", [P, KC, T], F32)
        self.xT = self.sb("xT", [P, KC, T], BF16)
        self.gains = self.sb("gains", [P, 5 * KC], F32)
        self.consts_f = self.sb("consts_f", [P, 3, 128], F32)
        self.consts = self.sb("consts_b", [P, 3, 128], BF16)
        self.epsc = self.sb("epsc", [P, 1], F32)
        self.onec = self.sb("onec", [P, 1], F32)
        self.flagb = self.sb("flagb", [P, 1], F32)
        self.nflagb = self.sb("nflagb", [P, 1], F32)
        self.sinks = self.sb("sinks", [P, 16], F32)
        self.esink = self.sb("esink", [P, 16], F32)
        self.lamv = self.sb("lamv", [P, 256], F32)
        self.lamt = self.sb("lamt", [P, 8], F32)
        self.subg = self.sb("subg", [P, 128], F32)
        self.kqneg = self.sb("kqneg", [P, 128], F32)
        self.negd0 = self.sb("negd0", [P, NREL], F32)
        base = self.sb_off
        self.rstd = [self.sb("rstd%d" % i, [P, TG], F32) for i in range(2)]
        self.lnv = [self.sb("lnv%d" % i, [P, TG], F32) for i in range(2)]
        self.sqb = [self.sb("sqb%d" % i, [P, TG], BF16) for i in range(3)]
        self.stage = [self.sb("stage%d" % i, [P, T], BF16) for i in range(2)]
        self.vstage = [self.sb("vstage%d" % i, [P, 256], BF16) for i in range(3)]
        self.relu_t = [self.sb("relu%d" % i, [P, TG], F32) for i in range(3)]
        self.wbuf = [self.sb("wbuf%d" % i, [P, 4096], BF16) for i in range(2)]
        self.hidT = self.sb("hidT", [P, 8, T], BF16)
        lin_end = self.sb_off
        self.sb_off = base
        self.kT = [[self.sb("kT%d_%d" % (i, c), [64, LP], BF16) for c in range(2)] for i in range(2)]
        self.qT = [[self.sb("qT%d_%d" % (i, c), [64, T], BF16) for c in range(2)] for i in range(2)]
        self.Vs = [self.sb("Vs%d" % i, [P, NGB, 130], BF16) for i in range(2)]
        bH_ = self.sb("biasH0", [P, NREL, 128], F32)
        self.biasH = [bH_, bH_]
        self.expH_off = self.sb_off
        self.expH = [self.sb("expH%d" % i, [P, NREL, 128], BF16) for i in range(2)]
        self.tabA = self.sb("tabA", [P, 7, 128], F32)
        self.tabB = self.sb("tabB", [P, 7, 128], F32)
        self.tabA0 = self.sb("tabA0", [P, NQB, 128], F32)
        self.maskb_f = self.sb("maskb_f", [P, 3, 128], F32)
        self.maskb = self.sb("maskb", [P, 3, 128], BF16)
        NS = 3
        eoff = self.expH_off
        self.tmpS2 = [[self.sb("tmpS%d_0" % t, [P, 4, 128], F32, off=eoff + t * 2048)] * 2 for t in range(NS)]
        self.pT2 = [[self.sb("pT%d_%d" % (t, i), [P, 4, 128], BF16) for i in range(3)] for t in range(NS)]
        self.spT2 = [[self.sb("spT%d_%d" % (t, i), [P, 4, 128], BF16) for i in range(2)] for t in range(NS)]
        self.R32s = [self.sb("R32_%d" % t, [P, 128], F32) for t in range(NS)]
        self.Rtmps = [self.sb("Rtmp_%d" % t, [P, 128], F32) for t in range(NS)]
        self.Rbfs = [[self.sb("Rbf%d_%d" % (t, i), [P, 128], BF16) for i in range(2)] for t in range(NS)]
        self.ofin3 = [self.sb("ofin%d" % t, [P, 128], F32) for t in range(NS)]
        self.ob3 = [self.sb("ob%d" % t, [P, 128], BF16) for t in range(NS)]
        self.small3 = [self.sb("small%d" % t, [P, 8], F32) for t in range(NS)]
        self.junk3 = [self.sb("junk%d" % t, [P, 128], F32) for t in range(NS)]
        self.junk = self.junk3[0]
        self.o0buf = [self.sb("o0buf%d" % t, [P, 132], F32) for t in range(NS)]
        self.tmprr = [0] * NS
        self.sprr = [0] * NS
        self.rbrr = [0] * NS
        att_end = self.sb_off
        self.sb_off = max(lin_end, att_end)
        assert self.sb_off <= 229344, self.sb_off
        self.ps = [nc.alloc_psum_tensor("ps%d" % i, [P, 512], F32) for i in range(6)]
        self.psTs = [nc.alloc_psum_tensor("psT%d" % i, [P, 1024], BF16) for i in range(2)]

    def u(self, name):
        self.uid += 1
        return (name, self.uid)

    def next_ps(self, n=6):
        i = self.psrr % n
        self.psrr += 1
        return i

    def load_consts(self):
        S = self.S
        S.op("sp", lambda e: e.dma_start(out=self.gains[:], in_=self.gains_d), writes=["gains"], dma=True)
        S.op("sp", lambda e: e.dma_start(out=self.consts_f[:], in_=self.consts_d), writes=["consts_f"], dma=True)
        S.op("sp", lambda e: e.dma_start(out=self.flagb[:], in_=self.flagb_d), writes=["flagb"], dma=True)
        S.op("sp", lambda e: e.dma_start(out=self.sinks[:], in_=self.sinks_d), writes=["sinks"], dma=True)
        S.op("sp", lambda e: e.dma_start(out=self.lamv[:], in_=self.lam_d), writes=["lamv"], dma=True)
        S.op("sp", lambda e: e.dma_start(out=self.subg[:], in_=self.subg_d), writes=["subg"], dma=True)
        S.op("sp", lambda e: e.dma_start(out=self.kqneg[:], in_=self.kqneg_d), writes=["kqneg"], dma=True)
        S.op("sp", lambda e: e.dma_start(out=self.negd0[:], in_=self.negd0_d), writes=["negd0"], dma=True)
        for tg in range(NTG):
            sl = slice(tg * TG, (tg + 1) * TG)
            S.op("sp", lambda e, sl=sl: e.dma_start(out=self.hT[:, :, sl],
                                                    in_=self.h0T.rearrange("(c p) t -> p c t", p=P)[:, :, sl]),
                 writes=[("hT", c, tg) for c in range(KC)], dma=True)
        S.op("dve", lambda e: e.tensor_copy(out=self.consts[:], in_=self.consts_f[:]), reads=["consts_f"], writes=["consts"])
        S.op("dve", lambda e: e.memset(self.epsc[:], EPS), writes=["epsc"])
        S.op("dve", lambda e: e.memset(self.onec[:], 1.0), writes=["onec"])
        S.op("dve", lambda e: e.tensor_scalar(out=self.nflagb[:], in0=self.flagb[:], scalar1=-1.0, scalar2=None, op0=ALU.mult),
             reads=["flagb"], writes=["nflagb"])
        S.op("act", lambda e: e.activation(out=self.esink[:], in_=self.sinks[:], func=AF.Exp), reads=["sinks"], writes=["esink"])
        S.op("dve", lambda e: e.tensor_tensor(out=self.junk[:, 0:64], in0=self.lamv[:, 0:64], in1=self.lamv[:, 64:128], op=ALU.mult),
             reads=["lamv"], writes=["junk"])
        S.op("dve", lambda e: e.tensor_reduce(out=self.lamt[:, 0:1], in_=self.junk[:, 0:64], axis=AX.X, op=ALU.add),
             reads=["junk"], writes=["lamt0"])
        S.op("dve", lambda e: e.tensor_tensor(out=self.junk[:, 64:128], in0=self.lamv[:, 128:192], in1=self.lamv[:, 192:256], op=ALU.mult),
             reads=["lamv"], writes=["junk2"])
        S.op("dve", lambda e: e.tensor_reduce(out=self.lamt[:, 1:2], in_=self.junk[:, 64:128], axis=AX.X, op=ALU.add),
             reads=["junk2"], writes=["lamt1"])
        S.op("act", lambda e: e.activation(out=self.lamt[:, 2:4], in_=self.lamt[:, 0:2], func=AF.Exp),
             reads=["lamt0", "lamt1"], writes=["lamt23"])
        S.op("dve", lambda e: e.tensor_tensor(out=self.lamt[:, 4:5], in0=self.lamt[:, 3:4], in1=self.lamt[:, 2:3], op=ALU.subtract),
             reads=["lamt23"], writes=["lamt4"])
        S.op("dve", lambda e: e.tensor_scalar(out=self.lamt[:, 5:6], in0=self.lamt[:, 4:5], scalar1=-LAMBDA_INIT, scalar2=None, op0=ALU.add),
             reads=["lamt4"], writes=["nlam"])
        S.op("dve", lambda e: e.tensor_scalar(out=self.subg[:], in0=self.subg[:], scalar1=1.0 - LAMBDA_INIT, scalar2=None, op0=ALU.mult),
             reads=["subg"], writes=["subg"])

    def rmsnorm(self, gi, dst_keyname="xT", final=False):
        S = self.S
        ones = self.consts[:, 0, :]
        for tg in range(NTG):
            sl = slice(tg * TG, (tg + 1) * TG)
            pi = self.next_ps()
            ps = self.ps[pi]
            for c in range(KC):
                sq = self.sqb[c % 3]
                S.op("act", lambda e, sq=sq, c=c, sl=sl: e.activation(out=sq[:], in_=self.hT[:, c, sl], func=AF.Square),
                     reads=[("hT", c, tg)], writes=[("sqb", c % 3)])
                S.op("pe", lambda e, sq=sq, c=c, ps=ps: e.matmul(ps[:, 0:TG], lhsT=ones, rhs=sq[:], start=(c == 0), stop=(c == KC - 1)),
                     reads=[("sqb", c % 3), "consts"], writes=[("ps", pi)])
            lnv = self.lnv[tg % 2]
            rstd = self.rstd[tg % 2]
            S.op("act", lambda e, ps=ps, lnv=lnv: e.activation(out=lnv[:], in_=ps[:, 0:TG], func=AF.Ln, bias=self.epsc[:, 0:1], scale=1.0 / D),
                 reads=[("ps", pi), "epsc"], writes=[("lnv", tg % 2)])
            S.op("act", lambda e, lnv=lnv, rstd=rstd: e.activation(out=rstd[:], in_=lnv[:], func=AF.Exp, scale=-0.5),
                 reads=[("lnv", tg % 2)], writes=[("rstd", tg % 2)])
            for c in range(KC):
                gcol = self.gains[:, gi * KC + c: gi * KC + c + 1]
                if final:
                    S.op("dve", lambda e, c=c, sl=sl, gcol=gcol, rstd=rstd: e.scalar_tensor_tensor(
                        out=self.hT[:, c, sl], in0=self.hT[:, c, sl], scalar=gcol, in1=rstd[:], op0=ALU.mult, op1=ALU.mult),
                        reads=[("hT", c, tg), ("rstd", tg % 2), "gains"], writes=[("hT", c, tg)])
                else:
                    S.op("dve", lambda e, c=c, sl=sl, gcol=gcol, rstd=rstd: e.scalar_tensor_tensor(
                        out=self.xT[:, c, sl], in0=self.hT[:, c, sl], scalar=gcol, in1=rstd[:], op0=ALU.mult, op1=ALU.mult),
                        reads=[("hT", c, tg), ("rstd", tg % 2), "gains"], writes=[("xT", c, tg)])

    def load_w(self, W, r0, nkc, c0, ncols):
        S = self.S
        self.wrr = getattr(self, "wrr", 0)
        slot = self.wrr % 2
        self.wrr += 1
        wb = self.wbuf[slot]
        view = wb[:, 0:nkc * ncols].rearrange("p (k n) -> p k n", n=ncols)
        src = W[r0:r0 + nkc * P, c0:c0 + ncols].rearrange("(k p) n -> p k n", p=P)
        half = nkc // 2
        S.op("pool", lambda e: e.dma_start(out=view[:, 0:half, :], in_=src[:, 0:half, :]), writes=[("wbuf", slot, 0)], dma=True)
        S.op("pool", lambda e: e.dma_start(out=view[:, half:nkc, :], in_=src[:, half:nkc, :]), writes=[("wbuf", slot, 1)], dma=True)
        return slot, view

    def evac_engine(self):
        self.evrr += 1
        return "act" if self.evrr % 2 == 0 else "dve"

    def copy_op(self, eng, out, in_, reads, writes, scale=None):
        S = self.S
        if eng == "act":
            if scale is None:
                S.op("act", lambda e: e.activation(out=out, in_=in_, func=AF.Copy), reads=reads, writes=writes)
            else:
                S.op("act", lambda e: e.activation(out=out, in_=in_, func=AF.Copy, scale=scale), reads=reads, writes=writes)
        else:
            if scale is None:
                S.op("dve", lambda e: e.tensor_copy(out=out, in_=in_), reads=reads, writes=writes)
            else:
                S.op("dve", lambda e: e.tensor_scalar(out=out, in0=in_, scalar1=scale, scalar2=None, op0=ALU.mult), reads=reads, writes=writes)

    def linear_fm(self, W, r0, nkc, c0, ncols_total, rhs_buf, rhs_key, kc0, evac, wcols=256):
        S = self.S
        ntile = ncols_total // wcols
        for wt in range(ntile):
            slot, view = self.load_w(W, r0, nkc, c0 + wt * wcols, wcols)
            for o in range(wcols // P):
                ot = wt * (wcols // P) + o
                for tg in range(NTG):
                    sl = slice(tg * TG, (tg + 1) * TG)
                    pi = self.next_ps()
                    ps = self.ps[pi]
                    for kc in range(nkc):
                        S.op("pe", lambda e, ps=ps, view=view, kc=kc, o=o, sl=sl: e.matmul(
                            ps[:, 0:TG], lhsT=view[:, kc, o * P:(o + 1) * P], rhs=rhs_buf[:, kc0 + kc, sl],
                            start=(kc == 0), stop=(kc == nkc - 1)),
                            reads=[("wbuf", slot, 0 if kc < nkc // 2 else 1), (rhs_key, kc0 + kc, tg)], writes=[("ps", pi)])
                    evac(ot, tg, ps, pi)

    def proj_qk(self, W, c0, ntiles, dst, drow0, scale=None):
        S = self.S

        def evac(ot, tg, ps, pi):
            st = self.stage[ot % 2]
            sl = slice(tg * TG, (tg + 1) * TG)
            self.copy_op(self.evac_engine(), st[:, sl], ps[:, 0:TG], [("ps", pi)], [("stage", ot % 2, tg)], scale=scale)
            if tg == NTG - 1:
                row = drow0 + ot * P
                if isinstance(dst, dict):
                    ci, w0, _ = self.ch_find(dst, row)
                    dap = dst["loc"][ci].ap()[w0:w0 + P, :]
                    key = (dst["name"], row)
                else:
                    dap = dst[row:row + P, :]
                    key = (dst.tensor.name, row)
                S.op("sp", lambda e: e.dma_start(out=dap, in_=st[:]),
                     reads=[("stage", ot % 2, t) for t in range(NTG)], writes=[key], dma=True)
                if isinstance(dst, dict):
                    dst["keys"][ci].append(key)
                    if row + P == dst["bounds"][ci + 1]:
                        self.emit_cc(dst, ci)

        self.linear_fm(W, 0, KC, c0, ntiles * P, self.xT, "xT", 0, evac)

    def proj_v(self, W, c0, ncols, dst, dcol0):
        S = self.S
        for wt in range(ncols // 256):
            slot, view = self.load_w(W, 0, KC, c0 + wt * 256, 256)
            for tb in range(NQB):
                qw = qw_of(tb)
                tsl = slice(tb * P, tb * P + qw)
                pi = self.next_ps()
                ps = self.ps[pi]
                for kc in range(KC):
                    S.op("pe", lambda e, ps=ps, view=view, kc=kc, tsl=tsl, qw=qw: e.matmul(
                        ps[0:qw, 0:256], lhsT=self.xT[:, kc, tsl], rhs=view[:, kc, :], start=(kc == 0), stop=(kc == KC - 1)),
                        reads=[("wbuf", slot, 0 if kc < 8 else 1)] + [("xT", kc, t) for t in range(NTG)], writes=[("ps", pi)])
                self.vsrr = getattr(self, "vsrr", 0) + 1
                vi = self.vsrr % 3
                vs = self.vstage[vi]
                self.copy_op(self.evac_engine(), vs[0:qw, :], ps[0:qw, 0:256], [("ps", pi)], [("vstage", vi)])
                ci, w0, _ = self.ch_find(dst, dcol0 + wt * 256)
                dap = dst["loc"][ci].ap()[tb * P: tb * P + qw, w0:w0 + 256]
                vkey = (dst["name"], "v", tb, dcol0 + wt * 256)
                S.op("sp", lambda e, vs=vs, qw=qw, dap=dap: e.dma_start(out=dap, in_=vs[0:qw, :]),
                     reads=[("vstage", vi)], writes=[vkey], dma=True)
                dst["keys"][ci].append(vkey)
            if dcol0 + (wt + 1) * 256 == dst["bounds"][ci + 1]:
                self.emit_cc(dst, ci)

    def emit_cc(self, ch, ci):
        S = self.S
        src, dst = ch["loc"][ci], ch["gat"][ci]
        S.op("pool", lambda e: e.collective_compute("AllGather", ALU.bypass, replica_groups=REPLICA_GROUPS,
                                                    ins=[src.ap().opt()], outs=[dst.ap().opt()]),
             reads=list(ch["keys"][ci]), writes=[("cc", ch["name"], ci)], cc=True)
        ch["done"][ci] = True

    def allgather(self, ch):
        assert all(ch["done"]), ch["name"]

    def add_into_h(self, ot, tg, ps, pi):
        sl = slice(tg * TG, (tg + 1) * TG)
        self.S.op("dve", lambda e: e.tensor_tensor(out=self.hT[:, ot, sl], in0=self.hT[:, ot, sl], in1=ps[:, 0:TG], op=ALU.add),
                  reads=[("ps", pi), ("hT", ot, tg)], writes=[("hT", ot, tg)])

    def out_proj(self, W):
        self.linear_fm(W, 0, KC, 0, D, self.xT, "xT", 0, self.add_into_h)

    def mlp(self, l):
        S = self.S
        W1 = self.w_mlp_in[l]
        W2 = self.w_mlp_out[l]
        for fc in range(8):
            def evac1(ot, tg, ps, pi):
                sl = slice(tg * TG, (tg + 1) * TG)
                self.rrr = getattr(self, "rrr", 0) + 1
                ri = self.rrr % 3
                rt = self.relu_t[ri]
                S.op("act", lambda e: e.activation(out=rt[:], in_=ps[:, 0:TG], func=AF.Relu), reads=[("ps", pi)], writes=[("relu", ri)])
                S.op("dve", lambda e: e.tensor_tensor(out=self.hidT[:, ot, sl], in0=rt[:], in1=rt[:], op=ALU.mult),
                     reads=[("relu", ri)], writes=[("hidT", ot, tg)])
            self.linear_fm(W1, 0, KC, fc * 1024, 1024, self.xT, "xT", 0, evac1)
            self.linear_fm(W2, fc * 1024, 8, 0, D, self.hidT, "hidT", 0, self.add_into_h, wcols=512)

    def load_k(self, slot, c, ch, row):
        S = self.S
        kt = self.kT[slot][c]
        ci, w0, w = self.ch_find(ch, row)
        g = ch["gat"][ci].ap()
        for r in range(2):
            S.op("sp", lambda e, r=r: e.dma_start(out=kt[:, r * T:(r + 1) * T], in_=g[r * w + w0: r * w + w0 + 64, :]),
                 reads=[("cc", ch["name"], ci)], writes=[("kT", slot, c, r)], dma=True)

    def load_q(self, slot, c, QTd, row):
        S = self.S
        qt = self.qT[slot][c]
        S.op("sp", lambda e: e.dma_start(out=qt[:], in_=QTd.ap()[row:row + 64, :]),
             reads=[(QTd.name, (row // P) * P)], writes=[("qT", slot, c)], dma=True)

    def load_v(self, slot, ch, col, ncol):
        S = self.S
        vs = self.Vs[slot]
        ci, w0, w = self.ch_find(ch, col)
        src = ch["gat"][ci].ap().rearrange("(g p) f -> p g f", p=P)
        for (g0, g1) in ((0, 9), (9, NGB)):
            S.op("sp", lambda e, g0=g0, g1=g1: e.dma_start(out=vs[:, g0:g1, 0:ncol], in_=src[:, g0:g1, w0:w0 + ncol]),
                 reads=[("cc", ch["name"], ci)], writes=[("Vs", slot, g0)], dma=True)

    def set_v_ones(self, slot, col):
        self.S.op("dve", lambda e: e.memset(self.Vs[slot][:, :, col:col + 1], 1.0), writes=[("Vs1", slot)],
                  reads=[])

    def transpose_out(self, obi, qw, ftile, j):
        S = self.S
        self.ptrr = getattr(self, "ptrr", 0) + 1
        pt = self.ptrr % 4
        ident = self.consts[:, 2, :]
        tsl = slice(j * P, j * P + qw)
        tgs = sorted(set([(j * P) // TG, (j * P + qw - 1) // TG]))
        S.op("pe", lambda e: e.transpose(self.psT[:, pt * 128: pt * 128 + qw], self.ob[obi][0:qw, :], ident[0:qw, 0:qw]),
             reads=[("ob", obi), "consts"], writes=[("psT", pt)])
        self.copy_op(self.evac_engine(), self.xT[:, ftile, tsl], self.psT[:, pt * 128: pt * 128 + qw], [("psT", pt)],
                     [("xT", ftile, t) for t in tgs])

    def run_tasks(self, factories, nslots):
        pending = list(factories)
        active = {}
        free = list(range(nslots))
        while pending or active:
            while pending and free:
                sl = free.pop(0)
                active[sl] = pending.pop(0)(sl)
            for sl in sorted(active.keys()):
                try:
                    next(active[sl])
                except StopIteration:
                    del active[sl]
                    free.append(sl)

    @staticmethod
    def split_groups(glist, split0=False, maxn=4):
        groups = []
        cur = []
        for g in glist:
            if cur and (g != cur[-1] + 1 or len(cur) == maxn or (split0 and cur[-1] == 0)):
                groups.append(cur)
                cur = []
            cur.append(g)
        if cur:
            groups.append(cur)
        return groups

    def softmax_task(self, ts, j, glist, bias_ap_of, kts, qts, vs, vcols, split0, fin_steps):
        S = self.S
        qw = qw_of(j)
        qsl = slice(j * P, j * P + qw)
        ncomp = len(kts)
        groups = self.split_groups(glist, split0)
        psS = self.ps[ts]
        psOb = self.ps[3 + ts][:, 0:vcols + 1]
        if ncomp == 1:
            psO = [psOb]
            okeys = [("psO", ts)]
        else:
            psO = [self.o0buf[ts][:, 0:vcols + 1], psOb]
            okeys = [("o0buf", ts), ("psO", ts)]
        seq = [(c, grp) for c in range(ncomp) for grp in groups]
        first = [True] * ncomp
        lastgrp = groups[-1]
        pend = None
        pend2 = None
        for si in range(len(seq) + 2):
            item = None
            if si < len(seq):
                c, grp = seq[si]
                n = len(grp)
                kt, ktkeys = kts[c]
                qt, qtkeys = qts[c]
                for i, g in enumerate(grp):
                    S.op("pe", lambda e, i=i, g=g: e.matmul(psS[:, i * 128: i * 128 + qw], lhsT=kt[:, g * P:(g + 1) * P], rhs=qt[:, qsl],
                                                            start=True, stop=True),
                         reads=ktkeys + qtkeys, writes=[("psS", ts)])
                self.tmprr[ts] += 1
                tb = self.tmprr[ts] % 3
                pt = self.pT2[ts][tb]
                b_ap, bkeys = bias_ap_of(grp, qw)
                p0 = self.spT2[ts][tb % 2]
                S.op("act", lambda e: e.activation(out=p0[:, 0:n, 0:qw], in_=psS[:, 0:n * 128].rearrange("p (n q) -> p n q", q=128)[:, :, 0:qw],
                                                   func=AF.Exp, scale=0.125),
                     reads=[("psS", ts)], writes=[("spT", ts, tb % 2)])
                S.op("dve", lambda e: e.tensor_tensor(out=pt[:, 0:n, 0:qw], in0=p0[:, 0:n, 0:qw], in1=b_ap, op=ALU.mult),
                     reads=[("spT", ts, tb % 2)] + bkeys, writes=[("pT", ts, tb)])
                item = (c, grp, pt, tb)
            if pend2 is not None:
                pc, pgrp, ppt, ptb = pend2
                for i, g in enumerate(pgrp):
                    st = first[pc]
                    first[pc] = False
                    sp_ = (pgrp is lastgrp and i == len(pgrp) - 1)
                    S.op("pe", lambda e, i=i, g=g, st=st, sp_=sp_: e.matmul(
                        psOb[0:qw, :], lhsT=ppt[:, i, 0:qw], rhs=vs[0][:, g, 0:vcols + 1], start=st, stop=sp_),
                        reads=[("pT", ts, ptb)] + vs[1], writes=[("psO", ts)])
                if ncomp == 2 and pc == 0 and pgrp is lastgrp:
                    self.copy_op(self.evac_engine(), self.o0buf[ts][0:qw, 0:vcols + 1], psOb[0:qw, :], [("psO", ts)], [("o0buf", ts)])
            pend2 = pend
            pend = item
            yield
        for step in fin_steps(ts, psO, okeys):
            step()
            yield

    def attn_c(self):
        S = self.S
        S.op("sp", lambda e: e.dma_start(out=self.tabA[:, 0:4, :], in_=self.negdist_c_d), writes=["tabA"], dma=True)
        S.op("sp", lambda e: e.dma_start(out=self.tabB[:, 0:4, :], in_=self.mask_c_d), writes=["tabB"], dma=True)
        for s_ in range(2):
            self.set_v_ones(s_, 128)
        facts = []
        for h in range(16):
            for j in range(NQB):
                facts.append(lambda ts, h=h, j=j: self.task_c(ts, h, j))
        self.run_tasks(facts, 3)

    def head_pre_c(self, h):
        S = self.S
        slot = h % 2
        for c in range(2):
            self.load_k(slot, c, self.KT1, h * 128 + c * 64)
            self.load_q(slot, c, self.QT1, h * 128 + c * 64)
        self.load_v(slot, self.V1, h * 128, 128)
        bH = self.biasH[slot]
        slope = C_SLOPES[h]
        for rel in range(NREL):
            if rel in C_NEAR:
                i = C_NEAR.index(rel)
                S.op("dve", lambda e: e.scalar_tensor_tensor(out=bH[:, rel, :], in0=self.tabA[:, i, :], scalar=slope,
                                                             in1=self.tabB[:, i, :], op0=ALU.mult, op1=ALU.add),
                     reads=["tabA", "tabB"], writes=[("biasH", 0, rel)])
            else:
                S.op("dve", lambda e: e.tensor_scalar(out=bH[:, rel, :], in0=self.kqneg[:], scalar1=self.negd0[:, rel:rel + 1],
                                                      scalar2=slope, op0=ALU.add, op1=ALU.mult),
                     reads=["kqneg", "negd0"], writes=[("biasH", 0, rel)])
        eH = self.expH[slot]
        S.op("act", lambda e: e.activation(out=eH[:], in_=bH[:], func=AF.Exp),
             reads=[("biasH", 0, r) for r in range(NREL)], writes=[("expH", slot, r) for r in range(NREL)])

    def task_c(self, ts, h, j):
        if j == 0:
            self.head_pre_c(h)
        slot = h % 2
        bH = self.biasH[slot]
        kts = [(self.kT[slot][c], [("kT", slot, c, 0), ("kT", slot, c, 1)]) for c in range(2)]
        qts = [(self.qT[slot][c], [("qT", slot, c)]) for c in range(2)]
        vs = (self.Vs[slot], [("Vs", slot, 0), ("Vs", slot, 9), ("Vs1", slot)])
        gmax = min(16, 9 + j)
        glist = list(range(0, gmax + 1))

        eH = self.expH[slot]

        def bias_ap_of(grp, qw):
            r0 = grp[0] - j + 8
            r1 = grp[-1] - j + 8
            return eH[:, r0:r1 + 1, 0:qw], [("expH", slot, r) for r in range(r0, r1 + 1)]

        def fin_steps(ts, psO, okeys):
            return self.fin_c_steps(ts, h, j, psO, okeys)

        return self.softmax_task(ts, j, glist, bias_ap_of, kts, qts, vs, 128, False, fin_steps)

    def fin_c_steps(self, ts, h, j, psO, okeys):
        S = self.S
        qw = qw_of(j)
        sm = self.small3[ts]
        of = self.ofin3[ts]
        ob = self.ob3[ts]
        jk = self.junk3[ts]
        kk = ("fin", ts)
        steps = []
        A = steps.append
        A(lambda: S.op("dve", lambda e: e.reciprocal(out=sm[0:qw, 0:1], in_=psO[0][0:qw, 128:129]), reads=[okeys[0]], writes=[(kk, 0)]))
        A(lambda: S.op("dve", lambda e: e.reciprocal(out=sm[0:qw, 1:2], in_=psO[1][0:qw, 128:129]), reads=[okeys[1]], writes=[(kk, 1)]))
        A(lambda: S.op("dve", lambda e: e.tensor_tensor(out=sm[0:qw, 2:3], in0=sm[0:qw, 1:2], in1=self.lamt[0:qw, 5:6], op=ALU.mult),
                       reads=[(kk, 1), "nlam"], writes=[(kk, 2)]))
        A(lambda: S.op("act", lambda e: e.activation(out=of[0:qw, :], in_=psO[0][0:qw, 0:128], func=AF.Copy, scale=sm[0:qw, 0:1]),
                       reads=[okeys[0], (kk, 0)], writes=[(kk, "of")]))
        A(lambda: S.op("dve", lambda e: e.scalar_tensor_tensor(out=of[0:qw, :], in0=psO[1][0:qw, 0:128], scalar=sm[0:qw, 2:3], in1=of[0:qw, :],
                                                               op0=ALU.mult, op1=ALU.add),
                       reads=[okeys[1], (kk, 2), (kk, "of")], writes=[(kk, "of")]))
        A(lambda: S.op("act", lambda e: e.activation(out=jk[0:qw, :], in_=of[0:qw, :], func=AF.Square),
                       reads=[(kk, "of")], writes=[(kk, "jk")]))
        A(lambda: S.op("dve", lambda e: e.tensor_reduce(out=sm[0:qw, 3:4], in_=jk[0:qw, :], axis=AX.X, op=ALU.add),
                       reads=[(kk, "jk")], writes=[(kk, 3)]))
        A(lambda: S.op("act", lambda e: e.activation(out=sm[0:qw, 4:5], in_=sm[0:qw, 3:4], func=AF.Ln, bias=self.epsc[0:qw, 0:1], scale=1.0 / 128),
                       reads=[(kk, 3), "epsc"], writes=[(kk, 4)]))
        A(lambda: S.op("act", lambda e: e.activation(out=sm[0:qw, 5:6], in_=sm[0:qw, 4:5], func=AF.Exp, scale=-0.5),
                       reads=[(kk, 4)], writes=[(kk, 5)]))
        A(lambda: S.op("dve", lambda e: e.scalar_tensor_tensor(out=ob[0:qw, :], in0=of[0:qw, :], scalar=sm[0:qw, 5:6], in1=self.subg[0:qw, :],
                                                               op0=ALU.mult, op1=ALU.mult),
                       reads=[(kk, "of"), (kk, 5), "subg"], writes=[("ob3", ts)]))
        A(lambda: self.transpose_evac(qw, h, j, self.transpose_pe(ob, [("ob3", ts)], qw)))
        return steps

    def transpose_pe(self, ob, obkeys, qw):
        S = self.S
        self.ptrr = getattr(self, "ptrr", 0) + 1
        pt = self.ptrr % 2
        ident = self.consts[:, 2, :]
        S.op("pe", lambda e: e.transpose(self.psTs[pt][:, 0:qw], ob[0:qw, :], ident[0:qw, 0:qw]),
             reads=obkeys + ["consts"], writes=[("psT", pt)])
        self.last_pt = pt
        return pt

    def transpose_evac(self, qw, ftile, j, pt=None):
        pt = self.last_pt if pt is None else pt
        tsl = slice(j * P, j * P + qw)
        tgs = sorted(set([(j * P) // TG, (j * P + qw - 1) // TG]))
        self.copy_op(self.evac_engine(), self.xT[:, ftile, tsl], self.psTs[pt][:, 0:qw], [("psT", pt)],
                     [("xT", ftile, t) for t in tgs])

    def attn_a(self):
        S = self.S
        S.op("sp", lambda e: e.dma_start(out=self.tabA[:], in_=self.negdist_a_d), writes=["tabA"], dma=True)
        S.op("sp", lambda e: e.dma_start(out=self.tabB[:], in_=self.mask_a_d), writes=["tabB"], dma=True)
        S.op("sp", lambda e: e.dma_start(out=self.tabA0[:], in_=self.mask_a0_d), writes=["tabA0"], dma=True)
        for s_ in range(2):
            self.set_v_ones(s_, 64)
        facts = []
        for h in range(16):
            for j in range(NQB):
                facts.append(lambda ts, h=h, j=j: self.task_a(ts, h, j))
        self.run_tasks(facts, 3)

    def head_pre_a(self, h):
        S = self.S
        kvh = h // 4
        kslot = kvh % 2
        slot = h % 2
        if h % 4 == 0:
            self.load_k(kslot, 0, self.KT0, kvh * 64)
            self.load_v(kslot, self.V0, kvh * 64, 64)
        self.load_q(slot, 0, self.QT0, h * 64)
        bH = self.biasH[slot]
        slope = A_SLOPES[h]
        for i, rel in enumerate(A_NEAR):
            S.op("dve", lambda e: e.scalar_tensor_tensor(out=bH[:, rel, :], in0=self.tabA[:, i, :], scalar=slope,
                                                         in1=self.tabB[:, i, :], op0=ALU.mult, op1=ALU.add),
                 reads=["tabA", "tabB"], writes=[("biasH", 0, rel)])
        for j in range(NQB):
            rel = 8 - j
            if rel in A_NEAR:
                i = A_NEAR.index(rel)
                S.op("dve", lambda e: e.scalar_tensor_tensor(out=bH[:, B0IDX[j], :], in0=self.tabA[:, i, :], scalar=slope,
                                                             in1=self.tabA0[:, j, :], op0=ALU.mult, op1=ALU.add),
                     reads=["tabA", "tabA0"], writes=[("biasH", 0, B0IDX[j])])
            else:
                S.op("dve", lambda e: e.tensor_scalar(out=bH[:, B0IDX[j], :], in0=self.kqneg[:], scalar1=self.negd0[:, rel:rel + 1],
                                                      scalar2=slope, op0=ALU.add, op1=ALU.mult),
                     reads=["kqneg", "negd0"], writes=[("biasH", 0, B0IDX[j])])
                S.op("dve", lambda e: e.tensor_tensor(out=bH[:, B0IDX[j], :], in0=bH[:, B0IDX[j], :], in1=self.tabA0[:, j, :], op=ALU.add),
                     reads=[("biasH", 0, B0IDX[j]), "tabA0"], writes=[("biasH", 0, B0IDX[j])])

    def task_a(self, ts, h, j):
        S = self.S
        if j == 0:
            self.head_pre_a(h)
            bH_, eH_ = self.biasH[h % 2], self.expH[h % 2]
            for (r0, r1) in ((0, 13), (15, 18)):
                S.op("act", lambda e: e.activation(out=eH_[:, r0:r1, :], in_=bH_[:, r0:r1, :], func=AF.Exp),
                     reads=[("biasH", 0, r) for r in range(r0, r1) if r in A_NEAR or r in B0IDX],
                     writes=[("expH", h % 2, r) for r in range(r0, r1)])
        kslot = (h // 4) % 2
        slot = h % 2
        bH = self.biasH[slot]
        kts = [(self.kT[kslot][0], [("kT", kslot, 0, 0), ("kT", kslot, 0, 1)])]
        qts = [(self.qT[slot][0], [("qT", slot, 0)])]
        vs = (self.Vs[kslot], [("Vs", kslot, 0), ("Vs", kslot, 9), ("Vs1", kslot)])
        near = sorted(set([g for g in list(range(j - 2, j + 2)) + list(range(j + 7, j + 10)) if 1 <= g <= 16]))
        glist = [0] + near

        eH = self.expH[slot]

        def bias_ap_of(grp, qw):
            if grp[0] == 0:
                assert len(grp) == 1
                return eH[:, B0IDX[j]:B0IDX[j] + 1, 0:qw], [("expH", slot, B0IDX[j])]
            r0 = grp[0] - j + 8
            r1 = grp[-1] - j + 8
            return eH[:, r0:r1 + 1, 0:qw], [("expH", slot, r) for r in range(r0, r1 + 1)]

        def fin_steps(ts, psO, okeys):
            qw = qw_of(j)
            sm = self.small3[ts]
            ob = self.obA[(h // 2) % 2][j]
            kk = ("finA", ts)
            obk = ("obA", (h // 2) % 2, j)
            steps = []
            A = steps.append
            A(lambda: S.op("dve", lambda e: e.tensor_tensor(out=sm[0:qw, 0:1], in0=psO[0][0:qw, 64:65], in1=self.esink[0:qw, h:h + 1], op=ALU.add),
                           reads=[okeys[0], "esink"], writes=[(kk, 0)]))
            A(lambda: S.op("dve", lambda e: e.reciprocal(out=sm[0:qw, 1:2], in_=sm[0:qw, 0:1]), reads=[(kk, 0)], writes=[(kk, 1)]))
            A(lambda: S.op("act", lambda e: e.activation(out=ob[0:qw, (h % 2) * 64:(h % 2) * 64 + 64], in_=psO[0][0:qw, 0:64], func=AF.Copy,
                                                         scale=sm[0:qw, 1:2]),
                           reads=[okeys[0], (kk, 1)], writes=[obk + (h % 2,)]))
            if h % 2 == 1:
                A(lambda: self.transpose_evac(qw, h // 2, j, self.transpose_pe(ob, [obk + (0,), obk + (1,)], qw)))
            return steps

        return self.softmax_task(ts, j, glist, bias_ap_of, kts, qts, vs, 64, True, fin_steps)

    def attn_b(self):
        S = self.S
        S.op("sp", lambda e: e.dma_start(out=self.maskb_f[:], in_=self.mask_b_d), writes=["maskb_f"], dma=True)
        S.op("dve", lambda e: e.tensor_copy(out=self.maskb[:], in_=self.maskb_f[:]), reads=["maskb_f"], writes=["maskb"])
        facts = []
        for h in range(16):
            for j in range(NQB):
                facts.append(lambda ts, h=h, j=j: self.task_b(ts, h, j))
        self.run_tasks(facts, 3)

    def task_b(self, ts, h, j):
        S = self.S
        ones = self.consts[:, 0, :]
        triu = self.consts[:, 1, :]
        slot = h % 2
        if j == 0:
            self.load_k(slot, 0, self.KT0, 256 + h * 64)
            self.load_v(slot, self.V0, 256 + h * 64, 64)
            self.load_q(slot, 0, self.QT0, 1024 + h * 64)
        kt = self.kT[slot][0]
        ktkeys = [("kT", slot, 0, 0), ("kT", slot, 0, 1)]
        qt = self.qT[slot][0]
        qtkeys = [("qT", slot, 0)]
        vsb = self.Vs[slot]
        vkeys = [("Vs", slot, 0), ("Vs", slot, 9)]
        qw = qw_of(j)
        qsl = slice(j * P, j * P + qw)
        gmax = min(16, 9 + j)
        groups = []
        cur = []
        curcls = None
        for g in range(gmax, -1, -1):
            rel = g - j + 8
            cls = "near" if rel in B_NEAR else ("flag" if rel >= 9 else "free")
            if cur and (cls != curcls or len(cur) == 4 or cls == "near"):
                groups.append((curcls, cur))
                cur = []
            cur.append(g)
            curcls = cls
        if cur:
            groups.append((curcls, cur))
        psS = self.ps[ts]
        psD = psS
        psO = self.ps[3 + ts][:, 0:64]
        okey = ("psOb", ts)
        R32 = self.R32s[ts]
        Rtmp = self.Rtmps[ts]
        S.op("dve", lambda e: e.memset(R32[:], 0.0), writes=[("R32", ts)])
        self.rbrr[ts] += 1
        rb = self.rbrr[ts] % 2
        S.op("dve", lambda e: e.memset(self.Rbfs[ts][rb][:], 0.0), writes=[("Rbf", ts, rb)])
        pend_pv = None
        first_pv = True
        ng = len(groups)

        def emit_pv(pend, first, last):
            wt, wkey, asc = pend
            n = len(asc)
            for i, g in enumerate(asc):
                st = first
                first = False
                sp_ = last and i == n - 1
                S.op("pe", lambda e: e.matmul(psO[0:qw, :], lhsT=wt[:, i, 0:qw], rhs=vsb[:, g, 0:64], start=st, stop=sp_),
                     reads=[wkey] + vkeys, writes=[okey])
            return first

        for gi, (cls, grp) in enumerate(groups):
            n = len(grp)
            asc = grp[::-1]
            if pend_pv is not None:
                first_pv = emit_pv(pend_pv, first_pv, False)
                pend_pv = None
            for i, g in enumerate(asc):
                S.op("pe", lambda e: e.matmul(psS[:, i * 128: i * 128 + qw], lhsT=kt[:, g * P:(g + 1) * P], rhs=qt[:, qsl], start=True, stop=True),
                     reads=ktkeys + qtkeys, writes=[("psS", ts)])
            et = self.tmpS2[ts][0]
            self.sprr[ts] += 1
            sb_ = self.sprr[ts] % 2
            spt = self.spT2[ts][sb_]
            spkey = ("spT", ts, sb_)
            psS3 = psS[:, 0:n * 128].rearrange("p (n q) -> p n q", q=128)[:, :, 0:qw]
            psD3 = psD[:, 0:n * 128].rearrange("p (n q) -> p n q", q=128)[:, :, 0:qw]
            if cls == "flag":
                S.op("act", lambda e: e.activation(out=et[:, 0:n, 0:qw], in_=psS3, func=AF.Exp, scale=-1.0, bias=self.flagb[:, 0:1]),
                     reads=[("psS", ts), "flagb"], writes=[("tmpS", ts, 0)])
            else:
                S.op("act", lambda e: e.activation(out=et[:, 0:n, 0:qw], in_=psS3, func=AF.Exp, scale=-1.0),
                     reads=[("psS", ts)], writes=[("tmpS", ts, 0)])
            S.op("act", lambda e: e.activation(out=spt[:, 0:n, 0:qw], in_=et[:, 0:n, 0:qw], func=AF.Ln, bias=self.onec[:, 0:1], scale=1.0),
                 reads=[("tmpS", ts, 0), "onec"], writes=[spkey])
            if cls == "near":
                mi = B_NEAR.index(grp[0] - j + 8)
                S.op("dve", lambda e: e.tensor_tensor(out=spt[:, 0, 0:qw], in0=spt[:, 0, 0:qw], in1=self.maskb[:, mi, 0:qw], op=ALU.mult),
                     reads=[spkey, "maskb"], writes=[spkey])
            yield
            for i, g in enumerate(asc):
                dsl = slice(i * 128, i * 128 + qw)
                S.op("pe", lambda e: e.matmul(psD[:, dsl], lhsT=triu, rhs=spt[:, i, 0:qw], start=False, stop=False, skip_group_check=True),
                     reads=[spkey, "consts", ("tmpS", ts, 0)], writes=[("psS", ts)])
                for i2 in range(i + 1, n):
                    S.op("pe", lambda e: e.matmul(psD[:, dsl], lhsT=ones, rhs=spt[:, i2, 0:qw], start=False, stop=False, skip_group_check=True),
                         reads=[spkey, "consts"], writes=[("psS", ts)])
                S.op("pe", lambda e: e.matmul(psD[:, dsl], lhsT=ones, rhs=self.Rbfs[ts][rb][:, 0:qw], start=False, stop=True, skip_group_check=True),
                     reads=[("Rbf", ts, rb), "consts"], writes=[("psS", ts)])
            if gi < ng - 1:
                if n > 1:
                    S.op("dve", lambda e: e.tensor_reduce(out=Rtmp[:, 0:qw], in_=spt[:, 0:n, 0:qw].rearrange("p n q -> p q n"), axis=AX.X, op=ALU.add),
                         reads=[spkey], writes=[("Rtmp", ts)])
                    S.op("dve", lambda e: e.tensor_tensor(out=R32[:, 0:qw], in0=R32[:, 0:qw], in1=Rtmp[:, 0:qw], op=ALU.add),
                         reads=[("Rtmp", ts), ("R32", ts)], writes=[("R32", ts)])
                else:
                    S.op("dve", lambda e: e.tensor_tensor(out=R32[:, 0:qw], in0=R32[:, 0:qw], in1=spt[:, 0, 0:qw], op=ALU.add),
                         reads=[spkey, ("R32", ts)], writes=[("R32", ts)])
                self.rbrr[ts] += 1
                rb = self.rbrr[ts] % 2
                S.op("dve", lambda e: e.tensor_copy(out=self.Rbfs[ts][rb][:, 0:qw], in_=R32[:, 0:qw]), reads=[("R32", ts)], writes=[("Rbf", ts, rb)])
            self.tmprr[ts] += 1
            wb = self.tmprr[ts] % 3
            wt = self.pT2[ts][wb]
            wkey = ("pT", ts, wb)
            if cls == "flag":
                S.op("act", lambda e: e.activation(out=wt[:, 0:n, 0:qw], in_=psD3, func=AF.Exp, scale=-1.0, bias=self.flagb[:, 0:1]),
                     reads=[("psS", ts), "flagb"], writes=[wkey])
            else:
                S.op("act", lambda e: e.activation(out=wt[:, 0:n, 0:qw], in_=psD3, func=AF.Exp, scale=-1.0),
                     reads=[("psS", ts)], writes=[wkey])
            if cls == "near":
                S.op("dve", lambda e: e.tensor_tensor(out=wt[:, 0, 0:qw], in0=wt[:, 0, 0:qw], in1=self.maskb[:, mi, 0:qw], op=ALU.mult),
                     reads=[wkey, "maskb"], writes=[wkey])
            pend_pv = (wt, wkey, asc)
            yield
        emit_pv(pend_pv, first_pv, True)
        yield
        ob = self.obA[(h // 2) % 2][j]
        obk = ("obA", (h // 2) % 2, j)
        self.copy_op(self.evac_engine(), ob[0:qw, (h % 2) * 64:(h % 2) * 64 + 64], psO[0:qw, :], [okey], [obk + (h % 2,)])
        yield
        if h % 2 == 1:
            pt_ = self.transpose_pe(ob, [obk + (0,), obk + (1,)], qw)
            self.transpose_evac(qw, 8 + h // 2, j, pt_)
            yield

    def dump_h(self):
        S = self.S
        outv = self.outT.rearrange("(c p) t -> p c t", p=P)
        for tg in range(NTG):
            sl = slice(tg * TG, (tg + 1) * TG)
            S.op("sp", lambda e, sl=sl: e.dma_start(out=outv[:, :, sl], in_=self.hT[:, :, sl]),
                 reads=[("hT", c, tg) for c in range(KC)], writes=[("out", tg)], dma=True)
        S.op("sp", None, reads=[("out", tg) for tg in range(NTG)])

    def dump_x(self):
        S = self.S
        outv = self.dbgx.rearrange("(c p) t -> p c t", p=P)
        for tg in range(NTG):
            sl = slice(tg * TG, (tg + 1) * TG)
            S.op("sp", lambda e, sl=sl: e.dma_start(out=outv[:, :, sl], in_=self.xT[:, :, sl]),
                 reads=[("xT", c, tg) for c in range(KC)], writes=[("outx", tg)], dma=True)
        S.op("sp", None, reads=[("outx", tg) for tg in range(NTG)])

    def stop_here(self, name):
        if self.stop != name:
            return False
        self.S.barrier()
        if name.startswith("x_"):
            self.dump_x()
        self.dump_h()
        self.emit_all()
        return True

    def build(self, stop=None):
        S = self.S
        nc = self.nc
        self.stop = stop
        if stop is not None:
            self.dbgx = nc.dram_tensor("dbgx", [D, T], BF16, kind="ExternalOutput").ap()
        self.obA = [[self.sb("obA%d_%d" % (i, j), [P, 128], BF16) for j in range(NQB)] for i in range(2)]
        assert self.sb_off <= 229344, self.sb_off
        print('sbuf end', self.sb_off)
        self.load_consts()
        self.rmsnorm(0)
        if self.stop_here("x_norm0"):
            return nc
        self.proj_qk(self.w_in_ab, 1024, 2, self.KT0, 0)
        if stop == "x_ka":
            S.barrier()
            S.op("sp", lambda e: e.dma_start(out=self.dbgx[0:256, :], in_=self.KT0["loc"][0].ap()[0:256, :]), reads=[], writes=[("outx", 0)], dma=True)
            S.op("sp", None, reads=[("outx", 0)])
            self.dump_h()
            self.emit_all()
            return nc
        self.proj_qk(self.w_in_ab, 2560, 8, self.KT0, 256)
        self.proj_v(self.w_in_ab, 1280, 256, self.V0, 0)
        if stop == "x_va":
            S.barrier()
            S.op("sp", lambda e: e.dma_start(out=self.dbgx[0:256, :], in_=self.V0.ap()[0:T, 0:256].rearrange("t f -> t f")), reads=[], writes=[("outx", 0)], dma=True) if False else None
            for f0 in range(0, 256, 32):
                S.op("sp", lambda e, f0=f0: e.dma_start(out=self.dbgx[f0:f0 + 32, :], in_=self.V0["loc"][0].ap()[:, f0:f0 + 32].rearrange("t f -> f t"), allow_slow_non_contiguous=True), reads=[], writes=[("outx", 0)], dma=True)
            S.op("sp", None, reads=[("outx", 0)])
            self.dump_h()
            self.emit_all()
            return nc
        self.proj_v(self.w_in_ab, 3584, 1024, self.V0, 256)
        self.allgather(self.KT0)
        self.allgather(self.V0)
        self.proj_qk(self.w_in_ab, 0, 8, self.QT0.ap(), 0)
        self.proj_qk(self.w_in_ab, 1536, 8, self.QT0.ap(), 1024, scale=-0.125)
        S.barrier()
        if stop == "x_proj0":
            S.op("sp", lambda e: e.dma_start(out=self.dbgx, in_=self.QT0.ap()), reads=[], writes=[("outx", 0)], dma=True)
            S.op("sp", None, reads=[("outx", 0)])
            self.dump_h()
            self.emit_all()
            return nc
        if stop == "x_kv0":
            S.op("sp", lambda e: e.dma_start(out=self.dbgx[0:640, :], in_=self.KT0["gat"][0].ap()[640:1280, :]), reads=[], writes=[("outx", 0)], dma=True)
            S.op("sp", None, reads=[("outx", 0)])
            self.dump_h()
            self.emit_all()
            return nc
        if stop == "x_attn_a":
            self.attn_a()
            self.stop_here("x_attn_a")
            return nc
        if stop == "x_attn_b":
            self.attn_b()
            self.stop_here("x_attn_b")
            return nc
        self.attn_a()
        S.barrier()
        self.attn_b()
        S.barrier()
        if self.stop_here("x_attn0"):
            return nc
        self.out_proj(self.w_out_ab)
        if self.stop_here("h_attn0"):
            return nc
        self.rmsnorm(3)
        self.mlp(0)
        if self.stop_here("h_l0"):
            return nc
        self.rmsnorm(1)
        self.proj_qk(self.w_in_c, 2048, 16, self.KT1, 0)
        self.proj_v(self.w_in_c, 4096, 2048, self.V1, 0)
        self.allgather(self.KT1)
        self.allgather(self.V1)
        self.proj_qk(self.w_in_c, 0, 16, self.QT1.ap(), 0)
        S.barrier()
        self.attn_c()
        S.barrier()
        if self.stop_here("x_attn1"):
            return nc
        self.out_proj(self.w_out_c)
        self.rmsnorm(4)
        self.mlp(1)
        if self.stop_here("h_l1"):
            return nc
        self.rmsnorm(2, final=True)
        self.dump_h()
        self.emit_all()
        return nc

    def emit_all(self):
        S = self.S
        nc = self.nc
        with ExitStack() as stack:
            S.finalize(nc, stack)
            block = stack.enter_context(nc.Block())

            @block.tensor
            def _(e):
                S.emit("pe", e)

            @block.scalar
            def _(e):
                S.emit("act", e)

            @block.vector
            def _(e):
                S.emit("dve", e)

            @block.gpsimd
            def _(e):
                S.emit("pool", e)

            @block.sync
            def _(e):
                S.emit("sp", e)


def chunk_of(p):
    return 1 + np.floor_divide(p - 16, 64)


def make_tables(rank):
    base = rank * T
    k = np.arange(128)[:, None]
    q = np.arange(128)[None, :]
    t = {}
    t["kqneg"] = (-(q - k)).astype(np.float32) * np.ones((128, 128), np.float32)
    negd0 = np.zeros((128, NREL), np.float32)
    for rel in range(NREL):
        dq0 = base - 128 * (rel - 8)
        negd0[:, rel] = -float(dq0) if dq0 >= 128 else -1.0e6
    t["negd0"] = negd0

    def posmats(rel):
        dq0 = base - 128 * (rel - 8)
        diff = dq0 + q - k
        return diff

    def absq(rel):
        jj = 8
        g = rel - 8 + jj
        qpos = base + 128 * jj + q + 0 * k
        kpos = 128 * g + k + 0 * q
        return qpos, kpos

    na = np.zeros((128, len(A_NEAR), 128), np.float32)
    ma = np.zeros((128, len(A_NEAR), 128), np.float32)
    for i, rel in enumerate(A_NEAR):
        qpos, kpos = absq(rel)
        qpos = qpos + 128 * 64
        kpos = kpos + 128 * 64
        na[:, i, :] = -np.abs(qpos - kpos)
        qc, kc = chunk_of(qpos), chunk_of(kpos)
        ok = (kc <= qc) & (kc >= qc - 2)
        ma[:, i, :] = np.where(ok, 0.0, NEGBIG)
    t["negdist_a"] = na
    t["mask_a"] = ma
    ma0 = np.zeros((128, NQB, 128), np.float32)
    for j in range(NQB):
        qpos = base + 128 * j + q + 0 * k
        kpos = k + 0 * q
        qc, kc = chunk_of(qpos), chunk_of(kpos)
        ok = (kpos < 16) | ((kpos >= 16) & (kc <= qc) & (kc >= qc - 2))
        ma0[:, j, :] = np.where(ok, 0.0, NEGBIG)
    t["mask_a0"] = ma0
    ncm = np.zeros((128, len(C_NEAR), 128), np.float32)
    mc = np.zeros((128, len(C_NEAR), 128), np.float32)
    for i, rel in enumerate(C_NEAR):
        qpos, kpos = absq(rel)
        qpos = qpos + 128 * 64
        kpos = kpos + 128 * 64
        ncm[:, i, :] = -np.abs(qpos - kpos)
        ok = chunk_of(kpos) <= chunk_of(qpos)
        mc[:, i, :] = np.where(ok, 0.0, NEGBIG)
    t["negdist_c"] = ncm
    t["mask_c"] = mc
    mb = np.zeros((128, len(B_NEAR), 128), np.float32)
    for i, rel in enumerate(B_NEAR):
        diff = posmats(rel)
        mb[:, i, :] = (diff > 0).astype(np.float32)
    t["mask_b"] = mb
    t["flagb"] = np.full((128, 1), NEGBIG if rank == 0 else 0.0, np.float32)
    cst = np.zeros((128, 3, 128), np.float32)
    cst[:, 0, :] = 1.0
    cst[:, 1, :] = (k >= q).astype(np.float32)
    cst[:, 2, :] = np.eye(128, dtype=np.float32)
    t["consts"] = cst
    return t


_NC_CACHE = {}


def get_nc(stop=None):
    key = "main" if stop is None else str(stop)
    if key not in _NC_CACHE:
        b = Builder()
        _NC_CACHE[key] = b.build(stop)
    return _NC_CACHE[key]


def make_in_maps(x, meta_tokens, ab_norm, w_in_ab, attn_sinks, w_out_ab, c_norm, w_in_c, diff_lambda, diff_subln,
                 w_out_c, mlp_norm, w_mlp_in, w_mlp_out, final_norm):
    f = lambda a: np.ascontiguousarray(np.asarray(a, dtype=np.float32))
    x = f(x)
    B = x.shape[0]
    meta = f(meta_tokens)
    gains = np.stack([f(ab_norm)[0], f(c_norm)[0], f(final_norm), f(mlp_norm)[0], f(mlp_norm)[1]], 0)
    gains_l = np.ascontiguousarray(gains.reshape(5, KC, P).transpose(2, 0, 1).reshape(P, 5 * KC))
    sinks = np.ascontiguousarray(np.broadcast_to(f(attn_sinks)[0][None, :], (P, 16)))
    lamv = np.ascontiguousarray(np.broadcast_to(f(diff_lambda)[0].reshape(1, 256), (P, 256)))
    subg = np.ascontiguousarray(np.broadcast_to(f(diff_subln)[0][None, :], (P, 128)))
    shared = {
        "w_in_ab": f(w_in_ab)[0], "w_out_ab": f(w_out_ab)[0], "w_in_c": f(w_in_c)[0], "w_out_c": f(w_out_c)[0],
        "w_mlp_in0": f(w_mlp_in)[0], "w_mlp_in1": f(w_mlp_in)[1], "w_mlp_out0": f(w_mlp_out)[0], "w_mlp_out1": f(w_mlp_out)[1],
        "gains": gains_l, "sinks": sinks, "lamv": lamv, "subg": subg,
    }
    tabs = [make_tables(0), make_tables(1)]
    in_maps = []
    for core in range(8):
        b, r = core // 2, core % 2
        seq = np.zeros((LP, D), np.float32)
        seq[0:16] = meta
        seq[16:16 + 2048] = x[b]
        h0T = np.ascontiguousarray(seq[r * T:(r + 1) * T].T)
        m = dict(shared)
        m["h0T"] = h0T
        m.update(tabs[r])
        in_maps.append(m)
    return in_maps


def assemble(results):
    out = np.zeros((4, 2048, D), np.float32)
    for core in range(8):
        b, r = core // 2, core % 2
        oT = np.asarray(results[core]["outT"])
        rows = oT.T
        pos0 = r * T
        lo = max(pos0, 16)
        hi = min(pos0 + T, 16 + 2048)
        out[b, lo - 16:hi - 16] = rows[lo - pos0:hi - pos0]
    return out


def kernel(**inputs):
    nc = get_nc()
    in_maps = make_in_maps(**inputs)
    res = run_bass_kernel_spmd(nc, in_maps, core_ids=list(range(8)))
    return assemble(res.results)
```

```python
import math
import types
from contextlib import ExitStack

import numpy as np
import concourse.bass as bass
import concourse.mybir as mybir
from concourse.bass_utils import run_bass_kernel_spmd

F32 = mybir.dt.float32
BF16 = mybir.dt.bfloat16
AF = mybir.ActivationFunctionType
ALU = mybir.AluOpType
AX = mybir.AxisListType

P = 128
D = 2048
KC = 16
T = 1088
LP = 2176
NTG = 4
TG = 272
NQB = 9
NGB = 17
DFF = 8192
EPS = 1e-6
NEGBIG = -30000.0
REPLICA_GROUPS = [[0, 1], [2, 3], [4, 5], [6, 7]]
NREL = 18
A_SLOPES = [2.0 ** (-8.0 * (i + 1) / 16) for i in range(16)]
C_SLOPES = A_SLOPES
LAMBDA_INIT = 0.8 - 0.6 * math.exp(-0.3 * 1)

A_NEAR = [6, 7, 8, 9, 15, 16, 17]
C_NEAR = [8, 9, 16, 17]
B_NEAR = [8, 16, 17]
B0IDX = [0, 1, 2, 3, 4, 5, 10, 11, 12]


def c_dead(h, rel):
    s_ = C_SLOPES[h]
    if rel in C_NEAR:
        return False
    dead = []
    for base in (0, T):
        dq0 = base - 128 * (rel - 8)
        if dq0 < 128:
            dead.append(base == 0 and rel >= 10)
        else:
            dead.append(s_ * (dq0 - 127) > 110.0)
    if not dead[0] and rel >= 10:
        return False
    return all(dead) and (s_ * 1.0e6 > 110.0)


def qw_of(j):
    return 128 if j < 8 else 64


def _freeze(fn):
    if fn is None or fn.__closure__ is None:
        return fn
    cells = []
    for c in fn.__closure__:
        try:
            cells.append(types.CellType(c.cell_contents))
        except ValueError:
            cells.append(c)
    return types.FunctionType(fn.__code__, fn.__globals__, fn.__name__, fn.__defaults__, tuple(cells))


class Op:
    __slots__ = ("eng", "fn", "dma", "waits", "sig", "sem", "val", "idx", "cc")

    def __init__(self, eng, fn, dma, cc=False):
        self.eng = eng
        self.fn = fn
        self.dma = dma
        self.cc = cc
        self.waits = {}
        self.sig = False
        self.sem = None
        self.val = 0


class Sched:
    ENGS = ("pe", "act", "dve", "pool", "sp")
    NDMASEM = 8

    def __init__(self):
        self.ops = {e: [] for e in self.ENGS}
        self.allops = []
        self.last_w = {}
        self.readers = {}
        self.dma_rr = {"pool": 0, "sp": 0}
        self.dma_last = {}

    def op(self, eng, fn, reads=(), writes=(), dma=False, cc=False):
        o = Op(eng, _freeze(fn), dma, cc)
        o.idx = len(self.allops)
        deps = set()
        for k in reads:
            w = self.last_w.get(k)
            if w is not None:
                deps.add(w)
        for k in writes:
            w = self.last_w.get(k)
            if w is not None:
                deps.add(w)
            for r in self.readers.get(k, ()):
                deps.add(r)
        if dma:
            slot = (eng, self.dma_rr[eng] % self.NDMASEM)
            self.dma_rr[eng] += 1
            o.sem = slot
            prev = self.dma_last.get(slot)
            if prev is not None:
                deps.add(prev)
            self.dma_last[slot] = o
        for d in deps:
            if d is o:
                continue
            if d.eng == "pe" and eng == "pe" and not d.dma:
                continue
            o.waits[d.idx] = d
            d.sig = True
        for k in reads:
            self.readers.setdefault(k, []).append(o)
        for k in writes:
            self.last_w[k] = o
            self.readers[k] = []
        self.ops[eng].append(o)
        self.allops.append(o)
        return o

    def barrier(self):
        last = []
        for e in self.ENGS:
            for o_ in reversed(self.ops[e]):
                if o_.fn is not None:
                    last.append(o_)
                    break
        outstanding = [o for o in self.dma_last.values()]
        key = ("__barrier__", len(self.allops))
        for e in self.ENGS:
            o = Op(e, None, False)
            o.idx = len(self.allops)
            for d in last + outstanding:
                if d.eng == e and not d.dma:
                    continue
                o.waits[d.idx] = d
                d.sig = True
            self.ops[e].append(o)
            self.allops.append(o)
        self.readers = {}

    def finalize(self, nc, stack):
        cnt = {e: 0 for e in self.ENGS}
        self.esem = {e: stack.enter_context(nc.semaphore("es_" + e)) for e in ("pe", "act", "dve", "pool", "sp")}
        self.dsem = {}
        for e in ("pool", "sp"):
            for i in range(self.NDMASEM):
                self.dsem[(e, i)] = stack.enter_context(nc.semaphore("ds_%s%d" % (e, i)))
        self.ccsem = stack.enter_context(nc.semaphore("ccsem"))
        dcnt = {}
        cccnt = 0
        for o in self.allops:
            if o.dma:
                dcnt[o.sem] = dcnt.get(o.sem, 0) + 16
                o.val = dcnt[o.sem]
                o.sem = self.dsem[o.sem]
                o.sig = True
            elif o.cc:
                cccnt += 1
                o.val = cccnt
                o.sem = self.ccsem
                o.sig = True
            elif o.sig:
                cnt[o.eng] += 1
                o.val = cnt[o.eng]
                o.sem = self.esem[o.eng]

    def emit(self, eng, e):
        waited = {}
        for o in self.ops[eng]:
            need = {}
            for d in o.waits.values():
                k = id(d.sem)
                if need.get(k, (None, 0))[1] < d.val:
                    need[k] = (d.sem, d.val)
            for k, (sem, val) in need.items():
                if waited.get(k, 0) >= val:
                    continue
                waited[k] = val
                e.wait_ge(sem, val)
            if o.fn is None:
                continue
            ins = o.fn(e)
            if o.sig:
                if o.dma:
                    ins.then_inc(o.sem, 16)
                elif o.cc:
                    ins.then_inc(o.sem)
                else:
                    ins.then_inc(o.sem, 1)


class Builder:
    def __init__(self, debug=None):
        self.debug = debug
        self.nc = nc = bass.Bass("TRN2", target_bir_lowering=False)
        self.S = Sched()
        self.sb_off = 16512
        self.psrr = 0
        self.uid = 0
        self.evrr = 0
        self.declare_io()
        self.alloc()

    def declare_io(self):
        nc = self.nc

        def inp(name, shape, dt=F32):
            return nc.dram_tensor(name, list(shape), dt, kind="ExternalInput").ap()

        self.h0T = inp("h0T", [D, T])
        self.w_in_ab = inp("w_in_ab", [D, 4608])
        self.w_out_ab = inp("w_out_ab", [D, D])
        self.w_in_c = inp("w_in_c", [D, 6144])
        self.w_out_c = inp("w_out_c", [D, D])
        self.w_mlp_in = [inp("w_mlp_in%d" % l, [D, DFF]) for l in range(2)]
        self.w_mlp_out = [inp("w_mlp_out%d" % l, [DFF, D]) for l in range(2)]
        self.gains_d = inp("gains", [P, 5 * KC])
        self.sinks_d = inp("sinks", [P, 16])
        self.lam_d = inp("lamv", [P, 256])
        self.subg_d = inp("subg", [P, 128])
        self.kqneg_d = inp("kqneg", [P, 128])
        self.negd0_d = inp("negd0", [P, NREL])
        self.negdist_a_d = inp("negdist_a", [P, len(A_NEAR), 128])
        self.mask_a_d = inp("mask_a", [P, len(A_NEAR), 128])
        self.mask_a0_d = inp("mask_a0", [P, NQB, 128])
        self.negdist_c_d = inp("negdist_c", [P, len(C_NEAR), 128])
        self.mask_c_d = inp("mask_c", [P, len(C_NEAR), 128])
        self.mask_b_d = inp("mask_b", [P, len(B_NEAR), 128])
        self.flagb_d = inp("flagb", [P, 1])
        self.consts_d = inp("consts", [P, 3, 128])
        self.outT = nc.dram_tensor("outT", [D, T], F32, kind="ExternalOutput").ap()
        self.QT0 = nc.dram_tensor("QT0", [2048, T], BF16)
        self.KT0 = self.chunked("KT0", [0, 640, 1280], True)
        self.V0 = self.chunked("V0", [0, 768, 1280], False)
        self.QT1 = nc.dram_tensor("QT1", [2048, T], BF16)
        self.KT1 = self.chunked("KT1", [0, 512, 1024, 1536, 2048], True)
        self.V1 = self.chunked("V1", [0, 512, 1024, 1536, 2048], False)
        if self.debug:
            self.dbg = nc.dram_tensor("dbg", list(self.debug["shape"]), F32, kind="ExternalOutput").ap()

    def chunked(self, name, bounds, is_k):
        nc = self.nc
        ch = {"name": name, "bounds": bounds, "is_k": is_k, "loc": [], "gat": [], "keys": [[] for _ in bounds[1:]], "done": [False] * (len(bounds) - 1)}
        for i in range(len(bounds) - 1):
            w = bounds[i + 1] - bounds[i]
            if is_k:
                ch["loc"].append(nc.dram_tensor("%s_l%d" % (name, i), [w, T], BF16))
                ch["gat"].append(nc.dram_tensor("%s_g%d" % (name, i), [2 * w, T], BF16))
            else:
                ch["loc"].append(nc.dram_tensor("%s_l%d" % (name, i), [T, w], BF16))
                ch["gat"].append(nc.dram_tensor("%s_g%d" % (name, i), [2 * T, w], BF16))
        return ch

    @staticmethod
    def ch_find(ch, f):
        b = ch["bounds"]
        for i in range(len(b) - 1):
            if b[i] <= f < b[i + 1]:
                return i, f - b[i], b[i + 1] - b[i]
        raise ValueError(f)

    def sb(self, name, shape, dt, off=None):
        n = 1
        for s in shape[1:]:
            n *= s
        nbytes = n * (4 if dt == F32 else 2)
        nbytes = (nbytes + 31) // 32 * 32
        if off is None:
            off = self.sb_off
            self.sb_off += nbytes
        t = self.nc.alloc_sbuf_tensor_at(name, list(shape), dt, offset=off)
        return t

    def alloc(self):
        nc = self.nc
        self.hT = self.sb("hT", [P, KC, T], F32)
        self.xT = self.sb("xT", [P, KC, T], BF16)
        self.gains = self.sb("gains", [P, 5 * KC], F32)
        self.consts_f = self.sb("consts_f", [P, 3, 128], F32)
        self.consts = self.sb("consts_b", [P, 3, 128], BF16)
        self.epsc = self.sb("epsc", [P, 1], F32)
        self.onec = self.sb("onec", [P, 1], F32)
        self.flagb = self.sb("flagb", [P, 1], F32)
        self.nflagb = self.sb("nflagb", [P, 1], F32)
        self.sinks = self.sb("sinks", [P, 16], F32)
        self.esink = self.sb("esink", [P, 16], F32)
        self.lamv = self.sb("lamv", [P, 256], F32)
        self.lamt = self.sb("lamt", [P, 8], F32)
        self.subg = self.sb("subg", [P, 128], F32)
        self.kqneg = self.sb("kqneg", [P, 128], F32)
        self.negd0 = self.sb("negd0", [P, NREL], F32)
        base = self.sb_off
        self.rstd = [self.sb("rstd%d" % i, [P, TG], F32) for i in range(2)]
        self.lnv = [self.sb("lnv%d" % i, [P, TG], F32) for i in range(2)]
        self.sqb = [self.sb("sqb%d" % i, [P, TG], BF16) for i in range(3)]
        self.stage = [self.sb("stage%d" % i, [P, T], BF16) for i in range(2)]
        self.vstage = [self.sb("vstage%d" % i, [P, 256], BF16) for i in range(3)]
        self.relu_t = [self.sb("relu%d" % i, [P, TG], F32) for i in range(3)]
        self.wbuf = [self.sb("wbuf%d" % i, [P, 4096], BF16) for i in range(2)]
        self.hidT = self.sb("hidT", [P, 8, T], BF16)
        lin_end = self.sb_off
        self.sb_off = base
        self.kT = [[self.sb("kT%d_%d" % (i, c), [64, LP], BF16) for c in range(2)] for i in range(2)]
        self.qT = [[self.sb("qT%d_%d" % (i, c), [64, T], BF16) for c in range(2)] for i in range(2)]
        self.Vs = [self.sb("Vs%d" % i, [P, NGB, 130], BF16) for i in range(2)]
        bH_ = self.sb("biasH0", [P, NREL, 128], F32)
        self.biasH = [bH_, bH_]
        self.expH_off = self.sb_off
        self.expH = [self.sb("expH%d" % i, [P, NREL, 128], BF16) for i in range(2)]
        self.tabA = self.sb("tabA", [P, 7, 128], F32)
        self.tabB = self.sb("tabB", [P, 7, 128], F32)
        self.tabA0 = self.sb("tabA0", [P, NQB, 128], F32)
        self.maskb_f = self.sb("maskb_f", [P, 3, 128], F32)
        self.maskb = self.sb("maskb", [P, 3, 128], BF16)
        NS = 3
        eoff = self.expH_off
        self.tmpS2 = [[self.sb("tmpS%d_0" % t, [P, 4, 128], F32, off=eoff + t * 2048)] * 2 for t in range(NS)]
        self.pT2 = [[self.sb("pT%d_%d" % (t, i), [P, 4, 128], BF16) for i in range(3)] for t in range(NS)]
        self.spT2 = [[self.sb("spT%d_%d" % (t, i), [P, 4, 128], BF16) for i in range(2)] for t in range(NS)]
        self.R32s = [self.sb("R32_%d" % t, [P, 128], F32) for t in range(NS)]
        self.Rtmps = [self.sb("Rtmp_%d" % t, [P, 128], F32) for t in range(NS)]
        self.Rbfs = [[self.sb("Rbf%d_%d" % (t, i), [P, 128], BF16) for i in range(2)] for t in range(NS)]
        self.ofin3 = [self.sb("ofin%d" % t, [P, 128], F32) for t in range(NS)]
        self.ob3 = [self.sb("ob%d" % t, [P, 128], BF16) for t in range(NS)]
        self.small3 = [self.sb("small%d" % t, [P, 8], F32) for t in range(NS)]
        self.junk3 = [self.sb("junk%d" % t, [P, 128], F32) for t in range(NS)]
        self.junk = self.junk3[0]
        self.o0buf = [self.sb("o0buf%d" % t, [P, 132], F32) for t in range(NS)]
        self.tmprr = [0] * NS
        self.sprr = [0] * NS
        self.rbrr = [0] * NS
        att_end = self.sb_off
        self.sb_off = max(lin_end, att_end)
        assert self.sb_off <= 229344, self.sb_off
        self.ps = [nc.alloc_psum_tensor("ps%d" % i, [P, 512], F32) for i in range(6)]
        self.psTs = [nc.alloc_psum_tensor("psT%d" % i, [P, 1024], BF16) for i in range(2)]

    def u(self, name):
        self.uid += 1
        return (name, self.uid)

    def next_ps(self, n=6):
        i = self.psrr % n
        self.psrr += 1
        return i

    def load_consts(self):
        S = self.S
        S.op("sp", lambda e: e.dma_start(out=self.gains[:], in_=self.gains_d), writes=["gains"], dma=True)
        S.op("sp", lambda e: e.dma_start(out=self.consts_f[:], in_=self.consts_d), writes=["consts_f"], dma=True)
        S.op("sp", lambda e: e.dma_start(out=self.flagb[:], in_=self.flagb_d), writes=["flagb"], dma=True)
        S.op("sp", lambda e: e.dma_start(out=self.sinks[:], in_=self.sinks_d), writes=["sinks"], dma=True)
        S.op("sp", lambda e: e.dma_start(out=self.lamv[:], in_=self.lam_d), writes=["lamv"], dma=True)
        S.op("sp", lambda e: e.dma_start(out=self.subg[:], in_=self.subg_d), writes=["subg"], dma=True)
        S.op("sp", lambda e: e.dma_start(out=self.kqneg[:], in_=self.kqneg_d), writes=["kqneg"], dma=True)
        S.op("sp", lambda e: e.dma_start(out=self.negd0[:], in_=self.negd0_d), writes=["negd0"], dma=True)
        for tg in range(NTG):
            sl = slice(tg * TG, (tg + 1) * TG)
            S.op("sp", lambda e, sl=sl: e.dma_start(out=self.hT[:, :, sl],
                                                    in_=self.h0T.rearrange("(c p) t -> p c t", p=P)[:, :, sl]),
                 writes=[("hT", c, tg) for c in range(KC)], dma=True)
        S.op("dve", lambda e: e.tensor_copy(out=self.consts[:], in_=self.consts_f[:]), reads=["consts_f"], writes=["consts"])
        S.op("dve", lambda e: e.memset(self.epsc[:], EPS), writes=["epsc"])
        S.op("dve", lambda e: e.memset(self.onec[:], 1.0), writes=["onec"])
        S.op("dve", lambda e: e.tensor_scalar(out=self.nflagb[:], in0=self.flagb[:], scalar1=-1.0, scalar2=None, op0=ALU.mult),
             reads=["flagb"], writes=["nflagb"])
        S.op("act", lambda e: e.activation(out=self.esink[:], in_=self.sinks[:], func=AF.Exp), reads=["sinks"], writes=["esink"])
        S.op("dve", lambda e: e.tensor_tensor(out=self.junk[:, 0:64], in0=self.lamv[:, 0:64], in1=self.lamv[:, 64:128], op=ALU.mult),
             reads=["lamv"], writes=["junk"])
        S.op("dve", lambda e: e.tensor_reduce(out=self.lamt[:, 0:1], in_=self.junk[:, 0:64], axis=AX.X, op=ALU.add),
             reads=["junk"], writes=["lamt0"])
        S.op("dve", lambda e: e.tensor_tensor(out=self.junk[:, 64:128], in0=self.lamv[:, 128:192], in1=self.lamv[:, 192:256], op=ALU.mult),
             reads=["lamv"], writes=["junk2"])
        S.op("dve", lambda e: e.tensor_reduce(out=self.lamt[:, 1:2], in_=self.junk[:, 64:128], axis=AX.X, op=ALU.add),
             reads=["junk2"], writes=["lamt1"])
        S.op("act", lambda e: e.activation(out=self.lamt[:, 2:4], in_=self.lamt[:, 0:2], func=AF.Exp),
             reads=["lamt0", "lamt1"], writes=["lamt23"])
        S.op("dve", lambda e: e.tensor_tensor(out=self.lamt[:, 4:5], in0=self.lamt[:, 3:4], in1=self.lamt[:, 2:3], op=ALU.subtract),
             reads=["lamt23"], writes=["lamt4"])
        S.op("dve", lambda e: e.tensor_scalar(out=self.lamt[:, 5:6], in0=self.lamt[:, 4:5], scalar1=-LAMBDA_INIT, scalar2=None, op0=ALU.add),
             reads=["lamt4"], writes=["nlam"])
        S.op("dve", lambda e: e.tensor_scalar(out=self.subg[:], in0=self.subg[:], scalar1=1.0 - LAMBDA_INIT, scalar2=None, op0=ALU.mult),
             reads=["subg"], writes=["subg"])

    def rmsnorm(self, gi, dst_keyname="xT", final=False):
        S = self.S
        ones = self.consts[:, 0, :]
        for tg in range(NTG):
            sl = slice(tg * TG, (tg + 1) * TG)
            pi = self.next_ps()
            ps = self.ps[pi]
            for c in range(KC):
                sq = self.sqb[c % 3]
                S.op("act", lambda e, sq=sq, c=c, sl=sl: e.activation(out=sq[:], in_=self.hT[:, c, sl], func=AF.Square),
                     reads=[("hT", c, tg)], writes=[("sqb", c % 3)])
                S.op("pe", lambda e, sq=sq, c=c, ps=ps: e.matmul(ps[:, 0:TG], lhsT=ones, rhs=sq[:], start=(c == 0), stop=(c == KC - 1)),
                     reads=[("sqb", c % 3), "consts"], writes=[("ps", pi)])
            lnv = self.lnv[tg % 2]
            rstd = self.rstd[tg % 2]
            S.op("act", lambda e, ps=ps, lnv=lnv: e.activation(out=lnv[:], in_=ps[:, 0:TG], func=AF.Ln, bias=self.epsc[:, 0:1], scale=1.0 / D),
                 reads=[("ps", pi), "epsc"], writes=[("lnv", tg % 2)])
            S.op("act", lambda e, lnv=lnv, rstd=rstd: e.activation(out=rstd[:], in_=lnv[:], func=AF.Exp, scale=-0.5),
                 reads=[("lnv", tg % 2)], writes=[("rstd", tg % 2)])
            for c in range(KC):
                gcol = self.gains[:, gi * KC + c: gi * KC + c + 1]
                if final:
                    S.op("dve", lambda e, c=c, sl=sl, gcol=gcol, rstd=rstd: e.scalar_tensor_tensor(
                        out=self.hT[:, c, sl], in0=self.hT[:, c, sl], scalar=gcol, in1=rstd[:], op0=ALU.mult, op1=ALU.mult),
                        reads=[("hT", c, tg), ("rstd", tg % 2), "gains"], writes=[("hT", c, tg)])
                else:
                    S.op("dve", lambda e, c=c, sl=sl, gcol=gcol, rstd=rstd: e.scalar_tensor_tensor(
                        out=self.xT[:, c, sl], in0=self.hT[:, c, sl], scalar=gcol, in1=rstd[:], op0=ALU.mult, op1=ALU.mult),
                        reads=[("hT", c, tg), ("rstd", tg % 2), "gains"], writes=[("xT", c, tg)])

    def load_w(self, W, r0, nkc, c0, ncols):
        S = self.S
        self.wrr = getattr(self, "wrr", 0)
        slot = self.wrr % 2
        self.wrr += 1
        wb = self.wbuf[slot]
        view = wb[:, 0:nkc * ncols].rearrange("p (k n) -> p k n", n=ncols)
        src = W[r0:r0 + nkc * P, c0:c0 + ncols].rearrange("(k p) n -> p k n", p=P)
        half = nkc // 2
        S.op("pool", lambda e: e.dma_start(out=view[:, 0:half, :], in_=src[:, 0:half, :]), writes=[("wbuf", slot, 0)], dma=True)
        S.op("pool", lambda e: e.dma_start(out=view[:, half:nkc, :], in_=src[:, half:nkc, :]), writes=[("wbuf", slot, 1)], dma=True)
        return slot, view

    def evac_engine(self):
        self.evrr += 1
        return "act" if self.evrr % 2 == 0 else "dve"

    def copy_op(self, eng, out, in_, reads, writes, scale=None):
        S = self.S
        if eng == "act":
            if scale is None:
                S.op("act", lambda e: e.activation(out=out, in_=in_, func=AF.Copy), reads=reads, writes=writes)
            else:
                S.op("act", lambda e: e.activation(out=out, in_=in_, func=AF.Copy, scale=scale), reads=reads, writes=writes)
        else:
            if scale is None:
                S.op("dve", lambda e: e.tensor_copy(out=out, in_=in_), reads=reads, writes=writes)
            else:
                S.op("dve", lambda e: e.tensor_scalar(out=out, in0=in_, scalar1=scale, scalar2=None, op0=ALU.mult), reads=reads, writes=writes)

    def linear_fm(self, W, r0, nkc, c0, ncols_total, rhs_buf, rhs_key, kc0, evac, wcols=256):
        S = self.S
        ntile = ncols_total // wcols
        for wt in range(ntile):
            slot, view = self.load_w(W, r0, nkc, c0 + wt * wcols, wcols)
            for o in range(wcols // P):
                ot = wt * (wcols // P) + o
                for tg in range(NTG):
                    sl = slice(tg * TG, (tg + 1) * TG)
                    pi = self.next_ps()
                    ps = self.ps[pi]
                    for kc in range(nkc):
                        S.op("pe", lambda e, ps=ps, view=view, kc=kc, o=o, sl=sl: e.matmul(
                            ps[:, 0:TG], lhsT=view[:, kc, o * P:(o + 1) * P], rhs=rhs_buf[:, kc0 + kc, sl],
                            start=(kc == 0), stop=(kc == nkc - 1)),
                            reads=[("wbuf", slot, 0 if kc < nkc // 2 else 1), (rhs_key, kc0 + kc, tg)], writes=[("ps", pi)])
                    evac(ot, tg, ps, pi)

    def proj_qk(self, W, c0, ntiles, dst, drow0, scale=None):
        S = self.S

        def evac(ot, tg, ps, pi):
            st = self.stage[ot % 2]
            sl = slice(tg * TG, (tg + 1) * TG)
            self.copy_op(self.evac_engine(), st[:, sl], ps[:, 0:TG], [("ps", pi)], [("stage", ot % 2, tg)], scale=scale)
            if tg == NTG - 1:
                row = drow0 + ot * P
                if isinstance(dst, dict):
                    ci, w0, _ = self.ch_find(dst, row)
                    dap = dst["loc"][ci].ap()[w0:w0 + P, :]
                    key = (dst["name"], row)
                else:
                    dap = dst[row:row + P, :]
                    key = (dst.tensor.name, row)
                S.op("sp", lambda e: e.dma_start(out=dap, in_=st[:]),
                     reads=[("stage", ot % 2, t) for t in range(NTG)], writes=[key], dma=True)
                if isinstance(dst, dict):
                    dst["keys"][ci].append(key)
                    if row + P == dst["bounds"][ci + 1]:
                        self.emit_cc(dst, ci)

        self.linear_fm(W, 0, KC, c0, ntiles * P, self.xT, "xT", 0, evac)

    def proj_v(self, W, c0, ncols, dst, dcol0):
        S = self.S
        for wt in range(ncols // 256):
            slot, view = self.load_w(W, 0, KC, c0 + wt * 256, 256)
            for tb in range(NQB):
                qw = qw_of(tb)
                tsl = slice(tb * P, tb * P + qw)
                pi = self.next_ps()
                ps = self.ps[pi]
                for kc in range(KC):
                    S.op("pe", lambda e, ps=ps, view=view, kc=kc, tsl=tsl, qw=qw: e.matmul(
                        ps[0:qw, 0:256], lhsT=self.xT[:, kc, tsl], rhs=view[:, kc, :], start=(kc == 0), stop=(kc == KC - 1)),
                        reads=[("wbuf", slot, 0 if kc < 8 else 1)] + [("xT", kc, t) for t in range(NTG)], writes=[("ps", pi)])
                self.vsrr = getattr(self, "vsrr", 0) + 1
                vi = self.vsrr % 3
                vs = self.vstage[vi]
                self.copy_op(self.evac_engine(), vs[0:qw, :], ps[0:qw, 0:256], [("ps", pi)], [("vstage", vi)])
                ci, w0, _ = self.ch_find(dst, dcol0 + wt * 256)
                dap = dst["loc"][ci].ap()[tb * P: tb * P + qw, w0:w0 + 256]
                vkey = (dst["name"], "v", tb, dcol0 + wt * 256)
                S.op("sp", lambda e, vs=vs, qw=qw, dap=dap: e.dma_start(out=dap, in_=vs[0:qw, :]),
                     reads=[("vstage", vi)], writes=[vkey], dma=True)
                dst["keys"][ci].append(vkey)
            if dcol0 + (wt + 1) * 256 == dst["bounds"][ci + 1]:
                self.emit_cc(dst, ci)

    def emit_cc(self, ch, ci):
        S = self.S
        src, dst = ch["loc"][ci], ch["gat"][ci]
        S.op("pool", lambda e: e.collective_compute("AllGather", ALU.bypass, replica_groups=REPLICA_GROUPS,
                                                    ins=[src.ap().opt()], outs=[dst.ap().opt()]),
             reads=list(ch["keys"][ci]), writes=[("cc", ch["name"], ci)], cc=True)
        ch["done"][ci] = True

    def allgather(self, ch):
        assert all(ch["done"]), ch["name"]

    def add_into_h(self, ot, tg, ps, pi):
        sl = slice(tg * TG, (tg + 1) * TG)
        self.S.op("dve", lambda e: e.tensor_tensor(out=self.hT[:, ot, sl], in0=self.hT[:, ot, sl], in1=ps[:, 0:TG], op=ALU.add),
                  reads=[("ps", pi), ("hT", ot, tg)], writes=[("hT", ot, tg)])

    def out_proj(self, W):
        self.linear_fm(W, 0, KC, 0, D, self.xT, "xT", 0, self.add_into_h)

    def mlp(self, l):
        S = self.S
        W1 = self.w_mlp_in[l]
        W2 = self.w_mlp_out[l]
        for fc in range(8):
            def evac1(ot, tg, ps, pi):
                sl = slice(tg * TG, (tg + 1) * TG)
                self.rrr = getattr(self, "rrr", 0) + 1
                ri = self.rrr % 3
                rt = self.relu_t[ri]
                S.op("act", lambda e: e.activation(out=rt[:], in_=ps[:, 0:TG], func=AF.Relu), reads=[("ps", pi)], writes=[("relu", ri)])
                S.op("dve", lambda e: e.tensor_tensor(out=self.hidT[:, ot, sl], in0=rt[:], in1=rt[:], op=ALU.mult),
                     reads=[("relu", ri)], writes=[("hidT", ot, tg)])
            self.linear_fm(W1, 0, KC, fc * 1024, 1024, self.xT, "xT", 0, evac1)
            self.linear_fm(W2, fc * 1024, 8, 0, D, self.hidT, "hidT", 0, self.add_into_h, wcols=512)

    def load_k(self, slot, c, ch, row):
        S = self.S
        kt = self.kT[slot][c]
        ci, w0, w = self.ch_find(ch, row)
        g = ch["gat"][ci].ap()
        for r in range(2):
            S.op("sp", lambda e, r=r: e.dma_start(out=kt[:, r * T:(r + 1) * T], in_=g[r * w + w0: r * w + w0 + 64, :]),
                 reads=[("cc", ch["name"], ci)], writes=[("kT", slot, c, r)], dma=True)

    def load_q(self, slot, c, QTd, row):
        S = self.S
        qt = self.qT[slot][c]
        S.op("sp", lambda e: e.dma_start(out=qt[:], in_=QTd.ap()[row:row + 64, :]),
             reads=[(QTd.name, (row // P) * P)], writes=[("qT", slot, c)], dma=True)

    def load_v(self, slot, ch, col, ncol):
        S = self.S
        vs = self.Vs[slot]
        ci, w0, w = self.ch_find(ch, col)
        src = ch["gat"][ci].ap().rearrange("(g p) f -> p g f", p=P)
        for (g0, g1) in ((0, 9), (9, NGB)):
            S.op("sp", lambda e, g0=g0, g1=g1: e.dma_start(out=vs[:, g0:g1, 0:ncol], in_=src[:, g0:g1, w0:w0 + ncol]),
                 reads=[("cc", ch["name"], ci)], writes=[("Vs", slot, g0)], dma=True)

    def set_v_ones(self, slot, col):
        self.S.op("dve", lambda e: e.memset(self.Vs[slot][:, :, col:col + 1], 1.0), writes=[("Vs1", slot)],
                  reads=[])

    def transpose_out(self, obi, qw, ftile, j):
        S = self.S
        self.ptrr = getattr(self, "ptrr", 0) + 1
        pt = self.ptrr % 4
        ident = self.consts[:, 2, :]
        tsl = slice(j * P, j * P + qw)
        tgs = sorted(set([(j * P) // TG, (j * P + qw - 1) // TG]))
        S.op("pe", lambda e: e.transpose(self.psT[:, pt * 128: pt * 128 + qw], self.ob[obi][0:qw, :], ident[0:qw, 0:qw]),
             reads=[("ob", obi), "consts"], writes=[("psT", pt)])
        self.copy_op(self.evac_engine(), self.xT[:, ftile, tsl], self.psT[:, pt * 128: pt * 128 + qw], [("psT", pt)],
                     [("xT", ftile, t) for t in tgs])

    def run_tasks(self, factories, nslots):
        pending = list(factories)
        active = {}
        free = list(range(nslots))
        while pending or active:
            while pending and free:
                sl = free.pop(0)
                active[sl] = pending.pop(0)(sl)
            for sl in sorted(active.keys()):
                try:
                    next(active[sl])
                except StopIteration:
                    del active[sl]
                    free.append(sl)

    @staticmethod
    def split_groups(glist, split0=False, maxn=4):
        groups = []
        cur = []
        for g in glist:
            if cur and (g != cur[-1] + 1 or len(cur) == maxn or (split0 and cur[-1] == 0)):
                groups.append(cur)
                cur = []
            cur.append(g)
        if cur:
            groups.append(cur)
        return groups

    def softmax_task(self, ts, j, glist, bias_ap_of, kts, qts, vs, vcols, split0, fin_steps):
        S = self.S
        qw = qw_of(j)
        qsl = slice(j * P, j * P + qw)
        ncomp = len(kts)
        groups = self.split_groups(glist, split0)
        psS = self.ps[ts]
        psOb = self.ps[3 + ts][:, 0:vcols + 1]
        if ncomp == 1:
            psO = [psOb]
            okeys = [("psO", ts)]
        else:
            psO = [self.o0buf[ts][:, 0:vcols + 1], psOb]
            okeys = [("o0buf", ts), ("psO", ts)]
        seq = [(c, grp) for c in range(ncomp) for grp in groups]
        first = [True] * ncomp
        lastgrp = groups[-1]
        pend = None
        pend2 = None
        for si in range(len(seq) + 2):
            item = None
            if si < len(seq):
                c, grp = seq[si]
                n = len(grp)
                kt, ktkeys = kts[c]
                qt, qtkeys = qts[c]
                for i, g in enumerate(grp):
                    S.op("pe", lambda e, i=i, g=g: e.matmul(psS[:, i * 128: i * 128 + qw], lhsT=kt[:, g * P:(g + 1) * P], rhs=qt[:, qsl],
                                                            start=True, stop=True),
                         reads=ktkeys + qtkeys, writes=[("psS", ts)])
                self.tmprr[ts] += 1
                tb = self.tmprr[ts] % 3
                pt = self.pT2[ts][tb]
                b_ap, bkeys = bias_ap_of(grp, qw)
                p0 = self.spT2[ts][tb % 2]
                S.op("act", lambda e: e.activation(out=p0[:, 0:n, 0:qw], in_=psS[:, 0:n * 128].rearrange("p (n q) -> p n q", q=128)[:, :, 0:qw],
                                                   func=AF.Exp, scale=0.125),
                     reads=[("psS", ts)], writes=[("spT", ts, tb % 2)])
                S.op("dve", lambda e: e.tensor_tensor(out=pt[:, 0:n, 0:qw], in0=p0[:, 0:n, 0:qw], in1=b_ap, op=ALU.mult),
                     reads=[("spT", ts, tb % 2)] + bkeys, writes=[("pT", ts, tb)])
                item = (c, grp, pt, tb)
            if pend2 is not None:
                pc, pgrp, ppt, ptb = pend2
                for i, g in enumerate(pgrp):
                    st = first[pc]
                    first[pc] = False
                    sp_ = (pgrp is lastgrp and i == len(pgrp) - 1)
                    S.op("pe", lambda e, i=i, g=g, st=st, sp_=sp_: e.matmul(
                        psOb[0:qw, :], lhsT=ppt[:, i, 0:qw], rhs=vs[0][:, g, 0:vcols + 1], start=st, stop=sp_),
                        reads=[("pT", ts, ptb)] + vs[1], writes=[("psO", ts)])
                if ncomp == 2 and pc == 0 and pgrp is lastgrp:
                    self.copy_op(self.evac_engine(), self.o0buf[ts][0:qw, 0:vcols + 1], psOb[0:qw, :], [("psO", ts)], [("o0buf", ts)])
            pend2 = pend
            pend = item
            yield
        for step in fin_steps(ts, psO, okeys):
            step()
            yield

    def attn_c(self):
        S = self.S
        S.op("sp", lambda e: e.dma_start(out=self.tabA[:, 0:4, :], in_=self.negdist_c_d), writes=["tabA"], dma=True)
        S.op("sp", lambda e: e.dma_start(out=self.tabB[:, 0:4, :], in_=self.mask_c_d), writes=["tabB"], dma=True)
        for s_ in range(2):
            self.set_v_ones(s_, 128)
        facts = []
        for h in range(16):
            for j in range(NQB):
                facts.append(lambda ts, h=h, j=j: self.task_c(ts, h, j))
        self.run_tasks(facts, 3)

    def head_pre_c(self, h):
        S = self.S
        slot = h % 2
        for c in range(2):
            self.load_k(slot, c, self.KT1, h * 128 + c * 64)
            self.load_q(slot, c, self.QT1, h * 128 + c * 64)
        self.load_v(slot, self.V1, h * 128, 128)
        bH = self.biasH[slot]
        slope = C_SLOPES[h]
        for rel in range(NREL):
            if rel in C_NEAR:
                i = C_NEAR.index(rel)
                S.op("dve", lambda e: e.scalar_tensor_tensor(out=bH[:, rel, :], in0=self.tabA[:, i, :], scalar=slope,
                                                             in1=self.tabB[:, i, :], op0=ALU.mult, op1=ALU.add),
                     reads=["tabA", "tabB"], writes=[("biasH", 0, rel)])
            else:
                S.op("dve", lambda e: e.tensor_scalar(out=bH[:, rel, :], in0=self.kqneg[:], scalar1=self.negd0[:, rel:rel + 1],
                                                      scalar2=slope, op0=ALU.add, op1=ALU.mult),
                     reads=["kqneg", "negd0"], writes=[("biasH", 0, rel)])
        eH = self.expH[slot]
        S.op("act", lambda e: e.activation(out=eH[:], in_=bH[:], func=AF.Exp),
             reads=[("biasH", 0, r) for r in range(NREL)], writes=[("expH", slot, r) for r in range(NREL)])

    def task_c(self, ts, h, j):
        if j == 0:
            self.head_pre_c(h)
        slot = h % 2
        bH = self.biasH[slot]
        kts = [(self.kT[slot][c], [("kT", slot, c, 0), ("kT", slot, c, 1)]) for c in range(2)]
        qts = [(self.qT[slot][c], [("qT", slot, c)]) for c in range(2)]
        vs = (self.Vs[slot], [("Vs", slot, 0), ("Vs", slot, 9), ("Vs1", slot)])
        gmax = min(16, 9 + j)
        glist = [g for g in range(0, gmax + 1) if not c_dead(h, g - j + 8)]

        eH = self.expH[slot]

        def bias_ap_of(grp, qw):
            r0 = grp[0] - j + 8
            r1 = grp[-1] - j + 8
            return eH[:, r0:r1 + 1, 0:qw], [("expH", slot, r) for r in range(r0, r1 + 1)]

        def fin_steps(ts, psO, okeys):
            return self.fin_c_steps(ts, h, j, psO, okeys)

        return self.softmax_task(ts, j, glist, bias_ap_of, kts, qts, vs, 128, False, fin_steps)

    def fin_c_steps(self, ts, h, j, psO, okeys):
        S = self.S
        qw = qw_of(j)
        sm = self.small3[ts]
        of = self.ofin3[ts]
        ob = self.ob3[ts]
        jk = self.junk3[ts]
        kk = ("fin", ts)
        steps = []
        A = steps.append
        A(lambda: S.op("dve", lambda e: e.reciprocal(out=sm[0:qw, 0:1], in_=psO[0][0:qw, 128:129]), reads=[okeys[0]], writes=[(kk, 0)]))
        A(lambda: S.op("dve", lambda e: e.reciprocal(out=sm[0:qw, 1:2], in_=psO[1][0:qw, 128:129]), reads=[okeys[1]], writes=[(kk, 1)]))
        A(lambda: S.op("dve", lambda e: e.tensor_tensor(out=sm[0:qw, 2:3], in0=sm[0:qw, 1:2], in1=self.lamt[0:qw, 5:6], op=ALU.mult),
                       reads=[(kk, 1), "nlam"], writes=[(kk, 2)]))
        A(lambda: S.op("act", lambda e: e.activation(out=of[0:qw, :], in_=psO[0][0:qw, 0:128], func=AF.Copy, scale=sm[0:qw, 0:1]),
                       reads=[okeys[0], (kk, 0)], writes=[(kk, "of")]))
        A(lambda: S.op("dve", lambda e: e.scalar_tensor_tensor(out=of[0:qw, :], in0=psO[1][0:qw, 0:128], scalar=sm[0:qw, 2:3], in1=of[0:qw, :],
                                                               op0=ALU.mult, op1=ALU.add),
                       reads=[okeys[1], (kk, 2), (kk, "of")], writes=[(kk, "of")]))
        A(lambda: S.op("act", lambda e: e.activation(out=jk[0:qw, :], in_=of[0:qw, :], func=AF.Square),
                       reads=[(kk, "of")], writes=[(kk, "jk")]))
        A(lambda: S.op("dve", lambda e: e.tensor_reduce(out=sm[0:qw, 3:4], in_=jk[0:qw, :], axis=AX.X, op=ALU.add),
                       reads=[(kk, "jk")], writes=[(kk, 3)]))
        A(lambda: S.op("act", lambda e: e.activation(out=sm[0:qw, 4:5], in_=sm[0:qw, 3:4], func=AF.Ln, bias=self.epsc[0:qw, 0:1], scale=1.0 / 128),
                       reads=[(kk, 3), "epsc"], writes=[(kk, 4)]))
        A(lambda: S.op("act", lambda e: e.activation(out=sm[0:qw, 5:6], in_=sm[0:qw, 4:5], func=AF.Exp, scale=-0.5),
                       reads=[(kk, 4)], writes=[(kk, 5)]))
        A(lambda: S.op("dve", lambda e: e.scalar_tensor_tensor(out=ob[0:qw, :], in0=of[0:qw, :], scalar=sm[0:qw, 5:6], in1=self.subg[0:qw, :],
                                                               op0=ALU.mult, op1=ALU.mult),
                       reads=[(kk, "of"), (kk, 5), "subg"], writes=[("ob3", ts)]))
        A(lambda: self.transpose_evac(qw, h, j, self.transpose_pe(ob, [("ob3", ts)], qw)))
        return steps

    def transpose_pe(self, ob, obkeys, qw):
        S = self.S
        self.ptrr = getattr(self, "ptrr", 0) + 1
        pt = self.ptrr % 2
        ident = self.consts[:, 2, :]
        S.op("pe", lambda e: e.transpose(self.psTs[pt][:, 0:qw], ob[0:qw, :], ident[0:qw, 0:qw]),
             reads=obkeys + ["consts"], writes=[("psT", pt)])
        self.last_pt = pt
        return pt

    def transpose_evac(self, qw, ftile, j, pt=None):
        pt = self.last_pt if pt is None else pt
        tsl = slice(j * P, j * P + qw)
        tgs = sorted(set([(j * P) // TG, (j * P + qw - 1) // TG]))
        self.copy_op(self.evac_engine(), self.xT[:, ftile, tsl], self.psTs[pt][:, 0:qw], [("psT", pt)],
                     [("xT", ftile, t) for t in tgs])

    def attn_a(self):
        S = self.S
        S.op("sp", lambda e: e.dma_start(out=self.tabA[:], in_=self.negdist_a_d), writes=["tabA"], dma=True)
        S.op("sp", lambda e: e.dma_start(out=self.tabB[:], in_=self.mask_a_d), writes=["tabB"], dma=True)
        S.op("sp", lambda e: e.dma_start(out=self.tabA0[:], in_=self.mask_a0_d), writes=["tabA0"], dma=True)
        for s_ in range(2):
            self.set_v_ones(s_, 64)
        facts = []
        for h in range(16):
            for j in range(NQB):
                facts.append(lambda ts, h=h, j=j: self.task_a(ts, h, j))
        self.run_tasks(facts, 3)

    def head_pre_a(self, h):
        S = self.S
        kvh = h // 4
        kslot = kvh % 2
        slot = h % 2
        if h % 4 == 0:
            self.load_k(kslot, 0, self.KT0, kvh * 64)
            self.load_v(kslot, self.V0, kvh * 64, 64)
        self.load_q(slot, 0, self.QT0, h * 64)
        bH = self.biasH[slot]
        slope = A_SLOPES[h]
        for i, rel in enumerate(A_NEAR):
            S.op("dve", lambda e: e.scalar_tensor_tensor(out=bH[:, rel, :], in0=self.tabA[:, i, :], scalar=slope,
                                                         in1=self.tabB[:, i, :], op0=ALU.mult, op1=ALU.add),
                 reads=["tabA", "tabB"], writes=[("biasH", 0, rel)])
        for j in range(NQB):
            rel = 8 - j
            if rel in A_NEAR:
                i = A_NEAR.index(rel)
                S.op("dve", lambda e: e.scalar_tensor_tensor(out=bH[:, B0IDX[j], :], in0=self.tabA[:, i, :], scalar=slope,
                                                             in1=self.tabA0[:, j, :], op0=ALU.mult, op1=ALU.add),
                     reads=["tabA", "tabA0"], writes=[("biasH", 0, B0IDX[j])])
            else:
                S.op("dve", lambda e: e.tensor_scalar(out=bH[:, B0IDX[j], :], in0=self.kqneg[:], scalar1=self.negd0[:, rel:rel + 1],
                                                      scalar2=slope, op0=ALU.add, op1=ALU.mult),
                     reads=["kqneg", "negd0"], writes=[("biasH", 0, B0IDX[j])])
                S.op("dve", lambda e: e.tensor_tensor(out=bH[:, B0IDX[j], :], in0=bH[:, B0IDX[j], :], in1=self.tabA0[:, j, :], op=ALU.add),
                     reads=[("biasH", 0, B0IDX[j]), "tabA0"], writes=[("biasH", 0, B0IDX[j])])

    def task_a(self, ts, h, j):
        S = self.S
        if j == 0:
            self.head_pre_a(h)
            bH_, eH_ = self.biasH[h % 2], self.expH[h % 2]
            for (r0, r1) in ((0, 13), (15, 18)):
                S.op("act", lambda e: e.activation(out=eH_[:, r0:r1, :], in_=bH_[:, r0:r1, :], func=AF.Exp),
                     reads=[("biasH", 0, r) for r in range(r0, r1) if r in A_NEAR or r in B0IDX],
                     writes=[("expH", h % 2, r) for r in range(r0, r1)])
        kslot = (h // 4) % 2
        slot = h % 2
        bH = self.biasH[slot]
        kts = [(self.kT[kslot][0], [("kT", kslot, 0, 0), ("kT", kslot, 0, 1)])]
        qts = [(self.qT[slot][0], [("qT", slot, 0)])]
        vs = (self.Vs[kslot], [("Vs", kslot, 0), ("Vs", kslot, 9), ("Vs1", kslot)])
        near = sorted(set([g for g in list(range(j - 2, j + 2)) + list(range(j + 7, j + 10)) if 1 <= g <= 16]))
        glist = [0] + near

        eH = self.expH[slot]

        def bias_ap_of(grp, qw):
            if grp[0] == 0:
                assert len(grp) == 1
                return eH[:, B0IDX[j]:B0IDX[j] + 1, 0:qw], [("expH", slot, B0IDX[j])]
            r0 = grp[0] - j + 8
            r1 = grp[-1] - j + 8
            return eH[:, r0:r1 + 1, 0:qw], [("expH", slot, r) for r in range(r0, r1 + 1)]

        def fin_steps(ts, psO, okeys):
            qw = qw_of(j)
            sm = self.small3[ts]
            ob = self.obA[(h // 2) % 2][j]
            kk = ("finA", ts)
            obk = ("obA", (h // 2) % 2, j)
            steps = []
            A = steps.append
            A(lambda: S.op("dve", lambda e: e.tensor_tensor(out=sm[0:qw, 0:1], in0=psO[0][0:qw, 64:65], in1=self.esink[0:qw, h:h + 1], op=ALU.add),
                           reads=[okeys[0], "esink"], writes=[(kk, 0)]))
            A(lambda: S.op("dve", lambda e: e.reciprocal(out=sm[0:qw, 1:2], in_=sm[0:qw, 0:1]), reads=[(kk, 0)], writes=[(kk, 1)]))
            A(lambda: S.op("act", lambda e: e.activation(out=ob[0:qw, (h % 2) * 64:(h % 2) * 64 + 64], in_=psO[0][0:qw, 0:64], func=AF.Copy,
                                                         scale=sm[0:qw, 1:2]),
                           reads=[okeys[0], (kk, 1)], writes=[obk + (h % 2,)]))
            if h % 2 == 1:
                A(lambda: self.transpose_evac(qw, h // 2, j, self.transpose_pe(ob, [obk + (0,), obk + (1,)], qw)))
            return steps

        return self.softmax_task(ts, j, glist, bias_ap_of, kts, qts, vs, 64, True, fin_steps)

    def attn_b(self):
        S = self.S
        S.op("sp", lambda e: e.dma_start(out=self.maskb_f[:], in_=self.mask_b_d), writes=["maskb_f"], dma=True)
        S.op("dve", lambda e: e.tensor_copy(out=self.maskb[:], in_=self.maskb_f[:]), reads=["maskb_f"], writes=["maskb"])
        facts = []
        for h in range(16):
            for j in range(NQB):
                facts.append(lambda ts, h=h, j=j: self.task_b(ts, h, j))
        self.run_tasks(facts, 3)

    def task_b(self, ts, h, j):
        S = self.S
        ones = self.consts[:, 0, :]
        triu = self.consts[:, 1, :]
        slot = h % 2
        if j == 0:
            self.load_k(slot, 0, self.KT0, 256 + h * 64)
            self.load_v(slot, self.V0, 256 + h * 64, 64)
            self.load_q(slot, 0, self.QT0, 1024 + h * 64)
        kt = self.kT[slot][0]
        ktkeys = [("kT", slot, 0, 0), ("kT", slot, 0, 1)]
        qt = self.qT[slot][0]
        qtkeys = [("qT", slot, 0)]
        vsb = self.Vs[slot]
        vkeys = [("Vs", slot, 0), ("Vs", slot, 9)]
        qw = qw_of(j)
        qsl = slice(j * P, j * P + qw)
        gmax = min(16, 9 + j)
        groups = []
        cur = []
        curcls = None
        for g in range(gmax, -1, -1):
            rel = g - j + 8
            cls = "near" if rel in B_NEAR else ("flag" if rel >= 9 else "free")
            if cur and (cls != curcls or len(cur) == 4 or cls == "near"):
                groups.append((curcls, cur))
                cur = []
            cur.append(g)
            curcls = cls
        if cur:
            groups.append((curcls, cur))
        psS = self.ps[ts]
        psD = psS
        psO = self.ps[3 + ts][:, 0:64]
        okey = ("psOb", ts)
        R32 = self.R32s[ts]
        Rtmp = self.Rtmps[ts]
        S.op("dve", lambda e: e.memset(R32[:], 0.0), writes=[("R32", ts)])
        self.rbrr[ts] += 1
        rb = self.rbrr[ts] % 2
        S.op("dve", lambda e: e.memset(self.Rbfs[ts][rb][:], 0.0), writes=[("Rbf", ts, rb)])
        pend_pv = None
        first_pv = True
        ng = len(groups)

        def emit_pv(pend, first, last):
            wt, wkey, asc = pend
            n = len(asc)
            for i, g in enumerate(asc):
                st = first
                first = False
                sp_ = last and i == n - 1
                S.op("pe", lambda e: e.matmul(psO[0:qw, :], lhsT=wt[:, i, 0:qw], rhs=vsb[:, g, 0:64], start=st, stop=sp_),
                     reads=[wkey] + vkeys, writes=[okey])
            return first

        for gi, (cls, grp) in enumerate(groups):
            n = len(grp)
            asc = grp[::-1]
            if pend_pv is not None:
                first_pv = emit_pv(pend_pv, first_pv, False)
                pend_pv = None
            for i, g in enumerate(asc):
                S.op("pe", lambda e: e.matmul(psS[:, i * 128: i * 128 + qw], lhsT=kt[:, g * P:(g + 1) * P], rhs=qt[:, qsl], start=True, stop=True),
                     reads=ktkeys + qtkeys, writes=[("psS", ts)])
            et = self.tmpS2[ts][0]
            self.sprr[ts] += 1
            sb_ = self.sprr[ts] % 2
            spt = self.spT2[ts][sb_]
            spkey = ("spT", ts, sb_)
            psS3 = psS[:, 0:n * 128].rearrange("p (n q) -> p n q", q=128)[:, :, 0:qw]
            psD3 = psD[:, 0:n * 128].rearrange("p (n q) -> p n q", q=128)[:, :, 0:qw]
            if cls == "flag":
                S.op("act", lambda e: e.activation(out=et[:, 0:n, 0:qw], in_=psS3, func=AF.Exp, scale=-1.0, bias=self.flagb[:, 0:1]),
                     reads=[("psS", ts), "flagb"], writes=[("tmpS", ts, 0)])
            else:
                S.op("act", lambda e: e.activation(out=et[:, 0:n, 0:qw], in_=psS3, func=AF.Exp, scale=-1.0),
                     reads=[("psS", ts)], writes=[("tmpS", ts, 0)])
            S.op("act", lambda e: e.activation(out=spt[:, 0:n, 0:qw], in_=et[:, 0:n, 0:qw], func=AF.Ln, bias=self.onec[:, 0:1], scale=1.0),
                 reads=[("tmpS", ts, 0), "onec"], writes=[spkey])
            if cls == "near":
                mi = B_NEAR.index(grp[0] - j + 8)
                S.op("dve", lambda e: e.tensor_tensor(out=spt[:, 0, 0:qw], in0=spt[:, 0, 0:qw], in1=self.maskb[:, mi, 0:qw], op=ALU.mult),
                     reads=[spkey, "maskb"], writes=[spkey])
            yield
            for i, g in enumerate(asc):
                dsl = slice(i * 128, i * 128 + qw)
                S.op("pe", lambda e: e.matmul(psD[:, dsl], lhsT=triu, rhs=spt[:, i, 0:qw], start=False, stop=False, skip_group_check=True),
                     reads=[spkey, "consts", ("tmpS", ts, 0)], writes=[("psS", ts)])
                for i2 in range(i + 1, n):
                    S.op("pe", lambda e: e.matmul(psD[:, dsl], lhsT=ones, rhs=spt[:, i2, 0:qw], start=False, stop=False, skip_group_check=True),
                         reads=[spkey, "consts"], writes=[("psS", ts)])
                S.op("pe", lambda e: e.matmul(psD[:, dsl], lhsT=ones, rhs=self.Rbfs[ts][rb][:, 0:qw], start=False, stop=True, skip_group_check=True),
                     reads=[("Rbf", ts, rb), "consts"], writes=[("psS", ts)])
            if gi < ng - 1:
                if n > 1:
                    S.op("dve", lambda e: e.tensor_reduce(out=Rtmp[:, 0:qw], in_=spt[:, 0:n, 0:qw].rearrange("p n q -> p q n"), axis=AX.X, op=ALU.add),
                         reads=[spkey], writes=[("Rtmp", ts)])
                    S.op("dve", lambda e: e.tensor_tensor(out=R32[:, 0:qw], in0=R32[:, 0:qw], in1=Rtmp[:, 0:qw], op=ALU.add),
                         reads=[("Rtmp", ts), ("R32", ts)], writes=[("R32", ts)])
                else:
                    S.op("dve", lambda e: e.tensor_tensor(out=R32[:, 0:qw], in0=R32[:, 0:qw], in1=spt[:, 0, 0:qw], op=ALU.add),
                         reads=[spkey, ("R32", ts)], writes=[("R32", ts)])
                self.rbrr[ts] += 1
                rb = self.rbrr[ts] % 2
                S.op("dve", lambda e: e.tensor_copy(out=self.Rbfs[ts][rb][:, 0:qw], in_=R32[:, 0:qw]), reads=[("R32", ts)], writes=[("Rbf", ts, rb)])
            self.tmprr[ts] += 1
            wb = self.tmprr[ts] % 3
            wt = self.pT2[ts][wb]
            wkey = ("pT", ts, wb)
            if cls == "flag":
                S.op("act", lambda e: e.activation(out=wt[:, 0:n, 0:qw], in_=psD3, func=AF.Exp, scale=-1.0, bias=self.flagb[:, 0:1]),
                     reads=[("psS", ts), "flagb"], writes=[wkey])
            else:
                S.op("act", lambda e: e.activation(out=wt[:, 0:n, 0:qw], in_=psD3, func=AF.Exp, scale=-1.0),
                     reads=[("psS", ts)], writes=[wkey])
            if cls == "near":
                S.op("dve", lambda e: e.tensor_tensor(out=wt[:, 0, 0:qw], in0=wt[:, 0, 0:qw], in1=self.maskb[:, mi, 0:qw], op=ALU.mult),
                     reads=[wkey, "maskb"], writes=[wkey])
            pend_pv = (wt, wkey, asc)
            yield
        emit_pv(pend_pv, first_pv, True)
        yield
        ob = self.obA[(h // 2) % 2][j]
        obk = ("obA", (h // 2) % 2, j)
        self.copy_op(self.evac_engine(), ob[0:qw, (h % 2) * 64:(h % 2) * 64 + 64], psO[0:qw, :], [okey], [obk + (h % 2,)])
        yield
        if h % 2 == 1:
            pt_ = self.transpose_pe(ob, [obk + (0,), obk + (1,)], qw)
            self.transpose_evac(qw, 8 + h // 2, j, pt_)
            yield

    def dump_h(self):
        S = self.S
        outv = self.outT.rearrange("(c p) t -> p c t", p=P)
        for tg in range(NTG):
            sl = slice(tg * TG, (tg + 1) * TG)
            S.op("sp", lambda e, sl=sl: e.dma_start(out=outv[:, :, sl], in_=self.hT[:, :, sl]),
                 reads=[("hT", c, tg) for c in range(KC)], writes=[("out", tg)], dma=True)
        S.op("sp", None, reads=[("out", tg) for tg in range(NTG)])

    def dump_x(self):
        S = self.S
        outv = self.dbgx.rearrange("(c p) t -> p c t", p=P)
        for tg in range(NTG):
            sl = slice(tg * TG, (tg + 1) * TG)
            S.op("sp", lambda e, sl=sl: e.dma_start(out=outv[:, :, sl], in_=self.xT[:, :, sl]),
                 reads=[("xT", c, tg) for c in range(KC)], writes=[("outx", tg)], dma=True)
        S.op("sp", None, reads=[("outx", tg) for tg in range(NTG)])

    def stop_here(self, name):
        if self.stop != name:
            return False
        self.S.barrier()
        if name.startswith("x_"):
            self.dump_x()
        self.dump_h()
        self.emit_all()
        return True

    def build(self, stop=None):
        S = self.S
        nc = self.nc
        self.stop = stop
        if stop is not None:
            self.dbgx = nc.dram_tensor("dbgx", [D, T], BF16, kind="ExternalOutput").ap()
        self.obA = [[self.sb("obA%d_%d" % (i, j), [P, 128], BF16) for j in range(NQB)] for i in range(2)]
        assert self.sb_off <= 229344, self.sb_off
        print('sbuf end', self.sb_off)
        self.load_consts()
        self.rmsnorm(0)
        if self.stop_here("x_norm0"):
            return nc
        self.proj_qk(self.w_in_ab, 1024, 2, self.KT0, 0)
        if stop == "x_ka":
            S.barrier()
            S.op("sp", lambda e: e.dma_start(out=self.dbgx[0:256, :], in_=self.KT0["loc"][0].ap()[0:256, :]), reads=[], writes=[("outx", 0)], dma=True)
            S.op("sp", None, reads=[("outx", 0)])
            self.dump_h()
            self.emit_all()
            return nc
        self.proj_qk(self.w_in_ab, 2560, 8, self.KT0, 256)
        self.proj_v(self.w_in_ab, 1280, 256, self.V0, 0)
        if stop == "x_va":
            S.barrier()
            S.op("sp", lambda e: e.dma_start(out=self.dbgx[0:256, :], in_=self.V0.ap()[0:T, 0:256].rearrange("t f -> t f")), reads=[], writes=[("outx", 0)], dma=True) if False else None
            for f0 in range(0, 256, 32):
                S.op("sp", lambda e, f0=f0: e.dma_start(out=self.dbgx[f0:f0 + 32, :], in_=self.V0["loc"][0].ap()[:, f0:f0 + 32].rearrange("t f -> f t"), allow_slow_non_contiguous=True), reads=[], writes=[("outx", 0)], dma=True)
            S.op("sp", None, reads=[("outx", 0)])
            self.dump_h()
            self.emit_all()
            return nc
        self.proj_v(self.w_in_ab, 3584, 1024, self.V0, 256)
        self.allgather(self.KT0)
        self.allgather(self.V0)
        self.proj_qk(self.w_in_ab, 0, 8, self.QT0.ap(), 0)
        self.proj_qk(self.w_in_ab, 1536, 8, self.QT0.ap(), 1024, scale=-0.125)
        S.barrier()
        if stop == "x_proj0":
            S.op("sp", lambda e: e.dma_start(out=self.dbgx, in_=self.QT0.ap()), reads=[], writes=[("outx", 0)], dma=True)
            S.op("sp", None, reads=[("outx", 0)])
            self.dump_h()
            self.emit_all()
            return nc
        if stop == "x_kv0":
            S.op("sp", lambda e: e.dma_start(out=self.dbgx[0:640, :], in_=self.KT0["gat"][0].ap()[640:1280, :]), reads=[], writes=[("outx", 0)], dma=True)
            S.op("sp", None, reads=[("outx", 0)])
            self.dump_h()
            self.emit_all()
            return nc
        if stop == "x_attn_a":
            self.attn_a()
            self.stop_here("x_attn_a")
            return nc
        if stop == "x_attn_b":
            self.attn_b()
            self.stop_here("x_attn_b")
            return nc
        self.attn_a()
        S.barrier()
        self.attn_b()
        S.barrier()
        if self.stop_here("x_attn0"):
            return nc
        self.out_proj(self.w_out_ab)
        if self.stop_here("h_attn0"):
            return nc
        self.rmsnorm(3)
        self.mlp(0)
        if self.stop_here("h_l0"):
            return nc
        self.rmsnorm(1)
        self.proj_qk(self.w_in_c, 2048, 16, self.KT1, 0)
        self.proj_v(self.w_in_c, 4096, 2048, self.V1, 0)
        self.allgather(self.KT1)
        self.allgather(self.V1)
        self.proj_qk(self.w_in_c, 0, 16, self.QT1.ap(), 0)
        S.barrier()
        self.attn_c()
        S.barrier()
        if self.stop_here("x_attn1"):
            return nc
        self.out_proj(self.w_out_c)
        self.rmsnorm(4)
        self.mlp(1)
        if self.stop_here("h_l1"):
            return nc
        self.rmsnorm(2, final=True)
        self.dump_h()
        self.emit_all()
        return nc

    def emit_all(self):
        S = self.S
        nc = self.nc
        with ExitStack() as stack:
            S.finalize(nc, stack)
            block = stack.enter_context(nc.Block())

            @block.tensor
            def _(e):
                S.emit("pe", e)

            @block.scalar
            def _(e):
                S.emit("act", e)

            @block.vector
            def _(e):
                S.emit("dve", e)

            @block.gpsimd
            def _(e):
                S.emit("pool", e)

            @block.sync
            def _(e):
                S.emit("sp", e)


def chunk_of(p):
    return 1 + np.floor_divide(p - 16, 64)


def make_tables(rank):
    base = rank * T
    k = np.arange(128)[:, None]
    q = np.arange(128)[None, :]
    t = {}
    t["kqneg"] = (-(q - k)).astype(np.float32) * np.ones((128, 128), np.float32)
    negd0 = np.zeros((128, NREL), np.float32)
    for rel in range(NREL):
        dq0 = base - 128 * (rel - 8)
        negd0[:, rel] = -float(dq0) if dq0 >= 128 else -1.0e6
    t["negd0"] = negd0

    def posmats(rel):
        dq0 = base - 128 * (rel - 8)
        diff = dq0 + q - k
        return diff

    def absq(rel):
        jj = 8
        g = rel - 8 + jj
        qpos = base + 128 * jj + q + 0 * k
        kpos = 128 * g + k + 0 * q
        return qpos, kpos

    na = np.zeros((128, len(A_NEAR), 128), np.float32)
    ma = np.zeros((128, len(A_NEAR), 128), np.float32)
    for i, rel in enumerate(A_NEAR):
        qpos, kpos = absq(rel)
        qpos = qpos + 128 * 64
        kpos = kpos + 128 * 64
        na[:, i, :] = -np.abs(qpos - kpos)
        qc, kc = chunk_of(qpos), chunk_of(kpos)
        ok = (kc <= qc) & (kc >= qc - 2)
        ma[:, i, :] = np.where(ok, 0.0, NEGBIG)
    t["negdist_a"] = na
    t["mask_a"] = ma
    ma0 = np.zeros((128, NQB, 128), np.float32)
    for j in range(NQB):
        qpos = base + 128 * j + q + 0 * k
        kpos = k + 0 * q
        qc, kc = chunk_of(qpos), chunk_of(kpos)
        ok = (kpos < 16) | ((kpos >= 16) & (kc <= qc) & (kc >= qc - 2))
        ma0[:, j, :] = np.where(ok, 0.0, NEGBIG)
    t["mask_a0"] = ma0
    ncm = np.zeros((128, len(C_NEAR), 128), np.float32)
    mc = np.zeros((128, len(C_NEAR), 128), np.float32)
    for i, rel in enumerate(C_NEAR):
        qpos, kpos = absq(rel)
        qpos = qpos + 128 * 64
        kpos = kpos + 128 * 64
        ncm[:, i, :] = -np.abs(qpos - kpos)
        ok = chunk_of(kpos) <= chunk_of(qpos)
        mc[:, i, :] = np.where(ok, 0.0, NEGBIG)
    t["negdist_c"] = ncm
    t["mask_c"] = mc
    mb = np.zeros((128, len(B_NEAR), 128), np.float32)
    for i, rel in enumerate(B_NEAR):
        diff = posmats(rel)
        mb[:, i, :] = (diff > 0).astype(np.float32)
    t["mask_b"] = mb
    t["flagb"] = np.full((128, 1), NEGBIG if rank == 0 else 0.0, np.float32)
    cst = np.zeros((128, 3, 128), np.float32)
    cst[:, 0, :] = 1.0
    cst[:, 1, :] = (k >= q).astype(np.float32)
    cst[:, 2, :] = np.eye(128, dtype=np.float32)
    t["consts"] = cst
    return t


_NC_CACHE = {}


def get_nc(stop=None):
    key = "main" if stop is None else str(stop)
    if key not in _NC_CACHE:
        b = Builder()
        _NC_CACHE[key] = b.build(stop)
    return _NC_CACHE[key]


def make_in_maps(x, meta_tokens, ab_norm, w_in_ab, attn_sinks, w_out_ab, c_norm, w_in_c, diff_lambda, diff_subln,
                 w_out_c, mlp_norm, w_mlp_in, w_mlp_out, final_norm):
    f = lambda a: np.ascontiguousarray(np.asarray(a, dtype=np.float32))
    x = f(x)
    B = x.shape[0]
    meta = f(meta_tokens)
    gains = np.stack([f(ab_norm)[0], f(c_norm)[0], f(final_norm), f(mlp_norm)[0], f(mlp_norm)[1]], 0)
    gains_l = np.ascontiguousarray(gains.reshape(5, KC, P).transpose(2, 0, 1).reshape(P, 5 * KC))
    sinks = np.ascontiguousarray(np.broadcast_to(f(attn_sinks)[0][None, :], (P, 16)))
    lamv = np.ascontiguousarray(np.broadcast_to(f(diff_lambda)[0].reshape(1, 256), (P, 256)))
    subg = np.ascontiguousarray(np.broadcast_to(f(diff_subln)[0][None, :], (P, 128)))
    shared = {
        "w_in_ab": f(w_in_ab)[0], "w_out_ab": f(w_out_ab)[0], "w_in_c": f(w_in_c)[0], "w_out_c": f(w_out_c)[0],
        "w_mlp_in0": f(w_mlp_in)[0], "w_mlp_in1": f(w_mlp_in)[1], "w_mlp_out0": f(w_mlp_out)[0], "w_mlp_out1": f(w_mlp_out)[1],
        "gains": gains_l, "sinks": sinks, "lamv": lamv, "subg": subg,
    }
    tabs = [make_tables(0), make_tables(1)]
    in_maps = []
    for core in range(8):
        b, r = core // 2, core % 2
        seq = np.zeros((LP, D), np.float32)
        seq[0:16] = meta
        seq[16:16 + 2048] = x[b]
        h0T = np.ascontiguousarray(seq[r * T:(r + 1) * T].T)
        m = dict(shared)
        m["h0T"] = h0T
        m.update(tabs[r])
        in_maps.append(m)
    return in_maps


def assemble(results):
    out = np.zeros((4, 2048, D), np.float32)
    for core in range(8):
        b, r = core // 2, core % 2
        oT = np.asarray(results[core]["outT"])
        rows = oT.T
        pos0 = r * T
        lo = max(pos0, 16)
        hi = min(pos0 + T, 16 + 2048)
        out[b, lo - 16:hi - 16] = rows[lo - pos0:hi - pos0]
    return out


def kernel(**inputs):
    nc = get_nc()
    in_maps = make_in_maps(**inputs)
    res = run_bass_kernel_spmd(nc, in_maps, core_ids=list(range(8)))
    return assemble(res.results)
```

```python
import math
import types
from contextlib import ExitStack

import numpy as np
import concourse.bass as bass
import concourse.mybir as mybir
from concourse.bass_utils import run_bass_kernel_spmd

F32 = mybir.dt.float32
BF16 = mybir.dt.bfloat16
AF = mybir.ActivationFunctionType
ALU = mybir.AluOpType
AX = mybir.AxisListType

P = 128
D = 2048
KC = 16
T = 1088
LP = 2176
NTG = 4
TG = 272
NQB = 9
NGB = 17
DFF = 8192
EPS = 1e-6
NEGBIG = -30000.0
REPLICA_GROUPS = [[0, 1], [2, 3], [4, 5], [6, 7]]
NREL = 18
A_SLOPES = [2.0 ** (-8.0 * (i + 1) / 16) for i in range(16)]
C_SLOPES = A_SLOPES
LAMBDA_INIT = 0.8 - 0.6 * math.exp(-0.3 * 1)

A_NEAR = [6, 7, 8, 9, 15, 16, 17]
C_NEAR = [8, 9, 16, 17]
B_NEAR = [8, 16, 17]
B0IDX = [0, 1, 2, 3, 4, 5, 10, 11, 12]


def c_dead(h, rel):
    s_ = C_SLOPES[h]
    if rel in C_NEAR:
        return False
    dead = []
    for base in (0, T):
        dq0 = base - 128 * (rel - 8)
        if dq0 < 128:
            dead.append(base == 0 and rel >= 10)
        else:
            dead.append(s_ * (dq0 - 127) > 110.0)
    if not dead[0] and rel >= 10:
        return False
    return all(dead) and (s_ * 1.0e6 > 110.0)


def qw_of(j):
    return 128 if j < 8 else 64


def _freeze(fn):
    if fn is None or fn.__closure__ is None:
        return fn
    cells = []
    for c in fn.__closure__:
        try:
            cells.append(types.CellType(c.cell_contents))
        except ValueError:
            cells.append(c)
    return types.FunctionType(fn.__code__, fn.__globals__, fn.__name__, fn.__defaults__, tuple(cells))


class Op:
    __slots__ = ("eng", "fn", "dma", "waits", "sig", "sem", "val", "idx", "cc")

    def __init__(self, eng, fn, dma, cc=False):
        self.eng = eng
        self.fn = fn
        self.dma = dma
        self.cc = cc
        self.waits = {}
        self.sig = False
        self.sem = None
        self.val = 0


class Sched:
    ENGS = ("pe", "act", "dve", "pool", "sp")
    NDMASEM = 8

    def __init__(self):
        self.ops = {e: [] for e in self.ENGS}
        self.allops = []
        self.last_w = {}
        self.readers = {}
        self.dma_rr = {"pool": 0, "sp": 0}
        self.dma_last = {}

    def op(self, eng, fn, reads=(), writes=(), dma=False, cc=False):
        o = Op(eng, _freeze(fn), dma, cc)
        o.idx = len(self.allops)
        deps = set()
        for k in reads:
            w = self.last_w.get(k)
            if w is not None:
                deps.add(w)
        for k in writes:
            w = self.last_w.get(k)
            if w is not None:
                deps.add(w)
            for r in self.readers.get(k, ()):
                deps.add(r)
        if dma:
            slot = (eng, self.dma_rr[eng] % self.NDMASEM)
            self.dma_rr[eng] += 1
            o.sem = slot
            prev = self.dma_last.get(slot)
            if prev is not None:
                deps.add(prev)
            self.dma_last[slot] = o
        for d in deps:
            if d is o:
                continue
            if d.eng == "pe" and eng == "pe" and not d.dma:
                continue
            o.waits[d.idx] = d
            d.sig = True
        for k in reads:
            self.readers.setdefault(k, []).append(o)
        for k in writes:
            self.last_w[k] = o
            self.readers[k] = []
        self.ops[eng].append(o)
        self.allops.append(o)
        return o

    def barrier(self):
        last = []
        for e in self.ENGS:
            for o_ in reversed(self.ops[e]):
                if o_.fn is not None:
                    last.append(o_)
                    break
        outstanding = [o for o in self.dma_last.values()]
        key = ("__barrier__", len(self.allops))
        for e in self.ENGS:
            o = Op(e, None, False)
            o.idx = len(self.allops)
            for d in last + outstanding:
                if d.eng == e and not d.dma:
                    continue
                o.waits[d.idx] = d
                d.sig = True
            self.ops[e].append(o)
            self.allops.append(o)
        self.readers = {}

    def finalize(self, nc, stack):
        cnt = {e: 0 for e in self.ENGS}
        self.esem = {e: stack.enter_context(nc.semaphore("es_" + e)) for e in ("pe", "act", "dve", "pool", "sp")}
        self.dsem = {}
        for e in ("pool", "sp"):
            for i in range(self.NDMASEM):
                self.dsem[(e, i)] = stack.enter_context(nc.semaphore("ds_%s%d" % (e, i)))
        self.ccsem = stack.enter_context(nc.semaphore("ccsem"))
        dcnt = {}
        cccnt = 0
        for o in self.allops:
            if o.dma:
                dcnt[o.sem] = dcnt.get(o.sem, 0) + 16
                o.val = dcnt[o.sem]
                o.sem = self.dsem[o.sem]
                o.sig = True
            elif o.cc:
                cccnt += 1
                o.val = cccnt
                o.sem = self.ccsem
                o.sig = True
            elif o.sig:
                cnt[o.eng] += 1
                o.val = cnt[o.eng]
                o.sem = self.esem[o.eng]

    def emit(self, eng, e):
        waited = {}
        for o in self.ops[eng]:
            need = {}
            for d in o.waits.values():
                k = id(d.sem)
                if need.get(k, (None, 0))[1] < d.val:
                    need[k] = (d.sem, d.val)
            for k, (sem, val) in need.items():
                if waited.get(k, 0) >= val:
                    continue
                waited[k] = val
                e.wait_ge(sem, val)
            if o.fn is None:
                continue
            ins = o.fn(e)
            if o.sig:
                if o.dma:
                    ins.then_inc(o.sem, 16)
                elif o.cc:
                    ins.then_inc(o.sem)
                else:
                    ins.then_inc(o.sem, 1)


class Builder:
    def __init__(self, debug=None):
        self.debug = debug
        self.nc = nc = bass.Bass("TRN2", target_bir_lowering=False)
        self.S = Sched()
        self.sb_off = 16512
        self.psrr = 0
        self.uid = 0
        self.evrr = 0
        self.declare_io()
        self.alloc()

    def declare_io(self):
        nc = self.nc

        def inp(name, shape, dt=F32):
            return nc.dram_tensor(name, list(shape), dt, kind="ExternalInput").ap()

        self.h0T = inp("h0T", [D, T])
        self.w_in_ab = inp("w_in_ab", [D, 4608])
        self.w_out_ab = inp("w_out_ab", [D, D])
        self.w_in_c = inp("w_in_c", [D, 6144])
        self.w_out_c = inp("w_out_c", [D, D])
        self.w_mlp_in = [inp("w_mlp_in%d" % l, [D, DFF]) for l in range(2)]
        self.w_mlp_out = [inp("w_mlp_out%d" % l, [DFF, D]) for l in range(2)]
        self.gains_d = inp("gains", [P, 5 * KC])
        self.sinks_d = inp("sinks", [P, 16])
        self.lam_d = inp("lamv", [P, 256])
        self.subg_d = inp("subg", [P, 128])
        self.kqneg_d = inp("kqneg", [P, 128])
        self.negd0_d = inp("negd0", [P, NREL])
        self.negdist_a_d = inp("negdist_a", [P, len(A_NEAR), 128])
        self.mask_a_d = inp("mask_a", [P, len(A_NEAR), 128])
        self.mask_a0_d = inp("mask_a0", [P, NQB, 128])
        self.negdist_c_d = inp("negdist_c", [P, len(C_NEAR), 128])
        self.mask_c_d = inp("mask_c", [P, len(C_NEAR), 128])
        self.mask_b_d = inp("mask_b", [P, NREL, 128])
        self.flagb_d = inp("flagb", [P, 1])
        self.consts_d = inp("consts", [P, 3, 128])
        self.outT = nc.dram_tensor("outT", [D, T], F32, kind="ExternalOutput").ap()
        self.QT0 = nc.dram_tensor("QT0", [2048, T], BF16)
        self.KT0 = self.chunked("KT0", [0, 640, 1280], True)
        self.V0 = self.chunked("V0", [0, 768, 1280], False)
        self.QT1 = nc.dram_tensor("QT1", [2048, T], BF16)
        self.KT1 = self.chunked("KT1", [0, 512, 1024, 1536, 2048], True)
        self.V1 = self.chunked("V1", [0, 512, 1024, 1536, 2048], False)
        if self.debug:
            self.dbg = nc.dram_tensor("dbg", list(self.debug["shape"]), F32, kind="ExternalOutput").ap()

    def chunked(self, name, bounds, is_k):
        nc = self.nc
        ch = {"name": name, "bounds": bounds, "is_k": is_k, "loc": [], "gat": [], "keys": [[] for _ in bounds[1:]], "done": [False] * (len(bounds) - 1)}
        for i in range(len(bounds) - 1):
            w = bounds[i + 1] - bounds[i]
            if is_k:
                ch["loc"].append(nc.dram_tensor("%s_l%d" % (name, i), [w, T], BF16))
                ch["gat"].append(nc.dram_tensor("%s_g%d" % (name, i), [2 * w, T], BF16))
            else:
                ch["loc"].append(nc.dram_tensor("%s_l%d" % (name, i), [T, w], BF16))
                ch["gat"].append(nc.dram_tensor("%s_g%d" % (name, i), [2 * T, w], BF16))
        return ch

    @staticmethod
    def ch_find(ch, f):
        b = ch["bounds"]
        for i in range(len(b) - 1):
            if b[i] <= f < b[i + 1]:
                return i, f - b[i], b[i + 1] - b[i]
        raise ValueError(f)

    def sb(self, name, shape, dt, off=None):
        n = 1
        for s in shape[1:]:
            n *= s
        nbytes = n * (4 if dt == F32 else 2)
        nbytes = (nbytes + 31) // 32 * 32
        if off is None:
            off = self.sb_off
            self.sb_off += nbytes
        t = self.nc.alloc_sbuf_tensor_at(name, list(shape), dt, offset=off)
        return t

    def alloc(self):
        nc = self.nc
        self.hT = self.sb("hT", [P, KC, T], F32)
        self.xT = self.sb("xT", [P, KC, T], BF16)
        self.gains = self.sb("gains", [P, 5 * KC], F32)
        self.consts_f = self.sb("consts_f", [P, 3, 128], F32)
        self.consts = self.sb("consts_b", [P, 3, 128], BF16)
        self.epsc = self.sb("epsc", [P, 1], F32)
        self.onec = self.sb("onec", [P, 1], F32)
        self.flagb = self.sb("flagb", [P, 1], F32)
        self.nflagb = self.sb("nflagb", [P, 1], F32)
        self.sinks = self.sb("sinks", [P, 16], F32)
        self.esink = self.sb("esink", [P, 16], F32)
        self.lamv = self.sb("lamv", [P, 256], F32)
        self.lamt = self.sb("lamt", [P, 8], F32)
        self.subg = self.sb("subg", [P, 128], F32)
        self.kqneg = self.sb("kqneg", [P, 128], F32)
        self.negd0 = self.sb("negd0", [P, NREL], F32)
        base = self.sb_off
        self.rstd = [self.sb("rstd%d" % i, [P, TG], F32) for i in range(2)]
        self.lnv = [self.sb("lnv%d" % i, [P, TG], F32) for i in range(2)]
        self.sqb = [self.sb("sqb%d" % i, [P, TG], BF16) for i in range(3)]
        self.stage = [self.sb("stage%d" % i, [P, T], BF16) for i in range(2)]
        self.vstage = [self.sb("vstage%d" % i, [P, 256], BF16) for i in range(3)]
        self.relu_t = [self.sb("relu%d" % i, [P, TG], F32) for i in range(3)]
        self.wbuf = [self.sb("wbuf%d" % i, [P, 4096], BF16) for i in range(2)]
        self.hidT = self.sb("hidT", [P, 8, T], BF16)
        lin_end = self.sb_off
        self.sb_off = base
        self.kT = [[self.sb("kT%d_%d" % (i, c), [64, LP], BF16) for c in range(2)] for i in range(2)]
        self.qT = [[self.sb("qT%d_%d" % (i, c), [64, T], BF16) for c in range(2)] for i in range(2)]
        self.Vs = [self.sb("Vs%d" % i, [P, NGB, 130], BF16) for i in range(2)]
        bH_ = self.sb("biasH0", [P, NREL, 128], F32)
        self.biasH = [bH_, bH_]
        self.expH_off = self.sb_off
        self.expH = [self.sb("expH%d" % i, [P, NREL, 128], BF16) for i in range(2)]
        self.tabA = self.sb("tabA", [P, 7, 128], F32)
        self.tabB = self.sb("tabB", [P, 7, 128], F32)
        self.tabA0 = self.sb("tabA0", [P, NQB, 128], F32)
        self.maskb = self.sb("maskb", [P, NREL, 128], BF16)
        NS = 3
        eoff = self.expH_off
        self.tmpS2 = [[self.sb("tmpS%d_0" % t, [P, 4, 128], F32, off=eoff + t * 2048)] * 2 for t in range(NS)]
        self.pT2 = [[self.sb("pT%d_%d" % (t, i), [P, 4, 128], BF16) for i in range(3)] for t in range(NS)]
        self.spT2 = [[self.sb("spT%d_%d" % (t, i), [P, 4, 128], BF16) for i in range(2)] for t in range(NS)]
        self.R32s = [self.sb("R32_%d" % t, [P, 128], F32) for t in range(NS)]
        self.Rtmps = [self.sb("Rtmp_%d" % t, [P, 128], F32) for t in range(NS)]
        self.Rbfs = [[self.sb("Rbf%d_%d" % (t, i), [P, 128], BF16) for i in range(2)] for t in range(NS)]
        self.ofin3 = [self.sb("ofin%d" % t, [P, 128], F32) for t in range(NS)]
        self.ob3 = [self.sb("ob%d" % t, [P, 128], BF16) for t in range(NS)]
        self.small3 = [self.sb("small%d" % t, [P, 8], F32) for t in range(NS)]
        self.junk3 = [self.sb("junk%d" % t, [P, 128], F32) for t in range(NS)]
        self.junk = self.junk3[0]
        self.o0buf = [self.sb("o0buf%d" % t, [P, 132], F32) for t in range(NS)]
        self.tmprr = [0] * NS
        self.sprr = [0] * NS
        self.rbrr = [0] * NS
        att_end = self.sb_off
        self.sb_off = max(lin_end, att_end)
        assert self.sb_off <= 229344, self.sb_off
        self.ps = [nc.alloc_psum_tensor("ps%d" % i, [P, 512], F32) for i in range(6)]
        self.psTs = [nc.alloc_psum_tensor("psT%d" % i, [P, 1024], BF16) for i in range(2)]

    def u(self, name):
        self.uid += 1
        return (name, self.uid)

    def next_ps(self, n=6):
        i = self.psrr % n
        self.psrr += 1
        return i

    def load_consts(self):
        S = self.S
        S.op("sp", lambda e: e.dma_start(out=self.gains[:], in_=self.gains_d), writes=["gains"], dma=True)
        S.op("sp", lambda e: e.dma_start(out=self.consts_f[:], in_=self.consts_d), writes=["consts_f"], dma=True)
        S.op("sp", lambda e: e.dma_start(out=self.flagb[:], in_=self.flagb_d), writes=["flagb"], dma=True)
        S.op("sp", lambda e: e.dma_start(out=self.sinks[:], in_=self.sinks_d), writes=["sinks"], dma=True)
        S.op("sp", lambda e: e.dma_start(out=self.lamv[:], in_=self.lam_d), writes=["lamv"], dma=True)
        S.op("sp", lambda e: e.dma_start(out=self.subg[:], in_=self.subg_d), writes=["subg"], dma=True)
        S.op("sp", lambda e: e.dma_start(out=self.kqneg[:], in_=self.kqneg_d), writes=["kqneg"], dma=True)
        S.op("sp", lambda e: e.dma_start(out=self.negd0[:], in_=self.negd0_d), writes=["negd0"], dma=True)
        for tg in range(NTG):
            sl = slice(tg * TG, (tg + 1) * TG)
            S.op("sp", lambda e, sl=sl: e.dma_start(out=self.hT[:, :, sl],
                                                    in_=self.h0T.rearrange("(c p) t -> p c t", p=P)[:, :, sl]),
                 writes=[("hT", c, tg) for c in range(KC)], dma=True)
        S.op("dve", lambda e: e.tensor_copy(out=self.consts[:], in_=self.consts_f[:]), reads=["consts_f"], writes=["consts"])
        S.op("dve", lambda e: e.memset(self.epsc[:], EPS), writes=["epsc"])
        S.op("dve", lambda e: e.memset(self.onec[:], 1.0), writes=["onec"])
        S.op("dve", lambda e: e.tensor_scalar(out=self.nflagb[:], in0=self.flagb[:], scalar1=-1.0, scalar2=None, op0=ALU.mult),
             reads=["flagb"], writes=["nflagb"])
        S.op("act", lambda e: e.activation(out=self.esink[:], in_=self.sinks[:], func=AF.Exp), reads=["sinks"], writes=["esink"])
        S.op("dve", lambda e: e.tensor_tensor(out=self.junk[:, 0:64], in0=self.lamv[:, 0:64], in1=self.lamv[:, 64:128], op=ALU.mult),
             reads=["lamv"], writes=["junk"])
        S.op("dve", lambda e: e.tensor_reduce(out=self.lamt[:, 0:1], in_=self.junk[:, 0:64], axis=AX.X, op=ALU.add),
             reads=["junk"], writes=["lamt0"])
        S.op("dve", lambda e: e.tensor_tensor(out=self.junk[:, 64:128], in0=self.lamv[:, 128:192], in1=self.lamv[:, 192:256], op=ALU.mult),
             reads=["lamv"], writes=["junk2"])
        S.op("dve", lambda e: e.tensor_reduce(out=self.lamt[:, 1:2], in_=self.junk[:, 64:128], axis=AX.X, op=ALU.add),
             reads=["junk2"], writes=["lamt1"])
        S.op("act", lambda e: e.activation(out=self.lamt[:, 2:4], in_=self.lamt[:, 0:2], func=AF.Exp),
             reads=["lamt0", "lamt1"], writes=["lamt23"])
        S.op("dve", lambda e: e.tensor_tensor(out=self.lamt[:, 4:5], in0=self.lamt[:, 3:4], in1=self.lamt[:, 2:3], op=ALU.subtract),
             reads=["lamt23"], writes=["lamt4"])
        S.op("dve", lambda e: e.tensor_scalar(out=self.lamt[:, 5:6], in0=self.lamt[:, 4:5], scalar1=-LAMBDA_INIT, scalar2=None, op0=ALU.add),
             reads=["lamt4"], writes=["nlam"])
        S.op("dve", lambda e: e.tensor_scalar(out=self.subg[:], in0=self.subg[:], scalar1=1.0 - LAMBDA_INIT, scalar2=None, op0=ALU.mult),
             reads=["subg"], writes=["subg"])

    def rmsnorm(self, gi, dst_keyname="xT", final=False):
        S = self.S
        ones = self.consts[:, 0, :]
        for tg in range(NTG):
            sl = slice(tg * TG, (tg + 1) * TG)
            pi = self.next_ps()
            ps = self.ps[pi]
            for c in range(KC):
                sq = self.sqb[c % 3]
                S.op("act", lambda e, sq=sq, c=c, sl=sl: e.activation(out=sq[:], in_=self.hT[:, c, sl], func=AF.Square),
                     reads=[("hT", c, tg)], writes=[("sqb", c % 3)])
                S.op("pe", lambda e, sq=sq, c=c, ps=ps: e.matmul(ps[:, 0:TG], lhsT=ones, rhs=sq[:], start=(c == 0), stop=(c == KC - 1)),
                     reads=[("sqb", c % 3), "consts"], writes=[("ps", pi)])
            lnv = self.lnv[tg % 2]
            rstd = self.rstd[tg % 2]
            S.op("act", lambda e, ps=ps, lnv=lnv: e.activation(out=lnv[:], in_=ps[:, 0:TG], func=AF.Ln, bias=self.epsc[:, 0:1], scale=1.0 / D),
                 reads=[("ps", pi), "epsc"], writes=[("lnv", tg % 2)])
            S.op("act", lambda e, lnv=lnv, rstd=rstd: e.activation(out=rstd[:], in_=lnv[:], func=AF.Exp, scale=-0.5),
                 reads=[("lnv", tg % 2)], writes=[("rstd", tg % 2)])
            for c in range(KC):
                gcol = self.gains[:, gi * KC + c: gi * KC + c + 1]
                if final:
                    S.op("dve", lambda e, c=c, sl=sl, gcol=gcol, rstd=rstd: e.scalar_tensor_tensor(
                        out=self.hT[:, c, sl], in0=self.hT[:, c, sl], scalar=gcol, in1=rstd[:], op0=ALU.mult, op1=ALU.mult),
                        reads=[("hT", c, tg), ("rstd", tg % 2), "gains"], writes=[("hT", c, tg)])
                else:
                    S.op("dve", lambda e, c=c, sl=sl, gcol=gcol, rstd=rstd: e.scalar_tensor_tensor(
                        out=self.xT[:, c, sl], in0=self.hT[:, c, sl], scalar=gcol, in1=rstd[:], op0=ALU.mult, op1=ALU.mult),
                        reads=[("hT", c, tg), ("rstd", tg % 2), "gains"], writes=[("xT", c, tg)])

    def load_w(self, W, r0, nkc, c0, ncols):
        S = self.S
        self.wrr = getattr(self, "wrr", 0)
        slot = self.wrr % 2
        self.wrr += 1
        wb = self.wbuf[slot]
        view = wb[:, 0:nkc * ncols].rearrange("p (k n) -> p k n", n=ncols)
        src = W[r0:r0 + nkc * P, c0:c0 + ncols].rearrange("(k p) n -> p k n", p=P)
        half = nkc // 2
        S.op("pool", lambda e: e.dma_start(out=view[:, 0:half, :], in_=src[:, 0:half, :]), writes=[("wbuf", slot, 0)], dma=True)
        S.op("pool", lambda e: e.dma_start(out=view[:, half:nkc, :], in_=src[:, half:nkc, :]), writes=[("wbuf", slot, 1)], dma=True)
        return slot, view

    def evac_engine(self):
        self.evrr += 1
        return "act" if self.evrr % 2 == 0 else "dve"

    def copy_op(self, eng, out, in_, reads, writes, scale=None):
        S = self.S
        if eng == "act":
            if scale is None:
                S.op("act", lambda e: e.activation(out=out, in_=in_, func=AF.Copy), reads=reads, writes=writes)
            else:
                S.op("act", lambda e: e.activation(out=out, in_=in_, func=AF.Copy, scale=scale), reads=reads, writes=writes)
        else:
            if scale is None:
                S.op("dve", lambda e: e.tensor_copy(out=out, in_=in_), reads=reads, writes=writes)
            else:
                S.op("dve", lambda e: e.tensor_scalar(out=out, in0=in_, scalar1=scale, scalar2=None, op0=ALU.mult), reads=reads, writes=writes)

    def linear_fm(self, W, r0, nkc, c0, ncols_total, rhs_buf, rhs_key, kc0, evac, wcols=256):
        S = self.S
        ntile = ncols_total // wcols
        for wt in range(ntile):
            slot, view = self.load_w(W, r0, nkc, c0 + wt * wcols, wcols)
            for o in range(wcols // P):
                ot = wt * (wcols // P) + o
                for tg in range(NTG):
                    sl = slice(tg * TG, (tg + 1) * TG)
                    pi = self.next_ps()
                    ps = self.ps[pi]
                    for kc in range(nkc):
                        S.op("pe", lambda e, ps=ps, view=view, kc=kc, o=o, sl=sl: e.matmul(
                            ps[:, 0:TG], lhsT=view[:, kc, o * P:(o + 1) * P], rhs=rhs_buf[:, kc0 + kc, sl],
                            start=(kc == 0), stop=(kc == nkc - 1)),
                            reads=[("wbuf", slot, 0 if kc < nkc // 2 else 1), (rhs_key, kc0 + kc, tg)], writes=[("ps", pi)])
                    evac(ot, tg, ps, pi)

    def proj_qk(self, W, c0, ntiles, dst, drow0, scale=None):
        S = self.S

        def evac(ot, tg, ps, pi):
            st = self.stage[ot % 2]
            sl = slice(tg * TG, (tg + 1) * TG)
            self.copy_op(self.evac_engine(), st[:, sl], ps[:, 0:TG], [("ps", pi)], [("stage", ot % 2, tg)], scale=scale)
            if tg == NTG - 1:
                row = drow0 + ot * P
                if isinstance(dst, dict):
                    ci, w0, _ = self.ch_find(dst, row)
                    dap = dst["loc"][ci].ap()[w0:w0 + P, :]
                    key = (dst["name"], row)
                else:
                    dap = dst[row:row + P, :]
                    key = (dst.tensor.name, row)
                S.op("sp", lambda e: e.dma_start(out=dap, in_=st[:]),
                     reads=[("stage", ot % 2, t) for t in range(NTG)], writes=[key], dma=True)
                if isinstance(dst, dict):
                    dst["keys"][ci].append(key)
                    if row + P == dst["bounds"][ci + 1]:
                        self.emit_cc(dst, ci)

        self.linear_fm(W, 0, KC, c0, ntiles * P, self.xT, "xT", 0, evac)

    def proj_v(self, W, c0, ncols, dst, dcol0):
        S = self.S
        for wt in range(ncols // 256):
            slot, view = self.load_w(W, 0, KC, c0 + wt * 256, 256)
            for tb in range(NQB):
                qw = qw_of(tb)
                tsl = slice(tb * P, tb * P + qw)
                pi = self.next_ps()
                ps = self.ps[pi]
                for kc in range(KC):
                    S.op("pe", lambda e, ps=ps, view=view, kc=kc, tsl=tsl, qw=qw: e.matmul(
                        ps[0:qw, 0:256], lhsT=self.xT[:, kc, tsl], rhs=view[:, kc, :], start=(kc == 0), stop=(kc == KC - 1)),
                        reads=[("wbuf", slot, 0 if kc < 8 else 1)] + [("xT", kc, t) for t in range(NTG)], writes=[("ps", pi)])
                self.vsrr = getattr(self, "vsrr", 0) + 1
                vi = self.vsrr % 3
                vs = self.vstage[vi]
                self.copy_op(self.evac_engine(), vs[0:qw, :], ps[0:qw, 0:256], [("ps", pi)], [("vstage", vi)])
                ci, w0, _ = self.ch_find(dst, dcol0 + wt * 256)
                dap = dst["loc"][ci].ap()[tb * P: tb * P + qw, w0:w0 + 256]
                vkey = (dst["name"], "v", tb, dcol0 + wt * 256)
                S.op("sp", lambda e, vs=vs, qw=qw, dap=dap: e.dma_start(out=dap, in_=vs[0:qw, :]),
                     reads=[("vstage", vi)], writes=[vkey], dma=True)
                dst["keys"][ci].append(vkey)
            if dcol0 + (wt + 1) * 256 == dst["bounds"][ci + 1]:
                self.emit_cc(dst, ci)

    def emit_cc(self, ch, ci):
        S = self.S
        src, dst = ch["loc"][ci], ch["gat"][ci]
        S.op("pool", lambda e: e.collective_compute("AllGather", ALU.bypass, replica_groups=REPLICA_GROUPS,
                                                    ins=[src.ap().opt()], outs=[dst.ap().opt()]),
             reads=list(ch["keys"][ci]), writes=[("cc", ch["name"], ci)], cc=True)
        ch["done"][ci] = True

    def allgather(self, ch):
        assert all(ch["done"]), ch["name"]

    def add_into_h(self, ot, tg, ps, pi):
        sl = slice(tg * TG, (tg + 1) * TG)
        self.S.op("dve", lambda e: e.tensor_tensor(out=self.hT[:, ot, sl], in0=self.hT[:, ot, sl], in1=ps[:, 0:TG], op=ALU.add),
                  reads=[("ps", pi), ("hT", ot, tg)], writes=[("hT", ot, tg)])

    def out_proj(self, W):
        self.linear_fm(W, 0, KC, 0, D, self.xT, "xT", 0, self.add_into_h)

    def mlp(self, l):
        S = self.S
        W1 = self.w_mlp_in[l]
        W2 = self.w_mlp_out[l]
        for fc in range(8):
            def evac1(ot, tg, ps, pi):
                sl = slice(tg * TG, (tg + 1) * TG)
                self.rrr = getattr(self, "rrr", 0) + 1
                ri = self.rrr % 3
                rt = self.relu_t[ri]
                S.op("act", lambda e: e.activation(out=rt[:], in_=ps[:, 0:TG], func=AF.Relu), reads=[("ps", pi)], writes=[("relu", ri)])
                S.op("dve", lambda e: e.tensor_tensor(out=self.hidT[:, ot, sl], in0=rt[:], in1=rt[:], op=ALU.mult),
                     reads=[("relu", ri)], writes=[("hidT", ot, tg)])
            self.linear_fm(W1, 0, KC, fc * 1024, 1024, self.xT, "xT", 0, evac1)
            self.linear_fm(W2, fc * 1024, 8, 0, D, self.hidT, "hidT", 0, self.add_into_h, wcols=512)

    def load_k(self, slot, c, ch, row):
        S = self.S
        kt = self.kT[slot][c]
        ci, w0, w = self.ch_find(ch, row)
        g = ch["gat"][ci].ap()
        for r in range(2):
            S.op("sp", lambda e, r=r: e.dma_start(out=kt[:, r * T:(r + 1) * T], in_=g[r * w + w0: r * w + w0 + 64, :]),
                 reads=[("cc", ch["name"], ci)], writes=[("kT", slot, c, r)], dma=True)

    def load_q(self, slot, c, QTd, row):
        S = self.S
        qt = self.qT[slot][c]
        S.op("sp", lambda e: e.dma_start(out=qt[:], in_=QTd.ap()[row:row + 64, :]),
             reads=[(QTd.name, (row // P) * P)], writes=[("qT", slot, c)], dma=True)

    def load_v(self, slot, ch, col, ncol):
        S = self.S
        vs = self.Vs[slot]
        ci, w0, w = self.ch_find(ch, col)
        src = ch["gat"][ci].ap().rearrange("(g p) f -> p g f", p=P)
        for (g0, g1) in ((0, 9), (9, NGB)):
            S.op("sp", lambda e, g0=g0, g1=g1: e.dma_start(out=vs[:, g0:g1, 0:ncol], in_=src[:, g0:g1, w0:w0 + ncol]),
                 reads=[("cc", ch["name"], ci)], writes=[("Vs", slot, g0)], dma=True)

    def set_v_ones(self, slot, col):
        self.S.op("dve", lambda e: e.memset(self.Vs[slot][:, :, col:col + 1], 1.0), writes=[("Vs1", slot)],
                  reads=[])

    def transpose_out(self, obi, qw, ftile, j):
        S = self.S
        self.ptrr = getattr(self, "ptrr", 0) + 1
        pt = self.ptrr % 4
        ident = self.consts[:, 2, :]
        tsl = slice(j * P, j * P + qw)
        tgs = sorted(set([(j * P) // TG, (j * P + qw - 1) // TG]))
        S.op("pe", lambda e: e.transpose(self.psT[:, pt * 128: pt * 128 + qw], self.ob[obi][0:qw, :], ident[0:qw, 0:qw]),
             reads=[("ob", obi), "consts"], writes=[("psT", pt)])
        self.copy_op(self.evac_engine(), self.xT[:, ftile, tsl], self.psT[:, pt * 128: pt * 128 + qw], [("psT", pt)],
                     [("xT", ftile, t) for t in tgs])

    def run_tasks(self, factories, nslots):
        pending = list(factories)
        active = {}
        free = list(range(nslots))
        while pending or active:
            while pending and free:
                sl = free.pop(0)
                active[sl] = pending.pop(0)(sl)
            for sl in sorted(active.keys()):
                try:
                    next(active[sl])
                except StopIteration:
                    del active[sl]
                    free.append(sl)

    @staticmethod
    def split_groups(glist, split0=False, maxn=4):
        groups = []
        cur = []
        for g in glist:
            if cur and (g != cur[-1] + 1 or len(cur) == maxn or (split0 and cur[-1] == 0)):
                groups.append(cur)
                cur = []
            cur.append(g)
        if cur:
            groups.append(cur)
        return groups

    def softmax_task(self, ts, j, glist, bias_ap_of, kts, qts, vs, vcols, split0, fin_steps):
        S = self.S
        qw = qw_of(j)
        qsl = slice(j * P, j * P + qw)
        ncomp = len(kts)
        groups = self.split_groups(glist, split0)
        psS = self.ps[ts]
        psOb = self.ps[3 + ts][:, 0:vcols + 1]
        if ncomp == 1:
            psO = [psOb]
            okeys = [("psO", ts)]
        else:
            psO = [self.o0buf[ts][:, 0:vcols + 1], psOb]
            okeys = [("o0buf", ts), ("psO", ts)]
        seq = [(c, grp) for c in range(ncomp) for grp in groups]
        first = [True] * ncomp
        lastgrp = groups[-1]
        pend = None
        pend2 = None
        for si in range(len(seq) + 2):
            item = None
            if si < len(seq):
                c, grp = seq[si]
                n = len(grp)
                kt, ktkeys = kts[c]
                qt, qtkeys = qts[c]
                for i, g in enumerate(grp):
                    S.op("pe", lambda e, i=i, g=g: e.matmul(psS[:, i * 128: i * 128 + qw], lhsT=kt[:, g * P:(g + 1) * P], rhs=qt[:, qsl],
                                                            start=True, stop=True),
                         reads=ktkeys + qtkeys, writes=[("psS", ts)])
                self.tmprr[ts] += 1
                tb = self.tmprr[ts] % 3
                pt = self.pT2[ts][tb]
                b_ap, bkeys = bias_ap_of(grp, qw)
                p0 = self.spT2[ts][tb % 2]
                S.op("act", lambda e: e.activation(out=p0[:, 0:n, 0:qw], in_=psS[:, 0:n * 128].rearrange("p (n q) -> p n q", q=128)[:, :, 0:qw],
                                                   func=AF.Exp, scale=0.125),
                     reads=[("psS", ts)], writes=[("spT", ts, tb % 2)])
                S.op("dve", lambda e: e.tensor_tensor(out=pt[:, 0:n, 0:qw], in0=p0[:, 0:n, 0:qw], in1=b_ap, op=ALU.mult),
                     reads=[("spT", ts, tb % 2)] + bkeys, writes=[("pT", ts, tb)])
                item = (c, grp, pt, tb)
            if pend2 is not None:
                pc, pgrp, ppt, ptb = pend2
                for i, g in enumerate(pgrp):
                    st = first[pc]
                    first[pc] = False
                    sp_ = (pgrp is lastgrp and i == len(pgrp) - 1)
                    S.op("pe", lambda e, i=i, g=g, st=st, sp_=sp_: e.matmul(
                        psOb[0:qw, :], lhsT=ppt[:, i, 0:qw], rhs=vs[0][:, g, 0:vcols + 1], start=st, stop=sp_),
                        reads=[("pT", ts, ptb)] + vs[1], writes=[("psO", ts)])
                if ncomp == 2 and pc == 0 and pgrp is lastgrp:
                    self.copy_op(self.evac_engine(), self.o0buf[ts][0:qw, 0:vcols + 1], psOb[0:qw, :], [("psO", ts)], [("o0buf", ts)])
            pend2 = pend
            pend = item
            yield
        for step in fin_steps(ts, psO, okeys):
            step()
            yield

    def attn_c(self):
        S = self.S
        S.op("sp", lambda e: e.dma_start(out=self.tabA[:, 0:4, :], in_=self.negdist_c_d), writes=["tabA"], dma=True)
        S.op("sp", lambda e: e.dma_start(out=self.tabB[:, 0:4, :], in_=self.mask_c_d), writes=["tabB"], dma=True)
        for s_ in range(2):
            self.set_v_ones(s_, 128)
        facts = []
        for h in range(16):
            for j in range(NQB):
                facts.append(lambda ts, h=h, j=j: self.task_c(ts, h, j))
        self.run_tasks(facts, 3)

    def head_pre_c(self, h):
        S = self.S
        slot = h % 2
        for c in range(2):
            self.load_k(slot, c, self.KT1, h * 128 + c * 64)
            self.load_q(slot, c, self.QT1, h * 128 + c * 64)
        self.load_v(slot, self.V1, h * 128, 128)
        bH = self.biasH[slot]
        slope = C_SLOPES[h]
        for rel in range(NREL):
            if rel in C_NEAR:
                i = C_NEAR.index(rel)
                S.op("dve", lambda e: e.scalar_tensor_tensor(out=bH[:, rel, :], in0=self.tabA[:, i, :], scalar=slope,
                                                             in1=self.tabB[:, i, :], op0=ALU.mult, op1=ALU.add),
                     reads=["tabA", "tabB"], writes=[("biasH", 0, rel)])
            else:
                S.op("dve", lambda e: e.tensor_scalar(out=bH[:, rel, :], in0=self.kqneg[:], scalar1=self.negd0[:, rel:rel + 1],
                                                      scalar2=slope, op0=ALU.add, op1=ALU.mult),
                     reads=["kqneg", "negd0"], writes=[("biasH", 0, rel)])
        eH = self.expH[slot]
        S.op("act", lambda e: e.activation(out=eH[:], in_=bH[:], func=AF.Exp),
             reads=[("biasH", 0, r) for r in range(NREL)], writes=[("expH", slot, r) for r in range(NREL)])

    def task_c(self, ts, h, j):
        if j == 0:
            self.head_pre_c(h)
        slot = h % 2
        bH = self.biasH[slot]
        kts = [(self.kT[slot][c], [("kT", slot, c, 0), ("kT", slot, c, 1)]) for c in range(2)]
        qts = [(self.qT[slot][c], [("qT", slot, c)]) for c in range(2)]
        vs = (self.Vs[slot], [("Vs", slot, 0), ("Vs", slot, 9), ("Vs1", slot)])
        gmax = min(16, 9 + j)
        glist = [g for g in range(0, gmax + 1) if not c_dead(h, g - j + 8)]

        eH = self.expH[slot]

        def bias_ap_of(grp, qw):
            r0 = grp[0] - j + 8
            r1 = grp[-1] - j + 8
            return eH[:, r0:r1 + 1, 0:qw], [("expH", slot, r) for r in range(r0, r1 + 1)]

        def fin_steps(ts, psO, okeys):
            return self.fin_c_steps(ts, h, j, psO, okeys)

        return self.softmax_task(ts, j, glist, bias_ap_of, kts, qts, vs, 128, False, fin_steps)

    def fin_c_steps(self, ts, h, j, psO, okeys):
        S = self.S
        qw = qw_of(j)
        sm = self.small3[ts]
        of = self.ofin3[ts]
        ob = self.ob3[ts]
        jk = self.junk3[ts]
        kk = ("fin", ts)
        steps = []
        A = steps.append
        A(lambda: S.op("dve", lambda e: e.reciprocal(out=sm[0:qw, 0:1], in_=psO[0][0:qw, 128:129]), reads=[okeys[0]], writes=[(kk, 0)]))
        A(lambda: S.op("dve", lambda e: e.reciprocal(out=sm[0:qw, 1:2], in_=psO[1][0:qw, 128:129]), reads=[okeys[1]], writes=[(kk, 1)]))
        A(lambda: S.op("dve", lambda e: e.tensor_tensor(out=sm[0:qw, 2:3], in0=sm[0:qw, 1:2], in1=self.lamt[0:qw, 5:6], op=ALU.mult),
                       reads=[(kk, 1), "nlam"], writes=[(kk, 2)]))
        A(lambda: S.op("act", lambda e: e.activation(out=of[0:qw, :], in_=psO[0][0:qw, 0:128], func=AF.Copy, scale=sm[0:qw, 0:1]),
                       reads=[okeys[0], (kk, 0)], writes=[(kk, "of")]))
        A(lambda: S.op("dve", lambda e: e.scalar_tensor_tensor(out=of[0:qw, :], in0=psO[1][0:qw, 0:128], scalar=sm[0:qw, 2:3], in1=of[0:qw, :],
                                                               op0=ALU.mult, op1=ALU.add),
                       reads=[okeys[1], (kk, 2), (kk, "of")], writes=[(kk, "of")]))
        A(lambda: S.op("act", lambda e: e.activation(out=jk[0:qw, :], in_=of[0:qw, :], func=AF.Square),
                       reads=[(kk, "of")], writes=[(kk, "jk")]))
        A(lambda: S.op("dve", lambda e: e.tensor_reduce(out=sm[0:qw, 3:4], in_=jk[0:qw, :], axis=AX.X, op=ALU.add),
                       reads=[(kk, "jk")], writes=[(kk, 3)]))
        A(lambda: S.op("act", lambda e: e.activation(out=sm[0:qw, 4:5], in_=sm[0:qw, 3:4], func=AF.Ln, bias=self.epsc[0:qw, 0:1], scale=1.0 / 128),
                       reads=[(kk, 3), "epsc"], writes=[(kk, 4)]))
        A(lambda: S.op("act", lambda e: e.activation(out=sm[0:qw, 5:6], in_=sm[0:qw, 4:5], func=AF.Exp, scale=-0.5),
                       reads=[(kk, 4)], writes=[(kk, 5)]))
        A(lambda: S.op("dve", lambda e: e.scalar_tensor_tensor(out=ob[0:qw, :], in0=of[0:qw, :], scalar=sm[0:qw, 5:6], in1=self.subg[0:qw, :],
                                                               op0=ALU.mult, op1=ALU.mult),
                       reads=[(kk, "of"), (kk, 5), "subg"], writes=[("ob3", ts)]))
        A(lambda: self.transpose_evac(qw, h, j, self.transpose_pe(ob, [("ob3", ts)], qw)))
        return steps

    def transpose_pe(self, ob, obkeys, qw):
        S = self.S
        self.ptrr = getattr(self, "ptrr", 0) + 1
        pt = self.ptrr % 2
        ident = self.consts[:, 2, :]
        S.op("pe", lambda e: e.transpose(self.psTs[pt][:, 0:qw], ob[0:qw, :], ident[0:qw, 0:qw]),
             reads=obkeys + ["consts"], writes=[("psT", pt)])
        self.last_pt = pt
        return pt

    def transpose_evac(self, qw, ftile, j, pt=None):
        pt = self.last_pt if pt is None else pt
        tsl = slice(j * P, j * P + qw)
        tgs = sorted(set([(j * P) // TG, (j * P + qw - 1) // TG]))
        self.copy_op(self.evac_engine(), self.xT[:, ftile, tsl], self.psTs[pt][:, 0:qw], [("psT", pt)],
                     [("xT", ftile, t) for t in tgs])

    def attn_a(self):
        S = self.S
        S.op("sp", lambda e: e.dma_start(out=self.tabA[:], in_=self.negdist_a_d), writes=["tabA"], dma=True)
        S.op("sp", lambda e: e.dma_start(out=self.tabB[:], in_=self.mask_a_d), writes=["tabB"], dma=True)
        S.op("sp", lambda e: e.dma_start(out=self.tabA0[:], in_=self.mask_a0_d), writes=["tabA0"], dma=True)
        for s_ in range(2):
            self.set_v_ones(s_, 64)
        facts = []
        for h in range(16):
            for j in range(NQB):
                facts.append(lambda ts, h=h, j=j: self.task_a(ts, h, j))
        self.run_tasks(facts, 3)

    def head_pre_a(self, h):
        S = self.S
        kvh = h // 4
        kslot = kvh % 2
        slot = h % 2
        if h % 4 == 0:
            self.load_k(kslot, 0, self.KT0, kvh * 64)
            self.load_v(kslot, self.V0, kvh * 64, 64)
        self.load_q(slot, 0, self.QT0, h * 64)
        bH = self.biasH[slot]
        slope = A_SLOPES[h]
        for i, rel in enumerate(A_NEAR):
            S.op("dve", lambda e: e.scalar_tensor_tensor(out=bH[:, rel, :], in0=self.tabA[:, i, :], scalar=slope,
                                                         in1=self.tabB[:, i, :], op0=ALU.mult, op1=ALU.add),
                 reads=["tabA", "tabB"], writes=[("biasH", 0, rel)])
        for j in range(NQB):
            rel = 8 - j
            if rel in A_NEAR:
                i = A_NEAR.index(rel)
                S.op("dve", lambda e: e.scalar_tensor_tensor(out=bH[:, B0IDX[j], :], in0=self.tabA[:, i, :], scalar=slope,
                                                             in1=self.tabA0[:, j, :], op0=ALU.mult, op1=ALU.add),
                     reads=["tabA", "tabA0"], writes=[("biasH", 0, B0IDX[j])])
            else:
                S.op("dve", lambda e: e.tensor_scalar(out=bH[:, B0IDX[j], :], in0=self.kqneg[:], scalar1=self.negd0[:, rel:rel + 1],
                                                      scalar2=slope, op0=ALU.add, op1=ALU.mult),
                     reads=["kqneg", "negd0"], writes=[("biasH", 0, B0IDX[j])])
                S.op("dve", lambda e: e.tensor_tensor(out=bH[:, B0IDX[j], :], in0=bH[:, B0IDX[j], :], in1=self.tabA0[:, j, :], op=ALU.add),
                     reads=[("biasH", 0, B0IDX[j]), "tabA0"], writes=[("biasH", 0, B0IDX[j])])

    def task_a(self, ts, h, j):
        S = self.S
        if j == 0:
            self.head_pre_a(h)
            bH_, eH_ = self.biasH[h % 2], self.expH[h % 2]
            for (r0, r1) in ((0, 13), (15, 18)):
                S.op("act", lambda e: e.activation(out=eH_[:, r0:r1, :], in_=bH_[:, r0:r1, :], func=AF.Exp),
                     reads=[("biasH", 0, r) for r in range(r0, r1) if r in A_NEAR or r in B0IDX],
                     writes=[("expH", h % 2, r) for r in range(r0, r1)])
        kslot = (h // 4) % 2
        slot = h % 2
        bH = self.biasH[slot]
        kts = [(self.kT[kslot][0], [("kT", kslot, 0, 0), ("kT", kslot, 0, 1)])]
        qts = [(self.qT[slot][0], [("qT", slot, 0)])]
        vs = (self.Vs[kslot], [("Vs", kslot, 0), ("Vs", kslot, 9), ("Vs1", kslot)])
        near = sorted(set([g for g in list(range(j - 2, j + 2)) + list(range(j + 7, j + 10)) if 1 <= g <= 16]))
        glist = [0] + near

        eH = self.expH[slot]

        def bias_ap_of(grp, qw):
            if grp[0] == 0:
                assert len(grp) == 1
                return eH[:, B0IDX[j]:B0IDX[j] + 1, 0:qw], [("expH", slot, B0IDX[j])]
            r0 = grp[0] - j + 8
            r1 = grp[-1] - j + 8
            return eH[:, r0:r1 + 1, 0:qw], [("expH", slot, r) for r in range(r0, r1 + 1)]

        def fin_steps(ts, psO, okeys):
            qw = qw_of(j)
            sm = self.small3[ts]
            ob = self.obA[(h // 2) % 2][j]
            kk = ("finA", ts)
            obk = ("obA", (h // 2) % 2, j)
            steps = []
            A = steps.append
            A(lambda: S.op("dve", lambda e: e.tensor_tensor(out=sm[0:qw, 0:1], in0=psO[0][0:qw, 64:65], in1=self.esink[0:qw, h:h + 1], op=ALU.add),
                           reads=[okeys[0], "esink"], writes=[(kk, 0)]))
            A(lambda: S.op("dve", lambda e: e.reciprocal(out=sm[0:qw, 1:2], in_=sm[0:qw, 0:1]), reads=[(kk, 0)], writes=[(kk, 1)]))
            A(lambda: S.op("act", lambda e: e.activation(out=ob[0:qw, (h % 2) * 64:(h % 2) * 64 + 64], in_=psO[0][0:qw, 0:64], func=AF.Copy,
                                                         scale=sm[0:qw, 1:2]),
                           reads=[okeys[0], (kk, 1)], writes=[obk + (h % 2,)]))
            if h % 2 == 1:
                A(lambda: self.transpose_evac(qw, h // 2, j, self.transpose_pe(ob, [obk + (0,), obk + (1,)], qw)))
            return steps

        return self.softmax_task(ts, j, glist, bias_ap_of, kts, qts, vs, 64, True, fin_steps)

    def attn_b(self):
        S = self.S
        S.op("pool", lambda e: e.dma_start(out=self.maskb[:], in_=self.mask_b_d), writes=["maskb"], dma=True)
        facts = []
        for h in range(16):
            for j in range(NQB):
                facts.append(lambda ts, h=h, j=j: self.task_b(ts, h, j))
        self.run_tasks(facts, 3)

    def task_b(self, ts, h, j):
        S = self.S
        ones = self.consts[:, 0, :]
        triu = self.consts[:, 1, :]
        slot = h % 2
        if j == 0:
            self.load_k(slot, 0, self.KT0, 256 + h * 64)
            self.load_v(slot, self.V0, 256 + h * 64, 64)
            self.load_q(slot, 0, self.QT0, 1024 + h * 64)
        kt = self.kT[slot][0]
        ktkeys = [("kT", slot, 0, 0), ("kT", slot, 0, 1)]
        qt = self.qT[slot][0]
        qtkeys = [("qT", slot, 0)]
        vsb = self.Vs[slot]
        vkeys = [("Vs", slot, 0), ("Vs", slot, 9)]
        qw = qw_of(j)
        qsl = slice(j * P, j * P + qw)
        gmax = min(16, 9 + j)
        groups = []
        cur = []
        curcls = None
        for g in range(gmax, -1, -1):
            rel = g - j + 8
            cls = "flag" if rel >= 9 else "free"
            if cur and (cls != curcls or len(cur) == 4):
                groups.append((curcls, cur))
                cur = []
            cur.append(g)
            curcls = cls
        if cur:
            groups.append((curcls, cur))
        psS = self.ps[ts]
        psD = psS
        psO = self.ps[3 + ts][:, 0:64]
        okey = ("psOb", ts)
        R32 = self.R32s[ts]
        Rtmp = self.Rtmps[ts]
        S.op("dve", lambda e: e.memset(R32[:], 0.0), writes=[("R32", ts)])
        self.rbrr[ts] += 1
        rb = self.rbrr[ts] % 2
        S.op("dve", lambda e: e.memset(self.Rbfs[ts][rb][:], 0.0), writes=[("Rbf", ts, rb)])
        pend_pv = None
        first_pv = True
        ng = len(groups)

        def emit_pv(pend, first, last):
            wt, wkey, asc = pend
            n = len(asc)
            for i, g in enumerate(asc):
                st = first
                first = False
                sp_ = last and i == n - 1
                S.op("pe", lambda e: e.matmul(psO[0:qw, :], lhsT=wt[:, i, 0:qw], rhs=vsb[:, g, 0:64], start=st, stop=sp_),
                     reads=[wkey] + vkeys, writes=[okey])
            return first

        for gi, (cls, grp) in enumerate(groups):
            n = len(grp)
            asc = grp[::-1]
            if pend_pv is not None:
                first_pv = emit_pv(pend_pv, first_pv, False)
                pend_pv = None
            for i, g in enumerate(asc):
                S.op("pe", lambda e: e.matmul(psS[:, i * 128: i * 128 + qw], lhsT=kt[:, g * P:(g + 1) * P], rhs=qt[:, qsl], start=(i == 0), stop=True,
                                              skip_group_check=True),
                     reads=ktkeys + qtkeys, writes=[("psS", ts)])
            et = self.tmpS2[ts][0]
            self.sprr[ts] += 1
            sb_ = self.sprr[ts] % 2
            spt = self.spT2[ts][sb_]
            spkey = ("spT", ts, sb_)
            psS3 = psS[:, 0:n * 128].rearrange("p (n q) -> p n q", q=128)[:, :, 0:qw]
            psD3 = psD[:, 0:n * 128].rearrange("p (n q) -> p n q", q=128)[:, :, 0:qw]
            if cls == "flag":
                S.op("act", lambda e: e.activation(out=et[:, 0:n, 0:qw], in_=psS3, func=AF.Exp, scale=-1.0, bias=self.flagb[:, 0:1]),
                     reads=[("psS", ts), "flagb"], writes=[("tmpS", ts, 0)])
            else:
                S.op("act", lambda e: e.activation(out=et[:, 0:n, 0:qw], in_=psS3, func=AF.Exp, scale=-1.0),
                     reads=[("psS", ts)], writes=[("tmpS", ts, 0)])
            S.op("act", lambda e: e.activation(out=spt[:, 0:n, 0:qw], in_=et[:, 0:n, 0:qw], func=AF.Ln, bias=self.onec[:, 0:1], scale=1.0),
                 reads=[("tmpS", ts, 0), "onec"], writes=[spkey])
            has_near = any((g - j + 8) in B_NEAR for g in asc)
            r0m = asc[0] - j + 8
            if has_near:
                S.op("dve", lambda e: e.tensor_tensor(out=spt[:, 0:n, 0:qw], in0=spt[:, 0:n, 0:qw], in1=self.maskb[:, r0m:r0m + n, 0:qw], op=ALU.mult),
                     reads=[spkey, "maskb"], writes=[spkey])
            yield
            for i, g in enumerate(asc):
                dsl = slice(i * 128, i * 128 + qw)
                S.op("pe", lambda e: e.matmul(psD[:, dsl], lhsT=triu, rhs=spt[:, i, 0:qw], start=False, stop=False, skip_group_check=True),
                     reads=[spkey, "consts", ("tmpS", ts, 0)], writes=[("psS", ts)])
                for i2 in range(i + 1, n):
                    S.op("pe", lambda e: e.matmul(psD[:, dsl], lhsT=ones, rhs=spt[:, i2, 0:qw], start=False, stop=False, skip_group_check=True),
                         reads=[spkey, "consts"], writes=[("psS", ts)])
                S.op("pe", lambda e: e.matmul(psD[:, dsl], lhsT=ones, rhs=self.Rbfs[ts][rb][:, 0:qw], start=False, stop=True, skip_group_check=True),
                     reads=[("Rbf", ts, rb), "consts"], writes=[("psS", ts)])
            if gi < ng - 1:
                if n > 1:
                    S.op("dve", lambda e: e.tensor_reduce(out=Rtmp[:, 0:qw], in_=spt[:, 0:n, 0:qw].rearrange("p n q -> p q n"), axis=AX.X, op=ALU.add),
                         reads=[spkey], writes=[("Rtmp", ts)])
                    S.op("dve", lambda e: e.tensor_tensor(out=R32[:, 0:qw], in0=R32[:, 0:qw], in1=Rtmp[:, 0:qw], op=ALU.add),
                         reads=[("Rtmp", ts), ("R32", ts)], writes=[("R32", ts)])
                else:
                    S.op("dve", lambda e: e.tensor_tensor(out=R32[:, 0:qw], in0=R32[:, 0:qw], in1=spt[:, 0, 0:qw], op=ALU.add),
                         reads=[spkey, ("R32", ts)], writes=[("R32", ts)])
                self.rbrr[ts] += 1
                rb = self.rbrr[ts] % 2
                S.op("dve", lambda e: e.tensor_copy(out=self.Rbfs[ts][rb][:, 0:qw], in_=R32[:, 0:qw]), reads=[("R32", ts)], writes=[("Rbf", ts, rb)])
            self.tmprr[ts] += 1
            wb = self.tmprr[ts] % 3
            wt = self.pT2[ts][wb]
            wkey = ("pT", ts, wb)
            if cls == "flag":
                S.op("act", lambda e: e.activation(out=wt[:, 0:n, 0:qw], in_=psD3, func=AF.Exp, scale=-1.0, bias=self.flagb[:, 0:1]),
                     reads=[("psS", ts), "flagb"], writes=[wkey])
            else:
                S.op("act", lambda e: e.activation(out=wt[:, 0:n, 0:qw], in_=psD3, func=AF.Exp, scale=-1.0),
                     reads=[("psS", ts)], writes=[wkey])
            if has_near:
                S.op("dve", lambda e: e.tensor_tensor(out=wt[:, 0:n, 0:qw], in0=wt[:, 0:n, 0:qw], in1=self.maskb[:, r0m:r0m + n, 0:qw], op=ALU.mult),
                     reads=[wkey, "maskb"], writes=[wkey])
            pend_pv = (wt, wkey, asc)
            yield
        emit_pv(pend_pv, first_pv, True)
        yield
        ob = self.obA[(h // 2) % 2][j]
        obk = ("obA", (h // 2) % 2, j)
        self.copy_op(self.evac_engine(), ob[0:qw, (h % 2) * 64:(h % 2) * 64 + 64], psO[0:qw, :], [okey], [obk + (h % 2,)])
        yield
        if h % 2 == 1:
            pt_ = self.transpose_pe(ob, [obk + (0,), obk + (1,)], qw)
            self.transpose_evac(qw, 8 + h // 2, j, pt_)
            yield

    def dump_h(self):
        S = self.S
        outv = self.outT.rearrange("(c p) t -> p c t", p=P)
        for tg in range(NTG):
            sl = slice(tg * TG, (tg + 1) * TG)
            S.op("sp", lambda e, sl=sl: e.dma_start(out=outv[:, :, sl], in_=self.hT[:, :, sl]),
                 reads=[("hT", c, tg) for c in range(KC)], writes=[("out", tg)], dma=True)
        S.op("sp", None, reads=[("out", tg) for tg in range(NTG)])

    def dump_x(self):
        S = self.S
        outv = self.dbgx.rearrange("(c p) t -> p c t", p=P)
        for tg in range(NTG):
            sl = slice(tg * TG, (tg + 1) * TG)
            S.op("sp", lambda e, sl=sl: e.dma_start(out=outv[:, :, sl], in_=self.xT[:, :, sl]),
                 reads=[("xT", c, tg) for c in range(KC)], writes=[("outx", tg)], dma=True)
        S.op("sp", None, reads=[("outx", tg) for tg in range(NTG)])

    def stop_here(self, name):
        if self.stop != name:
            return False
        self.S.barrier()
        if name.startswith("x_"):
            self.dump_x()
        self.dump_h()
        self.emit_all()
        return True

    def build(self, stop=None):
        S = self.S
        nc = self.nc
        self.stop = stop
        if stop is not None:
            self.dbgx = nc.dram_tensor("dbgx", [D, T], BF16, kind="ExternalOutput").ap()
        self.obA = [[self.sb("obA%d_%d" % (i, j), [P, 128], BF16) for j in range(NQB)] for i in range(2)]
        assert self.sb_off <= 229344, self.sb_off
        print('sbuf end', self.sb_off)
        self.load_consts()
        self.rmsnorm(0)
        if self.stop_here("x_norm0"):
            return nc
        self.proj_qk(self.w_in_ab, 1024, 2, self.KT0, 0)
        if stop == "x_ka":
            S.barrier()
            S.op("sp", lambda e: e.dma_start(out=self.dbgx[0:256, :], in_=self.KT0["loc"][0].ap()[0:256, :]), reads=[], writes=[("outx", 0)], dma=True)
            S.op("sp", None, reads=[("outx", 0)])
            self.dump_h()
            self.emit_all()
            return nc
        self.proj_qk(self.w_in_ab, 2560, 8, self.KT0, 256)
        self.proj_v(self.w_in_ab, 1280, 256, self.V0, 0)
        if stop == "x_va":
            S.barrier()
            S.op("sp", lambda e: e.dma_start(out=self.dbgx[0:256, :], in_=self.V0.ap()[0:T, 0:256].rearrange("t f -> t f")), reads=[], writes=[("outx", 0)], dma=True) if False else None
            for f0 in range(0, 256, 32):
                S.op("sp", lambda e, f0=f0: e.dma_start(out=self.dbgx[f0:f0 + 32, :], in_=self.V0["loc"][0].ap()[:, f0:f0 + 32].rearrange("t f -> f t"), allow_slow_non_contiguous=True), reads=[], writes=[("outx", 0)], dma=True)
            S.op("sp", None, reads=[("outx", 0)])
            self.dump_h()
            self.emit_all()
            return nc
        self.proj_v(self.w_in_ab, 3584, 1024, self.V0, 256)
        self.allgather(self.KT0)
        self.allgather(self.V0)
        self.proj_qk(self.w_in_ab, 0, 8, self.QT0.ap(), 0)
        self.proj_qk(self.w_in_ab, 1536, 8, self.QT0.ap(), 1024, scale=-0.125)
        S.barrier()
        if stop == "x_proj0":
            S.op("sp", lambda e: e.dma_start(out=self.dbgx, in_=self.QT0.ap()), reads=[], writes=[("outx", 0)], dma=True)
            S.op("sp", None, reads=[("outx", 0)])
            self.dump_h()
            self.emit_all()
            return nc
        if stop == "x_kv0":
            S.op("sp", lambda e: e.dma_start(out=self.dbgx[0:640, :], in_=self.KT0["gat"][0].ap()[640:1280, :]), reads=[], writes=[("outx", 0)], dma=True)
            S.op("sp", None, reads=[("outx", 0)])
            self.dump_h()
            self.emit_all()
            return nc
        if stop == "x_attn_a":
            self.attn_a()
            self.stop_here("x_attn_a")
            return nc
        if stop == "x_attn_b":
            self.attn_b()
            self.stop_here("x_attn_b")
            return nc
        self.attn_a()
        S.barrier()
        self.attn_b()
        S.barrier()
        if self.stop_here("x_attn0"):
            return nc
        self.out_proj(self.w_out_ab)
        if self.stop_here("h_attn0"):
            return nc
        self.rmsnorm(3)
        self.mlp(0)
        if self.stop_here("h_l0"):
            return nc
        self.rmsnorm(1)
        self.proj_qk(self.w_in_c, 2048, 16, self.KT1, 0)
        self.proj_v(self.w_in_c, 4096, 2048, self.V1, 0)
        self.allgather(self.KT1)
        self.allgather(self.V1)
        self.proj_qk(self.w_in_c, 0, 16, self.QT1.ap(), 0)
        S.barrier()
        self.attn_c()
        S.barrier()
        if self.stop_here("x_attn1"):
            return nc
        self.out_proj(self.w_out_c)
        self.rmsnorm(4)
        self.mlp(1)
        if self.stop_here("h_l1"):
            return nc
        self.rmsnorm(2, final=True)
        self.dump_h()
        self.emit_all()
        return nc

    def emit_all(self):
        S = self.S
        nc = self.nc
        with ExitStack() as stack:
            S.finalize(nc, stack)
            block = stack.enter_context(nc.Block())

            @block.tensor
            def _(e):
                S.emit("pe", e)

            @block.scalar
            def _(e):
                S.emit("act", e)

            @block.vector
            def _(e):
                S.emit("dve", e)

            @block.gpsimd
            def _(e):
                S.emit("pool", e)

            @block.sync
            def _(e):
                S.emit("sp", e)


def chunk_of(p):
    return 1 + np.floor_divide(p - 16, 64)


def make_tables(rank):
    base = rank * T
    k = np.arange(128)[:, None]
    q = np.arange(128)[None, :]
    t = {}
    t["kqneg"] = (-(q - k)).astype(np.float32) * np.ones((128, 128), np.float32)
    negd0 = np.zeros((128, NREL), np.float32)
    for rel in range(NREL):
        dq0 = base - 128 * (rel - 8)
        negd0[:, rel] = -float(dq0) if dq0 >= 128 else -1.0e6
    t["negd0"] = negd0

    def posmats(rel):
        dq0 = base - 128 * (rel - 8)
        diff = dq0 + q - k
        return diff

    def absq(rel):
        jj = 8
        g = rel - 8 + jj
        qpos = base + 128 * jj + q + 0 * k
        kpos = 128 * g + k + 0 * q
        return qpos, kpos

    na = np.zeros((128, len(A_NEAR), 128), np.float32)
    ma = np.zeros((128, len(A_NEAR), 128), np.float32)
    for i, rel in enumerate(A_NEAR):
        qpos, kpos = absq(rel)
        qpos = qpos + 128 * 64
        kpos = kpos + 128 * 64
        na[:, i, :] = -np.abs(qpos - kpos)
        qc, kc = chunk_of(qpos), chunk_of(kpos)
        ok = (kc <= qc) & (kc >= qc - 2)
        ma[:, i, :] = np.where(ok, 0.0, NEGBIG)
    t["negdist_a"] = na
    t["mask_a"] = ma
    ma0 = np.zeros((128, NQB, 128), np.float32)
    for j in range(NQB):
        qpos = base + 128 * j + q + 0 * k
        kpos = k + 0 * q
        qc, kc = chunk_of(qpos), chunk_of(kpos)
        ok = (kpos < 16) | ((kpos >= 16) & (kc <= qc) & (kc >= qc - 2))
        ma0[:, j, :] = np.where(ok, 0.0, NEGBIG)
    t["mask_a0"] = ma0
    ncm = np.zeros((128, len(C_NEAR), 128), np.float32)
    mc = np.zeros((128, len(C_NEAR), 128), np.float32)
    for i, rel in enumerate(C_NEAR):
        qpos, kpos = absq(rel)
        qpos = qpos + 128 * 64
        kpos = kpos + 128 * 64
        ncm[:, i, :] = -np.abs(qpos - kpos)
        ok = chunk_of(kpos) <= chunk_of(qpos)
        mc[:, i, :] = np.where(ok, 0.0, NEGBIG)
    t["negdist_c"] = ncm
    t["mask_c"] = mc
    mb = np.zeros((128, NREL, 128), np.float32)
    for rel in range(NREL):
        diff = posmats(rel)
        mb[:, rel, :] = (diff > 0).astype(np.float32)
    t["mask_b"] = mb
    t["flagb"] = np.full((128, 1), NEGBIG if rank == 0 else 0.0, np.float32)
    cst = np.zeros((128, 3, 128), np.float32)
    cst[:, 0, :] = 1.0
    cst[:, 1, :] = (k >= q).astype(np.float32)
    cst[:, 2, :] = np.eye(128, dtype=np.float32)
    t["consts"] = cst
    return t


_NC_CACHE = {}


def get_nc(stop=None):
    key = "main" if stop is None else str(stop)
    if key not in _NC_CACHE:
        b = Builder()
        _NC_CACHE[key] = b.build(stop)
    return _NC_CACHE[key]


def make_in_maps(x, meta_tokens, ab_norm, w_in_ab, attn_sinks, w_out_ab, c_norm, w_in_c, diff_lambda, diff_subln,
                 w_out_c, mlp_norm, w_mlp_in, w_mlp_out, final_norm):
    f = lambda a: np.ascontiguousarray(np.asarray(a, dtype=np.float32))
    x = f(x)
    B = x.shape[0]
    meta = f(meta_tokens)
    gains = np.stack([f(ab_norm)[0], f(c_norm)[0], f(final_norm), f(mlp_norm)[0], f(mlp_norm)[1]], 0)
    gains_l = np.ascontiguousarray(gains.reshape(5, KC, P).transpose(2, 0, 1).reshape(P, 5 * KC))
    sinks = np.ascontiguousarray(np.broadcast_to(f(attn_sinks)[0][None, :], (P, 16)))
    lamv = np.ascontiguousarray(np.broadcast_to(f(diff_lambda)[0].reshape(1, 256), (P, 256)))
    subg = np.ascontiguousarray(np.broadcast_to(f(diff_subln)[0][None, :], (P, 128)))
    shared = {
        "w_in_ab": f(w_in_ab)[0], "w_out_ab": f(w_out_ab)[0], "w_in_c": f(w_in_c)[0], "w_out_c": f(w_out_c)[0],
        "w_mlp_in0": f(w_mlp_in)[0], "w_mlp_in1": f(w_mlp_in)[1], "w_mlp_out0": f(w_mlp_out)[0], "w_mlp_out1": f(w_mlp_out)[1],
        "gains": gains_l, "sinks": sinks, "lamv": lamv, "subg": subg,
    }
    tabs = [make_tables(0), make_tables(1)]
    in_maps = []
    for core in range(8):
        b, r = core // 2, core % 2
        seq = np.zeros((LP, D), np.float32)
        seq[0:16] = meta
        seq[16:16 + 2048] = x[b]
        h0T = np.ascontiguousarray(seq[r * T:(r + 1) * T].T)
        m = dict(shared)
        m["h0T"] = h0T
        m.update(tabs[r])
        in_maps.append(m)
    return in_maps


def assemble(results):
    out = np.zeros((4, 2048, D), np.float32)
    for core in range(8):
        b, r = core // 2, core % 2
        oT = np.asarray(results[core]["outT"])
        rows = oT.T
        pos0 = r * T
        lo = max(pos0, 16)
        hi = min(pos0 + T, 16 + 2048)
        out[b, lo - 16:hi - 16] = rows[lo - pos0:hi - pos0]
    return out


def kernel(**inputs):
    nc = get_nc()
    in_maps = make_in_maps(**inputs)
    res = run_bass_kernel_spmd(nc, in_maps, core_ids=list(range(8)))
    return assemble(res.results)
```

```python
import math
import types
from contextlib import ExitStack

import numpy as np
import concourse.bass as bass
import concourse.mybir as mybir
from concourse.bass_utils import run_bass_kernel_spmd

F32 = mybir.dt.float32
BF16 = mybir.dt.bfloat16
AF = mybir.ActivationFunctionType
ALU = mybir.AluOpType
AX = mybir.AxisListType

P = 128
D = 2048
KC = 16
T = 1088
LP = 2176
NTG = 4
TG = 272
NQB = 9
NGB = 17
DFF = 8192
EPS = 1e-6
NEGBIG = -30000.0
REPLICA_GROUPS = [[0, 1], [2, 3], [4, 5], [6, 7]]
NREL = 18
A_SLOPES = [2.0 ** (-8.0 * (i + 1) / 16) for i in range(16)]
C_SLOPES = A_SLOPES
LAMBDA_INIT = 0.8 - 0.6 * math.exp(-0.3 * 1)

A_NEAR = [6, 7, 8, 9, 15, 16, 17]
C_NEAR = [8, 9, 16, 17]
B_NEAR = [8, 16, 17]
B0IDX = [0, 1, 2, 3, 4, 5, 10, 11, 12]


def c_dead(h, rel):
    s_ = C_SLOPES[h]
    if rel in C_NEAR:
        return False
    dead = []
    for base in (0, T):
        dq0 = base - 128 * (rel - 8)
        if dq0 < 128:
            dead.append(base == 0 and rel >= 10)
        else:
            dead.append(s_ * (dq0 - 127) > 110.0)
    if not dead[0] and rel >= 10:
        return False
    return all(dead) and (s_ * 1.0e6 > 110.0)


def qw_of(j):
    return 128 if j < 8 else 64


def _freeze(fn):
    if fn is None or fn.__closure__ is None:
        return fn
    cells = []
    for c in fn.__closure__:
        try:
            cells.append(types.CellType(c.cell_contents))
        except ValueError:
            cells.append(c)
    return types.FunctionType(fn.__code__, fn.__globals__, fn.__name__, fn.__defaults__, tuple(cells))


class Op:
    __slots__ = ("eng", "fn", "dma", "waits", "sig", "sem", "val", "idx", "cc")

    def __init__(self, eng, fn, dma, cc=False):
        self.eng = eng
        self.fn = fn
        self.dma = dma
        self.cc = cc
        self.waits = {}
        self.sig = False
        self.sem = None
        self.val = 0


class Sched:
    ENGS = ("pe", "act", "dve", "pool", "sp")
    NDMASEM = 8

    def __init__(self):
        self.ops = {e: [] for e in self.ENGS}
        self.allops = []
        self.last_w = {}
        self.readers = {}
        self.dma_rr = {"pool": 0, "sp": 0}
        self.dma_last = {}

    def op(self, eng, fn, reads=(), writes=(), dma=False, cc=False):
        o = Op(eng, _freeze(fn), dma, cc)
        o.idx = len(self.allops)
        deps = set()
        for k in reads:
            w = self.last_w.get(k)
            if w is not None:
                deps.add(w)
        for k in writes:
            w = self.last_w.get(k)
            if w is not None:
                deps.add(w)
            for r in self.readers.get(k, ()):
                deps.add(r)
        if dma:
            slot = (eng, self.dma_rr[eng] % self.NDMASEM)
            self.dma_rr[eng] += 1
            o.sem = slot
            prev = self.dma_last.get(slot)
            if prev is not None:
                deps.add(prev)
            self.dma_last[slot] = o
        for d in deps:
            if d is o:
                continue
            if d.eng == "pe" and eng == "pe" and not d.dma:
                continue
            o.waits[d.idx] = d
            d.sig = True
        for k in reads:
            self.readers.setdefault(k, []).append(o)
        for k in writes:
            self.last_w[k] = o
            self.readers[k] = []
        self.ops[eng].append(o)
        self.allops.append(o)
        return o

    def barrier(self):
        last = []
        for e in self.ENGS:
            for o_ in reversed(self.ops[e]):
                if o_.fn is not None:
                    last.append(o_)
                    break
        outstanding = [o for o in self.dma_last.values()]
        key = ("__barrier__", len(self.allops))
        for e in self.ENGS:
            o = Op(e, None, False)
            o.idx = len(self.allops)
            for d in last + outstanding:
                if d.eng == e and not d.dma:
                    continue
                o.waits[d.idx] = d
                d.sig = True
            self.ops[e].append(o)
            self.allops.append(o)
        self.readers = {}

    def finalize(self, nc, stack):
        cnt = {e: 0 for e in self.ENGS}
        self.esem = {e: stack.enter_context(nc.semaphore("es_" + e)) for e in ("pe", "act", "dve", "pool", "sp")}
        self.dsem = {}
        for e in ("pool", "sp"):
            for i in range(self.NDMASEM):
                self.dsem[(e, i)] = stack.enter_context(nc.semaphore("ds_%s%d" % (e, i)))
        self.ccsem = stack.enter_context(nc.semaphore("ccsem"))
        dcnt = {}
        cccnt = 0
        for o in self.allops:
            if o.dma:
                dcnt[o.sem] = dcnt.get(o.sem, 0) + 16
                o.val = dcnt[o.sem]
                o.sem = self.dsem[o.sem]
                o.sig = True
            elif o.cc:
                cccnt += 1
                o.val = cccnt
                o.sem = self.ccsem
                o.sig = True
            elif o.sig:
                cnt[o.eng] += 1
                o.val = cnt[o.eng]
                o.sem = self.esem[o.eng]

    def emit(self, eng, e):
        waited = {}
        for o in self.ops[eng]:
            need = {}
            for d in o.waits.values():
                k = id(d.sem)
                if need.get(k, (None, 0))[1] < d.val:
                    need[k] = (d.sem, d.val)
            for k, (sem, val) in need.items():
                if waited.get(k, 0) >= val:
                    continue
                waited[k] = val
                e.wait_ge(sem, val)
            if o.fn is None:
                continue
            ins = o.fn(e)
            if o.sig:
                if o.dma:
                    ins.then_inc(o.sem, 16)
                elif o.cc:
                    ins.then_inc(o.sem)
                else:
                    ins.then_inc(o.sem, 1)


class Builder:
    def __init__(self, debug=None):
        self.debug = debug
        self.nc = nc = bass.Bass("TRN2", target_bir_lowering=False)
        self.S = Sched()
        self.sb_off = 16512
        self.psrr = 0
        self.uid = 0
        self.evrr = 0
        self.declare_io()
        self.alloc()

    def declare_io(self):
        nc = self.nc

        def inp(name, shape, dt=F32):
            return nc.dram_tensor(name, list(shape), dt, kind="ExternalInput").ap()

        self.h0T = inp("h0T", [D, T])
        self.w_in_ab = inp("w_in_ab", [D, 4608])
        self.w_out_ab = inp("w_out_ab", [D, D])
        self.w_in_c = inp("w_in_c", [D, 6144])
        self.w_out_c = inp("w_out_c", [D, D])
        self.w_mlp_in = [inp("w_mlp_in%d" % l, [D, DFF]) for l in range(2)]
        self.w_mlp_out = [inp("w_mlp_out%d" % l, [DFF, D]) for l in range(2)]
        self.gains_d = inp("gains", [P, 5 * KC])
        self.sinks_d = inp("sinks", [P, 16])
        self.lam_d = inp("lamv", [P, 256])
        self.subg_d = inp("subg", [P, 128])
        self.kqneg_d = inp("kqneg", [P, 128])
        self.negd0_d = inp("negd0", [P, NREL])
        self.negdist_a_d = inp("negdist_a", [P, len(A_NEAR), 128])
        self.mask_a_d = inp("mask_a", [P, len(A_NEAR), 128])
        self.mask_a0_d = inp("mask_a0", [P, NQB, 128])
        self.negdist_c_d = inp("negdist_c", [P, len(C_NEAR), 128])
        self.mask_c_d = inp("mask_c", [P, len(C_NEAR), 128])
        self.mask_b_d = inp("mask_b", [P, NREL, 128])
        self.flagb_d = inp("flagb", [P, 1])
        self.consts_d = inp("consts", [P, 3, 128])
        self.outT = nc.dram_tensor("outT", [D, T], F32, kind="ExternalOutput").ap()
        self.QT0 = nc.dram_tensor("QT0", [2048, T], BF16)
        self.KT0 = self.chunked("KT0", [0, 640, 1280], True)
        self.V0 = self.chunked("V0", [0, 768, 1280], False)
        self.QT1 = nc.dram_tensor("QT1", [2048, T], BF16)
        self.KT1 = self.chunked("KT1", [0, 512, 1024, 1536, 2048], True)
        self.V1 = self.chunked("V1", [0, 512, 1024, 1536, 2048], False)
        if self.debug:
            self.dbg = nc.dram_tensor("dbg", list(self.debug["shape"]), F32, kind="ExternalOutput").ap()

    def chunked(self, name, bounds, is_k):
        nc = self.nc
        ch = {"name": name, "bounds": bounds, "is_k": is_k, "loc": [], "gat": [], "keys": [[] for _ in bounds[1:]], "done": [False] * (len(bounds) - 1)}
        for i in range(len(bounds) - 1):
            w = bounds[i + 1] - bounds[i]
            if is_k:
                ch["loc"].append(nc.dram_tensor("%s_l%d" % (name, i), [w, T], BF16))
                ch["gat"].append(nc.dram_tensor("%s_g%d" % (name, i), [2 * w, T], BF16))
            else:
                ch["loc"].append(nc.dram_tensor("%s_l%d" % (name, i), [T, w], BF16))
                ch["gat"].append(nc.dram_tensor("%s_g%d" % (name, i), [2 * T, w], BF16))
        return ch

    @staticmethod
    def ch_find(ch, f):
        b = ch["bounds"]
        for i in range(len(b) - 1):
            if b[i] <= f < b[i + 1]:
                return i, f - b[i], b[i + 1] - b[i]
        raise ValueError(f)

    def sb(self, name, shape, dt, off=None):
        n = 1
        for s in shape[1:]:
            n *= s
        nbytes = n * (4 if dt == F32 else 2)
        nbytes = (nbytes + 31) // 32 * 32
        if off is None:
            off = self.sb_off
            self.sb_off += nbytes
        t = self.nc.alloc_sbuf_tensor_at(name, list(shape), dt, offset=off)
        return t

    def alloc(self):
        nc = self.nc
        self.hT = self.sb("hT", [P, KC, T], F32)
        self.xT = self.sb("xT", [P, KC, T], BF16)
        self.gains = self.sb("gains", [P, 5 * KC], F32)
        self.consts_f = self.sb("consts_f", [P, 3, 128], F32)
        self.consts = self.sb("consts_b", [P, 3, 128], BF16)
        self.epsc = self.sb("epsc", [P, 1], F32)
        self.onec = self.sb("onec", [P, 1], F32)
        self.flagb = self.sb("flagb", [P, 1], F32)
        self.nflagb = self.sb("nflagb", [P, 1], F32)
        self.sinks = self.sb("sinks", [P, 16], F32)
        self.esink = self.sb("esink", [P, 16], F32)
        self.lamv = self.sb("lamv", [P, 256], F32)
        self.lamt = self.sb("lamt", [P, 8], F32)
        self.subg = self.sb("subg", [P, 128], F32)
        self.kqneg = self.sb("kqneg", [P, 128], F32)
        self.negd0 = self.sb("negd0", [P, NREL], F32)
        base = self.sb_off
        self.rstd = [self.sb("rstd%d" % i, [P, TG], F32) for i in range(2)]
        self.lnv = [self.sb("lnv%d" % i, [P, TG], F32) for i in range(2)]
        self.sqb = [self.sb("sqb%d" % i, [P, TG], BF16) for i in range(3)]
        self.stage = [self.sb("stage%d" % i, [P, T], BF16) for i in range(2)]
        self.vstage = [self.sb("vstage%d" % i, [P, 256], BF16) for i in range(3)]
        self.relu_t = [self.sb("relu%d" % i, [P, TG], F32) for i in range(3)]
        self.wbuf = [self.sb("wbuf%d" % i, [P, 4096], BF16) for i in range(2)]
        self.hidT = self.sb("hidT", [P, 8, T], BF16)
        lin_end = self.sb_off
        self.sb_off = base
        self.kT = [[self.sb("kT%d_%d" % (i, c), [64, LP], BF16) for c in range(2)] for i in range(2)]
        self.qT = [[self.sb("qT%d_%d" % (i, c), [64, T], BF16) for c in range(2)] for i in range(2)]
        self.Vs = [self.sb("Vs%d" % i, [P, NGB, 130], BF16) for i in range(2)]
        bH_ = self.sb("biasH0", [P, NREL, 128], F32)
        self.biasH = [bH_, bH_]
        self.expH_off = self.sb_off
        self.expH = [self.sb("expH%d" % i, [P, NREL, 128], BF16) for i in range(2)]
        self.tabA = self.sb("tabA", [P, 7, 128], F32)
        self.tabB = self.sb("tabB", [P, 7, 128], F32)
        self.tabA0 = self.sb("tabA0", [P, NQB, 128], F32)
        self.maskb = self.sb("maskb", [P, NREL, 128], BF16)
        NS = 3
        eoff = self.expH_off
        self.tmpS2 = [[self.sb("tmpS%d_0" % t, [P, 4, 128], F32, off=eoff + t * 2048)] * 2 for t in range(NS)]
        self.pT2 = [[self.sb("pT%d_%d" % (t, i), [P, 4, 128], BF16) for i in range(3)] for t in range(NS)]
        self.spT2 = [[self.sb("spT%d_%d" % (t, i), [P, 4, 128], BF16) for i in range(2)] for t in range(NS)]
        self.R32s = [self.sb("R32_%d" % t, [P, 128], F32) for t in range(NS)]
        self.Rtmps = [self.sb("Rtmp_%d" % t, [P, 128], F32) for t in range(NS)]
        self.Rbfs = [[self.sb("Rbf%d_%d" % (t, i), [P, 128], BF16) for i in range(2)] for t in range(NS)]
        self.ofin3 = [self.sb("ofin%d" % t, [P, 128], F32) for t in range(NS)]
        self.ob3 = [self.sb("ob%d" % t, [P, 128], BF16) for t in range(NS)]
        self.small3 = [self.sb("small%d" % t, [P, 8], F32) for t in range(NS)]
        self.junk3 = [self.sb("junk%d" % t, [P, 128], F32) for t in range(NS)]
        self.junk = self.junk3[0]
        self.o0buf = [self.sb("o0buf%d" % t, [P, 132], F32) for t in range(NS)]
        self.tmprr = [0] * NS
        self.sprr = [0] * NS
        self.rbrr = [0] * NS
        att_end = self.sb_off
        self.sb_off = max(lin_end, att_end)
        assert self.sb_off <= 229344, self.sb_off
        self.ps = [nc.alloc_psum_tensor("ps%d" % i, [P, 512], F32) for i in range(6)]
        self.psTs = [nc.alloc_psum_tensor("psT%d" % i, [P, 1024], BF16) for i in range(2)]

    def u(self, name):
        self.uid += 1
        return (name, self.uid)

    def next_ps(self, n=6):
        i = self.psrr % n
        self.psrr += 1
        return i

    def load_consts(self):
        S = self.S
        S.op("sp", lambda e: e.dma_start(out=self.gains[:], in_=self.gains_d), writes=["gains"], dma=True)
        S.op("sp", lambda e: e.dma_start(out=self.consts_f[:], in_=self.consts_d), writes=["consts_f"], dma=True)
        S.op("sp", lambda e: e.dma_start(out=self.flagb[:], in_=self.flagb_d), writes=["flagb"], dma=True)
        S.op("sp", lambda e: e.dma_start(out=self.sinks[:], in_=self.sinks_d), writes=["sinks"], dma=True)
        S.op("sp", lambda e: e.dma_start(out=self.lamv[:], in_=self.lam_d), writes=["lamv"], dma=True)
        S.op("sp", lambda e: e.dma_start(out=self.subg[:], in_=self.subg_d), writes=["subg"], dma=True)
        S.op("sp", lambda e: e.dma_start(out=self.kqneg[:], in_=self.kqneg_d), writes=["kqneg"], dma=True)
        S.op("sp", lambda e: e.dma_start(out=self.negd0[:], in_=self.negd0_d), writes=["negd0"], dma=True)
        for tg in range(NTG):
            sl = slice(tg * TG, (tg + 1) * TG)
            S.op("sp", lambda e, sl=sl: e.dma_start(out=self.hT[:, :, sl],
                                                    in_=self.h0T.rearrange("(c p) t -> p c t", p=P)[:, :, sl]),
                 writes=[("hT", c, tg) for c in range(KC)], dma=True)
        S.op("dve", lambda e: e.tensor_copy(out=self.consts[:], in_=self.consts_f[:]), reads=["consts_f"], writes=["consts"])
        S.op("dve", lambda e: e.memset(self.epsc[:], EPS), writes=["epsc"])
        S.op("dve", lambda e: e.memset(self.onec[:], 1.0), writes=["onec"])
        S.op("dve", lambda e: e.tensor_scalar(out=self.nflagb[:], in0=self.flagb[:], scalar1=-1.0, scalar2=None, op0=ALU.mult),
             reads=["flagb"], writes=["nflagb"])
        S.op("act", lambda e: e.activation(out=self.esink[:], in_=self.sinks[:], func=AF.Exp), reads=["sinks"], writes=["esink"])
        S.op("dve", lambda e: e.tensor_tensor(out=self.junk[:, 0:64], in0=self.lamv[:, 0:64], in1=self.lamv[:, 64:128], op=ALU.mult),
             reads=["lamv"], writes=["junk"])
        S.op("dve", lambda e: e.tensor_reduce(out=self.lamt[:, 0:1], in_=self.junk[:, 0:64], axis=AX.X, op=ALU.add),
             reads=["junk"], writes=["lamt0"])
        S.op("dve", lambda e: e.tensor_tensor(out=self.junk[:, 64:128], in0=self.lamv[:, 128:192], in1=self.lamv[:, 192:256], op=ALU.mult),
             reads=["lamv"], writes=["junk2"])
        S.op("dve", lambda e: e.tensor_reduce(out=self.lamt[:, 1:2], in_=self.junk[:, 64:128], axis=AX.X, op=ALU.add),
             reads=["junk2"], writes=["lamt1"])
        S.op("act", lambda e: e.activation(out=self.lamt[:, 2:4], in_=self.lamt[:, 0:2], func=AF.Exp),
             reads=["lamt0", "lamt1"], writes=["lamt23"])
        S.op("dve", lambda e: e.tensor_tensor(out=self.lamt[:, 4:5], in0=self.lamt[:, 3:4], in1=self.lamt[:, 2:3], op=ALU.subtract),
             reads=["lamt23"], writes=["lamt4"])
        S.op("dve", lambda e: e.tensor_scalar(out=self.lamt[:, 5:6], in0=self.lamt[:, 4:5], scalar1=-LAMBDA_INIT, scalar2=None, op0=ALU.add),
             reads=["lamt4"], writes=["nlam"])
        S.op("dve", lambda e: e.tensor_scalar(out=self.subg[:], in0=self.subg[:], scalar1=1.0 - LAMBDA_INIT, scalar2=None, op0=ALU.mult),
             reads=["subg"], writes=["subg"])

    def rmsnorm(self, gi, dst_keyname="xT", final=False):
        S = self.S
        ones = self.consts[:, 0, :]
        for tg in range(NTG):
            sl = slice(tg * TG, (tg + 1) * TG)
            pi = self.next_ps()
            ps = self.ps[pi]
            for c in range(KC):
                sq = self.sqb[c % 3]
                S.op("act", lambda e, sq=sq, c=c, sl=sl: e.activation(out=sq[:], in_=self.hT[:, c, sl], func=AF.Square),
                     reads=[("hT", c, tg)], writes=[("sqb", c % 3)])
                S.op("pe", lambda e, sq=sq, c=c, ps=ps: e.matmul(ps[:, 0:TG], lhsT=ones, rhs=sq[:], start=(c == 0), stop=(c == KC - 1)),
                     reads=[("sqb", c % 3), "consts"], writes=[("ps", pi)])
            lnv = self.lnv[tg % 2]
            rstd = self.rstd[tg % 2]
            S.op("act", lambda e, ps=ps, lnv=lnv: e.activation(out=lnv[:], in_=ps[:, 0:TG], func=AF.Ln, bias=self.epsc[:, 0:1], scale=1.0 / D),
                 reads=[("ps", pi), "epsc"], writes=[("lnv", tg % 2)])
            S.op("act", lambda e, lnv=lnv, rstd=rstd: e.activation(out=rstd[:], in_=lnv[:], func=AF.Exp, scale=-0.5),
                 reads=[("lnv", tg % 2)], writes=[("rstd", tg % 2)])
            for c in range(KC):
                gcol = self.gains[:, gi * KC + c: gi * KC + c + 1]
                if final:
                    S.op("dve", lambda e, c=c, sl=sl, gcol=gcol, rstd=rstd: e.scalar_tensor_tensor(
                        out=self.hT[:, c, sl], in0=self.hT[:, c, sl], scalar=gcol, in1=rstd[:], op0=ALU.mult, op1=ALU.mult),
                        reads=[("hT", c, tg), ("rstd", tg % 2), "gains"], writes=[("hT", c, tg)])
                else:
                    S.op("dve", lambda e, c=c, sl=sl, gcol=gcol, rstd=rstd: e.scalar_tensor_tensor(
                        out=self.xT[:, c, sl], in0=self.hT[:, c, sl], scalar=gcol, in1=rstd[:], op0=ALU.mult, op1=ALU.mult),
                        reads=[("hT", c, tg), ("rstd", tg % 2), "gains"], writes=[("xT", c, tg)])

    def load_w(self, W, r0, nkc, c0, ncols):
        S = self.S
        self.wrr = getattr(self, "wrr", 0)
        slot = self.wrr % 2
        self.wrr += 1
        wb = self.wbuf[slot]
        view = wb[:, 0:nkc * ncols].rearrange("p (k n) -> p k n", n=ncols)
        src = W[r0:r0 + nkc * P, c0:c0 + ncols].rearrange("(k p) n -> p k n", p=P)
        half = nkc // 2
        S.op("pool", lambda e: e.dma_start(out=view[:, 0:half, :], in_=src[:, 0:half, :]), writes=[("wbuf", slot, 0)], dma=True)
        S.op("pool", lambda e: e.dma_start(out=view[:, half:nkc, :], in_=src[:, half:nkc, :]), writes=[("wbuf", slot, 1)], dma=True)
        return slot, view

    def evac_engine(self):
        self.evrr += 1
        return "act" if self.evrr % 2 == 0 else "dve"

    def copy_op(self, eng, out, in_, reads, writes, scale=None):
        S = self.S
        if eng == "act":
            if scale is None:
                S.op("act", lambda e: e.activation(out=out, in_=in_, func=AF.Copy), reads=reads, writes=writes)
            else:
                S.op("act", lambda e: e.activation(out=out, in_=in_, func=AF.Copy, scale=scale), reads=reads, writes=writes)
        else:
            if scale is None:
                S.op("dve", lambda e: e.tensor_copy(out=out, in_=in_), reads=reads, writes=writes)
            else:
                S.op("dve", lambda e: e.tensor_scalar(out=out, in0=in_, scalar1=scale, scalar2=None, op0=ALU.mult), reads=reads, writes=writes)

    def linear_fm(self, W, r0, nkc, c0, ncols_total, rhs_buf, rhs_key, kc0, evac, wcols=256):
        S = self.S
        ntile = ncols_total // wcols
        for wt in range(ntile):
            slot, view = self.load_w(W, r0, nkc, c0 + wt * wcols, wcols)
            for o in range(wcols // P):
                ot = wt * (wcols // P) + o
                for tg in range(NTG):
                    sl = slice(tg * TG, (tg + 1) * TG)
                    pi = self.next_ps()
                    ps = self.ps[pi]
                    for kc in range(nkc):
                        S.op("pe", lambda e, ps=ps, view=view, kc=kc, o=o, sl=sl: e.matmul(
                            ps[:, 0:TG], lhsT=view[:, kc, o * P:(o + 1) * P], rhs=rhs_buf[:, kc0 + kc, sl],
                            start=(kc == 0), stop=(kc == nkc - 1)),
                            reads=[("wbuf", slot, 0 if kc < nkc // 2 else 1), (rhs_key, kc0 + kc, tg)], writes=[("ps", pi)])
                    evac(ot, tg, ps, pi)

    def proj_qk(self, W, c0, ntiles, dst, drow0, scale=None):
        S = self.S

        def evac(ot, tg, ps, pi):
            st = self.stage[ot % 2]
            sl = slice(tg * TG, (tg + 1) * TG)
            self.copy_op(self.evac_engine(), st[:, sl], ps[:, 0:TG], [("ps", pi)], [("stage", ot % 2, tg)], scale=scale)
            if tg == NTG - 1:
                row = drow0 + ot * P
                if isinstance(dst, dict):
                    ci, w0, _ = self.ch_find(dst, row)
                    dap = dst["loc"][ci].ap()[w0:w0 + P, :]
                    key = (dst["name"], row)
                else:
                    dap = dst[row:row + P, :]
                    key = (dst.tensor.name, row)
                S.op("sp", lambda e: e.dma_start(out=dap, in_=st[:]),
                     reads=[("stage", ot % 2, t) for t in range(NTG)], writes=[key], dma=True)
                if isinstance(dst, dict):
                    dst["keys"][ci].append(key)
                    if row + P == dst["bounds"][ci + 1]:
                        self.emit_cc(dst, ci)

        self.linear_fm(W, 0, KC, c0, ntiles * P, self.xT, "xT", 0, evac)

    def proj_v(self, W, c0, ncols, dst, dcol0):
        S = self.S
        for wt in range(ncols // 256):
            slot, view = self.load_w(W, 0, KC, c0 + wt * 256, 256)
            for tb in range(NQB):
                qw = qw_of(tb)
                tsl = slice(tb * P, tb * P + qw)
                pi = self.next_ps()
                ps = self.ps[pi]
                for kc in range(KC):
                    S.op("pe", lambda e, ps=ps, view=view, kc=kc, tsl=tsl, qw=qw: e.matmul(
                        ps[0:qw, 0:256], lhsT=self.xT[:, kc, tsl], rhs=view[:, kc, :], start=(kc == 0), stop=(kc == KC - 1)),
                        reads=[("wbuf", slot, 0 if kc < 8 else 1)] + [("xT", kc, t) for t in range(NTG)], writes=[("ps", pi)])
                self.vsrr = getattr(self, "vsrr", 0) + 1
                vi = self.vsrr % 3
                vs = self.vstage[vi]
                self.copy_op(self.evac_engine(), vs[0:qw, :], ps[0:qw, 0:256], [("ps", pi)], [("vstage", vi)])
                ci, w0, _ = self.ch_find(dst, dcol0 + wt * 256)
                dap = dst["loc"][ci].ap()[tb * P: tb * P + qw, w0:w0 + 256]
                vkey = (dst["name"], "v", tb, dcol0 + wt * 256)
                S.op("sp", lambda e, vs=vs, qw=qw, dap=dap: e.dma_start(out=dap, in_=vs[0:qw, :]),
                     reads=[("vstage", vi)], writes=[vkey], dma=True)
                dst["keys"][ci].append(vkey)
            if dcol0 + (wt + 1) * 256 == dst["bounds"][ci + 1]:
                self.emit_cc(dst, ci)

    def emit_cc(self, ch, ci):
        S = self.S
        src, dst = ch["loc"][ci], ch["gat"][ci]
        S.op("pool", lambda e: e.collective_compute("AllGather", ALU.bypass, replica_groups=REPLICA_GROUPS,
                                                    ins=[src.ap().opt()], outs=[dst.ap().opt()]),
             reads=list(ch["keys"][ci]), writes=[("cc", ch["name"], ci)], cc=True)
        ch["done"][ci] = True

    def allgather(self, ch):
        assert all(ch["done"]), ch["name"]

    def add_into_h(self, ot, tg, ps, pi):
        sl = slice(tg * TG, (tg + 1) * TG)
        self.S.op("dve", lambda e: e.tensor_tensor(out=self.hT[:, ot, sl], in0=self.hT[:, ot, sl], in1=ps[:, 0:TG], op=ALU.add),
                  reads=[("ps", pi), ("hT", ot, tg)], writes=[("hT", ot, tg)])

    def out_proj(self, W):
        self.linear_fm(W, 0, KC, 0, D, self.xT, "xT", 0, self.add_into_h)

    def mlp(self, l):
        S = self.S
        W1 = self.w_mlp_in[l]
        W2 = self.w_mlp_out[l]
        for fc in range(8):
            def evac1(ot, tg, ps, pi):
                sl = slice(tg * TG, (tg + 1) * TG)
                self.rrr = getattr(self, "rrr", 0) + 1
                ri = self.rrr % 3
                rt = self.relu_t[ri]
                S.op("act", lambda e: e.activation(out=rt[:], in_=ps[:, 0:TG], func=AF.Relu), reads=[("ps", pi)], writes=[("relu", ri)])
                S.op("dve", lambda e: e.tensor_tensor(out=self.hidT[:, ot, sl], in0=rt[:], in1=rt[:], op=ALU.mult),
                     reads=[("relu", ri)], writes=[("hidT", ot, tg)])
            self.linear_fm(W1, 0, KC, fc * 1024, 1024, self.xT, "xT", 0, evac1)
            self.linear_fm(W2, fc * 1024, 8, 0, D, self.hidT, "hidT", 0, self.add_into_h, wcols=512)

    def load_k(self, slot, c, ch, row):
        S = self.S
        kt = self.kT[slot][c]
        ci, w0, w = self.ch_find(ch, row)
        g = ch["gat"][ci].ap()
        for r in range(2):
            S.op("sp", lambda e, r=r: e.dma_start(out=kt[:, r * T:(r + 1) * T], in_=g[r * w + w0: r * w + w0 + 64, :]),
                 reads=[("cc", ch["name"], ci)], writes=[("kT", slot, c, r)], dma=True)

    def load_q(self, slot, c, QTd, row):
        S = self.S
        qt = self.qT[slot][c]
        S.op("sp", lambda e: e.dma_start(out=qt[:], in_=QTd.ap()[row:row + 64, :]),
             reads=[(QTd.name, (row // P) * P)], writes=[("qT", slot, c)], dma=True)

    def load_v(self, slot, ch, col, ncol):
        S = self.S
        vs = self.Vs[slot]
        ci, w0, w = self.ch_find(ch, col)
        src = ch["gat"][ci].ap().rearrange("(g p) f -> p g f", p=P)
        for (g0, g1) in ((0, 9), (9, NGB)):
            S.op("sp", lambda e, g0=g0, g1=g1: e.dma_start(out=vs[:, g0:g1, 0:ncol], in_=src[:, g0:g1, w0:w0 + ncol]),
                 reads=[("cc", ch["name"], ci)], writes=[("Vs", slot, g0)], dma=True)

    def set_v_ones(self, slot, col):
        self.S.op("dve", lambda e: e.memset(self.Vs[slot][:, :, col:col + 1], 1.0), writes=[("Vs1", slot)],
                  reads=[])

    def transpose_out(self, obi, qw, ftile, j):
        S = self.S
        self.ptrr = getattr(self, "ptrr", 0) + 1
        pt = self.ptrr % 4
        ident = self.consts[:, 2, :]
        tsl = slice(j * P, j * P + qw)
        tgs = sorted(set([(j * P) // TG, (j * P + qw - 1) // TG]))
        S.op("pe", lambda e: e.transpose(self.psT[:, pt * 128: pt * 128 + qw], self.ob[obi][0:qw, :], ident[0:qw, 0:qw]),
             reads=[("ob", obi), "consts"], writes=[("psT", pt)])
        self.copy_op(self.evac_engine(), self.xT[:, ftile, tsl], self.psT[:, pt * 128: pt * 128 + qw], [("psT", pt)],
                     [("xT", ftile, t) for t in tgs])

    def run_tasks(self, factories, nslots):
        pending = list(factories)
        active = {}
        free = list(range(nslots))
        while pending or active:
            while pending and free:
                sl = free.pop(0)
                active[sl] = pending.pop(0)(sl)
            for sl in sorted(active.keys()):
                try:
                    next(active[sl])
                except StopIteration:
                    del active[sl]
                    free.append(sl)

    @staticmethod
    def split_groups(glist, split0=False, maxn=4):
        groups = []
        cur = []
        for g in glist:
            if cur and (g != cur[-1] + 1 or len(cur) == maxn or (split0 and cur[-1] == 0)):
                groups.append(cur)
                cur = []
            cur.append(g)
        if cur:
            groups.append(cur)
        return groups

    def softmax_task(self, ts, j, glist, bias_ap_of, kts, qts, vs, vcols, split0, fin_steps):
        S = self.S
        qw = qw_of(j)
        qsl = slice(j * P, j * P + qw)
        ncomp = len(kts)
        groups = self.split_groups(glist, split0)
        psS = self.ps[ts]
        psOb = self.ps[3 + ts][:, 0:vcols + 1]
        if ncomp == 1:
            psO = [psOb]
            okeys = [("psO", ts)]
        else:
            psO = [self.o0buf[ts][:, 0:vcols + 1], psOb]
            okeys = [("o0buf", ts), ("psO", ts)]
        seq = [(c, grp) for c in range(ncomp) for grp in groups]
        first = [True] * ncomp
        lastgrp = groups[-1]
        pend = None
        pend2 = None
        for si in range(len(seq) + 2):
            item = None
            if si < len(seq):
                c, grp = seq[si]
                n = len(grp)
                kt, ktkeys = kts[c]
                qt, qtkeys = qts[c]
                for i, g in enumerate(grp):
                    S.op("pe", lambda e, i=i, g=g: e.matmul(psS[:, i * 128: i * 128 + qw], lhsT=kt[:, g * P:(g + 1) * P], rhs=qt[:, qsl],
                                                            start=True, stop=True),
                         reads=ktkeys + qtkeys, writes=[("psS", ts)])
                self.tmprr[ts] += 1
                tb = self.tmprr[ts] % 3
                pt = self.pT2[ts][tb]
                b_ap, bkeys = bias_ap_of(grp, qw)
                p0 = self.spT2[ts][tb % 2]
                S.op("act", lambda e: e.activation(out=p0[:, 0:n, 0:qw], in_=psS[:, 0:n * 128].rearrange("p (n q) -> p n q", q=128)[:, :, 0:qw],
                                                   func=AF.Exp, scale=0.125),
                     reads=[("psS", ts)], writes=[("spT", ts, tb % 2)])
                S.op("dve", lambda e: e.tensor_tensor(out=pt[:, 0:n, 0:qw], in0=p0[:, 0:n, 0:qw], in1=b_ap, op=ALU.mult),
                     reads=[("spT", ts, tb % 2)] + bkeys, writes=[("pT", ts, tb)])
                item = (c, grp, pt, tb)
            if pend2 is not None:
                pc, pgrp, ppt, ptb = pend2
                for i, g in enumerate(pgrp):
                    st = first[pc]
                    first[pc] = False
                    sp_ = (pgrp is lastgrp and i == len(pgrp) - 1)
                    S.op("pe", lambda e, i=i, g=g, st=st, sp_=sp_: e.matmul(
                        psOb[0:qw, :], lhsT=ppt[:, i, 0:qw], rhs=vs[0][:, g, 0:vcols + 1], start=st, stop=sp_),
                        reads=[("pT", ts, ptb)] + vs[1], writes=[("psO", ts)])
                if ncomp == 2 and pc == 0 and pgrp is lastgrp:
                    self.copy_op(self.evac_engine(), self.o0buf[ts][0:qw, 0:vcols + 1], psOb[0:qw, :], [("psO", ts)], [("o0buf", ts)])
            pend2 = pend
            pend = item
            yield
        for step in fin_steps(ts, psO, okeys):
            step()
            yield

    def attn_c(self):
        S = self.S
        S.op("sp", lambda e: e.dma_start(out=self.tabA[:, 0:4, :], in_=self.negdist_c_d), writes=["tabA"], dma=True)
        S.op("sp", lambda e: e.dma_start(out=self.tabB[:, 0:4, :], in_=self.mask_c_d), writes=["tabB"], dma=True)
        for s_ in range(2):
            self.set_v_ones(s_, 128)
        facts = []
        for h in range(16):
            for j in range(NQB):
                facts.append(lambda ts, h=h, j=j: self.task_c(ts, h, j))
        self.run_tasks(facts, 3)

    def head_pre_c(self, h):
        S = self.S
        slot = h % 2
        for c in range(2):
            self.load_k(slot, c, self.KT1, h * 128 + c * 64)
            self.load_q(slot, c, self.QT1, h * 128 + c * 64)
        self.load_v(slot, self.V1, h * 128, 128)
        bH = self.biasH[slot]
        slope = C_SLOPES[h]
        for rel in range(NREL):
            if rel in C_NEAR:
                i = C_NEAR.index(rel)
                S.op("dve", lambda e: e.scalar_tensor_tensor(out=bH[:, rel, :], in0=self.tabA[:, i, :], scalar=slope,
                                                             in1=self.tabB[:, i, :], op0=ALU.mult, op1=ALU.add),
                     reads=["tabA", "tabB"], writes=[("biasH", 0, rel)])
            else:
                S.op("dve", lambda e: e.tensor_scalar(out=bH[:, rel, :], in0=self.kqneg[:], scalar1=self.negd0[:, rel:rel + 1],
                                                      scalar2=slope, op0=ALU.add, op1=ALU.mult),
                     reads=["kqneg", "negd0"], writes=[("biasH", 0, rel)])
        eH = self.expH[slot]
        S.op("act", lambda e: e.activation(out=eH[:], in_=bH[:], func=AF.Exp),
             reads=[("biasH", 0, r) for r in range(NREL)], writes=[("expH", slot, r) for r in range(NREL)])

    def task_c(self, ts, h, j):
        if j == 0:
            self.head_pre_c(h)
        slot = h % 2
        bH = self.biasH[slot]
        kts = [(self.kT[slot][c], [("kT", slot, c, 0), ("kT", slot, c, 1)]) for c in range(2)]
        qts = [(self.qT[slot][c], [("qT", slot, c)]) for c in range(2)]
        vs = (self.Vs[slot], [("Vs", slot, 0), ("Vs", slot, 9), ("Vs1", slot)])
        gmax = min(16, 9 + j)
        glist = [g for g in range(0, gmax + 1) if not c_dead(h, g - j + 8)]

        eH = self.expH[slot]

        def bias_ap_of(grp, qw):
            r0 = grp[0] - j + 8
            r1 = grp[-1] - j + 8
            return eH[:, r0:r1 + 1, 0:qw], [("expH", slot, r) for r in range(r0, r1 + 1)]

        def fin_steps(ts, psO, okeys):
            return self.fin_c_steps(ts, h, j, psO, okeys)

        return self.softmax_task(ts, j, glist, bias_ap_of, kts, qts, vs, 128, False, fin_steps)

    def fin_c_steps(self, ts, h, j, psO, okeys):
        S = self.S
        qw = qw_of(j)
        sm = self.small3[ts]
        of = self.ofin3[ts]
        ob = self.ob3[ts]
        jk = self.junk3[ts]
        kk = ("fin", ts)
        steps = []
        A = steps.append
        A(lambda: S.op("dve", lambda e: e.reciprocal(out=sm[0:qw, 0:1], in_=psO[0][0:qw, 128:129]), reads=[okeys[0]], writes=[(kk, 0)]))
        A(lambda: S.op("dve", lambda e: e.reciprocal(out=sm[0:qw, 1:2], in_=psO[1][0:qw, 128:129]), reads=[okeys[1]], writes=[(kk, 1)]))
        A(lambda: S.op("dve", lambda e: e.tensor_tensor(out=sm[0:qw, 2:3], in0=sm[0:qw, 1:2], in1=self.lamt[0:qw, 5:6], op=ALU.mult),
                       reads=[(kk, 1), "nlam"], writes=[(kk, 2)]))
        A(lambda: S.op("act", lambda e: e.activation(out=of[0:qw, :], in_=psO[0][0:qw, 0:128], func=AF.Copy, scale=sm[0:qw, 0:1]),
                       reads=[okeys[0], (kk, 0)], writes=[(kk, "of")]))
        A(lambda: S.op("dve", lambda e: e.scalar_tensor_tensor(out=of[0:qw, :], in0=psO[1][0:qw, 0:128], scalar=sm[0:qw, 2:3], in1=of[0:qw, :],
                                                               op0=ALU.mult, op1=ALU.add),
                       reads=[okeys[1], (kk, 2), (kk, "of")], writes=[(kk, "of")]))
        A(lambda: S.op("act", lambda e: e.activation(out=jk[0:qw, :], in_=of[0:qw, :], func=AF.Square),
                       reads=[(kk, "of")], writes=[(kk, "jk")]))
        A(lambda: S.op("dve", lambda e: e.tensor_reduce(out=sm[0:qw, 3:4], in_=jk[0:qw, :], axis=AX.X, op=ALU.add),
                       reads=[(kk, "jk")], writes=[(kk, 3)]))
        A(lambda: S.op("act", lambda e: e.activation(out=sm[0:qw, 4:5], in_=sm[0:qw, 3:4], func=AF.Ln, bias=self.epsc[0:qw, 0:1], scale=1.0 / 128),
                       reads=[(kk, 3), "epsc"], writes=[(kk, 4)]))
        A(lambda: S.op("act", lambda e: e.activation(out=sm[0:qw, 5:6], in_=sm[0:qw, 4:5], func=AF.Exp, scale=-0.5),
                       reads=[(kk, 4)], writes=[(kk, 5)]))
        A(lambda: S.op("dve", lambda e: e.scalar_tensor_tensor(out=ob[0:qw, :], in0=of[0:qw, :], scalar=sm[0:qw, 5:6], in1=self.subg[0:qw, :],
                                                               op0=ALU.mult, op1=ALU.mult),
                       reads=[(kk, "of"), (kk, 5), "subg"], writes=[("ob3", ts)]))
        A(lambda: self.transpose_evac(qw, h, j, self.transpose_pe(ob, [("ob3", ts)], qw)))
        return steps

    def transpose_pe(self, ob, obkeys, qw):
        S = self.S
        self.ptrr = getattr(self, "ptrr", 0) + 1
        pt = self.ptrr % 2
        ident = self.consts[:, 2, :]
        S.op("pe", lambda e: e.transpose(self.psTs[pt][:, 0:qw], ob[0:qw, :], ident[0:qw, 0:qw]),
             reads=obkeys + ["consts"], writes=[("psT", pt)])
        self.last_pt = pt
        return pt

    def transpose_evac(self, qw, ftile, j, pt=None):
        pt = self.last_pt if pt is None else pt
        tsl = slice(j * P, j * P + qw)
        tgs = sorted(set([(j * P) // TG, (j * P + qw - 1) // TG]))
        self.copy_op(self.evac_engine(), self.xT[:, ftile, tsl], self.psTs[pt][:, 0:qw], [("psT", pt)],
                     [("xT", ftile, t) for t in tgs])

    def attn_a(self):
        S = self.S
        S.op("sp", lambda e: e.dma_start(out=self.tabA[:], in_=self.negdist_a_d), writes=["tabA"], dma=True)
        S.op("sp", lambda e: e.dma_start(out=self.tabB[:], in_=self.mask_a_d), writes=["tabB"], dma=True)
        S.op("sp", lambda e: e.dma_start(out=self.tabA0[:], in_=self.mask_a0_d), writes=["tabA0"], dma=True)
        for s_ in range(2):
            self.set_v_ones(s_, 64)
        facts = []
        for h in range(16):
            for j in range(NQB):
                facts.append(lambda ts, h=h, j=j: self.task_a(ts, h, j))
        self.run_tasks(facts, 3)

    def head_pre_a(self, h):
        S = self.S
        kvh = h // 4
        kslot = kvh % 2
        slot = h % 2
        if h % 4 == 0:
            self.load_k(kslot, 0, self.KT0, kvh * 64)
            self.load_v(kslot, self.V0, kvh * 64, 64)
        self.load_q(slot, 0, self.QT0, h * 64)
        bH = self.biasH[slot]
        slope = A_SLOPES[h]
        for i, rel in enumerate(A_NEAR):
            S.op("dve", lambda e: e.scalar_tensor_tensor(out=bH[:, rel, :], in0=self.tabA[:, i, :], scalar=slope,
                                                         in1=self.tabB[:, i, :], op0=ALU.mult, op1=ALU.add),
                 reads=["tabA", "tabB"], writes=[("biasH", 0, rel)])
        for j in range(NQB):
            rel = 8 - j
            if rel in A_NEAR:
                i = A_NEAR.index(rel)
                S.op("dve", lambda e: e.scalar_tensor_tensor(out=bH[:, B0IDX[j], :], in0=self.tabA[:, i, :], scalar=slope,
                                                             in1=self.tabA0[:, j, :], op0=ALU.mult, op1=ALU.add),
                     reads=["tabA", "tabA0"], writes=[("biasH", 0, B0IDX[j])])
            else:
                S.op("dve", lambda e: e.tensor_scalar(out=bH[:, B0IDX[j], :], in0=self.kqneg[:], scalar1=self.negd0[:, rel:rel + 1],
                                                      scalar2=slope, op0=ALU.add, op1=ALU.mult),
                     reads=["kqneg", "negd0"], writes=[("biasH", 0, B0IDX[j])])
                S.op("dve", lambda e: e.tensor_tensor(out=bH[:, B0IDX[j], :], in0=bH[:, B0IDX[j], :], in1=self.tabA0[:, j, :], op=ALU.add),
                     reads=[("biasH", 0, B0IDX[j]), "tabA0"], writes=[("biasH", 0, B0IDX[j])])

    def task_a(self, ts, h, j):
        S = self.S
        if j == 0:
            self.head_pre_a(h)
            bH_, eH_ = self.biasH[h % 2], self.expH[h % 2]
            for (r0, r1) in ((0, 13), (15, 18)):
                S.op("act", lambda e: e.activation(out=eH_[:, r0:r1, :], in_=bH_[:, r0:r1, :], func=AF.Exp),
                     reads=[("biasH", 0, r) for r in range(r0, r1) if r in A_NEAR or r in B0IDX],
                     writes=[("expH", h % 2, r) for r in range(r0, r1)])
        kslot = (h // 4) % 2
        slot = h % 2
        bH = self.biasH[slot]
        kts = [(self.kT[kslot][0], [("kT", kslot, 0, 0), ("kT", kslot, 0, 1)])]
        qts = [(self.qT[slot][0], [("qT", slot, 0)])]
        vs = (self.Vs[kslot], [("Vs", kslot, 0), ("Vs", kslot, 9), ("Vs1", kslot)])
        near = sorted(set([g for g in list(range(j - 2, j + 2)) + list(range(j + 7, j + 10)) if 1 <= g <= 16]))
        glist = [0] + near

        eH = self.expH[slot]

        def bias_ap_of(grp, qw):
            if grp[0] == 0:
                assert len(grp) == 1
                return eH[:, B0IDX[j]:B0IDX[j] + 1, 0:qw], [("expH", slot, B0IDX[j])]
            r0 = grp[0] - j + 8
            r1 = grp[-1] - j + 8
            return eH[:, r0:r1 + 1, 0:qw], [("expH", slot, r) for r in range(r0, r1 + 1)]

        def fin_steps(ts, psO, okeys):
            qw = qw_of(j)
            sm = self.small3[ts]
            ob = self.obA[(h // 2) % 2][j]
            kk = ("finA", ts)
            obk = ("obA", (h // 2) % 2, j)
            steps = []
            A = steps.append
            A(lambda: S.op("dve", lambda e: e.tensor_tensor(out=sm[0:qw, 0:1], in0=psO[0][0:qw, 64:65], in1=self.esink[0:qw, h:h + 1], op=ALU.add),
                           reads=[okeys[0], "esink"], writes=[(kk, 0)]))
            A(lambda: S.op("dve", lambda e: e.reciprocal(out=sm[0:qw, 1:2], in_=sm[0:qw, 0:1]), reads=[(kk, 0)], writes=[(kk, 1)]))
            A(lambda: S.op("act", lambda e: e.activation(out=ob[0:qw, (h % 2) * 64:(h % 2) * 64 + 64], in_=psO[0][0:qw, 0:64], func=AF.Copy,
                                                         scale=sm[0:qw, 1:2]),
                           reads=[okeys[0], (kk, 1)], writes=[obk + (h % 2,)]))
            if h % 2 == 1:
                A(lambda: self.transpose_evac(qw, h // 2, j, self.transpose_pe(ob, [obk + (0,), obk + (1,)], qw)))
            return steps

        return self.softmax_task(ts, j, glist, bias_ap_of, kts, qts, vs, 64, True, fin_steps)

    def attn_b(self):
        S = self.S
        S.op("pool", lambda e: e.dma_start(out=self.maskb[:], in_=self.mask_b_d), writes=["maskb"], dma=True)
        facts = []
        for h in range(16):
            for j in range(NQB):
                facts.append(lambda ts, h=h, j=j: self.task_b(ts, h, j))
        self.run_tasks(facts, 3)

    def task_b(self, ts, h, j):
        S = self.S
        ones = self.consts[:, 0, :]
        triu = self.consts[:, 1, :]
        slot = h % 2
        if j == 0:
            self.load_k(slot, 0, self.KT0, 256 + h * 64)
            self.load_v(slot, self.V0, 256 + h * 64, 64)
            self.load_q(slot, 0, self.QT0, 1024 + h * 64)
        kt = self.kT[slot][0]
        ktkeys = [("kT", slot, 0, 0), ("kT", slot, 0, 1)]
        qt = self.qT[slot][0]
        qtkeys = [("qT", slot, 0)]
        vsb = self.Vs[slot]
        vkeys = [("Vs", slot, 0), ("Vs", slot, 9)]
        qw = qw_of(j)
        qsl = slice(j * P, j * P + qw)
        gmax = min(16, 9 + j)
        gl = list(range(gmax, -1, -1))
        nflag = sum(1 for g in gl if g - j + 8 >= 9)
        nfull = (nflag // 4) * 4
        groups = [("flag", gl[a:a + 4]) for a in range(0, nfull, 4)]
        rest = gl[nfull:]
        for a in range(0, len(rest), 4):
            grp_ = rest[a:a + 4]
            groups.append(("mixed" if any(g - j + 8 >= 9 for g in grp_) else "free", grp_))
        psS = self.ps[ts]
        psD = psS
        psO = self.ps[3 + ts][:, 0:64]
        okey = ("psOb", ts)
        R32 = self.R32s[ts]
        Rtmp = self.Rtmps[ts]
        S.op("dve", lambda e: e.memset(R32[:], 0.0), writes=[("R32", ts)])
        self.rbrr[ts] += 1
        rb = self.rbrr[ts] % 2
        S.op("dve", lambda e: e.memset(self.Rbfs[ts][rb][:], 0.0), writes=[("Rbf", ts, rb)])
        pend_pv = None
        first_pv = True
        ng = len(groups)

        def emit_pv(pend, first, last):
            wt, wkey, asc = pend
            n = len(asc)
            for i, g in enumerate(asc):
                st = first
                first = False
                sp_ = last and i == n - 1
                S.op("pe", lambda e: e.matmul(psO[0:qw, :], lhsT=wt[:, i, 0:qw], rhs=vsb[:, g, 0:64], start=st, stop=sp_),
                     reads=[wkey] + vkeys, writes=[okey])
            return first

        for gi, (cls, grp) in enumerate(groups):
            n = len(grp)
            asc = grp[::-1]
            if pend_pv is not None:
                first_pv = emit_pv(pend_pv, first_pv, False)
                pend_pv = None
            for i, g in enumerate(asc):
                S.op("pe", lambda e: e.matmul(psS[:, i * 128: i * 128 + qw], lhsT=kt[:, g * P:(g + 1) * P], rhs=qt[:, qsl], start=(i == 0), stop=True,
                                              skip_group_check=True),
                     reads=ktkeys + qtkeys, writes=[("psS", ts)])
            et = self.tmpS2[ts][0]
            self.sprr[ts] += 1
            sb_ = self.sprr[ts] % 2
            spt = self.spT2[ts][sb_]
            spkey = ("spT", ts, sb_)
            psS3 = psS[:, 0:n * 128].rearrange("p (n q) -> p n q", q=128)[:, :, 0:qw]
            psD3 = psD[:, 0:n * 128].rearrange("p (n q) -> p n q", q=128)[:, :, 0:qw]
            if cls == "flag":
                S.op("act", lambda e: e.activation(out=et[:, 0:n, 0:qw], in_=psS3, func=AF.Exp, scale=-1.0, bias=self.flagb[:, 0:1]),
                     reads=[("psS", ts), "flagb"], writes=[("tmpS", ts, 0)])
            else:
                S.op("act", lambda e: e.activation(out=et[:, 0:n, 0:qw], in_=psS3, func=AF.Exp, scale=-1.0),
                     reads=[("psS", ts)], writes=[("tmpS", ts, 0)])
            S.op("act", lambda e: e.activation(out=spt[:, 0:n, 0:qw], in_=et[:, 0:n, 0:qw], func=AF.Ln, bias=self.onec[:, 0:1], scale=1.0),
                 reads=[("tmpS", ts, 0), "onec"], writes=[spkey])
            has_near = cls == "mixed" or any((g - j + 8) in B_NEAR for g in asc)
            r0m = asc[0] - j + 8
            if has_near:
                S.op("dve", lambda e: e.tensor_tensor(out=spt[:, 0:n, 0:qw], in0=spt[:, 0:n, 0:qw], in1=self.maskb[:, r0m:r0m + n, 0:qw], op=ALU.mult),
                     reads=[spkey, "maskb"], writes=[spkey])
            yield
            for i, g in enumerate(asc):
                dsl = slice(i * 128, i * 128 + qw)
                S.op("pe", lambda e: e.matmul(psD[:, dsl], lhsT=triu, rhs=spt[:, i, 0:qw], start=False, stop=False, skip_group_check=True),
                     reads=[spkey, "consts", ("tmpS", ts, 0)], writes=[("psS", ts)])
                for i2 in range(i + 1, n):
                    S.op("pe", lambda e: e.matmul(psD[:, dsl], lhsT=ones, rhs=spt[:, i2, 0:qw], start=False, stop=False, skip_group_check=True),
                         reads=[spkey, "consts"], writes=[("psS", ts)])
                S.op("pe", lambda e: e.matmul(psD[:, dsl], lhsT=ones, rhs=self.Rbfs[ts][rb][:, 0:qw], start=False, stop=True, skip_group_check=True),
                     reads=[("Rbf", ts, rb), "consts"], writes=[("psS", ts)])
            if gi < ng - 1:
                if n > 1:
                    S.op("dve", lambda e: e.tensor_reduce(out=Rtmp[:, 0:qw], in_=spt[:, 0:n, 0:qw].rearrange("p n q -> p q n"), axis=AX.X, op=ALU.add),
                         reads=[spkey], writes=[("Rtmp", ts)])
                    S.op("dve", lambda e: e.tensor_tensor(out=R32[:, 0:qw], in0=R32[:, 0:qw], in1=Rtmp[:, 0:qw], op=ALU.add),
                         reads=[("Rtmp", ts), ("R32", ts)], writes=[("R32", ts)])
                else:
                    S.op("dve", lambda e: e.tensor_tensor(out=R32[:, 0:qw], in0=R32[:, 0:qw], in1=spt[:, 0, 0:qw], op=ALU.add),
                         reads=[spkey, ("R32", ts)], writes=[("R32", ts)])
                self.rbrr[ts] += 1
                rb = self.rbrr[ts] % 2
                S.op("dve", lambda e: e.tensor_copy(out=self.Rbfs[ts][rb][:, 0:qw], in_=R32[:, 0:qw]), reads=[("R32", ts)], writes=[("Rbf", ts, rb)])
            self.tmprr[ts] += 1
            wb = self.tmprr[ts] % 3
            wt = self.pT2[ts][wb]
            wkey = ("pT", ts, wb)
            if cls == "flag":
                S.op("act", lambda e: e.activation(out=wt[:, 0:n, 0:qw], in_=psD3, func=AF.Exp, scale=-1.0, bias=self.flagb[:, 0:1]),
                     reads=[("psS", ts), "flagb"], writes=[wkey])
            else:
                S.op("act", lambda e: e.activation(out=wt[:, 0:n, 0:qw], in_=psD3, func=AF.Exp, scale=-1.0),
                     reads=[("psS", ts)], writes=[wkey])
            if has_near:
                S.op("dve", lambda e: e.tensor_tensor(out=wt[:, 0:n, 0:qw], in0=wt[:, 0:n, 0:qw], in1=self.maskb[:, r0m:r0m + n, 0:qw], op=ALU.mult),
                     reads=[wkey, "maskb"], writes=[wkey])
            pend_pv = (wt, wkey, asc)
            yield
        emit_pv(pend_pv, first_pv, True)
        yield
        ob = self.obA[(h // 2) % 2][j]
        obk = ("obA", (h // 2) % 2, j)
        self.copy_op(self.evac_engine(), ob[0:qw, (h % 2) * 64:(h % 2) * 64 + 64], psO[0:qw, :], [okey], [obk + (h % 2,)])
        yield
        if h % 2 == 1:
            pt_ = self.transpose_pe(ob, [obk + (0,), obk + (1,)], qw)
            self.transpose_evac(qw, 8 + h // 2, j, pt_)
            yield

    def dump_h(self):
        S = self.S
        outv = self.outT.rearrange("(c p) t -> p c t", p=P)
        for tg in range(NTG):
            sl = slice(tg * TG, (tg + 1) * TG)
            S.op("sp", lambda e, sl=sl: e.dma_start(out=outv[:, :, sl], in_=self.hT[:, :, sl]),
                 reads=[("hT", c, tg) for c in range(KC)], writes=[("out", tg)], dma=True)
        S.op("sp", None, reads=[("out", tg) for tg in range(NTG)])

    def dump_x(self):
        S = self.S
        outv = self.dbgx.rearrange("(c p) t -> p c t", p=P)
        for tg in range(NTG):
            sl = slice(tg * TG, (tg + 1) * TG)
            S.op("sp", lambda e, sl=sl: e.dma_start(out=outv[:, :, sl], in_=self.xT[:, :, sl]),
                 reads=[("xT", c, tg) for c in range(KC)], writes=[("outx", tg)], dma=True)
        S.op("sp", None, reads=[("outx", tg) for tg in range(NTG)])

    def stop_here(self, name):
        if self.stop != name:
            return False
        self.S.barrier()
        if name.startswith("x_"):
            self.dump_x()
        self.dump_h()
        self.emit_all()
        return True

    def build(self, stop=None):
        S = self.S
        nc = self.nc
        self.stop = stop
        if stop is not None:
            self.dbgx = nc.dram_tensor("dbgx", [D, T], BF16, kind="ExternalOutput").ap()
        self.obA = [[self.sb("obA%d_%d" % (i, j), [P, 128], BF16) for j in range(NQB)] for i in range(2)]
        assert self.sb_off <= 229344, self.sb_off
        print('sbuf end', self.sb_off)
        self.load_consts()
        self.rmsnorm(0)
        if self.stop_here("x_norm0"):
            return nc
        self.proj_qk(self.w_in_ab, 1024, 2, self.KT0, 0)
        if stop == "x_ka":
            S.barrier()
            S.op("sp", lambda e: e.dma_start(out=self.dbgx[0:256, :], in_=self.KT0["loc"][0].ap()[0:256, :]), reads=[], writes=[("outx", 0)], dma=True)
            S.op("sp", None, reads=[("outx", 0)])
            self.dump_h()
            self.emit_all()
            return nc
        self.proj_qk(self.w_in_ab, 2560, 8, self.KT0, 256)
        self.proj_v(self.w_in_ab, 1280, 256, self.V0, 0)
        if stop == "x_va":
            S.barrier()
            S.op("sp", lambda e: e.dma_start(out=self.dbgx[0:256, :], in_=self.V0.ap()[0:T, 0:256].rearrange("t f -> t f")), reads=[], writes=[("outx", 0)], dma=True) if False else None
            for f0 in range(0, 256, 32):
                S.op("sp", lambda e, f0=f0: e.dma_start(out=self.dbgx[f0:f0 + 32, :], in_=self.V0["loc"][0].ap()[:, f0:f0 + 32].rearrange("t f -> f t"), allow_slow_non_contiguous=True), reads=[], writes=[("outx", 0)], dma=True)
            S.op("sp", None, reads=[("outx", 0)])
            self.dump_h()
            self.emit_all()
            return nc
        self.proj_v(self.w_in_ab, 3584, 1024, self.V0, 256)
        self.allgather(self.KT0)
        self.allgather(self.V0)
        self.proj_qk(self.w_in_ab, 0, 8, self.QT0.ap(), 0)
        self.proj_qk(self.w_in_ab, 1536, 8, self.QT0.ap(), 1024, scale=-0.125)
        S.barrier()
        if stop == "x_proj0":
            S.op("sp", lambda e: e.dma_start(out=self.dbgx, in_=self.QT0.ap()), reads=[], writes=[("outx", 0)], dma=True)
            S.op("sp", None, reads=[("outx", 0)])
            self.dump_h()
            self.emit_all()
            return nc
        if stop == "x_kv0":
            S.op("sp", lambda e: e.dma_start(out=self.dbgx[0:640, :], in_=self.KT0["gat"][0].ap()[640:1280, :]), reads=[], writes=[("outx", 0)], dma=True)
            S.op("sp", None, reads=[("outx", 0)])
            self.dump_h()
            self.emit_all()
            return nc
        if stop == "x_attn_a":
            self.attn_a()
            self.stop_here("x_attn_a")
            return nc
        if stop == "x_attn_b":
            self.attn_b()
            self.stop_here("x_attn_b")
            return nc
        self.attn_a()
        S.barrier()
        self.attn_b()
        S.barrier()
        if self.stop_here("x_attn0"):
            return nc
        self.out_proj(self.w_out_ab)
        if self.stop_here("h_attn0"):
            return nc
        self.rmsnorm(3)
        self.mlp(0)
        if self.stop_here("h_l0"):
            return nc
        self.rmsnorm(1)
        self.proj_qk(self.w_in_c, 2048, 16, self.KT1, 0)
        self.proj_v(self.w_in_c, 4096, 2048, self.V1, 0)
        self.allgather(self.KT1)
        self.allgather(self.V1)
        self.proj_qk(self.w_in_c, 0, 16, self.QT1.ap(), 0)
        S.barrier()
        self.attn_c()
        S.barrier()
        if self.stop_here("x_attn1"):
            return nc
        self.out_proj(self.w_out_c)
        self.rmsnorm(4)
        self.mlp(1)
        if self.stop_here("h_l1"):
            return nc
        self.rmsnorm(2, final=True)
        self.dump_h()
        self.emit_all()
        return nc

    def emit_all(self):
        S = self.S
        nc = self.nc
        with ExitStack() as stack:
            S.finalize(nc, stack)
            block = stack.enter_context(nc.Block())

            @block.tensor
            def _(e):
                S.emit("pe", e)

            @block.scalar
            def _(e):
                S.emit("act", e)

            @block.vector
            def _(e):
                S.emit("dve", e)

            @block.gpsimd
            def _(e):
                S.emit("pool", e)

            @block.sync
            def _(e):
                S.emit("sp", e)


def chunk_of(p):
    return 1 + np.floor_divide(p - 16, 64)


def make_tables(rank):
    base = rank * T
    k = np.arange(128)[:, None]
    q = np.arange(128)[None, :]
    t = {}
    t["kqneg"] = (-(q - k)).astype(np.float32) * np.ones((128, 128), np.float32)
    negd0 = np.zeros((128, NREL), np.float32)
    for rel in range(NREL):
        dq0 = base - 128 * (rel - 8)
        negd0[:, rel] = -float(dq0) if dq0 >= 128 else -1.0e6
    t["negd0"] = negd0

    def posmats(rel):
        dq0 = base - 128 * (rel - 8)
        diff = dq0 + q - k
        return diff

    def absq(rel):
        jj = 8
        g = rel - 8 + jj
        qpos = base + 128 * jj + q + 0 * k
        kpos = 128 * g + k + 0 * q
        return qpos, kpos

    na = np.zeros((128, len(A_NEAR), 128), np.float32)
    ma = np.zeros((128, len(A_NEAR), 128), np.float32)
    for i, rel in enumerate(A_NEAR):
        qpos, kpos = absq(rel)
        qpos = qpos + 128 * 64
        kpos = kpos + 128 * 64
        na[:, i, :] = -np.abs(qpos - kpos)
        qc, kc = chunk_of(qpos), chunk_of(kpos)
        ok = (kc <= qc) & (kc >= qc - 2)
        ma[:, i, :] = np.where(ok, 0.0, NEGBIG)
    t["negdist_a"] = na
    t["mask_a"] = ma
    ma0 = np.zeros((128, NQB, 128), np.float32)
    for j in range(NQB):
        qpos = base + 128 * j + q + 0 * k
        kpos = k + 0 * q
        qc, kc = chunk_of(qpos), chunk_of(kpos)
        ok = (kpos < 16) | ((kpos >= 16) & (kc <= qc) & (kc >= qc - 2))
        ma0[:, j, :] = np.where(ok, 0.0, NEGBIG)
    t["mask_a0"] = ma0
    ncm = np.zeros((128, len(C_NEAR), 128), np.float32)
    mc = np.zeros((128, len(C_NEAR), 128), np.float32)
    for i, rel in enumerate(C_NEAR):
        qpos, kpos = absq(rel)
        qpos = qpos + 128 * 64
        kpos = kpos + 128 * 64
        ncm[:, i, :] = -np.abs(qpos - kpos)
        ok = chunk_of(kpos) <= chunk_of(qpos)
        mc[:, i, :] = np.where(ok, 0.0, NEGBIG)
    t["negdist_c"] = ncm
    t["mask_c"] = mc
    mb = np.zeros((128, NREL, 128), np.float32)
    for rel in range(NREL):
        diff = posmats(rel)
        mb[:, rel, :] = (diff > 0).astype(np.float32)
    t["mask_b"] = mb
    t["flagb"] = np.full((128, 1), NEGBIG if rank == 0 else 0.0, np.float32)
    cst = np.zeros((128, 3, 128), np.float32)
    cst[:, 0, :] = 1.0
    cst[:, 1, :] = (k >= q).astype(np.float32)
    cst[:, 2, :] = np.eye(128, dtype=np.float32)
    t["consts"] = cst
    return t


_NC_CACHE = {}


def get_nc(stop=None):
    key = "main" if stop is None else str(stop)
    if key not in _NC_CACHE:
        b = Builder()
        _NC_CACHE[key] = b.build(stop)
    return _NC_CACHE[key]


def make_in_maps(x, meta_tokens, ab_norm, w_in_ab, attn_sinks, w_out_ab, c_norm, w_in_c, diff_lambda, diff_subln,
                 w_out_c, mlp_norm, w_mlp_in, w_mlp_out, final_norm):
    f = lambda a: np.ascontiguousarray(np.asarray(a, dtype=np.float32))
    x = f(x)
    B = x.shape[0]
    meta = f(meta_tokens)
    gains = np.stack([f(ab_norm)[0], f(c_norm)[0], f(final_norm), f(mlp_norm)[0], f(mlp_norm)[1]], 0)
    gains_l = np.ascontiguousarray(gains.reshape(5, KC, P).transpose(2, 0, 1).reshape(P, 5 * KC))
    sinks = np.ascontiguousarray(np.broadcast_to(f(attn_sinks)[0][None, :], (P, 16)))
    lamv = np.ascontiguousarray(np.broadcast_to(f(diff_lambda)[0].reshape(1, 256), (P, 256)))
    subg = np.ascontiguousarray(np.broadcast_to(f(diff_subln)[0][None, :], (P, 128)))
    shared = {
        "w_in_ab": f(w_in_ab)[0], "w_out_ab": f(w_out_ab)[0], "w_in_c": f(w_in_c)[0], "w_out_c": f(w_out_c)[0],
        "w_mlp_in0": f(w_mlp_in)[0], "w_mlp_in1": f(w_mlp_in)[1], "w_mlp_out0": f(w_mlp_out)[0], "w_mlp_out1": f(w_mlp_out)[1],
        "gains": gains_l, "sinks": sinks, "lamv": lamv, "subg": subg,
    }
    tabs = [make_tables(0), make_tables(1)]
    in_maps = []
    for core in range(8):
        b, r = core // 2, core % 2
        seq = np.zeros((LP, D), np.float32)
        seq[0:16] = meta
        seq[16:16 + 2048] = x[b]
        h0T = np.ascontiguousarray(seq[r * T:(r + 1) * T].T)
        m = dict(shared)
        m["h0T"] = h0T
        m.update(tabs[r])
        in_maps.append(m)
    return in_maps


def assemble(results):
    out = np.zeros((4, 2048, D), np.float32)
    for core in range(8):
        b, r = core // 2, core % 2
        oT = np.asarray(results[core]["outT"])
        rows = oT.T
        pos0 = r * T
        lo = max(pos0, 16)
        hi = min(pos0 + T, 16 + 2048)
        out[b, lo - 16:hi - 16] = rows[lo - pos0:hi - pos0]
    return out


def kernel(**inputs):
    nc = get_nc()
    in_maps = make_in_maps(**inputs)
    res = run_bass_kernel_spmd(nc, in_maps, core_ids=list(range(8)))
    return assemble(res.results)
```

```python
import math
import types
from contextlib import ExitStack

import numpy as np
import concourse.bass as bass
import concourse.mybir as mybir
from concourse.bass_utils import run_bass_kernel_spmd

F32 = mybir.dt.float32
BF16 = mybir.dt.bfloat16
AF = mybir.ActivationFunctionType
ALU = mybir.AluOpType
AX = mybir.AxisListType

P = 128
D = 2048
KC = 16
T = 1088
LP = 2176
NTG = 4
TG = 272
NQB = 9
NGB = 17
DFF = 8192
EPS = 1e-6
NEGBIG = -30000.0
REPLICA_GROUPS = [[0, 1], [2, 3], [4, 5], [6, 7]]
NREL = 18
A_SLOPES = [2.0 ** (-8.0 * (i + 1) / 16) for i in range(16)]
C_SLOPES = A_SLOPES
LAMBDA_INIT = 0.8 - 0.6 * math.exp(-0.3 * 1)

A_NEAR = [6, 7, 8, 9, 15, 16, 17]
C_NEAR = [8, 9, 16, 17]
B_NEAR = [8, 16, 17]
B0IDX = [0, 1, 2, 3, 4, 5, 10, 11, 12]


def c_dead(h, rel):
    s_ = C_SLOPES[h]
    if rel in C_NEAR:
        return False
    dead = []
    for base in (0, T):
        dq0 = base - 128 * (rel - 8)
        if dq0 < 128:
            dead.append(base == 0 and rel >= 10)
        else:
            dead.append(s_ * (dq0 - 127) > 110.0)
    if not dead[0] and rel >= 10:
        return False
    return all(dead) and (s_ * 1.0e6 > 110.0)


def qw_of(j):
    return 128 if j < 8 else 64


def _freeze(fn):
    if fn is None or fn.__closure__ is None:
        return fn
    cells = []
    for c in fn.__closure__:
        try:
            cells.append(types.CellType(c.cell_contents))
        except ValueError:
            cells.append(c)
    return types.FunctionType(fn.__code__, fn.__globals__, fn.__name__, fn.__defaults__, tuple(cells))


class Op:
    __slots__ = ("eng", "fn", "dma", "waits", "sig", "sem", "val", "idx", "cc")

    def __init__(self, eng, fn, dma, cc=False):
        self.eng = eng
        self.fn = fn
        self.dma = dma
        self.cc = cc
        self.waits = {}
        self.sig = False
        self.sem = None
        self.val = 0


class Sched:
    ENGS = ("pe", "act", "dve", "pool", "sp")
    NDMASEM = 8

    def __init__(self):
        self.ops = {e: [] for e in self.ENGS}
        self.allops = []
        self.last_w = {}
        self.readers = {}
        self.dma_rr = {"pool": 0, "sp": 0}
        self.dma_last = {}

    def op(self, eng, fn, reads=(), writes=(), dma=False, cc=False):
        o = Op(eng, _freeze(fn), dma, cc)
        o.idx = len(self.allops)
        deps = set()
        for k in reads:
            w = self.last_w.get(k)
            if w is not None:
                deps.add(w)
        for k in writes:
            w = self.last_w.get(k)
            if w is not None:
                deps.add(w)
            for r in self.readers.get(k, ()):
                deps.add(r)
        if dma:
            slot = (eng, self.dma_rr[eng] % self.NDMASEM)
            self.dma_rr[eng] += 1
            o.sem = slot
            prev = self.dma_last.get(slot)
            if prev is not None:
                deps.add(prev)
            self.dma_last[slot] = o
        for d in deps:
            if d is o:
                continue
            if d.eng == "pe" and eng == "pe" and not d.dma:
                continue
            o.waits[d.idx] = d
            d.sig = True
        for k in reads:
            self.readers.setdefault(k, []).append(o)
        for k in writes:
            self.last_w[k] = o
            self.readers[k] = []
        self.ops[eng].append(o)
        self.allops.append(o)
        return o

    def barrier(self):
        last = []
        for e in self.ENGS:
            for o_ in reversed(self.ops[e]):
                if o_.fn is not None:
                    last.append(o_)
                    break
        outstanding = [o for o in self.dma_last.values()]
        key = ("__barrier__", len(self.allops))
        for e in self.ENGS:
            o = Op(e, None, False)
            o.idx = len(self.allops)
            for d in last + outstanding:
                if d.eng == e and not d.dma:
                    continue
                o.waits[d.idx] = d
                d.sig = True
            self.ops[e].append(o)
            self.allops.append(o)
        self.readers = {}

    def finalize(self, nc, stack):
        cnt = {e: 0 for e in self.ENGS}
        self.esem = {e: stack.enter_context(nc.semaphore("es_" + e)) for e in ("pe", "act", "dve", "pool", "sp")}
        self.dsem = {}
        for e in ("pool", "sp"):
            for i in range(self.NDMASEM):
                self.dsem[(e, i)] = stack.enter_context(nc.semaphore("ds_%s%d" % (e, i)))
        self.ccsem = stack.enter_context(nc.semaphore("ccsem"))
        dcnt = {}
        cccnt = 0
        for o in self.allops:
            if o.dma:
                dcnt[o.sem] = dcnt.get(o.sem, 0) + 16
                o.val = dcnt[o.sem]
                o.sem = self.dsem[o.sem]
                o.sig = True
            elif o.cc:
                cccnt += 1
                o.val = cccnt
                o.sem = self.ccsem
                o.sig = True
            elif o.sig:
                cnt[o.eng] += 1
                o.val = cnt[o.eng]
                o.sem = self.esem[o.eng]

    def emit(self, eng, e):
        waited = {}
        for o in self.ops[eng]:
            need = {}
            for d in o.waits.values():
                k = id(d.sem)
                if need.get(k, (None, 0))[1] < d.val:
                    need[k] = (d.sem, d.val)
            for k, (sem, val) in need.items():
                if waited.get(k, 0) >= val:
                    continue
                waited[k] = val
                e.wait_ge(sem, val)
            if o.fn is None:
                continue
            ins = o.fn(e)
            if o.sig:
                if o.dma:
                    ins.then_inc(o.sem, 16)
                elif o.cc:
                    ins.then_inc(o.sem)
                else:
                    ins.then_inc(o.sem, 1)


class Builder:
    def __init__(self, debug=None):
        self.debug = debug
        self.nc = nc = bass.Bass("TRN2", target_bir_lowering=False)
        self.S = Sched()
        self.sb_off = 16512
        self.psrr = 0
        self.uid = 0
        self.evrr = 0
        self.declare_io()
        self.alloc()

    def declare_io(self):
        nc = self.nc

        def inp(name, shape, dt=F32):
            return nc.dram_tensor(name, list(shape), dt, kind="ExternalInput").ap()

        self.h0T = inp("h0T", [D, T])
        self.w_in_ab = inp("w_in_ab", [D, 4608])
        self.w_out_ab = inp("w_out_ab", [D, D])
        self.w_in_c = inp("w_in_c", [D, 6144])
        self.w_out_c = inp("w_out_c", [D, D])
        self.w_mlp_in = [inp("w_mlp_in%d" % l, [D, DFF]) for l in range(2)]
        self.w_mlp_out = [inp("w_mlp_out%d" % l, [DFF, D]) for l in range(2)]
        self.gains_d = inp("gains", [P, 5 * KC])
        self.sinks_d = inp("sinks", [P, 16])
        self.lam_d = inp("lamv", [P, 256])
        self.subg_d = inp("subg", [P, 128])
        self.kqneg_d = inp("kqneg", [P, 128])
        self.negd0_d = inp("negd0", [P, NREL])
        self.negdist_a_d = inp("negdist_a", [P, len(A_NEAR), 128])
        self.mask_a_d = inp("mask_a", [P, len(A_NEAR), 128])
        self.mask_a0_d = inp("mask_a0", [P, NQB, 128])
        self.negdist_c_d = inp("negdist_c", [P, len(C_NEAR), 128])
        self.mask_c_d = inp("mask_c", [P, len(C_NEAR), 128])
        self.mask_b_d = inp("mask_b", [P, NREL, 128])
        self.flagb_d = inp("flagb", [P, 1])
        self.consts_d = inp("consts", [P, 3, 128])
        self.outT = nc.dram_tensor("outT", [D, T], F32, kind="ExternalOutput").ap()
        self.QT0 = nc.dram_tensor("QT0", [2048, T], BF16)
        self.KT0 = self.chunked("KT0", [0, 640, 1280], True)
        self.V0 = self.chunked("V0", [0, 768, 1280], False)
        self.QT1 = nc.dram_tensor("QT1", [2048, T], BF16)
        self.KT1 = self.chunked("KT1", [0, 512, 1024, 1536, 2048], True)
        self.V1 = self.chunked("V1", [0, 512, 1024, 1536, 2048], False)
        if self.debug:
            self.dbg = nc.dram_tensor("dbg", list(self.debug["shape"]), F32, kind="ExternalOutput").ap()

    def chunked(self, name, bounds, is_k):
        nc = self.nc
        ch = {"name": name, "bounds": bounds, "is_k": is_k, "loc": [], "gat": [], "keys": [[] for _ in bounds[1:]], "done": [False] * (len(bounds) - 1)}
        for i in range(len(bounds) - 1):
            w = bounds[i + 1] - bounds[i]
            if is_k:
                ch["loc"].append(nc.dram_tensor("%s_l%d" % (name, i), [w, T], BF16))
                ch["gat"].append(nc.dram_tensor("%s_g%d" % (name, i), [2 * w, T], BF16))
            else:
                ch["loc"].append(nc.dram_tensor("%s_l%d" % (name, i), [T, w], BF16))
                ch["gat"].append(nc.dram_tensor("%s_g%d" % (name, i), [2 * T, w], BF16))
        return ch

    @staticmethod
    def ch_find(ch, f):
        b = ch["bounds"]
        for i in range(len(b) - 1):
            if b[i] <= f < b[i + 1]:
                return i, f - b[i], b[i + 1] - b[i]
        raise ValueError(f)

    def sb(self, name, shape, dt, off=None):
        n = 1
        for s in shape[1:]:
            n *= s
        nbytes = n * (4 if dt == F32 else 2)
        nbytes = (nbytes + 31) // 32 * 32
        if off is None:
            off = self.sb_off
            self.sb_off += nbytes
        t = self.nc.alloc_sbuf_tensor_at(name, list(shape), dt, offset=off)
        return t

    def alloc(self):
        nc = self.nc
        self.hT = self.sb("hT", [P, KC, T], F32)
        self.xT = self.sb("xT", [P, KC, T], BF16)
        self.gains = self.sb("gains", [P, 5 * KC], F32)
        self.consts_f = self.sb("consts_f", [P, 3, 128], F32)
        self.consts = self.sb("consts_b", [P, 3, 128], BF16)
        self.epsc = self.sb("epsc", [P, 1], F32)
        self.onec = self.sb("onec", [P, 1], F32)
        self.flagb = self.sb("flagb", [P, 1], F32)
        self.nflagb = self.sb("nflagb", [P, 1], F32)
        self.sinks = self.sb("sinks", [P, 16], F32)
        self.esink = self.sb("esink", [P, 16], F32)
        self.lamv = self.sb("lamv", [P, 256], F32)
        self.lamt = self.sb("lamt", [P, 8], F32)
        self.subg = self.sb("subg", [P, 128], F32)
        self.kqneg = self.sb("kqneg", [P, 128], F32)
        self.negd0 = self.sb("negd0", [P, NREL], F32)
        base = self.sb_off
        self.rstd = [self.sb("rstd%d" % i, [P, TG], F32) for i in range(2)]
        self.lnv = [self.sb("lnv%d" % i, [P, TG], F32) for i in range(2)]
        self.sqb = [self.sb("sqb%d" % i, [P, TG], BF16) for i in range(3)]
        self.stage = [self.sb("stage%d" % i, [P, T], BF16) for i in range(2)]
        self.vstage = [self.sb("vstage%d" % i, [P, 256], BF16) for i in range(3)]
        self.relu_t = [self.sb("relu%d" % i, [P, TG], F32) for i in range(3)]
        self.wbuf = [self.sb("wbuf%d" % i, [P, 4096], BF16) for i in range(2)]
        self.hidT = self.sb("hidT", [P, 8, T], BF16)
        lin_end = self.sb_off
        self.sb_off = base
        self.kT = [[self.sb("kT%d_%d" % (i, c), [64, LP], BF16) for c in range(2)] for i in range(2)]
        self.qT = [[self.sb("qT%d_%d" % (i, c), [64, T], BF16) for c in range(2)] for i in range(2)]
        self.Vs = [self.sb("Vs%d" % i, [P, NGB, 130], BF16) for i in range(2)]
        bH_ = self.sb("biasH0", [P, NREL, 128], F32)
        self.biasH = [bH_, bH_]
        self.expH_off = self.sb_off
        self.expH = [self.sb("expH%d" % i, [P, NREL, 128], BF16) for i in range(2)]
        self.tabA = self.sb("tabA", [P, 7, 128], F32)
        self.tabB = self.sb("tabB", [P, 7, 128], F32)
        self.tabA0 = self.sb("tabA0", [P, NQB, 128], F32)
        self.maskb = self.sb("maskb", [P, NREL, 128], BF16)
        NS = 3
        eoff = self.expH_off
        self.tmpS2 = [[self.sb("tmpS%d_0" % t, [P, 4, 128], F32, off=eoff + t * 2048)] * 2 for t in range(NS)]
        self.pT2 = [[self.sb("pT%d_%d" % (t, i), [P, 4, 128], BF16) for i in range(3)] for t in range(NS)]
        self.spT2 = [[self.sb("spT%d_%d" % (t, i), [P, 4, 128], BF16) for i in range(2)] for t in range(NS)]
        self.R32s = [self.sb("R32_%d" % t, [P, 128], F32) for t in range(NS)]
        self.Rtmps = [self.sb("Rtmp_%d" % t, [P, 128], F32) for t in range(NS)]
        self.Rbfs = [[self.sb("Rbf%d_%d" % (t, i), [P, 128], BF16) for i in range(2)] for t in range(NS)]
        self.ofin3 = [self.sb("ofin%d" % t, [P, 128], F32) for t in range(NS)]
        self.ob3 = [self.sb("ob%d" % t, [P, 128], BF16) for t in range(NS)]
        self.small3 = [self.sb("small%d" % t, [P, 8], F32) for t in range(NS)]
        self.junk3 = [self.sb("junk%d" % t, [P, 128], F32) for t in range(NS)]
        self.junk = self.junk3[0]
        self.o0buf = [self.sb("o0buf%d" % t, [P, 132], F32) for t in range(NS)]
        self.tmprr = [0] * NS
        self.sprr = [0] * NS
        self.rbrr = [0] * NS
        att_end = self.sb_off
        self.sb_off = max(lin_end, att_end)
        assert self.sb_off <= 229344, self.sb_off
        self.ps = [nc.alloc_psum_tensor("ps%d" % i, [P, 512], F32) for i in range(6)]
        self.psTs = [nc.alloc_psum_tensor("psT%d" % i, [P, 1024], BF16) for i in range(2)]

    def u(self, name):
        self.uid += 1
        return (name, self.uid)

    def next_ps(self, n=6):
        i = self.psrr % n
        self.psrr += 1
        return i

    def load_consts(self):
        S = self.S
        S.op("sp", lambda e: e.dma_start(out=self.gains[:], in_=self.gains_d), writes=["gains"], dma=True)
        S.op("sp", lambda e: e.dma_start(out=self.consts_f[:], in_=self.consts_d), writes=["consts_f"], dma=True)
        S.op("sp", lambda e: e.dma_start(out=self.flagb[:], in_=self.flagb_d), writes=["flagb"], dma=True)
        S.op("sp", lambda e: e.dma_start(out=self.sinks[:], in_=self.sinks_d), writes=["sinks"], dma=True)
        S.op("sp", lambda e: e.dma_start(out=self.lamv[:], in_=self.lam_d), writes=["lamv"], dma=True)
        S.op("sp", lambda e: e.dma_start(out=self.subg[:], in_=self.subg_d), writes=["subg"], dma=True)
        S.op("sp", lambda e: e.dma_start(out=self.kqneg[:], in_=self.kqneg_d), writes=["kqneg"], dma=True)
        S.op("sp", lambda e: e.dma_start(out=self.negd0[:], in_=self.negd0_d), writes=["negd0"], dma=True)
        for tg in range(NTG):
            sl = slice(tg * TG, (tg + 1) * TG)
            S.op("sp", lambda e, sl=sl: e.dma_start(out=self.hT[:, :, sl],
                                                    in_=self.h0T.rearrange("(c p) t -> p c t", p=P)[:, :, sl]),
                 writes=[("hT", c, tg) for c in range(KC)], dma=True)
        S.op("dve", lambda e: e.tensor_copy(out=self.consts[:], in_=self.consts_f[:]), reads=["consts_f"], writes=["consts"])
        S.op("dve", lambda e: e.memset(self.epsc[:], EPS), writes=["epsc"])
        S.op("dve", lambda e: e.memset(self.onec[:], 1.0), writes=["onec"])
        S.op("dve", lambda e: e.tensor_scalar(out=self.nflagb[:], in0=self.flagb[:], scalar1=-1.0, scalar2=None, op0=ALU.mult),
             reads=["flagb"], writes=["nflagb"])
        S.op("act", lambda e: e.activation(out=self.esink[:], in_=self.sinks[:], func=AF.Exp), reads=["sinks"], writes=["esink"])
        S.op("dve", lambda e: e.tensor_tensor(out=self.junk[:, 0:64], in0=self.lamv[:, 0:64], in1=self.lamv[:, 64:128], op=ALU.mult),
             reads=["lamv"], writes=["junk"])
        S.op("dve", lambda e: e.tensor_reduce(out=self.lamt[:, 0:1], in_=self.junk[:, 0:64], axis=AX.X, op=ALU.add),
             reads=["junk"], writes=["lamt0"])
        S.op("dve", lambda e: e.tensor_tensor(out=self.junk[:, 64:128], in0=self.lamv[:, 128:192], in1=self.lamv[:, 192:256], op=ALU.mult),
             reads=["lamv"], writes=["junk2"])
        S.op("dve", lambda e: e.tensor_reduce(out=self.lamt[:, 1:2], in_=self.junk[:, 64:128], axis=AX.X, op=ALU.add),
             reads=["junk2"], writes=["lamt1"])
        S.op("act", lambda e: e.activation(out=self.lamt[:, 2:4], in_=self.lamt[:, 0:2], func=AF.Exp),
             reads=["lamt0", "lamt1"], writes=["lamt23"])
        S.op("dve", lambda e: e.tensor_tensor(out=self.lamt[:, 4:5], in0=self.lamt[:, 3:4], in1=self.lamt[:, 2:3], op=ALU.subtract),
             reads=["lamt23"], writes=["lamt4"])
        S.op("dve", lambda e: e.tensor_scalar(out=self.lamt[:, 5:6], in0=self.lamt[:, 4:5], scalar1=-LAMBDA_INIT, scalar2=None, op0=ALU.add),
             reads=["lamt4"], writes=["nlam"])
        S.op("dve", lambda e: e.tensor_scalar(out=self.subg[:], in0=self.subg[:], scalar1=1.0 - LAMBDA_INIT, scalar2=None, op0=ALU.mult),
             reads=["subg"], writes=["subg"])

    def rmsnorm(self, gi, dst_keyname="xT", final=False):
        S = self.S
        ones = self.consts[:, 0, :]
        for tg in range(NTG):
            sl = slice(tg * TG, (tg + 1) * TG)
            pi = self.next_ps()
            ps = self.ps[pi]
            for c in range(KC):
                sq = self.sqb[c % 3]
                S.op("act", lambda e, sq=sq, c=c, sl=sl: e.activation(out=sq[:], in_=self.hT[:, c, sl], func=AF.Square),
                     reads=[("hT", c, tg)], writes=[("sqb", c % 3)])
                S.op("pe", lambda e, sq=sq, c=c, ps=ps: e.matmul(ps[:, 0:TG], lhsT=ones, rhs=sq[:], start=(c == 0), stop=(c == KC - 1)),
                     reads=[("sqb", c % 3), "consts"], writes=[("ps", pi)])
            lnv = self.lnv[tg % 2]
            rstd = self.rstd[tg % 2]
            S.op("act", lambda e, ps=ps, lnv=lnv: e.activation(out=lnv[:], in_=ps[:, 0:TG], func=AF.Ln, bias=self.epsc[:, 0:1], scale=1.0 / D),
                 reads=[("ps", pi), "epsc"], writes=[("lnv", tg % 2)])
            S.op("act", lambda e, lnv=lnv, rstd=rstd: e.activation(out=rstd[:], in_=lnv[:], func=AF.Exp, scale=-0.5),
                 reads=[("lnv", tg % 2)], writes=[("rstd", tg % 2)])
            for c in range(KC):
                gcol = self.gains[:, gi * KC + c: gi * KC + c + 1]
                if final:
                    S.op("dve", lambda e, c=c, sl=sl, gcol=gcol, rstd=rstd: e.scalar_tensor_tensor(
                        out=self.hT[:, c, sl], in0=self.hT[:, c, sl], scalar=gcol, in1=rstd[:], op0=ALU.mult, op1=ALU.mult),
                        reads=[("hT", c, tg), ("rstd", tg % 2), "gains"], writes=[("hT", c, tg)])
                else:
                    S.op("dve", lambda e, c=c, sl=sl, gcol=gcol, rstd=rstd: e.scalar_tensor_tensor(
                        out=self.xT[:, c, sl], in0=self.hT[:, c, sl], scalar=gcol, in1=rstd[:], op0=ALU.mult, op1=ALU.mult),
                        reads=[("hT", c, tg), ("rstd", tg % 2), "gains"], writes=[("xT", c, tg)])

    def load_w(self, W, r0, nkc, c0, ncols):
        S = self.S
        self.wrr = getattr(self, "wrr", 0)
        slot = self.wrr % 2
        self.wrr += 1
        wb = self.wbuf[slot]
        view = wb[:, 0:nkc * ncols].rearrange("p (k n) -> p k n", n=ncols)
        src = W[r0:r0 + nkc * P, c0:c0 + ncols].rearrange("(k p) n -> p k n", p=P)
        half = nkc // 2
        S.op("pool", lambda e: e.dma_start(out=view[:, 0:half, :], in_=src[:, 0:half, :]), writes=[("wbuf", slot, 0)], dma=True)
        S.op("pool", lambda e: e.dma_start(out=view[:, half:nkc, :], in_=src[:, half:nkc, :]), writes=[("wbuf", slot, 1)], dma=True)
        return slot, view

    def evac_engine(self):
        self.evrr += 1
        return "act" if self.evrr % 2 == 0 else "dve"

    def copy_op(self, eng, out, in_, reads, writes, scale=None):
        S = self.S
        if eng == "act":
            if scale is None:
                S.op("act", lambda e: e.activation(out=out, in_=in_, func=AF.Copy), reads=reads, writes=writes)
            else:
                S.op("act", lambda e: e.activation(out=out, in_=in_, func=AF.Copy, scale=scale), reads=reads, writes=writes)
        else:
            if scale is None:
                S.op("dve", lambda e: e.tensor_copy(out=out, in_=in_), reads=reads, writes=writes)
            else:
                S.op("dve", lambda e: e.tensor_scalar(out=out, in0=in_, scalar1=scale, scalar2=None, op0=ALU.mult), reads=reads, writes=writes)

    def linear_fm(self, W, r0, nkc, c0, ncols_total, rhs_buf, rhs_key, kc0, evac, wcols=256):
        S = self.S
        ntile = ncols_total // wcols
        for wt in range(ntile):
            slot, view = self.load_w(W, r0, nkc, c0 + wt * wcols, wcols)
            for o in range(wcols // P):
                ot = wt * (wcols // P) + o
                for tg in range(NTG):
                    sl = slice(tg * TG, (tg + 1) * TG)
                    pi = self.next_ps()
                    ps = self.ps[pi]
                    for kc in range(nkc):
                        S.op("pe", lambda e, ps=ps, view=view, kc=kc, o=o, sl=sl: e.matmul(
                            ps[:, 0:TG], lhsT=view[:, kc, o * P:(o + 1) * P], rhs=rhs_buf[:, kc0 + kc, sl],
                            start=(kc == 0), stop=(kc == nkc - 1)),
                            reads=[("wbuf", slot, 0 if kc < nkc // 2 else 1), (rhs_key, kc0 + kc, tg)], writes=[("ps", pi)])
                    evac(ot, tg, ps, pi)

    def proj_qk(self, W, c0, ntiles, dst, drow0, scale=None):
        S = self.S

        def evac(ot, tg, ps, pi):
            st = self.stage[ot % 2]
            sl = slice(tg * TG, (tg + 1) * TG)
            self.copy_op(self.evac_engine(), st[:, sl], ps[:, 0:TG], [("ps", pi)], [("stage", ot % 2, tg)], scale=scale)
            if tg == NTG - 1:
                row = drow0 + ot * P
                if isinstance(dst, dict):
                    ci, w0, _ = self.ch_find(dst, row)
                    dap = dst["loc"][ci].ap()[w0:w0 + P, :]
                    key = (dst["name"], row)
                else:
                    dap = dst[row:row + P, :]
                    key = (dst.tensor.name, row)
                S.op("sp", lambda e: e.dma_start(out=dap, in_=st[:]),
                     reads=[("stage", ot % 2, t) for t in range(NTG)], writes=[key], dma=True)
                if isinstance(dst, dict):
                    dst["keys"][ci].append(key)
                    if row + P == dst["bounds"][ci + 1]:
                        self.emit_cc(dst, ci)

        self.linear_fm(W, 0, KC, c0, ntiles * P, self.xT, "xT", 0, evac)

    def proj_v(self, W, c0, ncols, dst, dcol0):
        S = self.S
        for wt in range(ncols // 256):
            slot, view = self.load_w(W, 0, KC, c0 + wt * 256, 256)
            for tb in range(NQB):
                qw = qw_of(tb)
                tsl = slice(tb * P, tb * P + qw)
                pi = self.next_ps()
                ps = self.ps[pi]
                for kc in range(KC):
                    S.op("pe", lambda e, ps=ps, view=view, kc=kc, tsl=tsl, qw=qw: e.matmul(
                        ps[0:qw, 0:256], lhsT=self.xT[:, kc, tsl], rhs=view[:, kc, :], start=(kc == 0), stop=(kc == KC - 1)),
                        reads=[("wbuf", slot, 0 if kc < 8 else 1)] + [("xT", kc, t) for t in range(NTG)], writes=[("ps", pi)])
                self.vsrr = getattr(self, "vsrr", 0) + 1
                vi = self.vsrr % 3
                vs = self.vstage[vi]
                self.copy_op(self.evac_engine(), vs[0:qw, :], ps[0:qw, 0:256], [("ps", pi)], [("vstage", vi)])
                ci, w0, _ = self.ch_find(dst, dcol0 + wt * 256)
                dap = dst["loc"][ci].ap()[tb * P: tb * P + qw, w0:w0 + 256]
                vkey = (dst["name"], "v", tb, dcol0 + wt * 256)
                S.op("sp", lambda e, vs=vs, qw=qw, dap=dap: e.dma_start(out=dap, in_=vs[0:qw, :]),
                     reads=[("vstage", vi)], writes=[vkey], dma=True)
                dst["keys"][ci].append(vkey)
            if dcol0 + (wt + 1) * 256 == dst["bounds"][ci + 1]:
                self.emit_cc(dst, ci)

    def emit_cc(self, ch, ci):
        S = self.S
        src, dst = ch["loc"][ci], ch["gat"][ci]
        S.op("pool", lambda e: e.collective_compute("AllGather", ALU.bypass, replica_groups=REPLICA_GROUPS,
                                                    ins=[src.ap().opt()], outs=[dst.ap().opt()]),
             reads=list(ch["keys"][ci]), writes=[("cc", ch["name"], ci)], cc=True)
        ch["done"][ci] = True

    def allgather(self, ch):
        assert all(ch["done"]), ch["name"]

    def add_into_h(self, ot, tg, ps, pi):
        sl = slice(tg * TG, (tg + 1) * TG)
        self.S.op("dve", lambda e: e.tensor_tensor(out=self.hT[:, ot, sl], in0=self.hT[:, ot, sl], in1=ps[:, 0:TG], op=ALU.add),
                  reads=[("ps", pi), ("hT", ot, tg)], writes=[("hT", ot, tg)])

    def out_proj(self, W):
        self.linear_fm(W, 0, KC, 0, D, self.xT, "xT", 0, self.add_into_h)

    def mlp(self, l):
        S = self.S
        W1 = self.w_mlp_in[l]
        W2 = self.w_mlp_out[l]
        for fc in range(8):
            def evac1(ot, tg, ps, pi):
                sl = slice(tg * TG, (tg + 1) * TG)
                self.rrr = getattr(self, "rrr", 0) + 1
                ri = self.rrr % 3
                rt = self.relu_t[ri]
                S.op("act", lambda e: e.activation(out=rt[:], in_=ps[:, 0:TG], func=AF.Relu), reads=[("ps", pi)], writes=[("relu", ri)])
                S.op("dve", lambda e: e.tensor_tensor(out=self.hidT[:, ot, sl], in0=rt[:], in1=rt[:], op=ALU.mult),
                     reads=[("relu", ri)], writes=[("hidT", ot, tg)])
            self.linear_fm(W1, 0, KC, fc * 1024, 1024, self.xT, "xT", 0, evac1)
            self.linear_fm(W2, fc * 1024, 8, 0, D, self.hidT, "hidT", 0, self.add_into_h, wcols=512)

    def load_k(self, slot, c, ch, row):
        S = self.S
        kt = self.kT[slot][c]
        ci, w0, w = self.ch_find(ch, row)
        g = ch["gat"][ci].ap()
        for r in range(2):
            S.op("sp", lambda e, r=r: e.dma_start(out=kt[:, r * T:(r + 1) * T], in_=g[r * w + w0: r * w + w0 + 64, :]),
                 reads=[("cc", ch["name"], ci)], writes=[("kT", slot, c, r)], dma=True)

    def load_q(self, slot, c, QTd, row):
        S = self.S
        qt = self.qT[slot][c]
        S.op("sp", lambda e: e.dma_start(out=qt[:], in_=QTd.ap()[row:row + 64, :]),
             reads=[(QTd.name, (row // P) * P)], writes=[("qT", slot, c)], dma=True)

    def load_v(self, slot, ch, col, ncol):
        S = self.S
        vs = self.Vs[slot]
        ci, w0, w = self.ch_find(ch, col)
        src = ch["gat"][ci].ap().rearrange("(g p) f -> p g f", p=P)
        for (g0, g1) in ((0, 9), (9, NGB)):
            S.op("sp", lambda e, g0=g0, g1=g1: e.dma_start(out=vs[:, g0:g1, 0:ncol], in_=src[:, g0:g1, w0:w0 + ncol]),
                 reads=[("cc", ch["name"], ci)], writes=[("Vs", slot, g0)], dma=True)

    def set_v_ones(self, slot, col):
        self.S.op("dve", lambda e: e.memset(self.Vs[slot][:, :, col:col + 1], 1.0), writes=[("Vs1", slot)],
                  reads=[])

    def transpose_out(self, obi, qw, ftile, j):
        S = self.S
        self.ptrr = getattr(self, "ptrr", 0) + 1
        pt = self.ptrr % 4
        ident = self.consts[:, 2, :]
        tsl = slice(j * P, j * P + qw)
        tgs = sorted(set([(j * P) // TG, (j * P + qw - 1) // TG]))
        S.op("pe", lambda e: e.transpose(self.psT[:, pt * 128: pt * 128 + qw], self.ob[obi][0:qw, :], ident[0:qw, 0:qw]),
             reads=[("ob", obi), "consts"], writes=[("psT", pt)])
        self.copy_op(self.evac_engine(), self.xT[:, ftile, tsl], self.psT[:, pt * 128: pt * 128 + qw], [("psT", pt)],
                     [("xT", ftile, t) for t in tgs])

    def run_tasks(self, factories, nslots):
        pending = list(factories)
        active = {}
        free = list(range(nslots))
        while pending or active:
            while pending and free:
                sl = free.pop(0)
                active[sl] = pending.pop(0)(sl)
            for sl in sorted(active.keys()):
                try:
                    next(active[sl])
                except StopIteration:
                    del active[sl]
                    free.append(sl)

    @staticmethod
    def split_groups(glist, split0=False, maxn=4):
        groups = []
        cur = []
        for g in glist:
            if cur and (g != cur[-1] + 1 or len(cur) == maxn or (split0 and cur[-1] == 0)):
                groups.append(cur)
                cur = []
            cur.append(g)
        if cur:
            groups.append(cur)
        return groups

    def softmax_task(self, ts, j, glist, bias_ap_of, kts, qts, vs, vcols, split0, fin_steps):
        S = self.S
        qw = qw_of(j)
        qsl = slice(j * P, j * P + qw)
        ncomp = len(kts)
        groups = self.split_groups(glist, split0)
        psS = self.ps[ts]
        psOb = self.ps[3 + ts][:, 0:vcols + 1]
        if ncomp == 1:
            psO = [psOb]
            okeys = [("psO", ts)]
        else:
            psO = [self.o0buf[ts][:, 0:vcols + 1], psOb]
            okeys = [("o0buf", ts), ("psO", ts)]
        seq = [(c, grp) for c in range(ncomp) for grp in groups]
        first = [True] * ncomp
        lastgrp = groups[-1]
        pend = None
        pend2 = None
        for si in range(len(seq) + 2):
            item = None
            if si < len(seq):
                c, grp = seq[si]
                n = len(grp)
                kt, ktkeys = kts[c]
                qt, qtkeys = qts[c]
                for i, g in enumerate(grp):
                    S.op("pe", lambda e, i=i, g=g: e.matmul(psS[:, i * 128: i * 128 + qw], lhsT=kt[:, g * P:(g + 1) * P], rhs=qt[:, qsl],
                                                            start=True, stop=True),
                         reads=ktkeys + qtkeys, writes=[("psS", ts)])
                self.tmprr[ts] += 1
                tb = self.tmprr[ts] % 3
                pt = self.pT2[ts][tb]
                b_ap, bkeys = bias_ap_of(grp, qw)
                p0 = self.spT2[ts][tb % 2]
                S.op("act", lambda e: e.activation(out=p0[:, 0:n, 0:qw], in_=psS[:, 0:n * 128].rearrange("p (n q) -> p n q", q=128)[:, :, 0:qw],
                                                   func=AF.Exp, scale=0.125),
                     reads=[("psS", ts)], writes=[("spT", ts, tb % 2)])
                S.op("dve", lambda e: e.tensor_tensor(out=pt[:, 0:n, 0:qw], in0=p0[:, 0:n, 0:qw], in1=b_ap, op=ALU.mult),
                     reads=[("spT", ts, tb % 2)] + bkeys, writes=[("pT", ts, tb)])
                item = (c, grp, pt, tb)
            if pend2 is not None:
                pc, pgrp, ppt, ptb = pend2
                for i, g in enumerate(pgrp):
                    st = first[pc]
                    first[pc] = False
                    sp_ = (pgrp is lastgrp and i == len(pgrp) - 1)
                    S.op("pe", lambda e, i=i, g=g, st=st, sp_=sp_: e.matmul(
                        psOb[0:qw, :], lhsT=ppt[:, i, 0:qw], rhs=vs[0][:, g, 0:vcols + 1], start=st, stop=sp_),
                        reads=[("pT", ts, ptb)] + vs[1], writes=[("psO", ts)])
                if ncomp == 2 and pc == 0 and pgrp is lastgrp:
                    self.copy_op(self.evac_engine(), self.o0buf[ts][0:qw, 0:vcols + 1], psOb[0:qw, :], [("psO", ts)], [("o0buf", ts)])
            pend2 = pend
            pend = item
            yield
        for k_, step in enumerate(fin_steps(ts, psO, okeys)):
            step()
            if k_ % 2 == 1:
                yield
        yield

    def attn_c(self):
        S = self.S
        S.op("sp", lambda e: e.dma_start(out=self.tabA[:, 0:4, :], in_=self.negdist_c_d), writes=["tabA"], dma=True)
        S.op("sp", lambda e: e.dma_start(out=self.tabB[:, 0:4, :], in_=self.mask_c_d), writes=["tabB"], dma=True)
        for s_ in range(2):
            self.set_v_ones(s_, 128)
        facts = []
        for h in range(16):
            for j in range(NQB):
                facts.append(lambda ts, h=h, j=j: self.task_c(ts, h, j))
        self.run_tasks(facts, 3)

    def head_pre_c(self, h):
        S = self.S
        slot = h % 2
        for c in range(2):
            self.load_k(slot, c, self.KT1, h * 128 + c * 64)
            self.load_q(slot, c, self.QT1, h * 128 + c * 64)
        self.load_v(slot, self.V1, h * 128, 128)
        bH = self.biasH[slot]
        slope = C_SLOPES[h]
        for rel in range(NREL):
            if rel in C_NEAR:
                i = C_NEAR.index(rel)
                S.op("dve", lambda e: e.scalar_tensor_tensor(out=bH[:, rel, :], in0=self.tabA[:, i, :], scalar=slope,
                                                             in1=self.tabB[:, i, :], op0=ALU.mult, op1=ALU.add),
                     reads=["tabA", "tabB"], writes=[("biasH", 0, rel)])
            else:
                S.op("dve", lambda e: e.tensor_scalar(out=bH[:, rel, :], in0=self.kqneg[:], scalar1=self.negd0[:, rel:rel + 1],
                                                      scalar2=slope, op0=ALU.add, op1=ALU.mult),
                     reads=["kqneg", "negd0"], writes=[("biasH", 0, rel)])
        eH = self.expH[slot]
        S.op("act", lambda e: e.activation(out=eH[:], in_=bH[:], func=AF.Exp),
             reads=[("biasH", 0, r) for r in range(NREL)], writes=[("expH", slot, r) for r in range(NREL)])

    def task_c(self, ts, h, j):
        if j == 0:
            self.head_pre_c(h)
        slot = h % 2
        bH = self.biasH[slot]
        kts = [(self.kT[slot][c], [("kT", slot, c, 0), ("kT", slot, c, 1)]) for c in range(2)]
        qts = [(self.qT[slot][c], [("qT", slot, c)]) for c in range(2)]
        vs = (self.Vs[slot], [("Vs", slot, 0), ("Vs", slot, 9), ("Vs1", slot)])
        gmax = min(16, 9 + j)
        glist = [g for g in range(0, gmax + 1) if not c_dead(h, g - j + 8)]

        eH = self.expH[slot]

        def bias_ap_of(grp, qw):
            r0 = grp[0] - j + 8
            r1 = grp[-1] - j + 8
            return eH[:, r0:r1 + 1, 0:qw], [("expH", slot, r) for r in range(r0, r1 + 1)]

        def fin_steps(ts, psO, okeys):
            return self.fin_c_steps(ts, h, j, psO, okeys)

        return self.softmax_task(ts, j, glist, bias_ap_of, kts, qts, vs, 128, False, fin_steps)

    def fin_c_steps(self, ts, h, j, psO, okeys):
        S = self.S
        qw = qw_of(j)
        sm = self.small3[ts]
        of = self.ofin3[ts]
        ob = self.ob3[ts]
        jk = self.junk3[ts]
        kk = ("fin", ts)
        steps = []
        A = steps.append
        A(lambda: S.op("dve", lambda e: e.reciprocal(out=sm[0:qw, 0:1], in_=psO[0][0:qw, 128:129]), reads=[okeys[0]], writes=[(kk, 0)]))
        A(lambda: S.op("dve", lambda e: e.reciprocal(out=sm[0:qw, 1:2], in_=psO[1][0:qw, 128:129]), reads=[okeys[1]], writes=[(kk, 1)]))
        A(lambda: S.op("dve", lambda e: e.tensor_tensor(out=sm[0:qw, 2:3], in0=sm[0:qw, 1:2], in1=self.lamt[0:qw, 5:6], op=ALU.mult),
                       reads=[(kk, 1), "nlam"], writes=[(kk, 2)]))
        A(lambda: S.op("act", lambda e: e.activation(out=of[0:qw, :], in_=psO[0][0:qw, 0:128], func=AF.Copy, scale=sm[0:qw, 0:1]),
                       reads=[okeys[0], (kk, 0)], writes=[(kk, "of")]))
        A(lambda: S.op("dve", lambda e: e.scalar_tensor_tensor(out=of[0:qw, :], in0=psO[1][0:qw, 0:128], scalar=sm[0:qw, 2:3], in1=of[0:qw, :],
                                                               op0=ALU.mult, op1=ALU.add),
                       reads=[okeys[1], (kk, 2), (kk, "of")], writes=[(kk, "of")]))
        A(lambda: S.op("act", lambda e: e.activation(out=jk[0:qw, :], in_=of[0:qw, :], func=AF.Square),
                       reads=[(kk, "of")], writes=[(kk, "jk")]))
        A(lambda: S.op("dve", lambda e: e.tensor_reduce(out=sm[0:qw, 3:4], in_=jk[0:qw, :], axis=AX.X, op=ALU.add),
                       reads=[(kk, "jk")], writes=[(kk, 3)]))
        A(lambda: S.op("act", lambda e: e.activation(out=sm[0:qw, 4:5], in_=sm[0:qw, 3:4], func=AF.Ln, bias=self.epsc[0:qw, 0:1], scale=1.0 / 128),
                       reads=[(kk, 3), "epsc"], writes=[(kk, 4)]))
        A(lambda: S.op("act", lambda e: e.activation(out=sm[0:qw, 5:6], in_=sm[0:qw, 4:5], func=AF.Exp, scale=-0.5),
                       reads=[(kk, 4)], writes=[(kk, 5)]))
        A(lambda: S.op("dve", lambda e: e.scalar_tensor_tensor(out=ob[0:qw, :], in0=of[0:qw, :], scalar=sm[0:qw, 5:6], in1=self.subg[0:qw, :],
                                                               op0=ALU.mult, op1=ALU.mult),
                       reads=[(kk, "of"), (kk, 5), "subg"], writes=[("ob3", ts)]))
        A(lambda: self.transpose_evac(qw, h, j, self.transpose_pe(ob, [("ob3", ts)], qw)))
        return steps

    def transpose_pe(self, ob, obkeys, qw):
        S = self.S
        self.ptrr = getattr(self, "ptrr", 0) + 1
        pt = self.ptrr % 2
        ident = self.consts[:, 2, :]
        S.op("pe", lambda e: e.transpose(self.psTs[pt][:, 0:qw], ob[0:qw, :], ident[0:qw, 0:qw]),
             reads=obkeys + ["consts"], writes=[("psT", pt)])
        self.last_pt = pt
        return pt

    def transpose_evac(self, qw, ftile, j, pt=None):
        pt = self.last_pt if pt is None else pt
        tsl = slice(j * P, j * P + qw)
        tgs = sorted(set([(j * P) // TG, (j * P + qw - 1) // TG]))
        self.copy_op(self.evac_engine(), self.xT[:, ftile, tsl], self.psTs[pt][:, 0:qw], [("psT", pt)],
                     [("xT", ftile, t) for t in tgs])

    def attn_a(self):
        S = self.S
        S.op("sp", lambda e: e.dma_start(out=self.tabA[:], in_=self.negdist_a_d), writes=["tabA"], dma=True)
        S.op("sp", lambda e: e.dma_start(out=self.tabB[:], in_=self.mask_a_d), writes=["tabB"], dma=True)
        S.op("sp", lambda e: e.dma_start(out=self.tabA0[:], in_=self.mask_a0_d), writes=["tabA0"], dma=True)
        for s_ in range(2):
            self.set_v_ones(s_, 64)
        facts = []
        for h in range(16):
            for j in range(NQB):
                facts.append(lambda ts, h=h, j=j: self.task_a(ts, h, j))
        self.run_tasks(facts, 3)

    def head_pre_a(self, h):
        S = self.S
        kvh = h // 4
        kslot = kvh % 2
        slot = h % 2
        if h % 4 == 0:
            self.load_k(kslot, 0, self.KT0, kvh * 64)
            self.load_v(kslot, self.V0, kvh * 64, 64)
        self.load_q(slot, 0, self.QT0, h * 64)
        bH = self.biasH[slot]
        slope = A_SLOPES[h]
        for i, rel in enumerate(A_NEAR):
            S.op("dve", lambda e: e.scalar_tensor_tensor(out=bH[:, rel, :], in0=self.tabA[:, i, :], scalar=slope,
                                                         in1=self.tabB[:, i, :], op0=ALU.mult, op1=ALU.add),
                 reads=["tabA", "tabB"], writes=[("biasH", 0, rel)])
        for j in range(NQB):
            rel = 8 - j
            if rel in A_NEAR:
                i = A_NEAR.index(rel)
                S.op("dve", lambda e: e.scalar_tensor_tensor(out=bH[:, B0IDX[j], :], in0=self.tabA[:, i, :], scalar=slope,
                                                             in1=self.tabA0[:, j, :], op0=ALU.mult, op1=ALU.add),
                     reads=["tabA", "tabA0"], writes=[("biasH", 0, B0IDX[j])])
            else:
                S.op("dve", lambda e: e.tensor_scalar(out=bH[:, B0IDX[j], :], in0=self.kqneg[:], scalar1=self.negd0[:, rel:rel + 1],
                                                      scalar2=slope, op0=ALU.add, op1=ALU.mult),
                     reads=["kqneg", "negd0"], writes=[("biasH", 0, B0IDX[j])])
                S.op("dve", lambda e: e.tensor_tensor(out=bH[:, B0IDX[j], :], in0=bH[:, B0IDX[j], :], in1=self.tabA0[:, j, :], op=ALU.add),
                     reads=[("biasH", 0, B0IDX[j]), "tabA0"], writes=[("biasH", 0, B0IDX[j])])

    def task_a(self, ts, h, j):
        S = self.S
        if j == 0:
            self.head_pre_a(h)
            bH_, eH_ = self.biasH[h % 2], self.expH[h % 2]
            for (r0, r1) in ((0, 13), (15, 18)):
                S.op("act", lambda e: e.activation(out=eH_[:, r0:r1, :], in_=bH_[:, r0:r1, :], func=AF.Exp),
                     reads=[("biasH", 0, r) for r in range(r0, r1) if r in A_NEAR or r in B0IDX],
                     writes=[("expH", h % 2, r) for r in range(r0, r1)])
        kslot = (h // 4) % 2
        slot = h % 2
        bH = self.biasH[slot]
        kts = [(self.kT[kslot][0], [("kT", kslot, 0, 0), ("kT", kslot, 0, 1)])]
        qts = [(self.qT[slot][0], [("qT", slot, 0)])]
        vs = (self.Vs[kslot], [("Vs", kslot, 0), ("Vs", kslot, 9), ("Vs1", kslot)])
        near = sorted(set([g for g in list(range(j - 2, j + 2)) + list(range(j + 7, j + 10)) if 1 <= g <= 16]))
        glist = [0] + near

        eH = self.expH[slot]

        def bias_ap_of(grp, qw):
            if grp[0] == 0:
                assert len(grp) == 1
                return eH[:, B0IDX[j]:B0IDX[j] + 1, 0:qw], [("expH", slot, B0IDX[j])]
            r0 = grp[0] - j + 8
            r1 = grp[-1] - j + 8
            return eH[:, r0:r1 + 1, 0:qw], [("expH", slot, r) for r in range(r0, r1 + 1)]

        def fin_steps(ts, psO, okeys):
            qw = qw_of(j)
            sm = self.small3[ts]
            ob = self.obA[(h // 2) % 2][j]
            kk = ("finA", ts)
            obk = ("obA", (h // 2) % 2, j)
            steps = []
            A = steps.append
            A(lambda: S.op("dve", lambda e: e.tensor_tensor(out=sm[0:qw, 0:1], in0=psO[0][0:qw, 64:65], in1=self.esink[0:qw, h:h + 1], op=ALU.add),
                           reads=[okeys[0], "esink"], writes=[(kk, 0)]))
            A(lambda: S.op("dve", lambda e: e.reciprocal(out=sm[0:qw, 1:2], in_=sm[0:qw, 0:1]), reads=[(kk, 0)], writes=[(kk, 1)]))
            A(lambda: S.op("act", lambda e: e.activation(out=ob[0:qw, (h % 2) * 64:(h % 2) * 64 + 64], in_=psO[0][0:qw, 0:64], func=AF.Copy,
                                                         scale=sm[0:qw, 1:2]),
                           reads=[okeys[0], (kk, 1)], writes=[obk + (h % 2,)]))
            if h % 2 == 1:
                A(lambda: self.transpose_evac(qw, h // 2, j, self.transpose_pe(ob, [obk + (0,), obk + (1,)], qw)))
            return steps

        return self.softmax_task(ts, j, glist, bias_ap_of, kts, qts, vs, 64, True, fin_steps)

    def attn_b(self):
        S = self.S
        S.op("pool", lambda e: e.dma_start(out=self.maskb[:], in_=self.mask_b_d), writes=["maskb"], dma=True)
        facts = []
        for h in range(16):
            for j in range(NQB):
                facts.append(lambda ts, h=h, j=j: self.task_b(ts, h, j))
        self.run_tasks(facts, 3)

    def task_b(self, ts, h, j):
        S = self.S
        ones = self.consts[:, 0, :]
        triu = self.consts[:, 1, :]
        slot = h % 2
        if j == 0:
            self.load_k(slot, 0, self.KT0, 256 + h * 64)
            self.load_v(slot, self.V0, 256 + h * 64, 64)
            self.load_q(slot, 0, self.QT0, 1024 + h * 64)
        kt = self.kT[slot][0]
        ktkeys = [("kT", slot, 0, 0), ("kT", slot, 0, 1)]
        qt = self.qT[slot][0]
        qtkeys = [("qT", slot, 0)]
        vsb = self.Vs[slot]
        vkeys = [("Vs", slot, 0), ("Vs", slot, 9)]
        qw = qw_of(j)
        qsl = slice(j * P, j * P + qw)
        gmax = min(16, 9 + j)
        gl = list(range(gmax, -1, -1))
        nflag = sum(1 for g in gl if g - j + 8 >= 9)
        nfull = (nflag // 4) * 4
        groups = [("flag", gl[a:a + 4]) for a in range(0, nfull, 4)]
        rest = gl[nfull:]
        for a in range(0, len(rest), 4):
            grp_ = rest[a:a + 4]
            groups.append(("mixed" if any(g - j + 8 >= 9 for g in grp_) else "free", grp_))
        psS = self.ps[ts]
        psD = psS
        psO = self.ps[3 + ts][:, 0:64]
        okey = ("psOb", ts)
        R32 = self.R32s[ts]
        Rtmp = self.Rtmps[ts]
        S.op("dve", lambda e: e.memset(R32[:], 0.0), writes=[("R32", ts)])
        self.rbrr[ts] += 1
        rb = self.rbrr[ts] % 2
        S.op("dve", lambda e: e.memset(self.Rbfs[ts][rb][:], 0.0), writes=[("Rbf", ts, rb)])
        pend_pv = None
        first_pv = True
        ng = len(groups)

        def emit_pv(pend, first, last):
            wt, wkey, asc = pend
            n = len(asc)
            for i, g in enumerate(asc):
                st = first
                first = False
                sp_ = last and i == n - 1
                S.op("pe", lambda e: e.matmul(psO[0:qw, :], lhsT=wt[:, i, 0:qw], rhs=vsb[:, g, 0:64], start=st, stop=sp_),
                     reads=[wkey] + vkeys, writes=[okey])
            return first

        for gi, (cls, grp) in enumerate(groups):
            n = len(grp)
            asc = grp[::-1]
            if pend_pv is not None:
                first_pv = emit_pv(pend_pv, first_pv, False)
                pend_pv = None
            for i, g in enumerate(asc):
                S.op("pe", lambda e: e.matmul(psS[:, i * 128: i * 128 + qw], lhsT=kt[:, g * P:(g + 1) * P], rhs=qt[:, qsl], start=(i == 0), stop=True,
                                              skip_group_check=True),
                     reads=ktkeys + qtkeys, writes=[("psS", ts)])
            et = self.tmpS2[ts][0]
            self.sprr[ts] += 1
            sb_ = self.sprr[ts] % 2
            spt = self.spT2[ts][sb_]
            spkey = ("spT", ts, sb_)
            psS3 = psS[:, 0:n * 128].rearrange("p (n q) -> p n q", q=128)[:, :, 0:qw]
            psD3 = psD[:, 0:n * 128].rearrange("p (n q) -> p n q", q=128)[:, :, 0:qw]
            if cls == "flag":
                S.op("act", lambda e: e.activation(out=et[:, 0:n, 0:qw], in_=psS3, func=AF.Exp, scale=-1.0, bias=self.flagb[:, 0:1]),
                     reads=[("psS", ts), "flagb"], writes=[("tmpS", ts, 0)])
            else:
                S.op("act", lambda e: e.activation(out=et[:, 0:n, 0:qw], in_=psS3, func=AF.Exp, scale=-1.0),
                     reads=[("psS", ts)], writes=[("tmpS", ts, 0)])
            S.op("act", lambda e: e.activation(out=spt[:, 0:n, 0:qw], in_=et[:, 0:n, 0:qw], func=AF.Ln, bias=self.onec[:, 0:1], scale=1.0),
                 reads=[("tmpS", ts, 0), "onec"], writes=[spkey])
            has_near = cls == "mixed" or any((g - j + 8) in B_NEAR for g in asc)
            r0m = asc[0] - j + 8
            if has_near:
                S.op("dve", lambda e: e.tensor_tensor(out=spt[:, 0:n, 0:qw], in0=spt[:, 0:n, 0:qw], in1=self.maskb[:, r0m:r0m + n, 0:qw], op=ALU.mult),
                     reads=[spkey, "maskb"], writes=[spkey])
            yield
            for i, g in enumerate(asc):
                dsl = slice(i * 128, i * 128 + qw)
                S.op("pe", lambda e: e.matmul(psD[:, dsl], lhsT=triu, rhs=spt[:, i, 0:qw], start=False, stop=False, skip_group_check=True),
                     reads=[spkey, "consts", ("tmpS", ts, 0)], writes=[("psS", ts)])
                for i2 in range(i + 1, n):
                    S.op("pe", lambda e: e.matmul(psD[:, dsl], lhsT=ones, rhs=spt[:, i2, 0:qw], start=False, stop=False, skip_group_check=True),
                         reads=[spkey, "consts"], writes=[("psS", ts)])
                S.op("pe", lambda e: e.matmul(psD[:, dsl], lhsT=ones, rhs=self.Rbfs[ts][rb][:, 0:qw], start=False, stop=True, skip_group_check=True),
                     reads=[("Rbf", ts, rb), "consts"], writes=[("psS", ts)])
            if gi < ng - 1:
                if n > 1:
                    S.op("dve", lambda e: e.tensor_reduce(out=Rtmp[:, 0:qw], in_=spt[:, 0:n, 0:qw].rearrange("p n q -> p q n"), axis=AX.X, op=ALU.add),
                         reads=[spkey], writes=[("Rtmp", ts)])
                    S.op("dve", lambda e: e.tensor_tensor(out=R32[:, 0:qw], in0=R32[:, 0:qw], in1=Rtmp[:, 0:qw], op=ALU.add),
                         reads=[("Rtmp", ts), ("R32", ts)], writes=[("R32", ts)])
                else:
                    S.op("dve", lambda e: e.tensor_tensor(out=R32[:, 0:qw], in0=R32[:, 0:qw], in1=spt[:, 0, 0:qw], op=ALU.add),
                         reads=[spkey, ("R32", ts)], writes=[("R32", ts)])
                self.rbrr[ts] += 1
                rb = self.rbrr[ts] % 2
                S.op("dve", lambda e: e.tensor_copy(out=self.Rbfs[ts][rb][:, 0:qw], in_=R32[:, 0:qw]), reads=[("R32", ts)], writes=[("Rbf", ts, rb)])
            self.tmprr[ts] += 1
            wb = self.tmprr[ts] % 3
            wt = self.pT2[ts][wb]
            wkey = ("pT", ts, wb)
            if cls == "flag":
                S.op("act", lambda e: e.activation(out=wt[:, 0:n, 0:qw], in_=psD3, func=AF.Exp, scale=-1.0, bias=self.flagb[:, 0:1]),
                     reads=[("psS", ts), "flagb"], writes=[wkey])
            else:
                S.op("act", lambda e: e.activation(out=wt[:, 0:n, 0:qw], in_=psD3, func=AF.Exp, scale=-1.0),
                     reads=[("psS", ts)], writes=[wkey])
            if has_near:
                S.op("dve", lambda e: e.tensor_tensor(out=wt[:, 0:n, 0:qw], in0=wt[:, 0:n, 0:qw], in1=self.maskb[:, r0m:r0m + n, 0:qw], op=ALU.mult),
                     reads=[wkey, "maskb"], writes=[wkey])
            pend_pv = (wt, wkey, asc)
            yield
        emit_pv(pend_pv, first_pv, True)
        ob = self.obA[(h // 2) % 2][j]
        obk = ("obA", (h // 2) % 2, j)
        self.copy_op(self.evac_engine(), ob[0:qw, (h % 2) * 64:(h % 2) * 64 + 64], psO[0:qw, :], [okey], [obk + (h % 2,)])
        yield
        if h % 2 == 1:
            pt_ = self.transpose_pe(ob, [obk + (0,), obk + (1,)], qw)
            self.transpose_evac(qw, 8 + h // 2, j, pt_)
            yield

    def dump_h(self):
        S = self.S
        outv = self.outT.rearrange("(c p) t -> p c t", p=P)
        for tg in range(NTG):
            sl = slice(tg * TG, (tg + 1) * TG)
            S.op("sp", lambda e, sl=sl: e.dma_start(out=outv[:, :, sl], in_=self.hT[:, :, sl]),
                 reads=[("hT", c, tg) for c in range(KC)], writes=[("out", tg)], dma=True)
        S.op("sp", None, reads=[("out", tg) for tg in range(NTG)])

    def dump_x(self):
        S = self.S
        outv = self.dbgx.rearrange("(c p) t -> p c t", p=P)
        for tg in range(NTG):
            sl = slice(tg * TG, (tg + 1) * TG)
            S.op("sp", lambda e, sl=sl: e.dma_start(out=outv[:, :, sl], in_=self.xT[:, :, sl]),
                 reads=[("xT", c, tg) for c in range(KC)], writes=[("outx", tg)], dma=True)
        S.op("sp", None, reads=[("outx", tg) for tg in range(NTG)])

    def stop_here(self, name):
        if self.stop != name:
            return False
        self.S.barrier()
        if name.startswith("x_"):
            self.dump_x()
        self.dump_h()
        self.emit_all()
        return True

    def build(self, stop=None):
        S = self.S
        nc = self.nc
        self.stop = stop
        if stop is not None:
            self.dbgx = nc.dram_tensor("dbgx", [D, T], BF16, kind="ExternalOutput").ap()
        self.obA = [[self.sb("obA%d_%d" % (i, j), [P, 128], BF16) for j in range(NQB)] for i in range(2)]
        assert self.sb_off <= 229344, self.sb_off
        print('sbuf end', self.sb_off)
        self.load_consts()
        self.rmsnorm(0)
        if self.stop_here("x_norm0"):
            return nc
        self.proj_qk(self.w_in_ab, 1024, 2, self.KT0, 0)
        if stop == "x_ka":
            S.barrier()
            S.op("sp", lambda e: e.dma_start(out=self.dbgx[0:256, :], in_=self.KT0["loc"][0].ap()[0:256, :]), reads=[], writes=[("outx", 0)], dma=True)
            S.op("sp", None, reads=[("outx", 0)])
            self.dump_h()
            self.emit_all()
            return nc
        self.proj_qk(self.w_in_ab, 2560, 8, self.KT0, 256)
        self.proj_v(self.w_in_ab, 1280, 256, self.V0, 0)
        if stop == "x_va":
            S.barrier()
            S.op("sp", lambda e: e.dma_start(out=self.dbgx[0:256, :], in_=self.V0.ap()[0:T, 0:256].rearrange("t f -> t f")), reads=[], writes=[("outx", 0)], dma=True) if False else None
            for f0 in range(0, 256, 32):
                S.op("sp", lambda e, f0=f0: e.dma_start(out=self.dbgx[f0:f0 + 32, :], in_=self.V0["loc"][0].ap()[:, f0:f0 + 32].rearrange("t f -> f t"), allow_slow_non_contiguous=True), reads=[], writes=[("outx", 0)], dma=True)
            S.op("sp", None, reads=[("outx", 0)])
            self.dump_h()
            self.emit_all()
            return nc
        self.proj_v(self.w_in_ab, 3584, 1024, self.V0, 256)
        self.allgather(self.KT0)
        self.allgather(self.V0)
        self.proj_qk(self.w_in_ab, 0, 8, self.QT0.ap(), 0)
        self.proj_qk(self.w_in_ab, 1536, 8, self.QT0.ap(), 1024, scale=-0.125)
        S.barrier()
        if stop == "x_proj0":
            S.op("sp", lambda e: e.dma_start(out=self.dbgx, in_=self.QT0.ap()), reads=[], writes=[("outx", 0)], dma=True)
            S.op("sp", None, reads=[("outx", 0)])
            self.dump_h()
            self.emit_all()
            return nc
        if stop == "x_kv0":
            S.op("sp", lambda e: e.dma_start(out=self.dbgx[0:640, :], in_=self.KT0["gat"][0].ap()[640:1280, :]), reads=[], writes=[("outx", 0)], dma=True)
            S.op("sp", None, reads=[("outx", 0)])
            self.dump_h()
            self.emit_all()
            return nc
        if stop == "x_attn_a":
            self.attn_a()
            self.stop_here("x_attn_a")
            return nc
        if stop == "x_attn_b":
            self.attn_b()
            self.stop_here("x_attn_b")
            return nc
        self.attn_a()
        S.barrier()
        self.attn_b()
        S.barrier()
        if self.stop_here("x_attn0"):
            return nc
        self.out_proj(self.w_out_ab)
        if self.stop_here("h_attn0"):
            return nc
        self.rmsnorm(3)
        self.mlp(0)
        if self.stop_here("h_l0"):
            return nc
        self.rmsnorm(1)
        self.proj_qk(self.w_in_c, 2048, 16, self.KT1, 0)
        self.proj_v(self.w_in_c, 4096, 2048, self.V1, 0)
        self.allgather(self.KT1)
        self.allgather(self.V1)
        self.proj_qk(self.w_in_c, 0, 16, self.QT1.ap(), 0)
        S.barrier()
        self.attn_c()
        S.barrier()
        if self.stop_here("x_attn1"):
            return nc
        self.out_proj(self.w_out_c)
        self.rmsnorm(4)
        self.mlp(1)
        if self.stop_here("h_l1"):
            return nc
        self.rmsnorm(2, final=True)
        self.dump_h()
        self.emit_all()
        return nc

    def emit_all(self):
        S = self.S
        nc = self.nc
        with ExitStack() as stack:
            S.finalize(nc, stack)
            block = stack.enter_context(nc.Block())

            @block.tensor
            def _(e):
                S.emit("pe", e)

            @block.scalar
            def _(e):
                S.emit("act", e)

            @block.vector
            def _(e):
                S.emit("dve", e)

            @block.gpsimd
            def _(e):
                S.emit("pool", e)

            @block.sync
            def _(e):
                S.emit("sp", e)


def chunk_of(p):
    return 1 + np.floor_divide(p - 16, 64)


def make_tables(rank):
    base = rank * T
    k = np.arange(128)[:, None]
    q = np.arange(128)[None, :]
    t = {}
    t["kqneg"] = (-(q - k)).astype(np.float32) * np.ones((128, 128), np.float32)
    negd0 = np.zeros((128, NREL), np.float32)
    for rel in range(NREL):
        dq0 = base - 128 * (rel - 8)
        negd0[:, rel] = -float(dq0) if dq0 >= 128 else -1.0e6
    t["negd0"] = negd0

    def posmats(rel):
        dq0 = base - 128 * (rel - 8)
        diff = dq0 + q - k
        return diff

    def absq(rel):
        jj = 8
        g = rel - 8 + jj
        qpos = base + 128 * jj + q + 0 * k
        kpos = 128 * g + k + 0 * q
        return qpos, kpos

    na = np.zeros((128, len(A_NEAR), 128), np.float32)
    ma = np.zeros((128, len(A_NEAR), 128), np.float32)
    for i, rel in enumerate(A_NEAR):
        qpos, kpos = absq(rel)
        qpos = qpos + 128 * 64
        kpos = kpos + 128 * 64
        na[:, i, :] = -np.abs(qpos - kpos)
        qc, kc = chunk_of(qpos), chunk_of(kpos)
        ok = (kc <= qc) & (kc >= qc - 2)
        ma[:, i, :] = np.where(ok, 0.0, NEGBIG)
    t["negdist_a"] = na
    t["mask_a"] = ma
    ma0 = np.zeros((128, NQB, 128), np.float32)
    for j in range(NQB):
        qpos = base + 128 * j + q + 0 * k
        kpos = k + 0 * q
        qc, kc = chunk_of(qpos), chunk_of(kpos)
        ok = (kpos < 16) | ((kpos >= 16) & (kc <= qc) & (kc >= qc - 2))
        ma0[:, j, :] = np.where(ok, 0.0, NEGBIG)
    t["mask_a0"] = ma0
    ncm = np.zeros((128, len(C_NEAR), 128), np.float32)
    mc = np.zeros((128, len(C_NEAR), 128), np.float32)
    for i, rel in enumerate(C_NEAR):
        qpos, kpos = absq(rel)
        qpos = qpos + 128 * 64
        kpos = kpos + 128 * 64
        ncm[:, i, :] = -np.abs(qpos - kpos)
        ok = chunk_of(kpos) <= chunk_of(qpos)
        mc[:, i, :] = np.where(ok, 0.0, NEGBIG)
    t["negdist_c"] = ncm
    t["mask_c"] = mc
    mb = np.zeros((128, NREL, 128), np.float32)
    for rel in range(NREL):
        diff = posmats(rel)
        mb[:, rel, :] = (diff > 0).astype(np.float32)
    t["mask_b"] = mb
    t["flagb"] = np.full((128, 1), NEGBIG if rank == 0 else 0.0, np.float32)
    cst = np.zeros((128, 3, 128), np.float32)
    cst[:, 0, :] = 1.0
    cst[:, 1, :] = (k >= q).astype(np.float32)
    cst[:, 2, :] = np.eye(128, dtype=np.float32)
    t["consts"] = cst
    return t


_NC_CACHE = {}


def get_nc(stop=None):
    key = "main" if stop is None else str(stop)
    if key not in _NC_CACHE:
        b = Builder()
        _NC_CACHE[key] = b.build(stop)
    return _NC_CACHE[key]


def make_in_maps(x, meta_tokens, ab_norm, w_in_ab, attn_sinks, w_out_ab, c_norm, w_in_c, diff_lambda, diff_subln,
                 w_out_c, mlp_norm, w_mlp_in, w_mlp_out, final_norm):
    f = lambda a: np.ascontiguousarray(np.asarray(a, dtype=np.float32))
    x = f(x)
    B = x.shape[0]
    meta = f(meta_tokens)
    gains = np.stack([f(ab_norm)[0], f(c_norm)[0], f(final_norm), f(mlp_norm)[0], f(mlp_norm)[1]], 0)
    gains_l = np.ascontiguousarray(gains.reshape(5, KC, P).transpose(2, 0, 1).reshape(P, 5 * KC))
    sinks = np.ascontiguousarray(np.broadcast_to(f(attn_sinks)[0][None, :], (P, 16)))
    lamv = np.ascontiguousarray(np.broadcast_to(f(diff_lambda)[0].reshape(1, 256), (P, 256)))
    subg = np.ascontiguousarray(np.broadcast_to(f(diff_subln)[0][None, :], (P, 128)))
    shared = {
        "w_in_ab": f(w_in_ab)[0], "w_out_ab": f(w_out_ab)[0], "w_in_c": f(w_in_c)[0], "w_out_c": f(w_out_c)[0],
        "w_mlp_in0": f(w_mlp_in)[0], "w_mlp_in1": f(w_mlp_in)[1], "w_mlp_out0": f(w_mlp_out)[0], "w_mlp_out1": f(w_mlp_out)[1],
        "gains": gains_l, "sinks": sinks, "lamv": lamv, "subg": subg,
    }
    tabs = [make_tables(0), make_tables(1)]
    in_maps = []
    for core in range(8):
        b, r = core // 2, core % 2
        seq = np.zeros((LP, D), np.float32)
        seq[0:16] = meta
        seq[16:16 + 2048] = x[b]
        h0T = np.ascontiguousarray(seq[r * T:(r + 1) * T].T)
        m = dict(shared)
        m["h0T"] = h0T
        m.update(tabs[r])
        in_maps.append(m)
    return in_maps


def assemble(results):
    out = np.zeros((4, 2048, D), np.float32)
    for core in range(8):
        b, r = core // 2, core % 2
        oT = np.asarray(results[core]["outT"])
        rows = oT.T
        pos0 = r * T
        lo = max(pos0, 16)
        hi = min(pos0 + T, 16 + 2048)
        out[b, lo - 16:hi - 16] = rows[lo - pos0:hi - pos0]
    return out


def kernel(**inputs):
    nc = get_nc()
    in_maps = make_in_maps(**inputs)
    res = run_bass_kernel_spmd(nc, in_maps, core_ids=list(range(8)))
    return assemble(res.results)
```
